# Optimizing a Trainium2 kernel written in Bass

```python
import math
import jax
import jax.numpy as jnp
from jax import lax
import numpy as np

D_MODEL = 1024
BATCH = 4
SEQ = 4096
DEPTH = 4

N_MIXERS = 3
N_ATTN = (DEPTH + 2) // 3
N_SSM = (DEPTH + 1) // 3
N_GMLP = DEPTH // 3

DA_HEADS = 8
DA_HEAD_DIM = D_MODEL // (2 * DA_HEADS)
DA_V_DIM = 2 * DA_HEAD_DIM
Q_BLOCK = 128
ROPE_THETA = 10000.0

S5_GROUP = 16
S5_GROUPS = D_MODEL // S5_GROUP
S5_STATE = 64
DT_MIN = 1e-3
DT_MAX = 1e-1

GM_FFN = 6 * D_MODEL
GM_HALF = GM_FFN // 2
GM_HEADS = 8
GM_HEAD_DIM = GM_HALF // GM_HEADS
GM_CHUNK = 128

MLP_HIDDEN = 4 * D_MODEL
EPS = 1e-6

kernel_name = "hybrid_diffattn_s5_gmlp_trunk"


def rms_norm(x, gain):
    x32 = x.astype(jnp.float32)
    y = x32 * lax.rsqrt(jnp.mean(jnp.square(x32), axis=-1, keepdims=True) + EPS)
    return (y * gain.astype(jnp.float32)).astype(x.dtype)


def rope_tables(positions):
    inv_freq = ROPE_THETA ** (-jnp.arange(0, DA_HEAD_DIM, 2, dtype=jnp.float32) / DA_HEAD_DIM)
    ang = positions.astype(jnp.float32)[..., None] * inv_freq
    return jnp.cos(ang), jnp.sin(ang)


def apply_rope(x, cos, sin):
    x32 = x.astype(jnp.float32)
    c = cos[:, :, None, None, :]
    s = sin[:, :, None, None, :]
    x1, x2 = jnp.split(x32, 2, axis=-1)
    return jnp.concatenate([x1 * c - x2 * s, x2 * c + x1 * s], axis=-1).astype(x.dtype)


def diff_attention(h, cos, sin, w_qkv, q_gain, k_gain, lam, sub_gain, w_o, lambda_init):
    B, L, _ = h.shape
    qkv = h @ w_qkv
    q, k, v = jnp.split(qkv, 3, axis=-1)
    q = q.reshape(B, L, DA_HEADS, 2, DA_HEAD_DIM)
    k = k.reshape(B, L, DA_HEADS, 2, DA_HEAD_DIM)
    v = v.reshape(B, L, DA_HEADS, DA_V_DIM)
    q = apply_rope(rms_norm(q, q_gain), cos, sin)
    k = apply_rope(rms_norm(k, k_gain), cos, sin)
    lam32 = lam.astype(jnp.float32)
    lam_full = (jnp.exp(jnp.sum(lam32[0] * lam32[1])) - jnp.exp(jnp.sum(lam32[2] * lam32[3]))
                + lambda_init)
    n_blocks = L // Q_BLOCK
    q_blocks = jnp.moveaxis(q.reshape(B, n_blocks, Q_BLOCK, DA_HEADS, 2, DA_HEAD_DIM), 1, 0)
    key_pos = jnp.arange(L)
    scale = DA_HEAD_DIM ** -0.5

    def block(args):
        qb, blk = args
        q_pos = blk * Q_BLOCK + jnp.arange(Q_BLOCK)
        s = jnp.einsum('bqhcd,bkhcd->bhcqk', qb, k,
                       preferred_element_type=jnp.float32) * scale
        s = jnp.where(key_pos[None, :] <= q_pos[:, None], s, -jnp.inf)
        p = jax.nn.softmax(s, axis=-1)
        a = p[:, :, 0] - lam_full * p[:, :, 1]
        return jnp.einsum('bhqk,bkhe->bqhe', a.astype(v.dtype), v)

    o = lax.map(block, (q_blocks, jnp.arange(n_blocks)))
    o = jnp.moveaxis(o, 0, 1).reshape(B, L, DA_HEADS, DA_V_DIM)
    o = rms_norm(o, sub_gain) * (1.0 - lambda_init)
    return o.reshape(B, L, D_MODEL) @ w_o


def _linear_recurrence(e_i, e_j):
    a_i, b_i = e_i
    a_j, b_j = e_j
    return a_j * a_i, a_j * b_i + b_j


def s5_mixer(h, w_in, a_re, a_im, b_re, b_im, c_re, c_im, d_skip, log_dt, w_glu, b_glu, w_out):
    B, L, _ = h.shape
    f32 = jnp.float32
    u = h @ w_in
    lam = lax.complex(jnp.minimum(a_re.astype(f32), -1e-4), a_im.astype(f32))
    dt = jnp.exp(log_dt.astype(f32))[:, None]
    lam_bar = jnp.exp(lam * dt)
    zoh = (lam_bar - 1.0) / lam
    b_cplx = lax.complex(b_re.astype(f32), b_im.astype(f32))
    b_bar = zoh[..., None] * b_cplx
    ug = u.astype(f32).reshape(B, L, S5_GROUPS, S5_GROUP)
    bu = lax.complex(jnp.einsum('blgh,gph->blgp', ug, jnp.real(b_bar)),
                     jnp.einsum('blgh,gph->blgp', ug, jnp.imag(b_bar)))
    a_elems = jnp.broadcast_to(lam_bar, (1, L, S5_GROUPS, S5_STATE))
    _, states = lax.associative_scan(_linear_recurrence, (a_elems, bu), axis=1)
    y = (jnp.einsum('ghp,blgp->blgh', c_re.astype(f32), jnp.real(states))
         - jnp.einsum('ghp,blgp->blgh', c_im.astype(f32), jnp.imag(states)))
    y = y.reshape(B, L, D_MODEL) + d_skip.astype(f32) * u.astype(f32)
    g = jax.nn.gelu(y.astype(h.dtype))
    g = g * jax.nn.sigmoid(g @ w_glu + b_glu)
    return g @ w_out


def gmlp_mixer(h, w_in, v_gain, w_s, b_s, w_out):
    B, L, _ = h.shape
    z = jax.nn.gelu(h @ w_in)
    u, v = jnp.split(z, 2, axis=-1)
    v = rms_norm(v, v_gain)
    n_chunks = L // GM_CHUNK
    v = v.reshape(B, n_chunks, GM_CHUNK, GM_HEADS, GM_HEAD_DIM)
    causal = jnp.tril(jnp.ones((GM_CHUNK, GM_CHUNK), dtype=bool))
    w_masked = jnp.where(causal[None], w_s, 0.0).astype(v.dtype)
    gate = (jnp.einsum('hts,bcshd->bcthd', w_masked, v)
            + b_s.T.astype(v.dtype)[None, None, :, :, None])
    gate = gate.reshape(B, L, GM_HALF)
    return (u * gate) @ w_out


def squared_relu_mlp(h, w1, w2):
    return jnp.square(jax.nn.relu(h @ w1)) @ w2


def setup_inputs(seed: int = 0) -> dict:
    key = jax.random.key(seed)
    ks = jax.random.split(key, 30)
    f32 = jnp.float32

    def dense(k, shape, fan_in):
        return jax.random.normal(k, shape, f32) * fan_in ** -0.5

    def gain(k, shape):
        return 1.0 + 0.02 * jax.random.normal(k, shape, f32)

    x = jax.random.normal(ks[0], (BATCH, SEQ, D_MODEL), f32)
    offset = jax.random.randint(ks[1], (BATCH, 1), 0, 2048, dtype=jnp.int32)
    positions = offset + jnp.arange(SEQ, dtype=jnp.int32)[None, :]
    n = jnp.arange(S5_STATE, dtype=f32)
    a_shape = (N_SSM, S5_GROUPS, S5_STATE)
    return {
        'x': x,
        'positions': positions,
        'norm_mix': gain(ks[2], (DEPTH, D_MODEL)),
        'norm_mlp': gain(ks[3], (DEPTH, D_MODEL)),
        'mlp_w1': dense(ks[4], (DEPTH, D_MODEL, MLP_HIDDEN), D_MODEL),
        'mlp_w2': dense(ks[5], (DEPTH, MLP_HIDDEN, D_MODEL), MLP_HIDDEN),
        'attn_w_qkv': dense(ks[6], (N_ATTN, D_MODEL, 3 * D_MODEL), D_MODEL),
        'attn_q_norm': gain(ks[7], (N_ATTN, DA_HEAD_DIM)),
        'attn_k_norm': gain(ks[8], (N_ATTN, DA_HEAD_DIM)),
        'attn_lambda': 0.1 * jax.random.normal(ks[9], (N_ATTN, 4, DA_HEAD_DIM), f32),
        'attn_sub_norm': gain(ks[10], (N_ATTN, DA_V_DIM)),
        'attn_w_o': dense(ks[11], (N_ATTN, D_MODEL, D_MODEL), D_MODEL),
        'ssm_w_in': dense(ks[12], (N_SSM, D_MODEL, D_MODEL), D_MODEL),
        'ssm_a_re': -0.5 + 0.01 * jax.random.normal(ks[13], a_shape, f32),
        'ssm_a_im': math.pi * n + 0.01 * jax.random.normal(ks[14], a_shape, f32),
        'ssm_b_re': dense(ks[15], (N_SSM, S5_GROUPS, S5_STATE, S5_GROUP), 2 * S5_GROUP),
        'ssm_b_im': dense(ks[16], (N_SSM, S5_GROUPS, S5_STATE, S5_GROUP), 2 * S5_GROUP),
        'ssm_c_re': 0.5 * jax.random.normal(ks[17], (N_SSM, S5_GROUPS, S5_GROUP, S5_STATE), f32),
        'ssm_c_im': 0.5 * jax.random.normal(ks[18], (N_SSM, S5_GROUPS, S5_GROUP, S5_STATE), f32),
        'ssm_d': jax.random.normal(ks[19], (N_SSM, D_MODEL), f32),
        'ssm_log_dt': jax.random.uniform(ks[20], (N_SSM, S5_GROUPS), f32,
                                         minval=math.log(DT_MIN), maxval=math.log(DT_MAX)),
        'ssm_w_glu': dense(ks[21], (N_SSM, D_MODEL, D_MODEL), D_MODEL),
        'ssm_b_glu': 0.02 * jax.random.normal(ks[22], (N_SSM, D_MODEL), f32),
        'ssm_w_out': dense(ks[23], (N_SSM, D_MODEL, D_MODEL), D_MODEL),
        'gm_w_in': dense(ks[24], (N_GMLP, D_MODEL, GM_FFN), D_MODEL),
        'gm_v_norm': gain(ks[25], (N_GMLP, GM_HALF)),
        'gm_w_s': dense(ks[26], (N_GMLP, GM_HEADS, GM_CHUNK, GM_CHUNK), GM_CHUNK),
        'gm_b_s': 1.0 + 0.1 * jax.random.normal(ks[27], (N_GMLP, GM_HEADS, GM_CHUNK), f32),
        'gm_w_out': dense(ks[28], (N_GMLP, GM_HALF, D_MODEL), GM_HALF),
    }


def reference(x, positions, norm_mix, norm_mlp, mlp_w1, mlp_w2,
              attn_w_qkv, attn_q_norm, attn_k_norm, attn_lambda, attn_sub_norm, attn_w_o,
              ssm_w_in, ssm_a_re, ssm_a_im, ssm_b_re, ssm_b_im, ssm_c_re, ssm_c_im,
              ssm_d, ssm_log_dt, ssm_w_glu, ssm_b_glu, ssm_w_out,
              gm_w_in, gm_v_norm, gm_w_s, gm_b_s, gm_w_out):
    cos, sin = rope_tables(positions)
    for i in range(DEPTH):
        kind = i % N_MIXERS
        j = i // N_MIXERS
        h = rms_norm(x, norm_mix[i])
        if kind == 0:
            lambda_init = 0.8 - 0.6 * math.exp(-0.3 * i)
            m = diff_attention(h, cos, sin, attn_w_qkv[j], attn_q_norm[j], attn_k_norm[j],
                               attn_lambda[j], attn_sub_norm[j], attn_w_o[j], lambda_init)
        elif kind == 1:
            m = s5_mixer(h, ssm_w_in[j], ssm_a_re[j], ssm_a_im[j], ssm_b_re[j], ssm_b_im[j],
                         ssm_c_re[j], ssm_c_im[j], ssm_d[j], ssm_log_dt[j],
                         ssm_w_glu[j], ssm_b_glu[j], ssm_w_out[j])
        else:
            m = gmlp_mixer(h, gm_w_in[j], gm_v_norm[j], gm_w_s[j], gm_b_s[j], gm_w_out[j])
        x = x + m
        x = x + squared_relu_mlp(rms_norm(x, norm_mlp[i]), mlp_w1[i], mlp_w2[i])
    return x
```

```python
import math
from contextlib import ExitStack

import numpy as np
import ml_dtypes

import concourse.bass as bass
import concourse.mybir as mybir
from concourse.bass_utils import run_bass_kernel_spmd

F32 = mybir.dt.float32
BF16 = mybir.dt.bfloat16
I32 = mybir.dt.int32
ALU = mybir.AluOpType
ACTF = mybir.ActivationFunctionType
AX = mybir.AxisListType

D = 1024
SEQ = 4096
BATCH = 4
DEPTH = 4
EPS = 1e-6
NCORES = 8


class Buf:
    __slots__ = ("name", "w", "r")

    def __init__(self, name):
        self.name = name
        self.w = None
        self.r = {}


class Region:
    def __init__(self, ap, bufs):
        self.ap = ap
        self.bufs = bufs


def _flat(items):
    out = []
    for it in items:
        if it is None:
            continue
        if isinstance(it, Buf):
            out.append(it)
        elif isinstance(it, Region):
            out.extend(it.bufs)
        else:
            out.extend(_flat(it))
    return out


class Sched:
    GEN = 24000
    POOL_INFLIGHT = 8

    def __init__(self, nc, stack, ndma=32):
        self.nc = nc
        self.stack = stack
        self.eng = dict(pe=nc.tensor, act=nc.scalar, dve=nc.vector, pool=nc.gpsimd, sp=nc.sync)
        self.stream = {k: [] for k in self.eng}
        self.cnt = {k: 0 for k in self.eng}
        self.gen = {k: 0 for k in self.eng}
        self.semh = {}
        for k in ("pe", "act", "dve", "pool"):
            self.semh[(k, 0)] = stack.enter_context(nc.semaphore(f"s_{k}0"))
        self.ndma = ndma
        for j in range(ndma):
            self.semh[("d", j)] = stack.enter_context(nc.semaphore(f"s_d{j}"))
        self.dcnt = [0] * ndma
        self.dnext = 0
        self.known = {k: {} for k in self.eng}
        self.ninstr = 0
        self.pool_hist = []

    def _collect(self, reads, writes):
        need = {}
        for b in reads:
            ev = b.w
            if ev is not None and need.get(ev[0], 0) < ev[1]:
                need[ev[0]] = ev[1]
        for b in writes:
            ev = b.w
            if ev is not None and need.get(ev[0], 0) < ev[1]:
                need[ev[0]] = ev[1]
            for k, v in b.r.items():
                if need.get(k, 0) < v:
                    need[k] = v
        return need

    def _waits(self, e, need, skip_self=False):
        kn = self.known[e]
        st = self.stream[e]
        for k, v in need.items():
            if skip_self and k[0] == e:
                continue
            if kn.get(k, 0) >= v:
                continue
            kn[k] = v
            sem = self.semh[k]
            self.eng[e].wait_ge(sem, v)
            self.ninstr += 1

    def op(self, e, fn, reads=(), writes=()):
        reads = _flat(reads)
        writes = _flat(writes)
        need = self._collect(reads, writes)
        self._waits(e, need, skip_self=(e == "pe"))
        if self.cnt[e] >= self.GEN:
            self.gen[e] += 1
            self.cnt[e] = 0
            self.semh[(e, self.gen[e])] = self.stack.enter_context(
                self.nc.semaphore(f"s_{e}{self.gen[e]}"))
        self.cnt[e] += 1
        key = (e, self.gen[e])
        v = self.cnt[e]
        sem = self.semh[key]
        fn(self.eng[e]).then_inc(sem, 1)
        self.ninstr += 1
        for b in reads:
            b.r[key] = v
        for b in writes:
            b.w = (key, v)
            b.r = {}

    def dma(self, q, out, in_, reads=(), writes=(), **kw):
        reads = _flat(reads)
        writes = _flat(writes)
        need = self._collect(reads, writes)
        if q == "pool":
            hist = self.pool_hist
            if len(hist) >= self.POOL_INFLIGHT:
                k0, v0 = hist[-self.POOL_INFLIGHT]
                if need.get(k0, 0) < v0:
                    need[k0] = v0
        self._waits(q, need)
        j = self.dnext
        self.dnext = (j + 1) % self.ndma
        self.dcnt[j] += 16
        key = ("d", j)
        v = self.dcnt[j]
        sem = self.semh[key]
        self.eng[q].dma_start(out=out, in_=in_, **kw).then_inc(sem, 16)
        self.ninstr += 1
        if q == "pool":
            self.pool_hist.append((key, v))
        for b in reads:
            b.r[key] = v
        for b in writes:
            b.w = (key, v)
            b.r = {}

    def wait_bufs(self, e, bufs):
        bufs = _flat(bufs)
        need = self._collect((), bufs)
        self._waits(e, need)

    def finish(self):
        return
        nc = self.nc
        with nc.Block() as block:
            for name, deco in (("pe", block.tensor), ("act", block.scalar), ("dve", block.vector),
                               ("pool", block.gpsimd), ("sp", block.sync)):
                lst = self.stream[name]

                @deco
                def _(eng, lst=lst):
                    for th in lst:
                        th(eng)


class Arena:
    def __init__(self, nc, stack, name, nelem, dtype, chunk):
        self.t = stack.enter_context(nc.sbuf_tensor(name, [128, nelem], dtype))
        self.n = nelem
        self.chunk = chunk
        self.bufs = [Buf(f"{name}{i}") for i in range((nelem + chunk - 1) // chunk)]
        self.off = 0
        self.name = name

    def reset(self, off=0):
        for b, cbs in getattr(self, "owned", []):
            for cb in cbs:
                if b.w is not None:
                    cb.r[b.w[0]] = max(cb.r.get(b.w[0], 0), b.w[1])
                for k, v in b.r.items():
                    cb.r[k] = max(cb.r.get(k, 0), v)
        self.owned = []
        self.off = off

    def own(self, reg):
        b = Buf("own")
        for cb in reg.bufs:
            if cb.w is not None:
                b.r[cb.w[0]] = max(b.r.get(cb.w[0], 0), cb.w[1])
            for k, v in cb.r.items():
                b.r[k] = max(b.r.get(k, 0), v)
        if not hasattr(self, "owned"):
            self.owned = []
        self.owned.append((b, reg.bufs))
        reg.bufs = [b]
        return reg

    def alloc(self, n, pattern=None, **kw):
        off = self.off
        assert off + n <= self.n, f"arena {self.name} overflow: {off}+{n} > {self.n}"
        self.off = off + n
        return self.view(off, n, pattern, **kw)

    def view(self, off, n, pattern=None, **kw):
        ap = self.t[:, off:off + n]
        if pattern is not None:
            ap = ap.rearrange(pattern, **kw)
        c0 = off // self.chunk
        c1 = (off + n - 1) // self.chunk
        r = Region(ap, self.bufs[c0:c1 + 1])
        r.off = off
        r.n = n
        r.arena = self
        return r

    def sub(self, reg, lo, hi):
        m = getattr(reg, "mul", 1)
        c0 = (reg.off + lo // m) // self.chunk
        c1 = (reg.off + (hi - 1) // m) // self.chunk
        return self.bufs[c0:c1 + 1]

    def alloc16(self, n, pattern=None, **kw):
        n32 = (n + 1) // 2
        off = self.off
        assert off + n32 <= self.n, f"arena {self.name} overflow: {off}+{n32} > {self.n}"
        self.off = off + n32
        ap = self.t[:, off:off + n32].bitcast(BF16)[:, 0:n]
        if pattern is not None:
            ap = ap.rearrange(pattern, **kw)
        c0 = off // self.chunk
        c1 = (off + n32 - 1) // self.chunk
        r = Region(ap, self.bufs[c0:c1 + 1])
        r.off = off; r.n = n32; r.arena = self; r.mul = 2
        return r

    def alloc32(self, n, pattern=None, **kw):
        r = self.alloc(2 * n)
        ap = r.ap.bitcast(F32)
        if pattern is not None:
            ap = ap.rearrange(pattern, **kw)
        r.ap = ap
        return r

    def alloci(self, n, pattern=None, **kw):
        r = self.alloc(n)
        ap = r.ap.bitcast(I32)
        if pattern is not None:
            ap = ap.rearrange(pattern, **kw)
        r.ap = ap
        return r


class Prog:
    def __init__(self, ntok=SEQ):
        self.ntok = ntok
        self.nc = bass.Bass("TRN2", target_bir_lowering=False)
        self.stack = ExitStack()
        nc = self.nc
        self.S = Sched(nc, self.stack)
        st = self.stack
        self.WA = Arena(nc, st, "wa", 72 * 1024, BF16, 2048)
        self.A = Arena(nc, st, "aa", 15 * 1024 + 512, F32, 256)
        self.psum_t = st.enter_context(nc.psum_tensor("ps", [128, 8, 512], F32))
        self.ps = [Region(self.psum_t[:, b, :], [Buf(f"ps{b}")]) for b in range(8)]
        self.dram_in = {}
        self.consts = {}
        self.dbuf = {}

    def din(self, name, shape, dtype=F32):
        t = self.nc.dram_tensor(name, list(shape), dtype, kind="ExternalInput").ap()
        self.dram_in[name] = t
        return t

    def dout(self, name, shape, dtype=F32):
        return self.nc.dram_tensor(name, list(shape), dtype, kind="ExternalOutput").ap()

    def dscratch(self, name, shape, dtype):
        return self.nc.dram_tensor(name, list(shape), dtype, kind="Internal").ap()

    def debug(self, name, reg, shape, dtype):
        d = self.nc.dram_tensor("dbg_" + name, list(shape), dtype, kind="ExternalOutput").ap()
        self.S.dma("sp", out=d, in_=reg.ap, reads=[reg], writes=[self.db("dbg_" + name)])
        self.dbg = getattr(self, "dbg", [])
        self.dbg.append(self.db("dbg_" + name))

    def db(self, name):
        b = self.dbuf.get(name)
        if b is None:
            b = self.dbuf[name] = Buf(name)
        return b


def ps_bf16(reg):
    return reg.ap.bitcast(BF16)


def emit_norm_T(P, xt_ap, xt_bufs, gam, hT, tok_off, ident, scr):
    S = P.S
    sq, ss, rstd, xn, pst = scr["sq"], scr["ss"], scr["rstd"], scr["xn"], scr["pst"]
    S.op("act", lambda e: e.activation(out=sq.ap, in_=xt_ap, func=ACTF.Square, scale=1.0 / 32.0,
                                       accum_out=ss.ap),
         reads=[xt_bufs], writes=[sq, ss])
    S.op("pool", lambda e: e.tensor_scalar(out=ss.ap, in0=ss.ap, scalar1=EPS, scalar2=None, op0=ALU.add),
         reads=[ss], writes=[ss])
    S.op("pool", lambda e: e.tensor_tensor(out=rstd.ap, in0=ss.ap, in1=P.C["mhalf"].ap, op=ALU.pow),
         reads=[ss, P.C["mhalf"]], writes=[rstd])
    S.op("dve", lambda e: e.scalar_tensor_tensor(out=xn.ap, in0=xt_ap, scalar=rstd.ap, in1=gam.ap,
                                                 op0=ALU.mult, op1=ALU.mult),
         reads=[xt_bufs, rstd, gam], writes=[xn])
    pv = ps_bf16(pst)
    for kc in range(8):
        S.op("pe", lambda e, kc=kc: e.transpose(out=pv[:, kc * 128:(kc + 1) * 128],
                                                in_=xn.ap[:, kc * 128:(kc + 1) * 128],
                                                identity=ident.ap),
             reads=[xn, ident], writes=[pst])
    S.op("act", lambda e: e.activation(out=hT.ap[:, :, tok_off:tok_off + 128],
                                       in_=pv.rearrange("p (k t) -> p k t", k=8), func=ACTF.Copy),
         reads=[pst], writes=[hT])


def load_weight_fast(P, reg, dram_ap, nk, ncols, nstage=4, piece=2048):
    S, A = P.S, P.A
    piece = min(piece, ncols)
    off_keep = A.off
    A.off = A.n - nstage * piece
    stg = [A.alloc(piece) for _ in range(nstage)]
    A.off = off_keep
    i = 0
    for kc in range(nk):
        for c0 in range(0, ncols, piece):
            c1 = min(ncols, c0 + piece)
            st = stg[i % nstage]
            S.dma("sp", out=st.ap[:, 0:c1 - c0], in_=dram_ap[kc * 128:(kc + 1) * 128, c0:c1], writes=[st])
            lo = kc * ncols + c0
            dst = reg.arena.sub(reg, lo, lo + (c1 - c0))
            if i % 2 == 0:
                S.op("act", lambda e: e.activation(out=reg.ap[:, kc, c0:c1], in_=st.ap[:, 0:c1 - c0], func=ACTF.Copy),
                     reads=[st], writes=[dst])
            else:
                S.op("dve", lambda e: e.tensor_copy(out=reg.ap[:, kc, c0:c1], in_=st.ap[:, 0:c1 - c0]),
                     reads=[st], writes=[dst])
            i += 1


def load_weight(P, reg, dram_ap, rows_per_part_dim, ncols, q="pool"):
    S = P.S
    nk = rows_per_part_dim
    for kc in range(nk):
        for c0 in range(0, ncols, 2048):
            c1 = min(ncols, c0 + 2048)
            lo = kc * ncols + c0
            S.dma(q, out=reg.ap[:, kc, c0:c1], in_=dram_ap[kc * 128:(kc + 1) * 128, c0:c1],
                  writes=[reg.arena.sub(reg, lo, lo + (c1 - c0))])


TWO_PI = 6.283179


def norm_scratch(P):
    A = P.A
    return dict(sq=A.alloc16(1024), ss=A.alloc(1), rstd=A.alloc(1), xn=A.alloc16(1024))


def load_gamma(P, gamd):
    gam = P.A.alloc(1024)
    P.S.dma("sp", out=gam.ap, in_=gamd.partition_broadcast(128), writes=[gam])
    return gam


def residual_out(P, xt, xb_ap_fn, W, nk, lhs_fn, lhs_reads_fn, banks):
    S = P.S
    nb = xt.ap.shape[1]
    i = 0
    for b in range(nb):
        for oc in range(2):
            pb = banks[i % len(banks)]
            i += 1
            for k in range(nk):
                S.op("pe", lambda e, k=k, b=b, oc=oc, pb=pb: e.matmul(
                    pb.ap, lhsT=lhs_fn(k, b), rhs=W.ap[:, k, oc * 512:(oc + 1) * 512],
                    start=(k == 0), stop=(k == nk - 1)),
                    reads=[lhs_reads_fn(k), P.WA.sub(W, k * 1024 + oc * 512, k * 1024 + oc * 512 + 512)],
                    writes=[pb])
            S.op("dve", lambda e, b=b, oc=oc, pb=pb: e.tensor_tensor(
                out=xt.ap[:, b, oc * 512:(oc + 1) * 512], in0=pb.ap,
                in1=xt.ap[:, b, oc * 512:(oc + 1) * 512], op=ALU.add),
                reads=[pb, xt], writes=[xt])


def phase_mlp(P, xd, w1d, w2d, gamd, preloaded=False):
    S, WA, A, C = P.S, P.WA, P.A, P.C
    TT = 256
    NTT = P.ntok // TT
    WA.reset(); A.reset(C["a0"])
    W1 = WA.alloc(8 * 4096, "p (k f) -> p k f", k=8)
    W2 = WA.alloc(32 * 1024, "p (k f) -> p k f", k=32)
    if not preloaded:
        load_weight_fast(P, W1, w1d, 8, 4096)
        load_weight_fast(P, W2, w2d, 32, 1024, piece=1024)
    gam = load_gamma(P, gamd)
    xts = [A.alloc(2 * 1024, "p (b d) -> p b d", b=2) for _ in range(2)]
    scr = norm_scratch(P)
    sqfs = [A.own(A.alloc(256)) for _ in range(3)]
    hTs = [A.alloc16(8 * TT, "p (k t) -> p k t", k=8) for _ in range(2)]
    actT = A.alloc16(32 * TT, "p (f t) -> p f t", f=32)
    ident = C["ident"]
    xv = xd.rearrange("(n b p) d -> n p b d", p=128, b=2)
    groups = [list(range(g, min(g + 3, 32))) for g in range(0, 32, 3)]

    def load_norm(it):
        xt = xts[it % 2]; hT = hTs[it % 2]
        S.dma("sp", out=xt.ap, in_=xv[it], reads=[P.db(f"x{it}")], writes=[xt])
        for b in range(2):
            emit_norm_T(P, xt.ap[:, b, :], xt, gam, hT, b * 128, ident, dict(scr, pst=P.ps[6 + b]))

    load_norm(0)
    nsq = 0
    for it in range(NTT):
        xt = xts[it % 2]; hT = hTs[it % 2]
        xb = P.db(f"x{it}")
        for gi, grp in enumerate(groups):
            banks = [P.ps[(gi % 2) * 3 + n] for n in range(len(grp))]
            for kc in range(8):
                for n, fc in enumerate(grp):
                    pb = banks[n]
                    S.op("pe", lambda e: e.matmul(
                        pb.ap[:, 0:TT], lhsT=W1.ap[:, kc, fc * 128:(fc + 1) * 128], rhs=hT.ap[:, kc, :],
                        start=(kc == 0), stop=(kc == 7)),
                        reads=[WA.sub(W1, kc * 4096 + fc * 128, kc * 4096 + fc * 128 + 128), hT], writes=[pb])
            for n, fc in enumerate(grp):
                pb = banks[n]
                sqf = sqfs[nsq % 3]
                nsq += 1
                S.op("act", lambda e: e.activation(out=sqf.ap, in_=pb.ap[:, 0:TT], func=ACTF.Relu),
                     reads=[pb], writes=[sqf])
                S.op("dve", lambda e: e.tensor_tensor(out=actT.ap[:, fc, :], in0=sqf.ap, in1=sqf.ap, op=ALU.mult),
                     reads=[sqf], writes=[A.sub(actT, fc * TT, fc * TT + TT)])
        if it + 1 < NTT:
            load_norm(it + 1)
        residual_out(P, xt, None, W2, 32,
                     lambda k, b: actT.ap[:, k, b * 128:(b + 1) * 128],
                     lambda k: A.sub(actT, k * TT, k * TT + TT), P.ps[0:4])
        S.dma("sp", out=xv[it], in_=xt.ap, reads=[xt], writes=[xb])


def phase_gmlp(P, xd, gamd, wind, vgd, wsd, bsd, woutd):
    S, WA, A, C = P.S, P.WA, P.A, P.C
    TT = 256
    NTT = P.ntok // TT
    WA.reset(); A.reset(C["a0"])
    Win = WA.alloc(8 * 6144, "p (k f) -> p k f", k=8)
    Wout = WA.alloc(24 * 1024, "p (k f) -> p k f", k=24)
    load_weight_fast(P, Win, wind, 8, 6144)
    load_weight_fast(P, Wout, woutd, 24, 1024, piece=1024)
    gam = load_gamma(P, gamd)
    ident = C["ident"]
    WmT = A.alloc16(1024, "p (h t) -> p h t", h=8)
    vg = A.alloc(24)
    bs = A.alloc(1024, "p (h t) -> p h t", h=8)
    off_keep = A.off
    A.off = A.n - 512
    wn = A.alloc16(1024, "p (h s) -> p h s", h=8)
    A.off = off_keep
    S.dma("pool", out=wn.ap, in_=wsd.rearrange("h t s -> t h s"), writes=[wn])
    S.op("dve", lambda e: e.tensor_tensor(out=wn.ap, in0=wn.ap,
                                          in1=C["tril"].ap.unsqueeze(1).broadcast_to([128, 8, 128]), op=ALU.mult),
         reads=[wn, C["tril"]], writes=[wn])
    pst = P.ps[7]
    pv = ps_bf16(pst)
    for h in range(8):
        S.op("pe", lambda e, h=h: e.transpose(out=pv[:, h * 128:(h + 1) * 128], in_=wn.ap[:, h, :], identity=ident.ap),
             reads=[wn, ident], writes=[pst])
    S.op("act", lambda e: e.activation(out=WmT.ap, in_=pv.rearrange("p (h t) -> p h t", h=8), func=ACTF.Copy),
         reads=[pst], writes=[WmT])
    S.dma("sp", out=vg.ap, in_=vgd.rearrange("o (c p) -> p (o c)", p=128), writes=[vg], allow_slow_non_contiguous=True)
    S.dma("sp", out=bs.ap, in_=bsd.rearrange("h t -> (h t)").partition_broadcast(128), writes=[bs])
    xts = [A.alloc(2 * 1024, "p (b d) -> p b d", b=2)] * 2
    scr = norm_scratch(P)
    hT = A.alloc16(8 * TT, "p (k t) -> p k t", k=8)
    gv = [A.alloc16(3072) for _ in range(2)]
    ssq = [A.alloc(8) for _ in range(2)]
    sst = [A.alloc(1) for _ in range(2)]
    rsv = [A.alloc(1) for _ in range(2)]
    junk = scr["sq"]
    WmTs = [A.alloc16(1024, "p (h t) -> p h t", h=8) for _ in range(2)]
    ug = [A.alloc(TT) for _ in range(2)]
    tmp = [A.alloc(TT) for _ in range(2)]
    uvT = A.alloc16(24 * TT, "p (c t) -> p c t", c=24)
    xv = xd.rearrange("(n b p) d -> n p b d", p=128, b=2)
    for it in range(NTT):
        xt = xts[it % 2]
        xb = P.db(f"x{it}")
        S.dma("sp", out=xt.ap, in_=xv[it], reads=[xb], writes=[xt])
        for b in range(2):
            emit_norm_T(P, xt.ap[:, b, :], xt, gam, hT, b * 128, ident, dict(scr, pst=P.ps[7]))
        for b in range(2):
            for j in range(6):
                pb = P.ps[(b * 6 + j) % 2]
                for kc in range(8):
                    c0 = 3072 + j * 512
                    S.op("pe", lambda e, kc=kc, b=b, pb=pb, c0=c0: e.matmul(
                        pb.ap, lhsT=hT.ap[:, kc, b * 128:(b + 1) * 128], rhs=Win.ap[:, kc, c0:c0 + 512],
                        start=(kc == 0), stop=(kc == 7)),
                        reads=[hT, WA.sub(Win, kc * 6144 + c0, kc * 6144 + c0 + 512)], writes=[pb])
                gs = A.sub(gv[b], j * 512, j * 512 + 512)
                S.op("act", lambda e, b=b, j=j, pb=pb: e.activation(
                    out=gv[b].ap[:, j * 512:(j + 1) * 512], in_=pb.ap, func=ACTF.Gelu_apprx_tanh),
                    reads=[pb], writes=[gs])
                S.op("dve", lambda e, b=b, j=j: e.scalar_tensor_tensor(
                    out=junk.ap[:, 0:512], in0=gv[b].ap[:, j * 512:(j + 1) * 512], scalar=1.0,
                    in1=gv[b].ap[:, j * 512:(j + 1) * 512], op0=ALU.mult, op1=ALU.mult,
                    accum_out=ssq[b].ap[:, j:j + 1]),
                    reads=[gs], writes=[junk, ssq[b]])
            S.op("dve", lambda e, b=b: e.tensor_reduce(out=sst[b].ap, in_=ssq[b].ap[:, 0:6], axis=AX.X, op=ALU.add),
                 reads=[ssq[b]], writes=[sst[b]])
            S.op("pool", lambda e, b=b: e.tensor_scalar(out=sst[b].ap, in0=sst[b].ap, scalar1=1.0 / 3072.0, scalar2=EPS,
                                                        op0=ALU.mult, op1=ALU.add),
                 reads=[sst[b]], writes=[sst[b]])
            S.op("pool", lambda e, b=b: e.tensor_tensor(out=rsv[b].ap, in0=sst[b].ap, in1=C["mhalf"].ap, op=ALU.pow),
                 reads=[sst[b], C["mhalf"]], writes=[rsv[b]])
            S.op("pool", lambda e, b=b: e.tensor_scalar(out=WmTs[b].ap, in0=WmT.ap, scalar1=rsv[b].ap, scalar2=None,
                                                        op0=ALU.mult),
                 reads=[WmT, rsv[b]], writes=[WmTs[b]])
        for c in range(24):
            hh = c // 3
            pu = P.ps[2 + c % 2]
            for kc in range(8):
                S.op("pe", lambda e, kc=kc, c=c, pu=pu: e.matmul(
                    pu.ap[:, 0:TT], lhsT=Win.ap[:, kc, c * 128:(c + 1) * 128], rhs=hT.ap[:, kc, :],
                    start=(kc == 0), stop=(kc == 7)),
                    reads=[hT, WA.sub(Win, kc * 6144 + c * 128, kc * 6144 + c * 128 + 128)], writes=[pu])
            S.op("act", lambda e, c=c, pu=pu: e.activation(out=ug[c % 2].ap, in_=pu.ap[:, 0:TT], func=ACTF.Gelu_apprx_tanh),
                 reads=[pu], writes=[ug[c % 2]])
            pg = P.ps[4 + c % 2]
            for b in range(2):
                S.op("pe", lambda e, c=c, b=b, pg=pg, hh=hh: e.matmul(
                    pg.ap[:, b * 128:(b + 1) * 128], lhsT=gv[b].ap[:, c * 128:(c + 1) * 128], rhs=WmTs[b].ap[:, hh, :],
                    start=True, stop=True),
                    reads=[A.sub(gv[b], c * 128, c * 128 + 128), WmTs[b]], writes=[pg])
            for b in range(2):
                S.op("dve", lambda e, c=c, b=b, pg=pg, hh=hh: e.scalar_tensor_tensor(
                    out=tmp[c % 2].ap[:, b * 128:(b + 1) * 128], in0=pg.ap[:, b * 128:(b + 1) * 128],
                    scalar=vg.ap[:, c:c + 1], in1=bs.ap[:, hh, :], op0=ALU.mult, op1=ALU.add),
                    reads=[pg, vg, bs], writes=[tmp[c % 2]])
            S.op("pool", lambda e, c=c: e.tensor_tensor(out=uvT.ap[:, c, :], in0=tmp[c % 2].ap, in1=ug[c % 2].ap, op=ALU.mult),
                 reads=[tmp[c % 2], ug[c % 2]], writes=[A.sub(uvT, c * TT, c * TT + TT)])
        residual_out(P, xt, None, Wout, 24,
                     lambda k, b: uvT.ap[:, k, b * 128:(b + 1) * 128],
                     lambda k: A.sub(uvT, k * TT, k * TT + TT), P.ps[0:2] + [P.ps[6]])
        S.dma("sp", out=xv[it], in_=xt.ap, reads=[xt], writes=[xb])


def emit_sincos(P, yy, sn, cs, ki, fr):
    S = P.S
    S.op("dve", lambda e: e.tensor_copy(out=ki.ap, in_=yy.ap), reads=[yy], writes=[ki])
    S.op("dve", lambda e: e.tensor_tensor(out=fr.ap, in0=yy.ap, in1=ki.ap, op=ALU.subtract),
         reads=[yy, ki], writes=[fr])
    S.op("act", lambda e: e.activation(out=sn.ap, in_=fr.ap, func=ACTF.Sin, scale=TWO_PI), reads=[fr], writes=[sn])
    S.op("dve", lambda e: e.tensor_scalar(out=ki.ap, in0=yy.ap, scalar1=0.25, scalar2=None, op0=ALU.add),
         reads=[yy, fr], writes=[ki])
    S.op("dve", lambda e: e.scalar_tensor_tensor(out=fr.ap, in0=yy.ap, scalar=0.25, in1=ki.ap,
                                                 op0=ALU.add, op1=ALU.subtract),
         reads=[yy, ki, sn], writes=[fr])
    S.op("act", lambda e: e.activation(out=cs.ap, in_=fr.ap, func=ACTF.Sin, scale=TWO_PI), reads=[fr], writes=[cs])


def phase_attn_qkv(P, xd, gamd, wqkvd, qgd, kgd, posd, sc):
    S, WA, A, C = P.S, P.WA, P.A, P.C
    TT = 512
    NB = TT // 128
    NTT = P.ntok // TT
    WA.reset(); A.reset(C["a0"])
    W = WA.alloc(8 * 3072, "p (k f) -> p k f", k=8)
    load_weight_fast(P, W, wqkvd, 8, 3072, piece=1536)
    gam = load_gamma(P, gamd)
    ident = C["ident"]
    gcol = A.alloc(2)
    for idx, gd in enumerate((qgd, kgd)):
        for half in range(2):
            S.dma("sp", out=gcol.ap[half * 64:(half + 1) * 64, idx:idx + 1], in_=gd, writes=[gcol])
    A.off = ((A.off + 255) // 256) * 256
    xt = A.alloc(NB * 1024, "p (b d) -> p b d", b=NB)
    scr = norm_scratch(P)
    hT = A.alloc16(8 * TT, "p (k t) -> p k t", k=8)
    posb = A.alloci(TT)
    yy = A.alloc(TT); ki = A.alloci(TT); fr = A.alloc(TT); sn = A.alloc(TT); cs = A.alloc(TT)
    va = [A.alloc16(8 * 129, "p (h e) -> p h e", h=8) for _ in range(2)]
    NS = 3
    sqb = [WA.own(WA.alloc(TT)) for _ in range(NS)]
    sd = [WA.own(WA.alloc32(TT)) for _ in range(NS)]
    rs = [WA.own(WA.alloc32(TT)) for _ in range(NS)]
    qn = [WA.own(WA.alloc(TT)) for _ in range(NS)]
    t1 = [WA.own(WA.alloc32(TT)) for _ in range(NS)]
    t2 = [WA.own(WA.alloc32(TT)) for _ in range(NS)]
    qf = [WA.own(WA.alloc(TT)) for _ in range(NS)]
    for v in va:
        S.op("pool", lambda e, v=v: e.memset(v.ap, 1.0), writes=[v])
    xv = xd.rearrange("(n b p) d -> n p b d", p=128, b=NB)
    pqb = P.ps[0:3]; pmb = P.ps[3:5]; prb = P.ps[5:7]
    combos = [(which, h) for which in range(2) for h in range(8)]
    gi = 0
    for it in range(NTT):
        xb = [P.db(f"x{it * 2}"), P.db(f"x{it * 2 + 1}")]
        S.dma("sp", out=xt.ap, in_=xv[it], reads=xb, writes=[xt])
        S.dma("sp", out=posb.ap, in_=posd[:, it * TT:(it + 1) * TT].partition_broadcast(128), writes=[posb])
        for b in range(NB):
            emit_norm_T(P, xt.ap[:, b, :], xt, gam, hT, b * 128, ident, dict(scr, pst=P.ps[7]))
        S.op("dve", lambda e: e.tensor_scalar(out=yy.ap, in0=posb.ap, scalar1=C["invf"].ap, scalar2=None, op0=ALU.mult),
             reads=[posb, C["invf"]], writes=[yy])
        emit_sincos(P, yy, sn, cs, ki, fr)

        def st0(i):
            which, h = combos[i]
            col0 = which * 1024 + h * 128
            pq = pqb[(gi + i) % 3]
            for kc in range(8):
                S.op("pe", lambda e: e.matmul(pq.ap, lhsT=W.ap[:, kc, col0:col0 + 128], rhs=hT.ap[:, kc, :],
                                              start=(kc == 0), stop=(kc == 7)),
                     reads=[hT, WA.sub(W, kc * 3072 + col0, kc * 3072 + col0 + 128)], writes=[pq])

        def st1(i):
            pq = pqb[(gi + i) % 3]; pm = pmb[(gi + i) % 2]; s_ = sqb[(gi + i) % NS]
            S.op("act", lambda e: e.activation(out=s_.ap, in_=pq.ap, func=ACTF.Square), reads=[pq], writes=[s_])
            S.op("pe", lambda e: e.matmul(pm.ap, lhsT=C["bones"].ap, rhs=s_.ap, start=True, stop=True),
                 reads=[s_, C["bones"]], writes=[pm])

        def st2(i):
            which, h = combos[i]
            k = (gi + i) % NS
            pq = pqb[(gi + i) % 3]; pm = pmb[(gi + i) % 2]; pr = prb[(gi + i) % 2]
            S.op("act", lambda e: e.activation(out=sd[k].ap, in_=pm.ap, func=ACTF.Ln, bias=C["epscol"].ap),
                 reads=[pm, C["epscol"]], writes=[sd[k]])
            S.op("act", lambda e: e.activation(out=rs[k].ap, in_=sd[k].ap, func=ACTF.Exp, scale=-0.5),
                 reads=[sd[k]], writes=[rs[k]])
            S.op("dve", lambda e: e.scalar_tensor_tensor(out=qn[k].ap, in0=pq.ap, scalar=gcol.ap[:, which:which + 1],
                                                         in1=rs[k].ap, op0=ALU.mult, op1=ALU.mult),
                 reads=[pq, gcol, rs[k]], writes=[qn[k]])
            S.op("pe", lambda e: e.matmul(pr.ap, lhsT=C["rrot"].ap, rhs=qn[k].ap, start=True, stop=True),
                 reads=[qn[k], C["rrot"]], writes=[pr])

        def st3(i):
            which, h = combos[i]
            k = (gi + i) % NS
            pr = prb[(gi + i) % 2]
            S.op("pool", lambda e: e.tensor_tensor(out=t1[k].ap, in0=qn[k].ap, in1=cs.ap, op=ALU.mult),
                 reads=[qn[k], cs], writes=[t1[k]])
            S.op("dve", lambda e: e.tensor_tensor(out=t2[k].ap, in0=pr.ap, in1=sn.ap, op=ALU.mult),
                 reads=[pr, sn], writes=[t2[k]])
            S.op("pool", lambda e: e.tensor_tensor(out=qf[k].ap, in0=t1[k].ap, in1=t2[k].ap, op=ALU.add),
                 reads=[t1[k], t2[k]], writes=[qf[k]])
            dst = sc["qT"] if which == 0 else sc["kT"]
            S.dma("sp", out=dst[h, :, it * TT:(it + 1) * TT], in_=qf[k].ap, reads=[qf[k]],
                  writes=[P.db(f"{'qk'[which]}T{h}_{it}")])

        n = len(combos)
        for s in range(n + 3):
            if s < n:
                st0(s)
            if 0 <= s - 1 < n:
                st1(s - 1)
            if 0 <= s - 2 < n:
                st2(s - 2)
            if 0 <= s - 3 < n:
                st3(s - 3)
        gi += n
        for b in range(NB):
            v = va[b % 2]
            for jj in range(2):
                pvb = P.ps[7]
                for kc in range(8):
                    c0 = 2048 + jj * 512
                    S.op("pe", lambda e: e.matmul(pvb.ap, lhsT=hT.ap[:, kc, b * 128:(b + 1) * 128], rhs=W.ap[:, kc, c0:c0 + 512],
                                                  start=(kc == 0), stop=(kc == 7)),
                         reads=[hT, WA.sub(W, kc * 3072 + c0, kc * 3072 + c0 + 512)], writes=[pvb])
                S.op("act", lambda e: e.activation(
                    out=v.ap[:, 4 * jj:4 * jj + 4, 0:128], in_=pvb.ap.rearrange("p (h e) -> p h e", h=4), func=ACTF.Copy),
                    reads=[pvb], writes=[v])
            blk = it * NB + b
            S.dma("sp", out=sc["v"][blk], in_=v.ap, reads=[v], writes=[P.db(f"v{blk}")])


def phase_attn_core(P, lamd, sgd, lambda_init, sc):
    S, WA, A, C = P.S, P.WA, P.A, P.C
    ntok = P.ntok
    NB = ntok // 128
    NG = ntok // 512
    NT256 = ntok // 512
    WA.reset(); A.reset(C["a0"])
    K0 = [WA.alloc(ntok) for _ in range(2)]
    K1 = [WA.alloc(ntok) for _ in range(2)]
    QT = [WA.alloc(ntok) for _ in range(2)]
    VA = [WA.alloc(NB * 128, "p (n e) -> p n e", e=128) for _ in range(2)]
    ones16 = WA.alloc(128)
    onesb = WA.alloc(128)
    S.op("pool", lambda e: e.memset(ones16.ap, 1.0), writes=[ones16])
    S.op("pool", lambda e: e.memset(onesb.ap, 1.0 / 128.0), writes=[onesb])
    for i in range(2):
        S.op("pool", lambda e, i=i: e.memset(K0[i].ap[64:128, :], 0.0), writes=[K0[i]])
        S.op("pool", lambda e, i=i: e.memset(K1[i].ap[0:64, :], 0.0), writes=[K1[i]])
    L = A.alloc(256, "p (a d) -> p a d", a=4)
    S.dma("sp", out=L.ap, in_=lamd.rearrange("a d -> (a d)").partition_broadcast(128), writes=[L])
    lj = A.alloc(64); s12 = A.alloc(2); e12 = A.alloc(2); neglam = A.alloc(1)
    for a in range(2):
        S.op("dve", lambda e, a=a: e.scalar_tensor_tensor(
            out=lj.ap, in0=L.ap[:, 2 * a, :], scalar=1.0, in1=L.ap[:, 2 * a + 1, :], op0=ALU.mult, op1=ALU.mult,
            accum_out=s12.ap[:, a:a + 1]), reads=[L], writes=[lj, s12])
    S.op("act", lambda e: e.activation(out=e12.ap, in_=s12.ap, func=ACTF.Exp), reads=[s12], writes=[e12])
    S.op("dve", lambda e: e.tensor_tensor(out=neglam.ap, in0=e12.ap[:, 1:2], in1=e12.ap[:, 0:1], op=ALU.subtract),
         reads=[e12], writes=[neglam])
    S.op("dve", lambda e: e.tensor_scalar(out=neglam.ap, in0=neglam.ap, scalar1=-float(lambda_init), scalar2=None,
                                          op0=ALU.add), reads=[neglam], writes=[neglam])
    sgc = A.alloc(1)
    S.dma("sp", out=sgc.ap, in_=sgd.rearrange("o e -> e o"), writes=[sgc], allow_slow_non_contiguous=True)
    S.op("dve", lambda e: e.tensor_scalar(out=sgc.ap, in0=sgc.ap, scalar1=float(1.0 - lambda_init), scalar2=None,
                                          op0=ALU.mult), reads=[sgc], writes=[sgc])
    A.off = ((A.off + 255) // 256) * 256
    PT = [[A.alloc16(512) for _ in range(2)] for _ in range(2)]
    ob = [[A.alloc(512) for _ in range(2)] for _ in range(2)]
    rl = [A.alloc(512) for _ in range(2)]
    tt = A.alloc(512); uu = A.alloc(512); oo = A.alloc(512)
    osq = A.alloc16(512); msb = A.alloc(512); rs = A.alloc(512)
    oT = [A.alloc16(512) for _ in range(2)]
    sb3 = [P.ps[0], P.ps[1], P.ps[2]]
    otb = [P.ps[3], P.ps[4]]
    plb = [P.ps[5], P.ps[6]]
    pmb = P.ps[7]
    lsb = [[A.alloc(512) for _ in range(2)] for _ in range(2)]
    mhb = C["mhalf"].ap.broadcast_to([128, 512])
    ng = 0
    for h in range(8):
        buf = h % 2
        S.dma("sp", out=K0[buf].ap[0:64, :], in_=sc["kT"][h, 0:64, :],
              reads=[P.db(f"kT{h}_{it}") for it in range(NT256)], writes=[K0[buf]])
        S.dma("sp", out=K1[buf].ap[64:128, :], in_=sc["kT"][h, 64:128, :],
              reads=[P.db(f"kT{h}_{it}") for it in range(NT256)], writes=[K1[buf]])
        S.dma("sp", out=QT[buf].ap, in_=sc["qT"][h],
              reads=[P.db(f"qT{h}_{it}") for it in range(NT256)], writes=[QT[buf]])
        S.dma("sp", out=VA[buf].ap, in_=sc["v"].rearrange("n p (h e) -> h p n e", h=8)[h][:, :, 0:128],
              reads=[P.db(f"v{blk}") for blk in range(NB)], writes=[VA[buf]])
        for G in range(NG):
            gb = ng % 2
            ng += 1
            njb = 4 * G + 4

            def geom(jb):
                nq0 = max(0, jb - 4 * G)
                return nq0, (4 - nq0) * 128, G * 512 + nq0 * 128

            def stage_a(jb, cs_=(0, 1)):
                nq0, N, qc0 = geom(jb)
                for c in cs_:
                    Kc = (K0 if c == 0 else K1)[buf]
                    pss = sb3[(2 * jb + c) % 3]
                    S.op("pe", lambda e: e.matmul(
                        pss.ap[:, 0:N], lhsT=Kc.ap[:, jb * 128:(jb + 1) * 128], rhs=QT[buf].ap[:, qc0:qc0 + N],
                        start=True, stop=True),
                        reads=[WA.sub(Kc, jb * 128, jb * 128 + 128), WA.sub(QT[buf], qc0, qc0 + N)], writes=[pss])

            def stage_b(jb, cs_=(0, 1)):
                nq0, N, qc0 = geom(jb)
                c0 = nq0 * 128
                for c in cs_:
                    pss = sb3[(2 * jb + c) % 3]
                    pt = PT[jb % 2][c]
                    S.op("act", lambda e: e.activation(out=pt.ap[:, 0:N], in_=pss.ap[:, 0:N], func=ACTF.Exp, scale=0.125),
                         reads=[pss], writes=[pt])
                    eng = "dve" if c == 0 else "pool"
                    if jb >= 4 * G:
                        S.op("dve", lambda e: e.tensor_tensor(out=pt.ap[:, 0:128], in0=pt.ap[:, 0:128],
                                                               in1=C["triu"].ap, op=ALU.mult),
                             reads=[pt, C["triu"]], writes=[pt])

            def stage_c(jb):
                nq0, N, qc0 = geom(jb)
                c0 = nq0 * 128
                for c in range(2):
                    pt = PT[jb % 2][c]
                    S.op("pe", lambda e: e.matmul(
                        otb[c].ap[:, c0:512], lhsT=VA[buf].ap[:, jb, :], rhs=pt.ap[:, 0:N],
                        start=(jb == 0), stop=(jb == njb - 1)),
                        reads=[pt, WA.sub(VA[buf], jb * 128, jb * 128 + 128)], writes=[otb[c]])
                    S.op("pe", lambda e: e.matmul(
                        plb[c].ap[:, c0:512], lhsT=ones16.ap, rhs=pt.ap[:, 0:N],
                        start=(jb == 0), stop=(jb == njb - 1)),
                        reads=[pt, ones16], writes=[plb[c]])

            stage_a(0)
            for jb in range(njb):
                if jb + 1 < njb:
                    stage_a(jb + 1, (0,))
                stage_b(jb, (0,))
                if jb + 1 < njb:
                    stage_a(jb + 1, (1,))
                stage_b(jb, (1,))
                stage_c(jb)
            for c in range(2):
                S.op("dve", lambda e, c=c: e.tensor_copy(out=ob[gb][c].ap, in_=otb[c].ap), reads=[otb[c]], writes=[ob[gb][c]])
                S.op("dve", lambda e, c=c: e.tensor_copy(out=lsb[gb][c].ap, in_=plb[c].ap), reads=[plb[c]], writes=[lsb[gb][c]])
            for c in range(2):
                S.op("dve", lambda e, c=c: e.reciprocal(out=rl[c].ap, in_=lsb[gb][c].ap), reads=[lsb[gb][c]], writes=[rl[c]])
            S.op("dve", lambda e: e.scalar_tensor_tensor(out=tt.ap, in0=ob[gb][1].ap, scalar=neglam.ap, in1=rl[1].ap,
                                                         op0=ALU.mult, op1=ALU.mult),
                 reads=[ob[gb][1], rl[1], neglam], writes=[tt])
            S.op("dve", lambda e: e.tensor_tensor(out=uu.ap, in0=ob[gb][0].ap, in1=rl[0].ap, op=ALU.mult),
                 reads=[ob[gb][0], rl[0]], writes=[uu])
            S.op("dve", lambda e: e.tensor_tensor(out=oo.ap, in0=uu.ap, in1=tt.ap, op=ALU.add), reads=[uu, tt], writes=[oo])
            S.op("dve", lambda e: e.tensor_tensor(out=osq.ap, in0=oo.ap, in1=oo.ap, op=ALU.mult), reads=[oo], writes=[osq])
            S.op("pe", lambda e: e.matmul(pmb.ap, lhsT=onesb.ap, rhs=osq.ap, start=True, stop=True),
                 reads=[onesb, osq], writes=[pmb])
            S.op("act", lambda e: e.activation(out=msb.ap, in_=pmb.ap, func=ACTF.Ln, bias=C["epscol"].ap),
                 reads=[pmb, C["epscol"]], writes=[msb])
            S.op("act", lambda e: e.activation(out=rs.ap, in_=msb.ap, func=ACTF.Exp, scale=-0.5),
                 reads=[msb], writes=[rs])
            oTt = oT[gb]
            S.op("dve", lambda e: e.scalar_tensor_tensor(out=oTt.ap, in0=oo.ap, scalar=sgc.ap, in1=rs.ap,
                                                         op0=ALU.mult, op1=ALU.mult),
                 reads=[oo, sgc, rs], writes=[oTt])
            S.dma("sp", out=sc["oT"][h, :, G * 512:(G + 1) * 512], in_=oTt.ap, reads=[oTt],
                  writes=[P.db(f"oT{h}_{G}")])


def phase_attn_out(P, xd, wod, sc, prefetch=None):
    S, WA, A, C = P.S, P.WA, P.A, P.C
    TT = 256
    NTT = P.ntok // TT
    WA.reset(); A.reset(C["a0"])
    WA.off = 64 * 1024
    Wo = WA.alloc(8 * 1024, "p (k f) -> p k f", k=8)
    load_weight_fast(P, Wo, wod, 8, 1024, piece=1024)
    if prefetch is not None:
        w1d, w2d = prefetch
        WA.off = 0
        W1 = WA.alloc(8 * 4096, "p (k f) -> p k f", k=8)
        W2 = WA.alloc(32 * 1024, "p (k f) -> p k f", k=32)
        load_weight(P, W1, w1d, 8, 4096)
        load_weight(P, W2, w2d, 32, 1024)
    xts = [A.alloc(2 * 1024, "p (b d) -> p b d", b=2) for _ in range(2)]
    oTs = [A.alloc16(8 * TT, "p (h t) -> p h t", h=8) for _ in range(2)]
    xv = xd.rearrange("(n b p) d -> n p b d", p=128, b=2)
    for it in range(NTT):
        xt = xts[it % 2]; ot = oTs[it % 2]
        xb = P.db(f"x{it}")
        S.dma("sp", out=xt.ap, in_=xv[it], reads=[xb], writes=[xt])
        S.dma("sp", out=ot.ap, in_=sc["oT"][:, :, it * TT:(it + 1) * TT].rearrange("h p t -> p h t"),
              reads=[P.db(f"oT{h}_{it // 2}") for h in range(8)], writes=[ot])
        residual_out(P, xt, None, Wo, 8, lambda k, b, ot=ot: ot.ap[:, k, b * 128:(b + 1) * 128],
                     lambda k, ot=ot: ot, P.ps[0:4])
        S.dma("sp", out=xv[it], in_=xt.ap, reads=[xt], writes=[xb])


def phase_s5(P, xd, gamd, wind, ared, aimd, bred, bimd, cred, cimd, dd, logdtd, wglud, bglud, woutd):
    S, WA, A, C = P.S, P.WA, P.A, P.C
    TT = 256
    NTT = P.ntok // TT
    WA.reset(); A.reset(C["a0"])
    ident = C["ident"]
    Win = WA.alloc(8 * 1024, "p (k f) -> p k f", k=8)
    Wglu = WA.alloc(8 * 1024, "p (k f) -> p k f", k=8)
    Wout = WA.alloc(8 * 1024, "p (k f) -> p k f", k=8)
    load_weight_fast(P, Win, wind, 8, 1024, piece=1024)
    load_weight_fast(P, Wglu, wglud, 8, 1024, piece=1024)
    load_weight_fast(P, Wout, woutd, 8, 1024, piece=1024)
    BL = WA.alloc(32 * 2 * 128, "p (q r m) -> p q r m", q=32, r=2)
    CBr = WA.alloc(32 * 3 * 128 + 2048)
    CBflat = CBr.ap
    Zr = [WA.alloc(512 + 256) for _ in range(2)]
    ZC = [WA.alloc(128, "p (g m) -> p g m", g=2) for _ in range(2)]
    rcol = A.alloc(32); ycol = A.alloc(32); carry = A.alloc(64, "p (q r) -> p q r", r=2)
    dcol = A.alloc(8); bgl = A.alloc(8)
    a_keep = A.off
    S.op("pool", lambda e: e.memset(carry.ap, 0.0), writes=[carry])
    S.dma("sp", out=dcol.ap, in_=dd.rearrange("o (c p) -> p (o c)", p=128), writes=[dcol], allow_slow_non_contiguous=True)
    S.dma("sp", out=bgl.ap, in_=bglud.rearrange("o (c p) -> p (o c)", p=128), writes=[bgl], allow_slow_non_contiguous=True)
    are = A.alloc(32); aim = A.alloc(32); ldt = A.alloc(32); dt = A.alloc(32)
    S.dma("sp", out=are.ap, in_=ared.rearrange("(q g2) p -> (g2 p) q", g2=2), writes=[are], allow_slow_non_contiguous=True)
    S.dma("sp", out=aim.ap, in_=aimd.rearrange("(q g2) p -> (g2 p) q", g2=2), writes=[aim], allow_slow_non_contiguous=True)
    ld2 = logdtd.rearrange("o (q g2) -> (o g2) q", g2=2)
    for g2 in range(2):
        S.dma("sp", out=ldt.ap[g2 * 64:(g2 + 1) * 64, :], in_=ld2[g2:g2 + 1, :].partition_broadcast(64), writes=[ldt],
              allow_slow_non_contiguous=True)
    S.op("act", lambda e: e.activation(out=dt.ap, in_=ldt.ap, func=ACTF.Exp), reads=[ldt], writes=[dt])
    S.op("dve", lambda e: e.tensor_scalar(out=are.ap, in0=are.ap, scalar1=-1e-4, scalar2=None, op0=ALU.min),
         reads=[are], writes=[are])
    rdt = A.alloc(32)
    S.op("dve", lambda e: e.tensor_tensor(out=rdt.ap, in0=are.ap, in1=dt.ap, op=ALU.mult), reads=[are, dt], writes=[rdt])
    S.op("act", lambda e: e.activation(out=rcol.ap, in_=rdt.ap, func=ACTF.Exp), reads=[rdt], writes=[rcol])
    S.op("dve", lambda e: e.scalar_tensor_tensor(out=ycol.ap, in0=aim.ap, scalar=float(1.0 / (2.0 * math.pi)), in1=dt.ap,
                                                 op0=ALU.mult, op1=ALU.mult), reads=[aim, dt], writes=[ycol])
    snt = A.alloc(32); cst = A.alloc(32); kit = A.alloci(32); frt = A.alloc(32)
    emit_sincos(P, ycol, snt, cst, kit, frt)
    nr = A.alloc(32); ni = A.alloc(32); den = A.alloc(32); t_a = A.alloc(32); t_b = A.alloc(32)
    zr = A.alloc(32); zi = A.alloc(32)
    S.op("dve", lambda e: e.tensor_tensor(out=nr.ap, in0=rcol.ap, in1=cst.ap, op=ALU.mult), reads=[rcol, cst], writes=[nr])
    S.op("dve", lambda e: e.tensor_scalar(out=nr.ap, in0=nr.ap, scalar1=-1.0, scalar2=None, op0=ALU.add), reads=[nr], writes=[nr])
    S.op("dve", lambda e: e.tensor_tensor(out=ni.ap, in0=rcol.ap, in1=snt.ap, op=ALU.mult), reads=[rcol, snt], writes=[ni])
    S.op("dve", lambda e: e.tensor_tensor(out=den.ap, in0=are.ap, in1=are.ap, op=ALU.mult), reads=[are], writes=[den])
    S.op("dve", lambda e: e.tensor_tensor(out=t_a.ap, in0=aim.ap, in1=aim.ap, op=ALU.mult), reads=[aim], writes=[t_a])
    S.op("dve", lambda e: e.tensor_tensor(out=den.ap, in0=den.ap, in1=t_a.ap, op=ALU.add), reads=[den, t_a], writes=[den])
    S.op("dve", lambda e: e.reciprocal(out=den.ap, in_=den.ap), reads=[den], writes=[den])
    S.op("dve", lambda e: e.tensor_tensor(out=t_a.ap, in0=nr.ap, in1=are.ap, op=ALU.mult), reads=[nr, are], writes=[t_a])
    S.op("dve", lambda e: e.tensor_tensor(out=t_b.ap, in0=ni.ap, in1=aim.ap, op=ALU.mult), reads=[ni, aim], writes=[t_b])
    S.op("dve", lambda e: e.tensor_tensor(out=t_a.ap, in0=t_a.ap, in1=t_b.ap, op=ALU.add), reads=[t_a, t_b], writes=[t_a])
    S.op("dve", lambda e: e.tensor_tensor(out=zr.ap, in0=t_a.ap, in1=den.ap, op=ALU.mult), reads=[t_a, den], writes=[zr])
    S.op("dve", lambda e: e.tensor_tensor(out=t_a.ap, in0=ni.ap, in1=are.ap, op=ALU.mult), reads=[ni, are], writes=[t_a])
    S.op("dve", lambda e: e.tensor_tensor(out=t_b.ap, in0=nr.ap, in1=aim.ap, op=ALU.mult), reads=[nr, aim], writes=[t_b])
    S.op("dve", lambda e: e.tensor_tensor(out=t_a.ap, in0=t_a.ap, in1=t_b.ap, op=ALU.subtract), reads=[t_a, t_b], writes=[t_a])
    S.op("dve", lambda e: e.tensor_tensor(out=zi.ap, in0=t_a.ap, in1=den.ap, op=ALU.mult), reads=[t_a, den], writes=[zi])
    Bre = A.alloc(512, "p (q h) -> p q h", h=16); Bim = A.alloc(512, "p (q h) -> p q h", h=16)
    S.dma("sp", out=Bre.ap, in_=bred.rearrange("(q g2) p h -> (g2 p) q h", g2=2), writes=[Bre])
    S.dma("sp", out=Bim.ap, in_=bimd.rearrange("(q g2) p h -> (g2 p) q h", g2=2), writes=[Bim])
    zrb = zr.ap.unsqueeze(2).broadcast_to([128, 32, 16])
    zib = zi.ap.unsqueeze(2).broadcast_to([128, 32, 16])
    M1 = A.alloc(512, "p (q h) -> p q h", h=16); M2 = A.alloc(512, "p (q h) -> p q h", h=16)
    Bb = [A.alloc(512, "p (q h) -> p q h", h=16) for _ in range(2)]
    S.op("dve", lambda e: e.tensor_tensor(out=M1.ap, in0=Bre.ap, in1=zrb, op=ALU.mult), reads=[Bre, zr], writes=[M1])
    S.op("dve", lambda e: e.tensor_tensor(out=M2.ap, in0=Bim.ap, in1=zib, op=ALU.mult), reads=[Bim, zi], writes=[M2])
    S.op("dve", lambda e: e.tensor_tensor(out=Bb[0].ap, in0=M1.ap, in1=M2.ap, op=ALU.subtract), reads=[M1, M2], writes=[Bb[0]])
    S.op("dve", lambda e: e.tensor_tensor(out=M1.ap, in0=Bim.ap, in1=zrb, op=ALU.mult), reads=[Bim, zr], writes=[M1])
    S.op("dve", lambda e: e.tensor_tensor(out=M2.ap, in0=Bre.ap, in1=zib, op=ALU.mult), reads=[Bre, zi], writes=[M2])
    S.op("dve", lambda e: e.tensor_tensor(out=Bb[1].ap, in0=M1.ap, in1=M2.ap, op=ALU.add), reads=[M1, M2], writes=[Bb[1]])
    pst = P.ps[7]
    pv = ps_bf16(pst)
    n = 0
    for k in range(8):
        for ri in range(2):
            Z = Zr[n % 2]
            n += 1
            S.op("pool", lambda e, Z=Z: e.memset(Z.ap, 0.0), writes=[Z])
            for g2 in range(2):
                dst = Z.ap[g2 * 64:(g2 + 1) * 64, g2 * 16:g2 * 16 + 640].rearrange("p (q m) -> p q m", m=160)[:, :, 0:16]
                S.op("dve", lambda e, dst=dst, g2=g2, ri=ri, k=k: e.tensor_copy(
                    out=dst, in_=Bb[ri].ap[g2 * 64:(g2 + 1) * 64, 4 * k:4 * k + 4, :]),
                    reads=[Bb[ri]], writes=[Z])
            for ql in range(4):
                S.op("pe", lambda e, Z=Z, ql=ql: e.transpose(out=pv[:, ql * 128:(ql + 1) * 128],
                                                            in_=Z.ap[:, ql * 128:(ql + 1) * 128], identity=ident.ap),
                     reads=[Z, ident], writes=[pst])
            S.op("act", lambda e, k=k, ri=ri: e.activation(out=BL.ap[:, 4 * k:4 * k + 4, ri, :],
                                                           in_=pv[:, 0:512].rearrange("p (q m) -> p q m", q=4),
                                                           func=ACTF.Copy),
                 reads=[pst], writes=[BL])
    Cn = [A.alloc(512, "p (k m) -> p k m", k=8) for _ in range(2)]
    S.dma("sp", out=Cn[0].ap, in_=cred.rearrange("(k gl) h p -> (gl h) k p", gl=8), writes=[Cn[0]])
    S.dma("sp", out=Cn[1].ap, in_=cimd.rearrange("(k gl) h p -> (gl h) k p", gl=8), writes=[Cn[1]])
    S.op("pool", lambda e: e.memset(CBflat, 0.0), writes=[CBr])
    pst2 = P.ps[6]
    pv2 = ps_bf16(pst2)
    n = 0
    for k in range(8):
        for ri in range(2):
            zc = ZC[n % 2]
            n += 1
            for g2 in range(2):
                S.op("dve", lambda e, zc=zc, g2=g2, ri=ri, k=k: e.tensor_scalar(
                    out=zc.ap[:, g2, :], in0=Cn[ri].ap[:, k, :], scalar1=C["par"].ap[:, g2:g2 + 1], scalar2=None,
                    op0=ALU.mult), reads=[Cn[ri], C["par"]], writes=[zc])
            S.op("pe", lambda e, zc=zc: e.transpose(out=pv2[:, 0:128], in_=zc.ap.rearrange("p g m -> p (g m)"),
                                                   identity=ident.ap), reads=[zc, ident], writes=[pst2])
            for var in ((0, 2) if ri == 0 else (1,)):
                base = 4 * k * 384 + var * 128
                dst = CBflat[:, base:base + 4 * 416].rearrange("p (q m) -> p q m", m=416)[:, :, 0:32]
                sgn = 1.0 if var == 0 else -1.0
                S.op("act", lambda e, dst=dst, sgn=sgn: e.activation(
                    out=dst, in_=pv2[:, 0:128].rearrange("p (q m) -> p q m", q=4), func=ACTF.Copy, scale=sgn),
                    reads=[pst2], writes=[CBr])
    CB = CBflat[:, 0:32 * 384].rearrange("p (q v m) -> p q v m", q=32, v=3)
    CS = WA.alloc(32 * TT, "p (q k) -> p q k", q=32)
    SN = WA.alloc(32 * TT, "p (q k) -> p q k", q=32)
    A.reset(a_keep)
    cT = A.alloc(32); sT = A.alloc(32)
    yT = A.alloc(32); kiT = A.alloci(32); frT = A.alloc(32)
    S.op("dve", lambda e: e.tensor_scalar(out=yT.ap, in0=ycol.ap, scalar1=float(TT), scalar2=None, op0=ALU.mult),
         reads=[ycol], writes=[yT])
    emit_sincos(P, yT, sT, cT, kiT, frT)
    a_keep2 = A.off
    yyt = [A.alloc(TT) for _ in range(2)]; kit2 = [A.alloci(TT) for _ in range(2)]; frt2 = [A.alloc(TT) for _ in range(2)]
    for q in range(32):
        yy_, ki_, fr_ = yyt[q % 2], kit2[q % 2], frt2[q % 2]
        S.op("dve", lambda e: e.tensor_scalar(out=yy_.ap, in0=C["tg0"].ap[:, 0:TT], scalar1=ycol.ap[:, q:q + 1], scalar2=None,
                                              op0=ALU.mult), reads=[C["tg0"], ycol], writes=[yy_])
        snq = Region(SN.ap[:, q, :], WA.sub(SN, q * TT, q * TT + TT))
        csq = Region(CS.ap[:, q, :], WA.sub(CS, q * TT, q * TT + TT))
        emit_sincos(P, yy_, snq, csq, ki_, fr_)
    A.reset(a_keep2)
    gam = load_gamma(P, gamd)
    xt = A.alloc(2 * 1024, "p (b d) -> p b d", b=2)
    scr = norm_scratch(P)
    hT = A.alloc16(8 * TT, "p (k t) -> p k t", k=8)
    uT = A.alloc16(8 * TT, "p (k t) -> p k t", k=8)
    gT = A.alloc16(8 * TT, "p (k t) -> p k t", k=8)
    ggT = A.alloc16(8 * TT, "p (k t) -> p k t", k=8)
    ydt = A.alloc(TT); sig = A.alloc(TT)
    NSET = 3

    def mkset(i):
        al = (lambda n: A.own(A.alloc(n))) if i == 0 else (lambda n: WA.own(WA.alloc32(n)))
        al16 = (lambda n: A.own(A.alloc16(n))) if i == 0 else (lambda n: WA.own(WA.alloc(n)))
        return dict(m=[al(TT) for _ in range(4)], w=[al(TT) for _ in range(2)], z=[al(TT) for _ in range(2)],
                    Y=[al16(TT) for _ in range(4)], ct=al(8))
    tsets = [mkset(0 if i < 2 else 1) for i in range(NSET)]
    cbufs = [Buf(f"carry{q}") for q in range(32)]
    xv = xd.rearrange("(n b p) d -> n p b d", p=128, b=2)
    gp = 0
    for it in range(NTT):
        xb = P.db(f"x{it}")
        S.dma("sp", out=xt.ap, in_=xv[it], reads=[xb], writes=[xt])
        for b in range(2):
            emit_norm_T(P, xt.ap[:, b, :], xt, gam, hT, b * 128, ident, dict(scr, pst=P.ps[6 + b]))
        for cc in range(8):
            pu = P.ps[cc % 2]
            for kc in range(8):
                S.op("pe", lambda e: e.matmul(
                    pu.ap[:, 0:TT], lhsT=Win.ap[:, kc, cc * 128:(cc + 1) * 128], rhs=hT.ap[:, kc, :],
                    start=(kc == 0), stop=(kc == 7)),
                    reads=[hT, WA.sub(Win, kc * 1024 + cc * 128, kc * 1024 + cc * 128 + 128)], writes=[pu])
            S.op("act", lambda e: e.activation(out=uT.ap[:, cc, :], in_=pu.ap[:, 0:TT], func=ACTF.Copy),
                 reads=[pu], writes=[A.sub(uT, cc * TT, cc * TT + TT)])

        def stB(q):
            cc = q // 4
            ts = tsets[(gp + q) % NSET]
            pb = [P.ps[2 + ((gp + q) % 2) * 2 + ri] for ri in range(2)]
            us = A.sub(uT, cc * TT, cc * TT + TT)
            m, w = ts["m"], ts["w"]
            csq = WA.sub(CS, q * TT, q * TT + TT); snq = WA.sub(SN, q * TT, q * TT + TT)
            for ri in range(2):
                S.op("pe", lambda e: e.matmul(pb[ri].ap[:, 0:TT], lhsT=BL.ap[:, q, ri, :], rhs=uT.ap[:, cc, :],
                                              start=True, stop=True), reads=[BL, us], writes=[pb[ri]])
            for idx, (ri, tab, tb) in enumerate(((0, CS, csq), (1, SN, snq), (1, CS, csq), (0, SN, snq))):
                S.op("dve", lambda e: e.tensor_tensor(out=m[idx].ap, in0=pb[ri].ap[:, 0:TT], in1=tab.ap[:, q, :], op=ALU.mult),
                     reads=[pb[ri], tb], writes=[m[idx]])
            S.op("pool", lambda e: e.tensor_tensor(out=w[0].ap, in0=m[0].ap, in1=m[1].ap, op=ALU.add),
                 reads=[m[0], m[1]], writes=[w[0]])
            S.op("pool", lambda e: e.tensor_tensor(out=w[1].ap, in0=m[2].ap, in1=m[3].ap, op=ALU.subtract),
                 reads=[m[2], m[3]], writes=[w[1]])

        def stC(q):
            ts = tsets[(gp + q) % NSET]
            w, z, ct = ts["w"], ts["z"], ts["ct"]
            rbc = rcol.ap[:, q:q + 1].broadcast_to([128, TT])
            for ri in range(2):
                S.op("dve", lambda e: e.tensor_tensor_scan(
                    out=z[ri].ap, data0=rbc, data1=w[ri].ap, initial=carry.ap[:, q, ri:ri + 1],
                    op0=ALU.mult, op1=ALU.add), reads=[rcol, w[ri], cbufs[q]], writes=[z[ri]])
            zl = [z[0].ap[:, TT - 1:TT], z[1].ap[:, TT - 1:TT]]
            for idx, (ri, tab) in enumerate(((0, cT), (1, sT), (1, cT), (0, sT))):
                S.op("act", lambda e: e.activation(out=ct.ap[:, idx:idx + 1], in_=zl[ri], func=ACTF.Copy,
                                                   scale=tab.ap[:, q:q + 1]),
                     reads=[z[ri], tab], writes=[ct])
            S.op("pool", lambda e: e.tensor_tensor(out=carry.ap[:, q, 0:1], in0=ct.ap[:, 0:1], in1=ct.ap[:, 1:2], op=ALU.subtract),
                 reads=[ct], writes=[cbufs[q]])
            S.op("pool", lambda e: e.tensor_tensor(out=carry.ap[:, q, 1:2], in0=ct.ap[:, 2:3], in1=ct.ap[:, 3:4], op=ALU.add),
                 reads=[ct], writes=[cbufs[q]])

        def stD(q):
            cc, ql = q // 4, q % 4
            ts = tsets[(gp + q) % NSET]
            z, Y = ts["z"], ts["Y"]
            py = P.ps[cc % 2]
            csq = WA.sub(CS, q * TT, q * TT + TT); snq = WA.sub(SN, q * TT, q * TT + TT)
            S.op("pool", lambda e: e.tensor_tensor(out=Y[0].ap, in0=z[0].ap, in1=CS.ap[:, q, :], op=ALU.mult),
                 reads=[z[0], csq], writes=[Y[0]])
            S.op("pool", lambda e: e.tensor_tensor(out=Y[1].ap, in0=z[1].ap, in1=CS.ap[:, q, :], op=ALU.mult),
                 reads=[z[1], csq], writes=[Y[1]])
            S.op("dve", lambda e: e.tensor_tensor(out=Y[2].ap, in0=z[0].ap, in1=SN.ap[:, q, :], op=ALU.mult),
                 reads=[z[0], snq], writes=[Y[2]])
            S.op("dve", lambda e: e.tensor_tensor(out=Y[3].ap, in0=z[1].ap, in1=SN.ap[:, q, :], op=ALU.mult),
                 reads=[z[1], snq], writes=[Y[3]])
            for n_, (var, yi) in enumerate(((0, 0), (1, 1), (1, 2), (2, 3))):
                S.op("pe", lambda e: e.matmul(py.ap[:, 0:TT], lhsT=CB[:, q, var, :], rhs=Y[yi].ap,
                                              start=(ql == 0 and n_ == 0), stop=(ql == 3 and n_ == 3)),
                     reads=[CBr, Y[yi]], writes=[py])
            if ql == 3:
                us = A.sub(uT, cc * TT, cc * TT + TT)
                S.op("dve", lambda e: e.scalar_tensor_tensor(
                    out=ydt.ap, in0=uT.ap[:, cc, :], scalar=dcol.ap[:, cc:cc + 1], in1=py.ap[:, 0:TT],
                    op0=ALU.mult, op1=ALU.add), reads=[us, dcol, py], writes=[ydt])
                S.op("act", lambda e: e.activation(out=gT.ap[:, cc, :], in_=ydt.ap, func=ACTF.Gelu_apprx_tanh),
                     reads=[ydt], writes=[A.sub(gT, cc * TT, cc * TT + TT)])

        for s_ in range(32 + 2):
            if s_ < 32:
                stB(s_)
            if 0 <= s_ - 1 < 32:
                stC(s_ - 1)
            if 0 <= s_ - 2 < 32:
                stD(s_ - 2)
        gp += 32
        for c2 in range(8):
            pz = P.ps[6 + c2 % 2]
            for cc in range(8):
                S.op("pe", lambda e: e.matmul(
                    pz.ap[:, 0:TT], lhsT=Wglu.ap[:, cc, c2 * 128:(c2 + 1) * 128], rhs=gT.ap[:, cc, :],
                    start=(cc == 0), stop=(cc == 7)),
                    reads=[A.sub(gT, cc * TT, cc * TT + TT), WA.sub(Wglu, cc * 1024 + c2 * 128, cc * 1024 + c2 * 128 + 128)],
                    writes=[pz])
            S.op("act", lambda e: e.activation(out=sig.ap, in_=pz.ap[:, 0:TT], func=ACTF.Sigmoid, bias=bgl.ap[:, c2:c2 + 1]),
                 reads=[pz, bgl], writes=[sig])
            S.op("pool", lambda e: e.tensor_tensor(out=ggT.ap[:, c2, :], in0=gT.ap[:, c2, :], in1=sig.ap, op=ALU.mult),
                 reads=[A.sub(gT, c2 * TT, c2 * TT + TT), sig], writes=[A.sub(ggT, c2 * TT, c2 * TT + TT)])
        residual_out(P, xt, None, Wout, 8, lambda k, b: ggT.ap[:, k, b * 128:(b + 1) * 128],
                     lambda k: A.sub(ggT, k * TT, k * TT + TT), P.ps[2:6])
        S.dma("sp", out=xv[it], in_=xt.ap, reads=[xt], writes=[xb])


def host_consts():
    bf = ml_dtypes.bfloat16
    c = {}
    c["c_ident"] = np.eye(128, dtype=np.float32).astype(bf)
    p = np.arange(128)
    c["c_bones"] = ((p[:, None] // 64) == (p[None, :] // 64)).astype(np.float32).astype(bf) * np.float32(1.0 / 64.0)
    c["c_bones"] = c["c_bones"].astype(bf)
    rr = np.zeros((128, 128), np.float32)
    for cc in range(2):
        for d in range(64):
            mcol = cc * 64 + d
            if d < 32:
                rr[cc * 64 + d + 32, mcol] = -1.0
            else:
                rr[cc * 64 + d - 32, mcol] = 1.0
    c["c_rrot"] = rr.astype(bf)
    invf = (10000.0 ** (-np.arange(0, 64, 2, dtype=np.float32) / 64.0)).astype(np.float32)
    c["c_invf"] = (invf[(p % 64) % 32] / np.float32(2.0 * math.pi)).astype(np.float32).reshape(128, 1)
    c["c_triu"] = (p[:, None] <= p[None, :]).astype(np.float32).astype(bf)
    c["c_tril"] = (p[None, :] <= p[:, None]).astype(np.float32).astype(bf)
    c["c_tg0"] = np.broadcast_to(np.arange(256, dtype=np.float32)[None, :], (128, 256)).copy()
    par = np.zeros((128, 2), np.float32)
    par[:, 0] = ((p // 16) % 2 == 0)
    par[:, 1] = ((p // 16) % 2 == 1)
    c["c_par"] = par
    return c


def setup_consts(P):
    S, A = P.S, P.A
    C = {}
    A.reset()

    def ld(name, n, dtype, shape):
        d = P.din("c_" + name, shape, dtype)
        r = A.alloc16(n) if dtype == BF16 else A.alloc(n)
        S.dma("sp", out=r.ap, in_=d, writes=[r])
        C[name] = r

    ld("ident", 128, BF16, [128, 128])
    ld("bones", 128, BF16, [128, 128])
    ld("rrot", 128, BF16, [128, 128])
    ld("triu", 128, BF16, [128, 128])
    ld("tril", 128, BF16, [128, 128])
    ld("invf", 1, F32, [128, 1])
    ld("tg0", 256, F32, [128, 256])
    ld("par", 2, F32, [128, 2])
    C["mhalf"] = A.alloc(1)
    S.op("pool", lambda e: e.memset(C["mhalf"].ap, -0.5), writes=[C["mhalf"]])
    C["epscol"] = A.alloc(1)
    S.op("pool", lambda e: e.memset(C["epscol"].ap, EPS), writes=[C["epscol"]])
    A.off = ((A.off + 255) // 256) * 256
    C["a0"] = A.off
    P.C = C
    return C


def lambda_init_of(li):
    return 0.8 - 0.6 * math.exp(-0.3 * li)


def build(spec, ntok=SEQ, debug=False):
    P = Prog(ntok)
    P.DEBUG = debug
    S = P.S
    xin = P.din("x", [ntok, D])
    xout = P.dout("y", [ntok, D])
    setup_consts(P)
    nt = ntok // 256
    xiv = xin.rearrange("(n t) d -> n t d", t=256)
    xov = xout.rearrange("(n t) d -> n t d", t=256)
    for it in range(nt):
        S.dma("sp", out=xov[it], in_=xiv[it], writes=[P.db(f"x{it}")])
    sc = None
    posd = None
    preloaded = set()
    for si, (kind, li) in enumerate(spec):
        if kind == "mlp":
            phase_mlp(P, xout, P.din(f"mlp_w1_{li}", [D, 4 * D]) if li not in preloaded else None,
                      P.din(f"mlp_w2_{li}", [4 * D, D]) if li not in preloaded else None,
                      P.din(f"norm_mlp_{li}", [1, D]), preloaded=(li in preloaded))
            continue
        gamd = P.din(f"norm_mix_{li}", [1, D])
        mk = li % 3
        j = li // 3
        if mk == 0:
            if sc is None:
                sc = dict(qT=P.dscratch("sc_qT", [8, 128, ntok], BF16), kT=P.dscratch("sc_kT", [8, 128, ntok], BF16),
                          v=P.dscratch("sc_v", [ntok // 128, 128, 8 * 129], BF16),
                          oT=P.dscratch("sc_oT", [8, 128, ntok], BF16))
                posd = P.din("positions", [1, ntok], I32)
            phase_attn_qkv(P, xout, gamd, P.din(f"attn_w_qkv_{j}", [D, 3 * D]), P.din(f"attn_q_norm_{j}", [64, 1]),
                           P.din(f"attn_k_norm_{j}", [64, 1]), posd, sc)
            phase_attn_core(P, P.din(f"attn_lambda_{j}", [4, 64]), P.din(f"attn_sub_norm_{j}", [1, 128]),
                            lambda_init_of(li), sc)
            pf = None
            if si + 1 < len(spec) and spec[si + 1] == ("mlp", li):
                pf = (P.din(f"mlp_w1_{li}", [D, 4 * D]), P.din(f"mlp_w2_{li}", [4 * D, D]))
                preloaded.add(li)
            phase_attn_out(P, xout, P.din(f"attn_w_o_{j}", [D, D]), sc, prefetch=pf)
        elif mk == 1:
            phase_s5(P, xout, gamd, P.din(f"ssm_w_in_{j}", [D, D]), P.din(f"ssm_a_re_{j}", [64, 64]),
                     P.din(f"ssm_a_im_{j}", [64, 64]), P.din(f"ssm_b_re_{j}", [64, 64, 16]),
                     P.din(f"ssm_b_im_{j}", [64, 64, 16]), P.din(f"ssm_c_re_{j}", [64, 16, 64]),
                     P.din(f"ssm_c_im_{j}", [64, 16, 64]), P.din(f"ssm_d_{j}", [1, D]),
                     P.din(f"ssm_log_dt_{j}", [1, 64]), P.din(f"ssm_w_glu_{j}", [D, D]),
                     P.din(f"ssm_b_glu_{j}", [1, D]), P.din(f"ssm_w_out_{j}", [D, D]))
        else:
            phase_gmlp(P, xout, gamd, P.din(f"gm_w_in_{j}", [D, 6 * D]), P.din(f"gm_v_norm_{j}", [1, 3 * D]),
                       P.din(f"gm_w_s_{j}", [8, 128, 128]), P.din(f"gm_b_s_{j}", [8, 128]),
                       P.din(f"gm_w_out_{j}", [3 * D, D]))
    S.wait_bufs("sp", [P.db(f"x{it}") for it in range(nt)] + getattr(P, "dbg", []))
    return P


FULL_SPEC = [("mix", 0), ("mlp", 0), ("mix", 1), ("mlp", 1), ("mix", 2), ("mlp", 2), ("mix", 3), ("mlp", 3)]

_SHAPES = {
    "norm_mix": (1, D), "norm_mlp": (1, D), "attn_q_norm": (64, 1), "attn_k_norm": (64, 1), "attn_sub_norm": (1, 128),
    "ssm_d": (1, D), "ssm_log_dt": (1, 64), "ssm_b_glu": (1, D), "gm_v_norm": (1, 3 * D),
}


def make_in_map(P, inputs, b, ntok=SEQ):
    m = dict(host_consts())
    out = {}
    for name in P.dram_in:
        if name in m:
            out[name] = m[name]
        elif name == "x":
            out[name] = np.ascontiguousarray(inputs["x"][b, :ntok])
        elif name == "positions":
            out[name] = np.ascontiguousarray(inputs["positions"][b, :ntok].reshape(1, ntok)).astype(np.int32)
        else:
            base, idx = name.rsplit("_", 1)
            arr = np.asarray(inputs[base])[int(idx)]
            shp = _SHAPES.get(base)
            if shp is not None:
                arr = arr.reshape(shp)
            out[name] = np.ascontiguousarray(arr, dtype=np.float32)
    return out


_PROG = {}


def kernel(**inputs):
    if "full" not in _PROG:
        _PROG["full"] = build(FULL_SPEC, SEQ)
    P = _PROG["full"]
    inputs = {k: np.asarray(v) for k, v in inputs.items()}
    maps = [make_in_map(P, inputs, c % BATCH) for c in range(NCORES)]
    res = run_bass_kernel_spmd(P.nc, maps, core_ids=list(range(NCORES)))
    return np.stack([np.asarray(res.results[b]["y"], dtype=np.float32) for b in range(BATCH)], axis=0)
```

```python
import math
from contextlib import ExitStack

import numpy as np
import ml_dtypes

import concourse.bass as bass
import concourse.mybir as mybir
from concourse.bass_utils import run_bass_kernel_spmd

F32 = mybir.dt.float32
BF16 = mybir.dt.bfloat16
I32 = mybir.dt.int32
ALU = mybir.AluOpType
ACTF = mybir.ActivationFunctionType
AX = mybir.AxisListType

D = 1024
SEQ = 4096
BATCH = 4
DEPTH = 4
EPS = 1e-6
NCORES = 8


class Buf:
    __slots__ = ("name", "w", "r")

    def __init__(self, name):
        self.name = name
        self.w = None
        self.r = {}


class Region:
    def __init__(self, ap, bufs):
        self.ap = ap
        self.bufs = bufs


def _flat(items):
    out = []
    for it in items:
        if it is None:
            continue
        if isinstance(it, Buf):
            out.append(it)
        elif isinstance(it, Region):
            out.extend(it.bufs)
        else:
            out.extend(_flat(it))
    return out


class Sched:
    GEN = 24000
    POOL_INFLIGHT = 8

    def __init__(self, nc, stack, ndma=32):
        self.nc = nc
        self.stack = stack
        self.eng = dict(pe=nc.tensor, act=nc.scalar, dve=nc.vector, pool=nc.gpsimd, sp=nc.sync)
        self.stream = {k: [] for k in self.eng}
        self.cnt = {k: 0 for k in self.eng}
        self.gen = {k: 0 for k in self.eng}
        self.semh = {}
        for k in ("pe", "act", "dve", "pool"):
            self.semh[(k, 0)] = stack.enter_context(nc.semaphore(f"s_{k}0"))
        self.ndma = ndma
        for j in range(ndma):
            self.semh[("d", j)] = stack.enter_context(nc.semaphore(f"s_d{j}"))
        self.dcnt = [0] * ndma
        self.dnext = 0
        self.known = {k: {} for k in self.eng}
        self.ninstr = 0
        self.pool_hist = []

    def _collect(self, reads, writes):
        need = {}
        for b in reads:
            ev = b.w
            if ev is not None and need.get(ev[0], 0) < ev[1]:
                need[ev[0]] = ev[1]
        for b in writes:
            ev = b.w
            if ev is not None and need.get(ev[0], 0) < ev[1]:
                need[ev[0]] = ev[1]
            for k, v in b.r.items():
                if need.get(k, 0) < v:
                    need[k] = v
        return need

    def _waits(self, e, need, skip_self=False):
        kn = self.known[e]
        st = self.stream[e]
        for k, v in need.items():
            if skip_self and k[0] == e:
                continue
            if kn.get(k, 0) >= v:
                continue
            kn[k] = v
            sem = self.semh[k]
            self.eng[e].wait_ge(sem, v)
            self.ninstr += 1

    def op(self, e, fn, reads=(), writes=()):
        reads = _flat(reads)
        writes = _flat(writes)
        need = self._collect(reads, writes)
        self._waits(e, need, skip_self=(e == "pe"))
        if self.cnt[e] >= self.GEN:
            self.gen[e] += 1
            self.cnt[e] = 0
            self.semh[(e, self.gen[e])] = self.stack.enter_context(
                self.nc.semaphore(f"s_{e}{self.gen[e]}"))
        self.cnt[e] += 1
        key = (e, self.gen[e])
        v = self.cnt[e]
        sem = self.semh[key]
        fn(self.eng[e]).then_inc(sem, 1)
        self.ninstr += 1
        for b in reads:
            b.r[key] = v
        for b in writes:
            b.w = (key, v)
            b.r = {}

    def dma(self, q, out, in_, reads=(), writes=(), **kw):
        reads = _flat(reads)
        writes = _flat(writes)
        need = self._collect(reads, writes)
        if q == "pool":
            hist = self.pool_hist
            if len(hist) >= self.POOL_INFLIGHT:
                k0, v0 = hist[-self.POOL_INFLIGHT]
                if need.get(k0, 0) < v0:
                    need[k0] = v0
        self._waits(q, need)
        j = self.dnext
        self.dnext = (j + 1) % self.ndma
        self.dcnt[j] += 16
        key = ("d", j)
        v = self.dcnt[j]
        sem = self.semh[key]
        self.eng[q].dma_start(out=out, in_=in_, **kw).then_inc(sem, 16)
        self.ninstr += 1
        if q == "pool":
            self.pool_hist.append((key, v))
        for b in reads:
            b.r[key] = v
        for b in writes:
            b.w = (key, v)
            b.r = {}

    def wait_bufs(self, e, bufs):
        bufs = _flat(bufs)
        need = self._collect((), bufs)
        self._waits(e, need)

    def finish(self):
        return
        nc = self.nc
        with nc.Block() as block:
            for name, deco in (("pe", block.tensor), ("act", block.scalar), ("dve", block.vector),
                               ("pool", block.gpsimd), ("sp", block.sync)):
                lst = self.stream[name]

                @deco
                def _(eng, lst=lst):
                    for th in lst:
                        th(eng)


class Arena:
    def __init__(self, nc, stack, name, nelem, dtype, chunk):
        self.t = stack.enter_context(nc.sbuf_tensor(name, [128, nelem], dtype))
        self.n = nelem
        self.chunk = chunk
        self.bufs = [Buf(f"{name}{i}") for i in range((nelem + chunk - 1) // chunk)]
        self.off = 0
        self.name = name

    def reset(self, off=0):
        for b, cbs in getattr(self, "owned", []):
            for cb in cbs:
                if b.w is not None:
                    cb.r[b.w[0]] = max(cb.r.get(b.w[0], 0), b.w[1])
                for k, v in b.r.items():
                    cb.r[k] = max(cb.r.get(k, 0), v)
        self.owned = []
        self.off = off

    def own(self, reg):
        b = Buf("own")
        for cb in reg.bufs:
            if cb.w is not None:
                b.r[cb.w[0]] = max(b.r.get(cb.w[0], 0), cb.w[1])
            for k, v in cb.r.items():
                b.r[k] = max(b.r.get(k, 0), v)
        if not hasattr(self, "owned"):
            self.owned = []
        self.owned.append((b, reg.bufs))
        reg.bufs = [b]
        return reg

    def alloc(self, n, pattern=None, **kw):
        off = self.off
        assert off + n <= self.n, f"arena {self.name} overflow: {off}+{n} > {self.n}"
        self.off = off + n
        return self.view(off, n, pattern, **kw)

    def view(self, off, n, pattern=None, **kw):
        ap = self.t[:, off:off + n]
        if pattern is not None:
            ap = ap.rearrange(pattern, **kw)
        c0 = off // self.chunk
        c1 = (off + n - 1) // self.chunk
        r = Region(ap, self.bufs[c0:c1 + 1])
        r.off = off
        r.n = n
        r.arena = self
        return r

    def sub(self, reg, lo, hi):
        m = getattr(reg, "mul", 1)
        c0 = (reg.off + lo // m) // self.chunk
        c1 = (reg.off + (hi - 1) // m) // self.chunk
        return self.bufs[c0:c1 + 1]

    def alloc16(self, n, pattern=None, **kw):
        n32 = (n + 1) // 2
        off = self.off
        assert off + n32 <= self.n, f"arena {self.name} overflow: {off}+{n32} > {self.n}"
        self.off = off + n32
        ap = self.t[:, off:off + n32].bitcast(BF16)[:, 0:n]
        if pattern is not None:
            ap = ap.rearrange(pattern, **kw)
        c0 = off // self.chunk
        c1 = (off + n32 - 1) // self.chunk
        r = Region(ap, self.bufs[c0:c1 + 1])
        r.off = off; r.n = n32; r.arena = self; r.mul = 2
        return r

    def alloc32(self, n, pattern=None, **kw):
        r = self.alloc(2 * n)
        ap = r.ap.bitcast(F32)
        if pattern is not None:
            ap = ap.rearrange(pattern, **kw)
        r.ap = ap
        return r

    def alloci(self, n, pattern=None, **kw):
        r = self.alloc(n)
        ap = r.ap.bitcast(I32)
        if pattern is not None:
            ap = ap.rearrange(pattern, **kw)
        r.ap = ap
        return r


class Prog:
    def __init__(self, ntok=SEQ):
        self.ntok = ntok
        self.nc = bass.Bass("TRN2", target_bir_lowering=False)
        self.stack = ExitStack()
        nc = self.nc
        self.S = Sched(nc, self.stack)
        st = self.stack
        self.WA = Arena(nc, st, "wa", 72 * 1024, BF16, 2048)
        self.A = Arena(nc, st, "aa", 15 * 1024 + 512, F32, 256)
        self.psum_t = st.enter_context(nc.psum_tensor("ps", [128, 8, 512], F32))
        self.ps = [Region(self.psum_t[:, b, :], [Buf(f"ps{b}")]) for b in range(8)]
        self.dram_in = {}
        self.consts = {}
        self.dbuf = {}

    def din(self, name, shape, dtype=F32):
        t = self.nc.dram_tensor(name, list(shape), dtype, kind="ExternalInput").ap()
        self.dram_in[name] = t
        return t

    def dout(self, name, shape, dtype=F32):
        return self.nc.dram_tensor(name, list(shape), dtype, kind="ExternalOutput").ap()

    def dscratch(self, name, shape, dtype):
        return self.nc.dram_tensor(name, list(shape), dtype, kind="Internal").ap()

    def debug(self, name, reg, shape, dtype):
        d = self.nc.dram_tensor("dbg_" + name, list(shape), dtype, kind="ExternalOutput").ap()
        self.S.dma("sp", out=d, in_=reg.ap, reads=[reg], writes=[self.db("dbg_" + name)])
        self.dbg = getattr(self, "dbg", [])
        self.dbg.append(self.db("dbg_" + name))

    def db(self, name):
        b = self.dbuf.get(name)
        if b is None:
            b = self.dbuf[name] = Buf(name)
        return b


def ps_bf16(reg):
    return reg.ap.bitcast(BF16)


def emit_norm_T(P, xt_ap, xt_bufs, gam, hT, tok_off, ident, scr):
    S = P.S
    sq, ss, rstd, xn, pst = scr["sq"], scr["ss"], scr["rstd"], scr["xn"], scr["pst"]
    S.op("act", lambda e: e.activation(out=sq.ap, in_=xt_ap, func=ACTF.Square, scale=1.0 / 32.0,
                                       accum_out=ss.ap),
         reads=[xt_bufs], writes=[sq, ss])
    S.op("pool", lambda e: e.tensor_scalar(out=ss.ap, in0=ss.ap, scalar1=EPS, scalar2=None, op0=ALU.add),
         reads=[ss], writes=[ss])
    S.op("pool", lambda e: e.tensor_tensor(out=rstd.ap, in0=ss.ap, in1=P.C["mhalf"].ap, op=ALU.pow),
         reads=[ss, P.C["mhalf"]], writes=[rstd])
    S.op("dve", lambda e: e.scalar_tensor_tensor(out=xn.ap, in0=xt_ap, scalar=rstd.ap, in1=gam.ap,
                                                 op0=ALU.mult, op1=ALU.mult),
         reads=[xt_bufs, rstd, gam], writes=[xn])
    pv = ps_bf16(pst)
    for kc in range(8):
        S.op("pe", lambda e, kc=kc: e.transpose(out=pv[:, kc * 128:(kc + 1) * 128],
                                                in_=xn.ap[:, kc * 128:(kc + 1) * 128],
                                                identity=ident.ap),
             reads=[xn, ident], writes=[pst])
    S.op("act", lambda e: e.activation(out=hT.ap[:, :, tok_off:tok_off + 128],
                                       in_=pv.rearrange("p (k t) -> p k t", k=8), func=ACTF.Copy),
         reads=[pst], writes=[hT])


def load_weight_fast(P, reg, dram_ap, nk, ncols, nstage=4, piece=2048):
    S, A = P.S, P.A
    piece = min(piece, ncols)
    off_keep = A.off
    A.off = A.n - nstage * piece
    stg = [A.alloc(piece) for _ in range(nstage)]
    A.off = off_keep
    i = 0
    for kc in range(nk):
        for c0 in range(0, ncols, piece):
            c1 = min(ncols, c0 + piece)
            st = stg[i % nstage]
            S.dma("sp", out=st.ap[:, 0:c1 - c0], in_=dram_ap[kc * 128:(kc + 1) * 128, c0:c1], writes=[st])
            lo = kc * ncols + c0
            dst = reg.arena.sub(reg, lo, lo + (c1 - c0))
            if i % 2 == 0:
                S.op("act", lambda e: e.activation(out=reg.ap[:, kc, c0:c1], in_=st.ap[:, 0:c1 - c0], func=ACTF.Copy),
                     reads=[st], writes=[dst])
            else:
                S.op("dve", lambda e: e.tensor_copy(out=reg.ap[:, kc, c0:c1], in_=st.ap[:, 0:c1 - c0]),
                     reads=[st], writes=[dst])
            i += 1


def load_weight(P, reg, dram_ap, rows_per_part_dim, ncols, q="pool"):
    S = P.S
    nk = rows_per_part_dim
    for kc in range(nk):
        for c0 in range(0, ncols, 2048):
            c1 = min(ncols, c0 + 2048)
            lo = kc * ncols + c0
            S.dma(q, out=reg.ap[:, kc, c0:c1], in_=dram_ap[kc * 128:(kc + 1) * 128, c0:c1],
                  writes=[reg.arena.sub(reg, lo, lo + (c1 - c0))])


TWO_PI = 6.283179


def norm_scratch(P):
    A = P.A
    return dict(sq=A.alloc16(1024), ss=A.alloc(1), rstd=A.alloc(1), xn=A.alloc16(1024))


def load_gamma(P, gamd):
    gam = P.A.alloc(1024)
    P.S.dma("sp", out=gam.ap, in_=gamd.partition_broadcast(128), writes=[gam])
    return gam


def residual_out(P, xt, xb_ap_fn, W, nk, lhs_fn, lhs_reads_fn, banks):
    S = P.S
    nb = xt.ap.shape[1]
    i = 0
    for b in range(nb):
        for oc in range(2):
            pb = banks[i % len(banks)]
            i += 1
            for k in range(nk):
                S.op("pe", lambda e, k=k, b=b, oc=oc, pb=pb: e.matmul(
                    pb.ap, lhsT=lhs_fn(k, b), rhs=W.ap[:, k, oc * 512:(oc + 1) * 512],
                    start=(k == 0), stop=(k == nk - 1)),
                    reads=[lhs_reads_fn(k), P.WA.sub(W, k * 1024 + oc * 512, k * 1024 + oc * 512 + 512)],
                    writes=[pb])
            S.op("dve", lambda e, b=b, oc=oc, pb=pb: e.tensor_tensor(
                out=xt.ap[:, b, oc * 512:(oc + 1) * 512], in0=pb.ap,
                in1=xt.ap[:, b, oc * 512:(oc + 1) * 512], op=ALU.add),
                reads=[pb, xt], writes=[xt])


def phase_mlp(P, xd, w1d, w2d, gamd, preloaded=False):
    S, WA, A, C = P.S, P.WA, P.A, P.C
    TT = 256
    NTT = P.ntok // TT
    WA.reset(); A.reset(C["a0"])
    W1 = WA.alloc(8 * 4096, "p (k f) -> p k f", k=8)
    W2 = WA.alloc(32 * 1024, "p (k f) -> p k f", k=32)
    if not preloaded:
        load_weight_fast(P, W1, w1d, 8, 4096)
        load_weight_fast(P, W2, w2d, 32, 1024, piece=1024)
    gam = load_gamma(P, gamd)
    xts = [A.alloc(2 * 1024, "p (b d) -> p b d", b=2) for _ in range(2)]
    scr = norm_scratch(P)
    sqfs = [A.own(A.alloc(256)) for _ in range(3)]
    hTs = [A.alloc16(8 * TT, "p (k t) -> p k t", k=8) for _ in range(2)]
    actT = A.alloc16(32 * TT, "p (f t) -> p f t", f=32)
    ident = C["ident"]
    xv = xd.rearrange("(n b p) d -> n p b d", p=128, b=2)
    groups = [list(range(g, min(g + 3, 32))) for g in range(0, 32, 3)]

    def load_norm(it):
        xt = xts[it % 2]; hT = hTs[it % 2]
        S.dma("sp", out=xt.ap, in_=xv[it], reads=[P.db(f"x{it}")], writes=[xt])
        for b in range(2):
            emit_norm_T(P, xt.ap[:, b, :], xt, gam, hT, b * 128, ident, dict(scr, pst=P.ps[6 + b]))

    load_norm(0)
    nsq = 0
    for it in range(NTT):
        xt = xts[it % 2]; hT = hTs[it % 2]
        xb = P.db(f"x{it}")
        for gi, grp in enumerate(groups):
            banks = [P.ps[(gi % 2) * 3 + n] for n in range(len(grp))]
            for kc in range(8):
                for n, fc in enumerate(grp):
                    pb = banks[n]
                    S.op("pe", lambda e: e.matmul(
                        pb.ap[:, 0:TT], lhsT=W1.ap[:, kc, fc * 128:(fc + 1) * 128], rhs=hT.ap[:, kc, :],
                        start=(kc == 0), stop=(kc == 7)),
                        reads=[WA.sub(W1, kc * 4096 + fc * 128, kc * 4096 + fc * 128 + 128), hT], writes=[pb])
            for n, fc in enumerate(grp):
                pb = banks[n]
                sqf = sqfs[nsq % 3]
                nsq += 1
                S.op("act", lambda e: e.activation(out=sqf.ap, in_=pb.ap[:, 0:TT], func=ACTF.Relu),
                     reads=[pb], writes=[sqf])
                S.op("dve", lambda e: e.tensor_tensor(out=actT.ap[:, fc, :], in0=sqf.ap, in1=sqf.ap, op=ALU.mult),
                     reads=[sqf], writes=[A.sub(actT, fc * TT, fc * TT + TT)])
        if it + 1 < NTT:
            load_norm(it + 1)
        residual_out(P, xt, None, W2, 32,
                     lambda k, b: actT.ap[:, k, b * 128:(b + 1) * 128],
                     lambda k: A.sub(actT, k * TT, k * TT + TT), P.ps[0:4])
        S.dma("sp", out=xv[it], in_=xt.ap, reads=[xt], writes=[xb])


def phase_gmlp(P, xd, gamd, wind, vgd, wsd, bsd, woutd):
    S, WA, A, C = P.S, P.WA, P.A, P.C
    TT = 256
    NTT = P.ntok // TT
    WA.reset(); A.reset(C["a0"])
    Win = WA.alloc(8 * 6144, "p (k f) -> p k f", k=8)
    Wout = WA.alloc(24 * 1024, "p (k f) -> p k f", k=24)
    load_weight_fast(P, Win, wind, 8, 6144)
    load_weight_fast(P, Wout, woutd, 24, 1024, piece=1024)
    gam = load_gamma(P, gamd)
    ident = C["ident"]
    WmT = A.alloc16(1024, "p (h t) -> p h t", h=8)
    vg = A.alloc(24)
    bs = A.alloc(1024, "p (h t) -> p h t", h=8)
    off_keep = A.off
    A.off = A.n - 512
    wn = A.alloc16(1024, "p (h s) -> p h s", h=8)
    A.off = off_keep
    S.dma("pool", out=wn.ap, in_=wsd.rearrange("h t s -> t h s"), writes=[wn])
    S.op("dve", lambda e: e.tensor_tensor(out=wn.ap, in0=wn.ap,
                                          in1=C["tril"].ap.unsqueeze(1).broadcast_to([128, 8, 128]), op=ALU.mult),
         reads=[wn, C["tril"]], writes=[wn])
    pst = P.ps[7]
    pv = ps_bf16(pst)
    for h in range(8):
        S.op("pe", lambda e, h=h: e.transpose(out=pv[:, h * 128:(h + 1) * 128], in_=wn.ap[:, h, :], identity=ident.ap),
             reads=[wn, ident], writes=[pst])
    S.op("act", lambda e: e.activation(out=WmT.ap, in_=pv.rearrange("p (h t) -> p h t", h=8), func=ACTF.Copy),
         reads=[pst], writes=[WmT])
    S.dma("sp", out=vg.ap, in_=vgd.rearrange("o (c p) -> p (o c)", p=128), writes=[vg], allow_slow_non_contiguous=True)
    S.dma("sp", out=bs.ap, in_=bsd.rearrange("h t -> (h t)").partition_broadcast(128), writes=[bs])
    xts = [A.alloc(2 * 1024, "p (b d) -> p b d", b=2)] * 2
    scr = norm_scratch(P)
    hT = A.alloc16(8 * TT, "p (k t) -> p k t", k=8)
    gv = [A.alloc16(3072) for _ in range(2)]
    ssq = [A.alloc(8) for _ in range(2)]
    sst = [A.alloc(1) for _ in range(2)]
    rsv = [A.alloc(1) for _ in range(2)]
    junk = scr["sq"]
    WmTs = [A.alloc16(1024, "p (h t) -> p h t", h=8) for _ in range(2)]
    ug = [A.alloc(TT) for _ in range(2)]
    tmp = [A.alloc(TT) for _ in range(2)]
    uvT = A.alloc16(24 * TT, "p (c t) -> p c t", c=24)
    xv = xd.rearrange("(n b p) d -> n p b d", p=128, b=2)
    for it in range(NTT):
        xt = xts[it % 2]
        xb = P.db(f"x{it}")
        S.dma("sp", out=xt.ap, in_=xv[it], reads=[xb], writes=[xt])
        for b in range(2):
            emit_norm_T(P, xt.ap[:, b, :], xt, gam, hT, b * 128, ident, dict(scr, pst=P.ps[7]))
        for b in range(2):
            for j in range(6):
                pb = P.ps[(b * 6 + j) % 2]
                for kc in range(8):
                    c0 = 3072 + j * 512
                    S.op("pe", lambda e, kc=kc, b=b, pb=pb, c0=c0: e.matmul(
                        pb.ap, lhsT=hT.ap[:, kc, b * 128:(b + 1) * 128], rhs=Win.ap[:, kc, c0:c0 + 512],
                        start=(kc == 0), stop=(kc == 7)),
                        reads=[hT, WA.sub(Win, kc * 6144 + c0, kc * 6144 + c0 + 512)], writes=[pb])
                gs = A.sub(gv[b], j * 512, j * 512 + 512)
                S.op("act", lambda e, b=b, j=j, pb=pb: e.activation(
                    out=gv[b].ap[:, j * 512:(j + 1) * 512], in_=pb.ap, func=ACTF.Gelu_apprx_tanh),
                    reads=[pb], writes=[gs])
                S.op("dve", lambda e, b=b, j=j: e.scalar_tensor_tensor(
                    out=junk.ap[:, 0:512], in0=gv[b].ap[:, j * 512:(j + 1) * 512], scalar=1.0,
                    in1=gv[b].ap[:, j * 512:(j + 1) * 512], op0=ALU.mult, op1=ALU.mult,
                    accum_out=ssq[b].ap[:, j:j + 1]),
                    reads=[gs], writes=[junk, ssq[b]])
            S.op("dve", lambda e, b=b: e.tensor_reduce(out=sst[b].ap, in_=ssq[b].ap[:, 0:6], axis=AX.X, op=ALU.add),
                 reads=[ssq[b]], writes=[sst[b]])
            S.op("pool", lambda e, b=b: e.tensor_scalar(out=sst[b].ap, in0=sst[b].ap, scalar1=1.0 / 3072.0, scalar2=EPS,
                                                        op0=ALU.mult, op1=ALU.add),
                 reads=[sst[b]], writes=[sst[b]])
            S.op("pool", lambda e, b=b: e.tensor_tensor(out=rsv[b].ap, in0=sst[b].ap, in1=C["mhalf"].ap, op=ALU.pow),
                 reads=[sst[b], C["mhalf"]], writes=[rsv[b]])
            S.op("pool", lambda e, b=b: e.tensor_scalar(out=WmTs[b].ap, in0=WmT.ap, scalar1=rsv[b].ap, scalar2=None,
                                                        op0=ALU.mult),
                 reads=[WmT, rsv[b]], writes=[WmTs[b]])
        for c in range(24):
            hh = c // 3
            pu = P.ps[2 + c % 2]
            for kc in range(8):
                S.op("pe", lambda e, kc=kc, c=c, pu=pu: e.matmul(
                    pu.ap[:, 0:TT], lhsT=Win.ap[:, kc, c * 128:(c + 1) * 128], rhs=hT.ap[:, kc, :],
                    start=(kc == 0), stop=(kc == 7)),
                    reads=[hT, WA.sub(Win, kc * 6144 + c * 128, kc * 6144 + c * 128 + 128)], writes=[pu])
            S.op("act", lambda e, c=c, pu=pu: e.activation(out=ug[c % 2].ap, in_=pu.ap[:, 0:TT], func=ACTF.Gelu_apprx_tanh),
                 reads=[pu], writes=[ug[c % 2]])
            pg = P.ps[4 + c % 2]
            for b in range(2):
                S.op("pe", lambda e, c=c, b=b, pg=pg, hh=hh: e.matmul(
                    pg.ap[:, b * 128:(b + 1) * 128], lhsT=gv[b].ap[:, c * 128:(c + 1) * 128], rhs=WmTs[b].ap[:, hh, :],
                    start=True, stop=True),
                    reads=[A.sub(gv[b], c * 128, c * 128 + 128), WmTs[b]], writes=[pg])
            for b in range(2):
                S.op("dve", lambda e, c=c, b=b, pg=pg, hh=hh: e.scalar_tensor_tensor(
                    out=tmp[c % 2].ap[:, b * 128:(b + 1) * 128], in0=pg.ap[:, b * 128:(b + 1) * 128],
                    scalar=vg.ap[:, c:c + 1], in1=bs.ap[:, hh, :], op0=ALU.mult, op1=ALU.add),
                    reads=[pg, vg, bs], writes=[tmp[c % 2]])
            S.op("pool", lambda e, c=c: e.tensor_tensor(out=uvT.ap[:, c, :], in0=tmp[c % 2].ap, in1=ug[c % 2].ap, op=ALU.mult),
                 reads=[tmp[c % 2], ug[c % 2]], writes=[A.sub(uvT, c * TT, c * TT + TT)])
        residual_out(P, xt, None, Wout, 24,
                     lambda k, b: uvT.ap[:, k, b * 128:(b + 1) * 128],
                     lambda k: A.sub(uvT, k * TT, k * TT + TT), P.ps[0:2] + [P.ps[6]])
        S.dma("sp", out=xv[it], in_=xt.ap, reads=[xt], writes=[xb])


def emit_sincos(P, yy, sn, cs, ki, fr):
    S = P.S
    S.op("dve", lambda e: e.tensor_copy(out=ki.ap, in_=yy.ap), reads=[yy], writes=[ki])
    S.op("dve", lambda e: e.tensor_tensor(out=fr.ap, in0=yy.ap, in1=ki.ap, op=ALU.subtract),
         reads=[yy, ki], writes=[fr])
    S.op("act", lambda e: e.activation(out=sn.ap, in_=fr.ap, func=ACTF.Sin, scale=TWO_PI), reads=[fr], writes=[sn])
    S.op("dve", lambda e: e.tensor_scalar(out=ki.ap, in0=yy.ap, scalar1=0.25, scalar2=None, op0=ALU.add),
         reads=[yy, fr], writes=[ki])
    S.op("dve", lambda e: e.scalar_tensor_tensor(out=fr.ap, in0=yy.ap, scalar=0.25, in1=ki.ap,
                                                 op0=ALU.add, op1=ALU.subtract),
         reads=[yy, ki, sn], writes=[fr])
    S.op("act", lambda e: e.activation(out=cs.ap, in_=fr.ap, func=ACTF.Sin, scale=TWO_PI), reads=[fr], writes=[cs])


def phase_attn_qkv(P, xd, gamd, wqkvd, qgd, kgd, posd, sc):
    S, WA, A, C = P.S, P.WA, P.A, P.C
    TT = 512
    NB = TT // 128
    NTT = P.ntok // TT
    WA.reset(); A.reset(C["a0"])
    W = WA.alloc(8 * 3072, "p (k f) -> p k f", k=8)
    load_weight_fast(P, W, wqkvd, 8, 3072, piece=1536)
    gam = load_gamma(P, gamd)
    ident = C["ident"]
    gcol = A.alloc(2)
    for idx, gd in enumerate((qgd, kgd)):
        for half in range(2):
            S.dma("sp", out=gcol.ap[half * 64:(half + 1) * 64, idx:idx + 1], in_=gd, writes=[gcol])
    A.off = ((A.off + 255) // 256) * 256
    xt = A.alloc(NB * 1024, "p (b d) -> p b d", b=NB)
    scr = norm_scratch(P)
    hT = A.alloc16(8 * TT, "p (k t) -> p k t", k=8)
    posb = A.alloci(TT)
    yy = A.alloc(TT); ki = A.alloci(TT); fr = A.alloc(TT); sn = A.alloc(TT); cs = A.alloc(TT)
    va = [A.alloc16(8 * 129, "p (h e) -> p h e", h=8) for _ in range(2)]
    NS = 3
    sqb = [WA.own(WA.alloc(TT)) for _ in range(NS)]
    sd = [WA.own(WA.alloc32(TT)) for _ in range(NS)]
    rs = [WA.own(WA.alloc32(TT)) for _ in range(NS)]
    qn = [WA.own(WA.alloc(TT)) for _ in range(NS)]
    t1 = [WA.own(WA.alloc32(TT)) for _ in range(NS)]
    t2 = [WA.own(WA.alloc32(TT)) for _ in range(NS)]
    qf = [WA.own(WA.alloc(TT)) for _ in range(NS)]
    for v in va:
        S.op("pool", lambda e, v=v: e.memset(v.ap, 1.0), writes=[v])
    xv = xd.rearrange("(n b p) d -> n p b d", p=128, b=NB)
    pqb = P.ps[0:3]; pmb = P.ps[3:5]; prb = P.ps[5:7]
    combos = [(which, h) for which in range(2) for h in range(8)]
    gi = 0
    for it in range(NTT):
        xb = [P.db(f"x{it * 2}"), P.db(f"x{it * 2 + 1}")]
        S.dma("sp", out=xt.ap, in_=xv[it], reads=xb, writes=[xt])
        S.dma("sp", out=posb.ap, in_=posd[:, it * TT:(it + 1) * TT].partition_broadcast(128), writes=[posb])
        for b in range(NB):
            emit_norm_T(P, xt.ap[:, b, :], xt, gam, hT, b * 128, ident, dict(scr, pst=P.ps[7]))
        S.op("dve", lambda e: e.tensor_scalar(out=yy.ap, in0=posb.ap, scalar1=C["invf"].ap, scalar2=None, op0=ALU.mult),
             reads=[posb, C["invf"]], writes=[yy])
        emit_sincos(P, yy, sn, cs, ki, fr)

        def st0(i):
            which, h = combos[i]
            col0 = which * 1024 + h * 128
            pq = pqb[(gi + i) % 3]
            for kc in range(8):
                S.op("pe", lambda e: e.matmul(pq.ap, lhsT=W.ap[:, kc, col0:col0 + 128], rhs=hT.ap[:, kc, :],
                                              start=(kc == 0), stop=(kc == 7)),
                     reads=[hT, WA.sub(W, kc * 3072 + col0, kc * 3072 + col0 + 128)], writes=[pq])

        def st1(i):
            pq = pqb[(gi + i) % 3]; pm = pmb[(gi + i) % 2]; s_ = sqb[(gi + i) % NS]
            S.op("act", lambda e: e.activation(out=s_.ap, in_=pq.ap, func=ACTF.Square), reads=[pq], writes=[s_])
            S.op("pe", lambda e: e.matmul(pm.ap, lhsT=C["bones"].ap, rhs=s_.ap, start=True, stop=True),
                 reads=[s_, C["bones"]], writes=[pm])

        def st2(i):
            which, h = combos[i]
            k = (gi + i) % NS
            pq = pqb[(gi + i) % 3]; pm = pmb[(gi + i) % 2]; pr = prb[(gi + i) % 2]
            S.op("act", lambda e: e.activation(out=sd[k].ap, in_=pm.ap, func=ACTF.Ln, bias=C["epscol"].ap),
                 reads=[pm, C["epscol"]], writes=[sd[k]])
            S.op("act", lambda e: e.activation(out=rs[k].ap, in_=sd[k].ap, func=ACTF.Exp, scale=-0.5),
                 reads=[sd[k]], writes=[rs[k]])
            S.op("dve", lambda e: e.scalar_tensor_tensor(out=qn[k].ap, in0=pq.ap, scalar=gcol.ap[:, which:which + 1],
                                                         in1=rs[k].ap, op0=ALU.mult, op1=ALU.mult),
                 reads=[pq, gcol, rs[k]], writes=[qn[k]])
            S.op("pe", lambda e: e.matmul(pr.ap, lhsT=C["rrot"].ap, rhs=qn[k].ap, start=True, stop=True),
                 reads=[qn[k], C["rrot"]], writes=[pr])

        def st3(i):
            which, h = combos[i]
            k = (gi + i) % NS
            pr = prb[(gi + i) % 2]
            S.op("pool", lambda e: e.tensor_tensor(out=t1[k].ap, in0=qn[k].ap, in1=cs.ap, op=ALU.mult),
                 reads=[qn[k], cs], writes=[t1[k]])
            S.op("dve", lambda e: e.tensor_tensor(out=t2[k].ap, in0=pr.ap, in1=sn.ap, op=ALU.mult),
                 reads=[pr, sn], writes=[t2[k]])
            S.op("pool", lambda e: e.tensor_tensor(out=qf[k].ap, in0=t1[k].ap, in1=t2[k].ap, op=ALU.add),
                 reads=[t1[k], t2[k]], writes=[qf[k]])
            dst = sc["qT"] if which == 0 else sc["kT"]
            S.dma("sp", out=dst[h, :, it * TT:(it + 1) * TT], in_=qf[k].ap, reads=[qf[k]],
                  writes=[P.db(f"{'qk'[which]}T{h}_{it}")])

        n = len(combos)
        for s in range(n + 3):
            if s < n:
                st0(s)
            if 0 <= s - 1 < n:
                st1(s - 1)
            if 0 <= s - 2 < n:
                st2(s - 2)
            if 0 <= s - 3 < n:
                st3(s - 3)
        gi += n
        for b in range(NB):
            v = va[b % 2]
            for jj in range(2):
                pvb = P.ps[7]
                for kc in range(8):
                    c0 = 2048 + jj * 512
                    S.op("pe", lambda e: e.matmul(pvb.ap, lhsT=hT.ap[:, kc, b * 128:(b + 1) * 128], rhs=W.ap[:, kc, c0:c0 + 512],
                                                  start=(kc == 0), stop=(kc == 7)),
                         reads=[hT, WA.sub(W, kc * 3072 + c0, kc * 3072 + c0 + 512)], writes=[pvb])
                S.op("act", lambda e: e.activation(
                    out=v.ap[:, 4 * jj:4 * jj + 4, 0:128], in_=pvb.ap.rearrange("p (h e) -> p h e", h=4), func=ACTF.Copy),
                    reads=[pvb], writes=[v])
            blk = it * NB + b
            S.dma("sp", out=sc["v"][blk], in_=v.ap, reads=[v], writes=[P.db(f"v{blk}")])


def phase_attn_core(P, lamd, sgd, lambda_init, sc):
    S, WA, A, C = P.S, P.WA, P.A, P.C
    ntok = P.ntok
    NB = ntok // 128
    NG = ntok // 512
    NT256 = ntok // 512
    WA.reset(); A.reset(C["a0"])
    K0 = [WA.alloc(ntok) for _ in range(2)]
    K1 = [WA.alloc(ntok) for _ in range(2)]
    QT = [WA.alloc(ntok) for _ in range(2)]
    VA = [WA.alloc(NB * 128, "p (n e) -> p n e", e=128) for _ in range(2)]
    ones16 = WA.alloc(128)
    onesb = WA.alloc(128)
    S.op("pool", lambda e: e.memset(ones16.ap, 1.0), writes=[ones16])
    S.op("pool", lambda e: e.memset(onesb.ap, 1.0 / 128.0), writes=[onesb])
    for i in range(2):
        S.op("pool", lambda e, i=i: e.memset(K0[i].ap[64:128, :], 0.0), writes=[K0[i]])
        S.op("pool", lambda e, i=i: e.memset(K1[i].ap[0:64, :], 0.0), writes=[K1[i]])
    L = A.alloc(256, "p (a d) -> p a d", a=4)
    S.dma("sp", out=L.ap, in_=lamd.rearrange("a d -> (a d)").partition_broadcast(128), writes=[L])
    lj = A.alloc(64); s12 = A.alloc(2); e12 = A.alloc(2); neglam = A.alloc(1)
    for a in range(2):
        S.op("dve", lambda e, a=a: e.scalar_tensor_tensor(
            out=lj.ap, in0=L.ap[:, 2 * a, :], scalar=1.0, in1=L.ap[:, 2 * a + 1, :], op0=ALU.mult, op1=ALU.mult,
            accum_out=s12.ap[:, a:a + 1]), reads=[L], writes=[lj, s12])
    S.op("act", lambda e: e.activation(out=e12.ap, in_=s12.ap, func=ACTF.Exp), reads=[s12], writes=[e12])
    S.op("dve", lambda e: e.tensor_tensor(out=neglam.ap, in0=e12.ap[:, 1:2], in1=e12.ap[:, 0:1], op=ALU.subtract),
         reads=[e12], writes=[neglam])
    S.op("dve", lambda e: e.tensor_scalar(out=neglam.ap, in0=neglam.ap, scalar1=-float(lambda_init), scalar2=None,
                                          op0=ALU.add), reads=[neglam], writes=[neglam])
    sgc = A.alloc(1)
    S.dma("sp", out=sgc.ap, in_=sgd.rearrange("o e -> e o"), writes=[sgc], allow_slow_non_contiguous=True)
    S.op("dve", lambda e: e.tensor_scalar(out=sgc.ap, in0=sgc.ap, scalar1=float(1.0 - lambda_init), scalar2=None,
                                          op0=ALU.mult), reads=[sgc], writes=[sgc])
    A.off = ((A.off + 255) // 256) * 256
    PT = [[A.alloc16(512) for _ in range(2)] for _ in range(2)]
    ob = [[A.alloc(512) for _ in range(2)] for _ in range(2)]
    rl = [A.alloc(512) for _ in range(2)]
    tt = A.alloc(512); uu = A.alloc(512); oo = A.alloc(512)
    osq = A.alloc16(512); msb = A.alloc(512); rs = A.alloc(512)
    oT = [A.alloc16(512) for _ in range(2)]
    sb3 = [P.ps[0], P.ps[1], P.ps[2]]
    otb = [P.ps[3], P.ps[4]]
    plb = [P.ps[5], P.ps[6]]
    pmb = P.ps[7]
    lsb = [[A.alloc(512) for _ in range(2)] for _ in range(2)]
    mhb = C["mhalf"].ap.broadcast_to([128, 512])
    ng = 0
    for h in range(8):
        buf = h % 2
        S.dma("sp", out=K0[buf].ap[0:64, :], in_=sc["kT"][h, 0:64, :],
              reads=[P.db(f"kT{h}_{it}") for it in range(NT256)], writes=[K0[buf]])
        S.dma("sp", out=K1[buf].ap[64:128, :], in_=sc["kT"][h, 64:128, :],
              reads=[P.db(f"kT{h}_{it}") for it in range(NT256)], writes=[K1[buf]])
        S.dma("sp", out=QT[buf].ap, in_=sc["qT"][h],
              reads=[P.db(f"qT{h}_{it}") for it in range(NT256)], writes=[QT[buf]])
        S.dma("sp", out=VA[buf].ap, in_=sc["v"].rearrange("n p (h e) -> h p n e", h=8)[h][:, :, 0:128],
              reads=[P.db(f"v{blk}") for blk in range(NB)], writes=[VA[buf]])
        for G in range(NG):
            gb = ng % 2
            ng += 1
            njb = 4 * G + 4

            def geom(jb):
                nq0 = max(0, jb - 4 * G)
                return nq0, (4 - nq0) * 128, G * 512 + nq0 * 128

            def stage_a(jb, cs_=(0, 1)):
                nq0, N, qc0 = geom(jb)
                for c in cs_:
                    Kc = (K0 if c == 0 else K1)[buf]
                    pss = sb3[(2 * jb + c) % 3]
                    S.op("pe", lambda e: e.matmul(
                        pss.ap[:, 0:N], lhsT=Kc.ap[:, jb * 128:(jb + 1) * 128], rhs=QT[buf].ap[:, qc0:qc0 + N],
                        start=True, stop=True),
                        reads=[WA.sub(Kc, jb * 128, jb * 128 + 128), WA.sub(QT[buf], qc0, qc0 + N)], writes=[pss])

            def stage_b(jb, cs_=(0, 1)):
                nq0, N, qc0 = geom(jb)
                c0 = nq0 * 128
                for c in cs_:
                    pss = sb3[(2 * jb + c) % 3]
                    pt = PT[jb % 2][c]
                    S.op("act", lambda e: e.activation(out=pt.ap[:, 0:N], in_=pss.ap[:, 0:N], func=ACTF.Exp, scale=0.125),
                         reads=[pss], writes=[pt])
                    eng = "dve" if c == 0 else "pool"
                    if jb >= 4 * G:
                        S.op("dve", lambda e: e.tensor_tensor(out=pt.ap[:, 0:128], in0=pt.ap[:, 0:128],
                                                               in1=C["triu"].ap, op=ALU.mult),
                             reads=[pt, C["triu"]], writes=[pt])

            def stage_c(jb):
                nq0, N, qc0 = geom(jb)
                c0 = nq0 * 128
                for c in range(2):
                    pt = PT[jb % 2][c]
                    S.op("pe", lambda e: e.matmul(
                        otb[c].ap[:, c0:512], lhsT=VA[buf].ap[:, jb, :], rhs=pt.ap[:, 0:N],
                        start=(jb == 0), stop=(jb == njb - 1)),
                        reads=[pt, WA.sub(VA[buf], jb * 128, jb * 128 + 128)], writes=[otb[c]])
                    S.op("pe", lambda e: e.matmul(
                        plb[c].ap[:, c0:512], lhsT=ones16.ap, rhs=pt.ap[:, 0:N],
                        start=(jb == 0), stop=(jb == njb - 1)),
                        reads=[pt, ones16], writes=[plb[c]])

            stage_a(0)
            for jb in range(njb):
                if jb + 1 < njb:
                    stage_a(jb + 1, (0,))
                stage_b(jb, (0,))
                if jb + 1 < njb:
                    stage_a(jb + 1, (1,))
                stage_b(jb, (1,))
                stage_c(jb)
            for c in range(2):
                S.op("act", lambda e, c=c: e.activation(out=lsb[gb][c].ap, in_=plb[c].ap, func=ACTF.Ln),
                     reads=[plb[c]], writes=[lsb[gb][c]])
                S.op("dve", lambda e, c=c: e.tensor_copy(out=ob[gb][c].ap, in_=otb[c].ap), reads=[otb[c]], writes=[ob[gb][c]])
            for c in range(2):
                S.op("act", lambda e, c=c: e.activation(out=rl[c].ap, in_=lsb[gb][c].ap, func=ACTF.Exp, scale=-1.0),
                     reads=[lsb[gb][c]], writes=[rl[c]])
            S.op("dve", lambda e: e.scalar_tensor_tensor(out=tt.ap, in0=ob[gb][1].ap, scalar=neglam.ap, in1=rl[1].ap,
                                                         op0=ALU.mult, op1=ALU.mult),
                 reads=[ob[gb][1], rl[1], neglam], writes=[tt])
            S.op("dve", lambda e: e.tensor_tensor(out=uu.ap, in0=ob[gb][0].ap, in1=rl[0].ap, op=ALU.mult),
                 reads=[ob[gb][0], rl[0]], writes=[uu])
            S.op("dve", lambda e: e.tensor_tensor(out=oo.ap, in0=uu.ap, in1=tt.ap, op=ALU.add), reads=[uu, tt], writes=[oo])
            S.op("dve", lambda e: e.tensor_tensor(out=osq.ap, in0=oo.ap, in1=oo.ap, op=ALU.mult), reads=[oo], writes=[osq])
            S.op("pe", lambda e: e.matmul(pmb.ap, lhsT=onesb.ap, rhs=osq.ap, start=True, stop=True),
                 reads=[onesb, osq], writes=[pmb])
            S.op("act", lambda e: e.activation(out=msb.ap, in_=pmb.ap, func=ACTF.Ln, bias=C["epscol"].ap),
                 reads=[pmb, C["epscol"]], writes=[msb])
            S.op("act", lambda e: e.activation(out=rs.ap, in_=msb.ap, func=ACTF.Exp, scale=-0.5),
                 reads=[msb], writes=[rs])
            oTt = oT[gb]
            S.op("dve", lambda e: e.scalar_tensor_tensor(out=oTt.ap, in0=oo.ap, scalar=sgc.ap, in1=rs.ap,
                                                         op0=ALU.mult, op1=ALU.mult),
                 reads=[oo, sgc, rs], writes=[oTt])
            S.dma("sp", out=sc["oT"][h, :, G * 512:(G + 1) * 512], in_=oTt.ap, reads=[oTt],
                  writes=[P.db(f"oT{h}_{G}")])


def phase_attn_out(P, xd, wod, sc, prefetch=None):
    S, WA, A, C = P.S, P.WA, P.A, P.C
    TT = 256
    NTT = P.ntok // TT
    WA.reset(); A.reset(C["a0"])
    WA.off = 64 * 1024
    Wo = WA.alloc(8 * 1024, "p (k f) -> p k f", k=8)
    load_weight_fast(P, Wo, wod, 8, 1024, piece=1024)
    if prefetch is not None:
        w1d, w2d = prefetch
        WA.off = 0
        W1 = WA.alloc(8 * 4096, "p (k f) -> p k f", k=8)
        W2 = WA.alloc(32 * 1024, "p (k f) -> p k f", k=32)
        load_weight(P, W1, w1d, 8, 4096)
        load_weight(P, W2, w2d, 32, 1024)
    xts = [A.alloc(2 * 1024, "p (b d) -> p b d", b=2) for _ in range(2)]
    oTs = [A.alloc16(8 * TT, "p (h t) -> p h t", h=8) for _ in range(2)]
    xv = xd.rearrange("(n b p) d -> n p b d", p=128, b=2)
    for it in range(NTT):
        xt = xts[it % 2]; ot = oTs[it % 2]
        xb = P.db(f"x{it}")
        S.dma("sp", out=xt.ap, in_=xv[it], reads=[xb], writes=[xt])
        S.dma("sp", out=ot.ap, in_=sc["oT"][:, :, it * TT:(it + 1) * TT].rearrange("h p t -> p h t"),
              reads=[P.db(f"oT{h}_{it // 2}") for h in range(8)], writes=[ot])
        residual_out(P, xt, None, Wo, 8, lambda k, b, ot=ot: ot.ap[:, k, b * 128:(b + 1) * 128],
                     lambda k, ot=ot: ot, P.ps[0:4])
        S.dma("sp", out=xv[it], in_=xt.ap, reads=[xt], writes=[xb])


def phase_s5(P, xd, gamd, wind, ared, aimd, bred, bimd, cred, cimd, dd, logdtd, wglud, bglud, woutd):
    S, WA, A, C = P.S, P.WA, P.A, P.C
    TT = 256
    NTT = P.ntok // TT
    WA.reset(); A.reset(C["a0"])
    ident = C["ident"]
    Win = WA.alloc(8 * 1024, "p (k f) -> p k f", k=8)
    Wglu = WA.alloc(8 * 1024, "p (k f) -> p k f", k=8)
    Wout = WA.alloc(8 * 1024, "p (k f) -> p k f", k=8)
    load_weight_fast(P, Win, wind, 8, 1024, piece=1024)
    load_weight_fast(P, Wglu, wglud, 8, 1024, piece=1024)
    load_weight_fast(P, Wout, woutd, 8, 1024, piece=1024)
    BL = WA.alloc(32 * 2 * 128, "p (q r m) -> p q r m", q=32, r=2)
    CBr = WA.alloc(32 * 3 * 128 + 2048)
    CBflat = CBr.ap
    Zr = [WA.alloc(512 + 256) for _ in range(2)]
    ZC = [WA.alloc(128, "p (g m) -> p g m", g=2) for _ in range(2)]
    rcol = A.alloc(32); ycol = A.alloc(32); carry = A.alloc(64, "p (q r) -> p q r", r=2)
    dcol = A.alloc(8); bgl = A.alloc(8)
    a_keep = A.off
    S.op("pool", lambda e: e.memset(carry.ap, 0.0), writes=[carry])
    S.dma("sp", out=dcol.ap, in_=dd.rearrange("o (c p) -> p (o c)", p=128), writes=[dcol], allow_slow_non_contiguous=True)
    S.dma("sp", out=bgl.ap, in_=bglud.rearrange("o (c p) -> p (o c)", p=128), writes=[bgl], allow_slow_non_contiguous=True)
    are = A.alloc(32); aim = A.alloc(32); ldt = A.alloc(32); dt = A.alloc(32)
    S.dma("sp", out=are.ap, in_=ared.rearrange("(q g2) p -> (g2 p) q", g2=2), writes=[are], allow_slow_non_contiguous=True)
    S.dma("sp", out=aim.ap, in_=aimd.rearrange("(q g2) p -> (g2 p) q", g2=2), writes=[aim], allow_slow_non_contiguous=True)
    ld2 = logdtd.rearrange("o (q g2) -> (o g2) q", g2=2)
    for g2 in range(2):
        S.dma("sp", out=ldt.ap[g2 * 64:(g2 + 1) * 64, :], in_=ld2[g2:g2 + 1, :].partition_broadcast(64), writes=[ldt],
              allow_slow_non_contiguous=True)
    S.op("act", lambda e: e.activation(out=dt.ap, in_=ldt.ap, func=ACTF.Exp), reads=[ldt], writes=[dt])
    S.op("dve", lambda e: e.tensor_scalar(out=are.ap, in0=are.ap, scalar1=-1e-4, scalar2=None, op0=ALU.min),
         reads=[are], writes=[are])
    rdt = A.alloc(32)
    S.op("dve", lambda e: e.tensor_tensor(out=rdt.ap, in0=are.ap, in1=dt.ap, op=ALU.mult), reads=[are, dt], writes=[rdt])
    S.op("act", lambda e: e.activation(out=rcol.ap, in_=rdt.ap, func=ACTF.Exp), reads=[rdt], writes=[rcol])
    S.op("dve", lambda e: e.scalar_tensor_tensor(out=ycol.ap, in0=aim.ap, scalar=float(1.0 / (2.0 * math.pi)), in1=dt.ap,
                                                 op0=ALU.mult, op1=ALU.mult), reads=[aim, dt], writes=[ycol])
    snt = A.alloc(32); cst = A.alloc(32); kit = A.alloci(32); frt = A.alloc(32)
    emit_sincos(P, ycol, snt, cst, kit, frt)
    nr = A.alloc(32); ni = A.alloc(32); den = A.alloc(32); t_a = A.alloc(32); t_b = A.alloc(32)
    zr = A.alloc(32); zi = A.alloc(32)
    S.op("dve", lambda e: e.tensor_tensor(out=nr.ap, in0=rcol.ap, in1=cst.ap, op=ALU.mult), reads=[rcol, cst], writes=[nr])
    S.op("dve", lambda e: e.tensor_scalar(out=nr.ap, in0=nr.ap, scalar1=-1.0, scalar2=None, op0=ALU.add), reads=[nr], writes=[nr])
    S.op("dve", lambda e: e.tensor_tensor(out=ni.ap, in0=rcol.ap, in1=snt.ap, op=ALU.mult), reads=[rcol, snt], writes=[ni])
    S.op("dve", lambda e: e.tensor_tensor(out=den.ap, in0=are.ap, in1=are.ap, op=ALU.mult), reads=[are], writes=[den])
    S.op("dve", lambda e: e.tensor_tensor(out=t_a.ap, in0=aim.ap, in1=aim.ap, op=ALU.mult), reads=[aim], writes=[t_a])
    S.op("dve", lambda e: e.tensor_tensor(out=den.ap, in0=den.ap, in1=t_a.ap, op=ALU.add), reads=[den, t_a], writes=[den])
    S.op("dve", lambda e: e.reciprocal(out=den.ap, in_=den.ap), reads=[den], writes=[den])
    S.op("dve", lambda e: e.tensor_tensor(out=t_a.ap, in0=nr.ap, in1=are.ap, op=ALU.mult), reads=[nr, are], writes=[t_a])
    S.op("dve", lambda e: e.tensor_tensor(out=t_b.ap, in0=ni.ap, in1=aim.ap, op=ALU.mult), reads=[ni, aim], writes=[t_b])
    S.op("dve", lambda e: e.tensor_tensor(out=t_a.ap, in0=t_a.ap, in1=t_b.ap, op=ALU.add), reads=[t_a, t_b], writes=[t_a])
    S.op("dve", lambda e: e.tensor_tensor(out=zr.ap, in0=t_a.ap, in1=den.ap, op=ALU.mult), reads=[t_a, den], writes=[zr])
    S.op("dve", lambda e: e.tensor_tensor(out=t_a.ap, in0=ni.ap, in1=are.ap, op=ALU.mult), reads=[ni, are], writes=[t_a])
    S.op("dve", lambda e: e.tensor_tensor(out=t_b.ap, in0=nr.ap, in1=aim.ap, op=ALU.mult), reads=[nr, aim], writes=[t_b])
    S.op("dve", lambda e: e.tensor_tensor(out=t_a.ap, in0=t_a.ap, in1=t_b.ap, op=ALU.subtract), reads=[t_a, t_b], writes=[t_a])
    S.op("dve", lambda e: e.tensor_tensor(out=zi.ap, in0=t_a.ap, in1=den.ap, op=ALU.mult), reads=[t_a, den], writes=[zi])
    Bre = A.alloc(512, "p (q h) -> p q h", h=16); Bim = A.alloc(512, "p (q h) -> p q h", h=16)
    S.dma("sp", out=Bre.ap, in_=bred.rearrange("(q g2) p h -> (g2 p) q h", g2=2), writes=[Bre])
    S.dma("sp", out=Bim.ap, in_=bimd.rearrange("(q g2) p h -> (g2 p) q h", g2=2), writes=[Bim])
    zrb = zr.ap.unsqueeze(2).broadcast_to([128, 32, 16])
    zib = zi.ap.unsqueeze(2).broadcast_to([128, 32, 16])
    M1 = A.alloc(512, "p (q h) -> p q h", h=16); M2 = A.alloc(512, "p (q h) -> p q h", h=16)
    Bb = [A.alloc(512, "p (q h) -> p q h", h=16) for _ in range(2)]
    S.op("dve", lambda e: e.tensor_tensor(out=M1.ap, in0=Bre.ap, in1=zrb, op=ALU.mult), reads=[Bre, zr], writes=[M1])
    S.op("dve", lambda e: e.tensor_tensor(out=M2.ap, in0=Bim.ap, in1=zib, op=ALU.mult), reads=[Bim, zi], writes=[M2])
    S.op("dve", lambda e: e.tensor_tensor(out=Bb[0].ap, in0=M1.ap, in1=M2.ap, op=ALU.subtract), reads=[M1, M2], writes=[Bb[0]])
    S.op("dve", lambda e: e.tensor_tensor(out=M1.ap, in0=Bim.ap, in1=zrb, op=ALU.mult), reads=[Bim, zr], writes=[M1])
    S.op("dve", lambda e: e.tensor_tensor(out=M2.ap, in0=Bre.ap, in1=zib, op=ALU.mult), reads=[Bre, zi], writes=[M2])
    S.op("dve", lambda e: e.tensor_tensor(out=Bb[1].ap, in0=M1.ap, in1=M2.ap, op=ALU.add), reads=[M1, M2], writes=[Bb[1]])
    pst = P.ps[7]
    pv = ps_bf16(pst)
    n = 0
    for k in range(8):
        for ri in range(2):
            Z = Zr[n % 2]
            n += 1
            S.op("pool", lambda e, Z=Z: e.memset(Z.ap, 0.0), writes=[Z])
            for g2 in range(2):
                dst = Z.ap[g2 * 64:(g2 + 1) * 64, g2 * 16:g2 * 16 + 640].rearrange("p (q m) -> p q m", m=160)[:, :, 0:16]
                S.op("dve", lambda e, dst=dst, g2=g2, ri=ri, k=k: e.tensor_copy(
                    out=dst, in_=Bb[ri].ap[g2 * 64:(g2 + 1) * 64, 4 * k:4 * k + 4, :]),
                    reads=[Bb[ri]], writes=[Z])
            for ql in range(4):
                S.op("pe", lambda e, Z=Z, ql=ql: e.transpose(out=pv[:, ql * 128:(ql + 1) * 128],
                                                            in_=Z.ap[:, ql * 128:(ql + 1) * 128], identity=ident.ap),
                     reads=[Z, ident], writes=[pst])
            S.op("act", lambda e, k=k, ri=ri: e.activation(out=BL.ap[:, 4 * k:4 * k + 4, ri, :],
                                                           in_=pv[:, 0:512].rearrange("p (q m) -> p q m", q=4),
                                                           func=ACTF.Copy),
                 reads=[pst], writes=[BL])
    Cn = [A.alloc(512, "p (k m) -> p k m", k=8) for _ in range(2)]
    S.dma("sp", out=Cn[0].ap, in_=cred.rearrange("(k gl) h p -> (gl h) k p", gl=8), writes=[Cn[0]])
    S.dma("sp", out=Cn[1].ap, in_=cimd.rearrange("(k gl) h p -> (gl h) k p", gl=8), writes=[Cn[1]])
    S.op("pool", lambda e: e.memset(CBflat, 0.0), writes=[CBr])
    pst2 = P.ps[6]
    pv2 = ps_bf16(pst2)
    n = 0
    for k in range(8):
        for ri in range(2):
            zc = ZC[n % 2]
            n += 1
            for g2 in range(2):
                S.op("dve", lambda e, zc=zc, g2=g2, ri=ri, k=k: e.tensor_scalar(
                    out=zc.ap[:, g2, :], in0=Cn[ri].ap[:, k, :], scalar1=C["par"].ap[:, g2:g2 + 1], scalar2=None,
                    op0=ALU.mult), reads=[Cn[ri], C["par"]], writes=[zc])
            S.op("pe", lambda e, zc=zc: e.transpose(out=pv2[:, 0:128], in_=zc.ap.rearrange("p g m -> p (g m)"),
                                                   identity=ident.ap), reads=[zc, ident], writes=[pst2])
            for var in ((0, 2) if ri == 0 else (1,)):
                base = 4 * k * 384 + var * 128
                dst = CBflat[:, base:base + 4 * 416].rearrange("p (q m) -> p q m", m=416)[:, :, 0:32]
                sgn = 1.0 if var == 0 else -1.0
                S.op("act", lambda e, dst=dst, sgn=sgn: e.activation(
                    out=dst, in_=pv2[:, 0:128].rearrange("p (q m) -> p q m", q=4), func=ACTF.Copy, scale=sgn),
                    reads=[pst2], writes=[CBr])
    CB = CBflat[:, 0:32 * 384].rearrange("p (q v m) -> p q v m", q=32, v=3)
    CS = WA.alloc(32 * TT, "p (q k) -> p q k", q=32)
    SN = WA.alloc(32 * TT, "p (q k) -> p q k", q=32)
    A.reset(a_keep)
    cT = A.alloc(32); sT = A.alloc(32)
    yT = A.alloc(32); kiT = A.alloci(32); frT = A.alloc(32)
    S.op("dve", lambda e: e.tensor_scalar(out=yT.ap, in0=ycol.ap, scalar1=float(TT), scalar2=None, op0=ALU.mult),
         reads=[ycol], writes=[yT])
    emit_sincos(P, yT, sT, cT, kiT, frT)
    a_keep2 = A.off
    yyt = [A.alloc(TT) for _ in range(2)]; kit2 = [A.alloci(TT) for _ in range(2)]; frt2 = [A.alloc(TT) for _ in range(2)]
    for q in range(32):
        yy_, ki_, fr_ = yyt[q % 2], kit2[q % 2], frt2[q % 2]
        S.op("dve", lambda e: e.tensor_scalar(out=yy_.ap, in0=C["tg0"].ap[:, 0:TT], scalar1=ycol.ap[:, q:q + 1], scalar2=None,
                                              op0=ALU.mult), reads=[C["tg0"], ycol], writes=[yy_])
        snq = Region(SN.ap[:, q, :], WA.sub(SN, q * TT, q * TT + TT))
        csq = Region(CS.ap[:, q, :], WA.sub(CS, q * TT, q * TT + TT))
        emit_sincos(P, yy_, snq, csq, ki_, fr_)
    A.reset(a_keep2)
    gam = load_gamma(P, gamd)
    xt = A.alloc(2 * 1024, "p (b d) -> p b d", b=2)
    scr = norm_scratch(P)
    hT = A.alloc16(8 * TT, "p (k t) -> p k t", k=8)
    uT = A.alloc16(8 * TT, "p (k t) -> p k t", k=8)
    gT = A.alloc16(8 * TT, "p (k t) -> p k t", k=8)
    ggT = A.alloc16(8 * TT, "p (k t) -> p k t", k=8)
    ydt = A.alloc(TT); sig = A.alloc(TT)
    NSET = 3

    def mkset(i):
        al = (lambda n: A.own(A.alloc(n))) if i == 0 else (lambda n: WA.own(WA.alloc32(n)))
        al16 = (lambda n: A.own(A.alloc16(n))) if i == 0 else (lambda n: WA.own(WA.alloc(n)))
        return dict(m=[al(TT) for _ in range(4)], w=[al(TT) for _ in range(2)], z=[al(TT) for _ in range(2)],
                    Y=[al16(TT) for _ in range(4)], ct=al(8))
    tsets = [mkset(0 if i < 2 else 1) for i in range(NSET)]
    cbufs = [Buf(f"carry{q}") for q in range(32)]
    xv = xd.rearrange("(n b p) d -> n p b d", p=128, b=2)
    gp = 0
    for it in range(NTT):
        xb = P.db(f"x{it}")
        S.dma("sp", out=xt.ap, in_=xv[it], reads=[xb], writes=[xt])
        for b in range(2):
            emit_norm_T(P, xt.ap[:, b, :], xt, gam, hT, b * 128, ident, dict(scr, pst=P.ps[6 + b]))
        for cc in range(8):
            pu = P.ps[cc % 2]
            for kc in range(8):
                S.op("pe", lambda e: e.matmul(
                    pu.ap[:, 0:TT], lhsT=Win.ap[:, kc, cc * 128:(cc + 1) * 128], rhs=hT.ap[:, kc, :],
                    start=(kc == 0), stop=(kc == 7)),
                    reads=[hT, WA.sub(Win, kc * 1024 + cc * 128, kc * 1024 + cc * 128 + 128)], writes=[pu])
            S.op("act", lambda e: e.activation(out=uT.ap[:, cc, :], in_=pu.ap[:, 0:TT], func=ACTF.Copy),
                 reads=[pu], writes=[A.sub(uT, cc * TT, cc * TT + TT)])

        def stB(q, part):
            cc = q // 4
            ts = tsets[(gp + q) % NSET]
            pb = [P.ps[2 + ((gp + q) % 2) * 2 + ri] for ri in range(2)]
            us = A.sub(uT, cc * TT, cc * TT + TT)
            m, w = ts["m"], ts["w"]
            csq = WA.sub(CS, q * TT, q * TT + TT); snq = WA.sub(SN, q * TT, q * TT + TT)
            if part == 0:
                for ri in range(2):
                    S.op("pe", lambda e: e.matmul(pb[ri].ap[:, 0:TT], lhsT=BL.ap[:, q, ri, :], rhs=uT.ap[:, cc, :],
                                                  start=True, stop=True), reads=[BL, us], writes=[pb[ri]])
                return
            for idx, (ri, tab, tb) in enumerate(((0, CS, csq), (1, SN, snq), (1, CS, csq), (0, SN, snq))):
                S.op("dve", lambda e: e.tensor_tensor(out=m[idx].ap, in0=pb[ri].ap[:, 0:TT], in1=tab.ap[:, q, :], op=ALU.mult),
                     reads=[pb[ri], tb], writes=[m[idx]])
            S.op("pool", lambda e: e.tensor_tensor(out=w[0].ap, in0=m[0].ap, in1=m[1].ap, op=ALU.add),
                 reads=[m[0], m[1]], writes=[w[0]])
            S.op("pool", lambda e: e.tensor_tensor(out=w[1].ap, in0=m[2].ap, in1=m[3].ap, op=ALU.subtract),
                 reads=[m[2], m[3]], writes=[w[1]])

        def stC(q):
            ts = tsets[(gp + q) % NSET]
            w, z, ct = ts["w"], ts["z"], ts["ct"]
            rbc = rcol.ap[:, q:q + 1].broadcast_to([128, TT])
            for ri in range(2):
                S.op("dve", lambda e: e.tensor_tensor_scan(
                    out=z[ri].ap, data0=rbc, data1=w[ri].ap, initial=carry.ap[:, q, ri:ri + 1],
                    op0=ALU.mult, op1=ALU.add), reads=[rcol, w[ri], cbufs[q]], writes=[z[ri]])
            zl = [z[0].ap[:, TT - 1:TT], z[1].ap[:, TT - 1:TT]]
            for idx, (ri, tab) in enumerate(((0, cT), (1, sT), (1, cT), (0, sT))):
                S.op("act", lambda e: e.activation(out=ct.ap[:, idx:idx + 1], in_=zl[ri], func=ACTF.Copy,
                                                   scale=tab.ap[:, q:q + 1]),
                     reads=[z[ri], tab], writes=[ct])
            S.op("pool", lambda e: e.tensor_tensor(out=carry.ap[:, q, 0:1], in0=ct.ap[:, 0:1], in1=ct.ap[:, 1:2], op=ALU.subtract),
                 reads=[ct], writes=[cbufs[q]])
            S.op("pool", lambda e: e.tensor_tensor(out=carry.ap[:, q, 1:2], in0=ct.ap[:, 2:3], in1=ct.ap[:, 3:4], op=ALU.add),
                 reads=[ct], writes=[cbufs[q]])

        def stD(q):
            cc, ql = q // 4, q % 4
            ts = tsets[(gp + q) % NSET]
            z, Y = ts["z"], ts["Y"]
            py = P.ps[cc % 2]
            csq = WA.sub(CS, q * TT, q * TT + TT); snq = WA.sub(SN, q * TT, q * TT + TT)
            S.op("pool", lambda e: e.tensor_tensor(out=Y[0].ap, in0=z[0].ap, in1=CS.ap[:, q, :], op=ALU.mult),
                 reads=[z[0], csq], writes=[Y[0]])
            S.op("pool", lambda e: e.tensor_tensor(out=Y[1].ap, in0=z[1].ap, in1=CS.ap[:, q, :], op=ALU.mult),
                 reads=[z[1], csq], writes=[Y[1]])
            S.op("dve", lambda e: e.tensor_tensor(out=Y[2].ap, in0=z[0].ap, in1=SN.ap[:, q, :], op=ALU.mult),
                 reads=[z[0], snq], writes=[Y[2]])
            S.op("dve", lambda e: e.tensor_tensor(out=Y[3].ap, in0=z[1].ap, in1=SN.ap[:, q, :], op=ALU.mult),
                 reads=[z[1], snq], writes=[Y[3]])
            for n_, (var, yi) in enumerate(((0, 0), (1, 1), (1, 2), (2, 3))):
                S.op("pe", lambda e: e.matmul(py.ap[:, 0:TT], lhsT=CB[:, q, var, :], rhs=Y[yi].ap,
                                              start=(ql == 0 and n_ == 0), stop=(ql == 3 and n_ == 3)),
                     reads=[CBr, Y[yi]], writes=[py])
            if ql == 3:
                us = A.sub(uT, cc * TT, cc * TT + TT)
                S.op("dve", lambda e: e.scalar_tensor_tensor(
                    out=ydt.ap, in0=uT.ap[:, cc, :], scalar=dcol.ap[:, cc:cc + 1], in1=py.ap[:, 0:TT],
                    op0=ALU.mult, op1=ALU.add), reads=[us, dcol, py], writes=[ydt])
                S.op("act", lambda e: e.activation(out=gT.ap[:, cc, :], in_=ydt.ap, func=ACTF.Gelu_apprx_tanh),
                     reads=[ydt], writes=[A.sub(gT, cc * TT, cc * TT + TT)])

        for s_ in range(32 + 2):
            if s_ < 32:
                stB(s_, 0)
            if 0 <= s_ - 2 < 32:
                stD(s_ - 2)
            if 0 <= s_ - 1 < 32:
                stC(s_ - 1)
            if s_ < 32:
                stB(s_, 1)
        gp += 32
        for c2 in range(8):
            pz = P.ps[6 + c2 % 2]
            for cc in range(8):
                S.op("pe", lambda e: e.matmul(
                    pz.ap[:, 0:TT], lhsT=Wglu.ap[:, cc, c2 * 128:(c2 + 1) * 128], rhs=gT.ap[:, cc, :],
                    start=(cc == 0), stop=(cc == 7)),
                    reads=[A.sub(gT, cc * TT, cc * TT + TT), WA.sub(Wglu, cc * 1024 + c2 * 128, cc * 1024 + c2 * 128 + 128)],
                    writes=[pz])
            S.op("act", lambda e: e.activation(out=sig.ap, in_=pz.ap[:, 0:TT], func=ACTF.Sigmoid, bias=bgl.ap[:, c2:c2 + 1]),
                 reads=[pz, bgl], writes=[sig])
            S.op("pool", lambda e: e.tensor_tensor(out=ggT.ap[:, c2, :], in0=gT.ap[:, c2, :], in1=sig.ap, op=ALU.mult),
                 reads=[A.sub(gT, c2 * TT, c2 * TT + TT), sig], writes=[A.sub(ggT, c2 * TT, c2 * TT + TT)])
        residual_out(P, xt, None, Wout, 8, lambda k, b: ggT.ap[:, k, b * 128:(b + 1) * 128],
                     lambda k: A.sub(ggT, k * TT, k * TT + TT), P.ps[2:6])
        S.dma("sp", out=xv[it], in_=xt.ap, reads=[xt], writes=[xb])


def host_consts():
    bf = ml_dtypes.bfloat16
    c = {}
    c["c_ident"] = np.eye(128, dtype=np.float32).astype(bf)
    p = np.arange(128)
    c["c_bones"] = ((p[:, None] // 64) == (p[None, :] // 64)).astype(np.float32).astype(bf) * np.float32(1.0 / 64.0)
    c["c_bones"] = c["c_bones"].astype(bf)
    rr = np.zeros((128, 128), np.float32)
    for cc in range(2):
        for d in range(64):
            mcol = cc * 64 + d
            if d < 32:
                rr[cc * 64 + d + 32, mcol] = -1.0
            else:
                rr[cc * 64 + d - 32, mcol] = 1.0
    c["c_rrot"] = rr.astype(bf)
    invf = (10000.0 ** (-np.arange(0, 64, 2, dtype=np.float32) / 64.0)).astype(np.float32)
    c["c_invf"] = (invf[(p % 64) % 32] / np.float32(2.0 * math.pi)).astype(np.float32).reshape(128, 1)
    c["c_triu"] = (p[:, None] <= p[None, :]).astype(np.float32).astype(bf)
    c["c_tril"] = (p[None, :] <= p[:, None]).astype(np.float32).astype(bf)
    c["c_tg0"] = np.broadcast_to(np.arange(256, dtype=np.float32)[None, :], (128, 256)).copy()
    par = np.zeros((128, 2), np.float32)
    par[:, 0] = ((p // 16) % 2 == 0)
    par[:, 1] = ((p // 16) % 2 == 1)
    c["c_par"] = par
    return c


def setup_consts(P):
    S, A = P.S, P.A
    C = {}
    A.reset()

    def ld(name, n, dtype, shape):
        d = P.din("c_" + name, shape, dtype)
        r = A.alloc16(n) if dtype == BF16 else A.alloc(n)
        S.dma("sp", out=r.ap, in_=d, writes=[r])
        C[name] = r

    ld("ident", 128, BF16, [128, 128])
    ld("bones", 128, BF16, [128, 128])
    ld("rrot", 128, BF16, [128, 128])
    ld("triu", 128, BF16, [128, 128])
    ld("tril", 128, BF16, [128, 128])
    ld("invf", 1, F32, [128, 1])
    ld("tg0", 256, F32, [128, 256])
    ld("par", 2, F32, [128, 2])
    C["mhalf"] = A.alloc(1)
    S.op("pool", lambda e: e.memset(C["mhalf"].ap, -0.5), writes=[C["mhalf"]])
    C["epscol"] = A.alloc(1)
    S.op("pool", lambda e: e.memset(C["epscol"].ap, EPS), writes=[C["epscol"]])
    A.off = ((A.off + 255) // 256) * 256
    C["a0"] = A.off
    P.C = C
    return C


def lambda_init_of(li):
    return 0.8 - 0.6 * math.exp(-0.3 * li)


def build(spec, ntok=SEQ, debug=False):
    P = Prog(ntok)
    P.DEBUG = debug
    S = P.S
    xin = P.din("x", [ntok, D])
    xout = P.dout("y", [ntok, D])
    setup_consts(P)
    nt = ntok // 256
    xiv = xin.rearrange("(n t) d -> n t d", t=256)
    xov = xout.rearrange("(n t) d -> n t d", t=256)
    for it in range(nt):
        S.dma("sp", out=xov[it], in_=xiv[it], writes=[P.db(f"x{it}")])
    sc = None
    posd = None
    preloaded = set()
    for si, (kind, li) in enumerate(spec):
        if kind == "mlp":
            phase_mlp(P, xout, P.din(f"mlp_w1_{li}", [D, 4 * D]) if li not in preloaded else None,
                      P.din(f"mlp_w2_{li}", [4 * D, D]) if li not in preloaded else None,
                      P.din(f"norm_mlp_{li}", [1, D]), preloaded=(li in preloaded))
            continue
        gamd = P.din(f"norm_mix_{li}", [1, D])
        mk = li % 3
        j = li // 3
        if mk == 0:
            if sc is None:
                sc = dict(qT=P.dscratch("sc_qT", [8, 128, ntok], BF16), kT=P.dscratch("sc_kT", [8, 128, ntok], BF16),
                          v=P.dscratch("sc_v", [ntok // 128, 128, 8 * 129], BF16),
                          oT=P.dscratch("sc_oT", [8, 128, ntok], BF16))
                posd = P.din("positions", [1, ntok], I32)
            phase_attn_qkv(P, xout, gamd, P.din(f"attn_w_qkv_{j}", [D, 3 * D]), P.din(f"attn_q_norm_{j}", [64, 1]),
                           P.din(f"attn_k_norm_{j}", [64, 1]), posd, sc)
            phase_attn_core(P, P.din(f"attn_lambda_{j}", [4, 64]), P.din(f"attn_sub_norm_{j}", [1, 128]),
                            lambda_init_of(li), sc)
            pf = None
            if si + 1 < len(spec) and spec[si + 1] == ("mlp", li):
                pf = (P.din(f"mlp_w1_{li}", [D, 4 * D]), P.din(f"mlp_w2_{li}", [4 * D, D]))
                preloaded.add(li)
            phase_attn_out(P, xout, P.din(f"attn_w_o_{j}", [D, D]), sc, prefetch=pf)
        elif mk == 1:
            phase_s5(P, xout, gamd, P.din(f"ssm_w_in_{j}", [D, D]), P.din(f"ssm_a_re_{j}", [64, 64]),
                     P.din(f"ssm_a_im_{j}", [64, 64]), P.din(f"ssm_b_re_{j}", [64, 64, 16]),
                     P.din(f"ssm_b_im_{j}", [64, 64, 16]), P.din(f"ssm_c_re_{j}", [64, 16, 64]),
                     P.din(f"ssm_c_im_{j}", [64, 16, 64]), P.din(f"ssm_d_{j}", [1, D]),
                     P.din(f"ssm_log_dt_{j}", [1, 64]), P.din(f"ssm_w_glu_{j}", [D, D]),
                     P.din(f"ssm_b_glu_{j}", [1, D]), P.din(f"ssm_w_out_{j}", [D, D]))
        else:
            phase_gmlp(P, xout, gamd, P.din(f"gm_w_in_{j}", [D, 6 * D]), P.din(f"gm_v_norm_{j}", [1, 3 * D]),
                       P.din(f"gm_w_s_{j}", [8, 128, 128]), P.din(f"gm_b_s_{j}", [8, 128]),
                       P.din(f"gm_w_out_{j}", [3 * D, D]))
    S.wait_bufs("sp", [P.db(f"x{it}") for it in range(nt)] + getattr(P, "dbg", []))
    return P


FULL_SPEC = [("mix", 0), ("mlp", 0), ("mix", 1), ("mlp", 1), ("mix", 2), ("mlp", 2), ("mix", 3), ("mlp", 3)]

_SHAPES = {
    "norm_mix": (1, D), "norm_mlp": (1, D), "attn_q_norm": (64, 1), "attn_k_norm": (64, 1), "attn_sub_norm": (1, 128),
    "ssm_d": (1, D), "ssm_log_dt": (1, 64), "ssm_b_glu": (1, D), "gm_v_norm": (1, 3 * D),
}


def make_in_map(P, inputs, b, ntok=SEQ):
    m = dict(host_consts())
    out = {}
    for name in P.dram_in:
        if name in m:
            out[name] = m[name]
        elif name == "x":
            out[name] = np.ascontiguousarray(inputs["x"][b, :ntok])
        elif name == "positions":
            out[name] = np.ascontiguousarray(inputs["positions"][b, :ntok].reshape(1, ntok)).astype(np.int32)
        else:
            base, idx = name.rsplit("_", 1)
            arr = np.asarray(inputs[base])[int(idx)]
            shp = _SHAPES.get(base)
            if shp is not None:
                arr = arr.reshape(shp)
            out[name] = np.ascontiguousarray(arr, dtype=np.float32)
    return out


_PROG = {}


def kernel(**inputs):
    if "full" not in _PROG:
        _PROG["full"] = build(FULL_SPEC, SEQ)
    P = _PROG["full"]
    inputs = {k: np.asarray(v) for k, v in inputs.items()}
    maps = [make_in_map(P, inputs, c % BATCH) for c in range(NCORES)]
    res = run_bass_kernel_spmd(P.nc, maps, core_ids=list(range(NCORES)))
    return np.stack([np.asarray(res.results[b]["y"], dtype=np.float32) for b in range(BATCH)], axis=0)
```

```python
import math
from contextlib import ExitStack

import numpy as np
import ml_dtypes

import concourse.bass as bass
import concourse.mybir as mybir
from concourse.bass_utils import run_bass_kernel_spmd

F32 = mybir.dt.float32
BF16 = mybir.dt.bfloat16
I32 = mybir.dt.int32
ALU = mybir.AluOpType
ACTF = mybir.ActivationFunctionType
AX = mybir.AxisListType

D = 1024
SEQ = 4096
BATCH = 4
DEPTH = 4
EPS = 1e-6
NCORES = 8


class Buf:
    __slots__ = ("name", "w", "r")

    def __init__(self, name):
        self.name = name
        self.w = None
        self.r = {}


class Region:
    def __init__(self, ap, bufs):
        self.ap = ap
        self.bufs = bufs


def _flat(items):
    out = []
    for it in items:
        if it is None:
            continue
        if isinstance(it, Buf):
            out.append(it)
        elif isinstance(it, Region):
            out.extend(it.bufs)
        else:
            out.extend(_flat(it))
    return out


class Sched:
    GEN = 24000
    POOL_INFLIGHT = 8

    def __init__(self, nc, stack, ndma=32):
        self.nc = nc
        self.stack = stack
        self.eng = dict(pe=nc.tensor, act=nc.scalar, dve=nc.vector, pool=nc.gpsimd, sp=nc.sync)
        self.stream = {k: [] for k in self.eng}
        self.cnt = {k: 0 for k in self.eng}
        self.gen = {k: 0 for k in self.eng}
        self.semh = {}
        for k in ("pe", "act", "dve", "pool"):
            self.semh[(k, 0)] = stack.enter_context(nc.semaphore(f"s_{k}0"))
        self.ndma = ndma
        for j in range(ndma):
            self.semh[("d", j)] = stack.enter_context(nc.semaphore(f"s_d{j}"))
        self.dcnt = [0] * ndma
        self.dnext = 0
        self.known = {k: {} for k in self.eng}
        self.ninstr = 0
        self.pool_hist = []

    def _collect(self, reads, writes):
        need = {}
        for b in reads:
            ev = b.w
            if ev is not None and need.get(ev[0], 0) < ev[1]:
                need[ev[0]] = ev[1]
        for b in writes:
            ev = b.w
            if ev is not None and need.get(ev[0], 0) < ev[1]:
                need[ev[0]] = ev[1]
            for k, v in b.r.items():
                if need.get(k, 0) < v:
                    need[k] = v
        return need

    def _waits(self, e, need, skip_self=False):
        kn = self.known[e]
        st = self.stream[e]
        for k, v in need.items():
            if skip_self and k[0] == e:
                continue
            if kn.get(k, 0) >= v:
                continue
            kn[k] = v
            sem = self.semh[k]
            self.eng[e].wait_ge(sem, v)
            self.ninstr += 1

    def op(self, e, fn, reads=(), writes=()):
        reads = _flat(reads)
        writes = _flat(writes)
        need = self._collect(reads, writes)
        self._waits(e, need, skip_self=(e == "pe"))
        if self.cnt[e] >= self.GEN:
            self.gen[e] += 1
            self.cnt[e] = 0
            self.semh[(e, self.gen[e])] = self.stack.enter_context(
                self.nc.semaphore(f"s_{e}{self.gen[e]}"))
        self.cnt[e] += 1
        key = (e, self.gen[e])
        v = self.cnt[e]
        sem = self.semh[key]
        fn(self.eng[e]).then_inc(sem, 1)
        self.ninstr += 1
        for b in reads:
            b.r[key] = v
        for b in writes:
            b.w = (key, v)
            b.r = {}

    def dma(self, q, out, in_, reads=(), writes=(), **kw):
        reads = _flat(reads)
        writes = _flat(writes)
        need = self._collect(reads, writes)
        if q == "pool":
            hist = self.pool_hist
            if len(hist) >= self.POOL_INFLIGHT:
                k0, v0 = hist[-self.POOL_INFLIGHT]
                if need.get(k0, 0) < v0:
                    need[k0] = v0
        self._waits(q, need)
        j = self.dnext
        self.dnext = (j + 1) % self.ndma
        self.dcnt[j] += 16
        key = ("d", j)
        v = self.dcnt[j]
        sem = self.semh[key]
        self.eng[q].dma_start(out=out, in_=in_, **kw).then_inc(sem, 16)
        self.ninstr += 1
        if q == "pool":
            self.pool_hist.append((key, v))
        for b in reads:
            b.r[key] = v
        for b in writes:
            b.w = (key, v)
            b.r = {}

    def wait_bufs(self, e, bufs):
        bufs = _flat(bufs)
        need = self._collect((), bufs)
        self._waits(e, need)

    def finish(self):
        return
        nc = self.nc
        with nc.Block() as block:
            for name, deco in (("pe", block.tensor), ("act", block.scalar), ("dve", block.vector),
                               ("pool", block.gpsimd), ("sp", block.sync)):
                lst = self.stream[name]

                @deco
                def _(eng, lst=lst):
                    for th in lst:
                        th(eng)


class Arena:
    def __init__(self, nc, stack, name, nelem, dtype, chunk):
        self.t = stack.enter_context(nc.sbuf_tensor(name, [128, nelem], dtype))
        self.n = nelem
        self.chunk = chunk
        self.bufs = [Buf(f"{name}{i}") for i in range((nelem + chunk - 1) // chunk)]
        self.off = 0
        self.name = name

    def reset(self, off=0):
        for b, cbs in getattr(self, "owned", []):
            for cb in cbs:
                if b.w is not None:
                    cb.r[b.w[0]] = max(cb.r.get(b.w[0], 0), b.w[1])
                for k, v in b.r.items():
                    cb.r[k] = max(cb.r.get(k, 0), v)
        self.owned = []
        self.off = off

    def own(self, reg):
        b = Buf("own")
        for cb in reg.bufs:
            if cb.w is not None:
                b.r[cb.w[0]] = max(b.r.get(cb.w[0], 0), cb.w[1])
            for k, v in cb.r.items():
                b.r[k] = max(b.r.get(k, 0), v)
        if not hasattr(self, "owned"):
            self.owned = []
        self.owned.append((b, reg.bufs))
        reg.bufs = [b]
        return reg

    def alloc(self, n, pattern=None, **kw):
        off = self.off
        assert off + n <= self.n, f"arena {self.name} overflow: {off}+{n} > {self.n}"
        self.off = off + n
        return self.view(off, n, pattern, **kw)

    def view(self, off, n, pattern=None, **kw):
        ap = self.t[:, off:off + n]
        if pattern is not None:
            ap = ap.rearrange(pattern, **kw)
        c0 = off // self.chunk
        c1 = (off + n - 1) // self.chunk
        r = Region(ap, self.bufs[c0:c1 + 1])
        r.off = off
        r.n = n
        r.arena = self
        return r

    def sub(self, reg, lo, hi):
        m = getattr(reg, "mul", 1)
        c0 = (reg.off + lo // m) // self.chunk
        c1 = (reg.off + (hi - 1) // m) // self.chunk
        return self.bufs[c0:c1 + 1]

    def alloc16(self, n, pattern=None, **kw):
        n32 = (n + 1) // 2
        off = self.off
        assert off + n32 <= self.n, f"arena {self.name} overflow: {off}+{n32} > {self.n}"
        self.off = off + n32
        ap = self.t[:, off:off + n32].bitcast(BF16)[:, 0:n]
        if pattern is not None:
            ap = ap.rearrange(pattern, **kw)
        c0 = off // self.chunk
        c1 = (off + n32 - 1) // self.chunk
        r = Region(ap, self.bufs[c0:c1 + 1])
        r.off = off; r.n = n32; r.arena = self; r.mul = 2
        return r

    def alloc32(self, n, pattern=None, **kw):
        r = self.alloc(2 * n)
        ap = r.ap.bitcast(F32)
        if pattern is not None:
            ap = ap.rearrange(pattern, **kw)
        r.ap = ap
        return r

    def alloci(self, n, pattern=None, **kw):
        r = self.alloc(n)
        ap = r.ap.bitcast(I32)
        if pattern is not None:
            ap = ap.rearrange(pattern, **kw)
        r.ap = ap
        return r


class Prog:
    def __init__(self, ntok=SEQ):
        self.ntok = ntok
        self.nc = bass.Bass("TRN2", target_bir_lowering=False)
        self.stack = ExitStack()
        nc = self.nc
        self.S = Sched(nc, self.stack)
        st = self.stack
        self.WA = Arena(nc, st, "wa", 72 * 1024, BF16, 2048)
        self.A = Arena(nc, st, "aa", 15 * 1024 + 512, F32, 256)
        self.psum_t = st.enter_context(nc.psum_tensor("ps", [128, 8, 512], F32))
        self.ps = [Region(self.psum_t[:, b, :], [Buf(f"ps{b}")]) for b in range(8)]
        self.dram_in = {}
        self.consts = {}
        self.dbuf = {}

    def din(self, name, shape, dtype=F32):
        t = self.nc.dram_tensor(name, list(shape), dtype, kind="ExternalInput").ap()
        self.dram_in[name] = t
        return t

    def dout(self, name, shape, dtype=F32):
        return self.nc.dram_tensor(name, list(shape), dtype, kind="ExternalOutput").ap()

    def dscratch(self, name, shape, dtype):
        return self.nc.dram_tensor(name, list(shape), dtype, kind="Internal").ap()

    def debug(self, name, reg, shape, dtype):
        d = self.nc.dram_tensor("dbg_" + name, list(shape), dtype, kind="ExternalOutput").ap()
        self.S.dma("sp", out=d, in_=reg.ap, reads=[reg], writes=[self.db("dbg_" + name)])
        self.dbg = getattr(self, "dbg", [])
        self.dbg.append(self.db("dbg_" + name))

    def db(self, name):
        b = self.dbuf.get(name)
        if b is None:
            b = self.dbuf[name] = Buf(name)
        return b


def ps_bf16(reg):
    return reg.ap.bitcast(BF16)


def emit_norm_T(P, xt_ap, xt_bufs, gam, hT, tok_off, ident, scr):
    S = P.S
    sq, ss, rstd, xn, pst = scr["sq"], scr["ss"], scr["rstd"], scr["xn"], scr["pst"]
    S.op("act", lambda e: e.activation(out=sq.ap, in_=xt_ap, func=ACTF.Square, scale=1.0 / 32.0,
                                       accum_out=ss.ap),
         reads=[xt_bufs], writes=[sq, ss])
    S.op("pool", lambda e: e.tensor_scalar(out=ss.ap, in0=ss.ap, scalar1=EPS, scalar2=None, op0=ALU.add),
         reads=[ss], writes=[ss])
    S.op("pool", lambda e: e.tensor_tensor(out=rstd.ap, in0=ss.ap, in1=P.C["mhalf"].ap, op=ALU.pow),
         reads=[ss, P.C["mhalf"]], writes=[rstd])
    S.op("dve", lambda e: e.scalar_tensor_tensor(out=xn.ap, in0=xt_ap, scalar=rstd.ap, in1=gam.ap,
                                                 op0=ALU.mult, op1=ALU.mult),
         reads=[xt_bufs, rstd, gam], writes=[xn])
    pv = ps_bf16(pst)
    for kc in range(8):
        S.op("pe", lambda e, kc=kc: e.transpose(out=pv[:, kc * 128:(kc + 1) * 128],
                                                in_=xn.ap[:, kc * 128:(kc + 1) * 128],
                                                identity=ident.ap),
             reads=[xn, ident], writes=[pst])
    S.op("act", lambda e: e.activation(out=hT.ap[:, :, tok_off:tok_off + 128],
                                       in_=pv.rearrange("p (k t) -> p k t", k=8), func=ACTF.Copy),
         reads=[pst], writes=[hT])


def load_weight_fast(P, reg, dram_ap, nk, ncols, nstage=4, piece=2048):
    S, A = P.S, P.A
    piece = min(piece, ncols)
    off_keep = A.off
    A.off = A.n - nstage * piece
    stg = [A.alloc(piece) for _ in range(nstage)]
    A.off = off_keep
    i = 0
    for kc in range(nk):
        for c0 in range(0, ncols, piece):
            c1 = min(ncols, c0 + piece)
            st = stg[i % nstage]
            S.dma("sp", out=st.ap[:, 0:c1 - c0], in_=dram_ap[kc * 128:(kc + 1) * 128, c0:c1], writes=[st])
            lo = kc * ncols + c0
            dst = reg.arena.sub(reg, lo, lo + (c1 - c0))
            if i % 2 == 0:
                S.op("act", lambda e: e.activation(out=reg.ap[:, kc, c0:c1], in_=st.ap[:, 0:c1 - c0], func=ACTF.Copy),
                     reads=[st], writes=[dst])
            else:
                S.op("dve", lambda e: e.tensor_copy(out=reg.ap[:, kc, c0:c1], in_=st.ap[:, 0:c1 - c0]),
                     reads=[st], writes=[dst])
            i += 1


def load_weight(P, reg, dram_ap, rows_per_part_dim, ncols, q="pool"):
    S = P.S
    nk = rows_per_part_dim
    for kc in range(nk):
        for c0 in range(0, ncols, 2048):
            c1 = min(ncols, c0 + 2048)
            lo = kc * ncols + c0
            S.dma(q, out=reg.ap[:, kc, c0:c1], in_=dram_ap[kc * 128:(kc + 1) * 128, c0:c1],
                  writes=[reg.arena.sub(reg, lo, lo + (c1 - c0))])


TWO_PI = 6.283179


def norm_scratch(P):
    A = P.A
    return dict(sq=A.alloc16(1024), ss=A.alloc(1), rstd=A.alloc(1), xn=A.alloc16(1024))


def load_gamma(P, gamd):
    gam = P.A.alloc(1024)
    P.S.dma("sp", out=gam.ap, in_=gamd.partition_broadcast(128), writes=[gam])
    return gam


def residual_out(P, xt, xb_ap_fn, W, nk, lhs_fn, lhs_reads_fn, banks):
    S = P.S
    nb = xt.ap.shape[1]
    i = 0
    for b in range(nb):
        for oc in range(2):
            pb = banks[i % len(banks)]
            i += 1
            for k in range(nk):
                S.op("pe", lambda e, k=k, b=b, oc=oc, pb=pb: e.matmul(
                    pb.ap, lhsT=lhs_fn(k, b), rhs=W.ap[:, k, oc * 512:(oc + 1) * 512],
                    start=(k == 0), stop=(k == nk - 1)),
                    reads=[lhs_reads_fn(k), P.WA.sub(W, k * 1024 + oc * 512, k * 1024 + oc * 512 + 512)],
                    writes=[pb])
            S.op("dve", lambda e, b=b, oc=oc, pb=pb: e.tensor_tensor(
                out=xt.ap[:, b, oc * 512:(oc + 1) * 512], in0=pb.ap,
                in1=xt.ap[:, b, oc * 512:(oc + 1) * 512], op=ALU.add),
                reads=[pb, xt], writes=[xt])


def phase_mlp(P, xd, w1d, w2d, gamd, preloaded=False):
    S, WA, A, C = P.S, P.WA, P.A, P.C
    TT = 256
    NTT = P.ntok // TT
    WA.reset(); A.reset(C["a0"])
    W1 = WA.alloc(8 * 4096, "p (k f) -> p k f", k=8)
    W2 = WA.alloc(32 * 1024, "p (k f) -> p k f", k=32)
    if not preloaded:
        load_weight_fast(P, W1, w1d, 8, 4096)
        load_weight_fast(P, W2, w2d, 32, 1024, piece=1024)
    gam = load_gamma(P, gamd)
    xts = [A.alloc(2 * 1024, "p (b d) -> p b d", b=2) for _ in range(2)]
    scr = norm_scratch(P)
    sqfs = [A.own(A.alloc(256)) for _ in range(3)]
    hTs = [A.alloc16(8 * TT, "p (k t) -> p k t", k=8) for _ in range(2)]
    actT = A.alloc16(32 * TT, "p (f t) -> p f t", f=32)
    ident = C["ident"]
    xv = xd.rearrange("(n b p) d -> n p b d", p=128, b=2)
    groups = [list(range(g, min(g + 3, 32))) for g in range(0, 32, 3)]

    def load_norm(it):
        xt = xts[it % 2]; hT = hTs[it % 2]
        S.dma("sp", out=xt.ap, in_=xv[it], reads=[P.db(f"x{it}")], writes=[xt])
        for b in range(2):
            emit_norm_T(P, xt.ap[:, b, :], xt, gam, hT, b * 128, ident, dict(scr, pst=P.ps[6 + b]))

    load_norm(0)
    nsq = 0
    for it in range(NTT):
        xt = xts[it % 2]; hT = hTs[it % 2]
        xb = P.db(f"x{it}")
        for gi, grp in enumerate(groups):
            banks = [P.ps[(gi % 2) * 3 + n] for n in range(len(grp))]
            for kc in range(8):
                for n, fc in enumerate(grp):
                    pb = banks[n]
                    S.op("pe", lambda e: e.matmul(
                        pb.ap[:, 0:TT], lhsT=W1.ap[:, kc, fc * 128:(fc + 1) * 128], rhs=hT.ap[:, kc, :],
                        start=(kc == 0), stop=(kc == 7)),
                        reads=[WA.sub(W1, kc * 4096 + fc * 128, kc * 4096 + fc * 128 + 128), hT], writes=[pb])
            for n, fc in enumerate(grp):
                pb = banks[n]
                sqf = sqfs[nsq % 3]
                nsq += 1
                S.op("act", lambda e: e.activation(out=sqf.ap, in_=pb.ap[:, 0:TT], func=ACTF.Relu),
                     reads=[pb], writes=[sqf])
                S.op("dve", lambda e: e.tensor_tensor(out=actT.ap[:, fc, :], in0=sqf.ap, in1=sqf.ap, op=ALU.mult),
                     reads=[sqf], writes=[A.sub(actT, fc * TT, fc * TT + TT)])
        if it + 1 < NTT:
            load_norm(it + 1)
        residual_out(P, xt, None, W2, 32,
                     lambda k, b: actT.ap[:, k, b * 128:(b + 1) * 128],
                     lambda k: A.sub(actT, k * TT, k * TT + TT), P.ps[0:4])
        S.dma("sp", out=xv[it], in_=xt.ap, reads=[xt], writes=[xb])


def phase_gmlp(P, xd, gamd, wind, vgd, wsd, bsd, woutd):
    S, WA, A, C = P.S, P.WA, P.A, P.C
    TT = 256
    NTT = P.ntok // TT
    WA.reset(); A.reset(C["a0"])
    Win = WA.alloc(8 * 6144, "p (k f) -> p k f", k=8)
    Wout = WA.alloc(24 * 1024, "p (k f) -> p k f", k=24)
    load_weight_fast(P, Win, wind, 8, 6144)
    load_weight_fast(P, Wout, woutd, 24, 1024, piece=1024)
    gam = load_gamma(P, gamd)
    ident = C["ident"]
    WmT = A.alloc16(1024, "p (h t) -> p h t", h=8)
    vg = A.alloc(24)
    bs = A.alloc(1024, "p (h t) -> p h t", h=8)
    off_keep = A.off
    A.off = A.n - 512
    wn = A.alloc16(1024, "p (h s) -> p h s", h=8)
    A.off = off_keep
    S.dma("pool", out=wn.ap, in_=wsd.rearrange("h t s -> t h s"), writes=[wn])
    S.op("dve", lambda e: e.tensor_tensor(out=wn.ap, in0=wn.ap,
                                          in1=C["tril"].ap.unsqueeze(1).broadcast_to([128, 8, 128]), op=ALU.mult),
         reads=[wn, C["tril"]], writes=[wn])
    pst = P.ps[7]
    pv = ps_bf16(pst)
    for h in range(8):
        S.op("pe", lambda e, h=h: e.transpose(out=pv[:, h * 128:(h + 1) * 128], in_=wn.ap[:, h, :], identity=ident.ap),
             reads=[wn, ident], writes=[pst])
    S.op("act", lambda e: e.activation(out=WmT.ap, in_=pv.rearrange("p (h t) -> p h t", h=8), func=ACTF.Copy),
         reads=[pst], writes=[WmT])
    S.dma("sp", out=vg.ap, in_=vgd.rearrange("o (c p) -> p (o c)", p=128), writes=[vg], allow_slow_non_contiguous=True)
    S.dma("sp", out=bs.ap, in_=bsd.rearrange("h t -> (h t)").partition_broadcast(128), writes=[bs])
    xts = [A.alloc(2 * 1024, "p (b d) -> p b d", b=2)] * 2
    scr = norm_scratch(P)
    hT = A.alloc16(8 * TT, "p (k t) -> p k t", k=8)
    gv = [A.alloc16(3072) for _ in range(2)]
    ssq = [A.alloc(8) for _ in range(2)]
    sst = [A.alloc(1) for _ in range(2)]
    rsv = [A.alloc(1) for _ in range(2)]
    junk = scr["sq"]
    WmTs = [A.alloc16(1024, "p (h t) -> p h t", h=8) for _ in range(2)]
    ug = [A.own(A.alloc(TT)) for _ in range(2)]
    tmp = [A.own(A.alloc(TT)) for _ in range(2)]
    uvT = A.alloc16(24 * TT, "p (c t) -> p c t", c=24)
    xv = xd.rearrange("(n b p) d -> n p b d", p=128, b=2)
    hTs = [hT, hT]

    def load_norm(it):
        xt_ = xts[it % 2]
        S.dma("sp", out=xt_.ap, in_=xv[it], reads=[P.db(f"x{it}")], writes=[xt_])
        for b in range(2):
            emit_norm_T(P, xt_.ap[:, b, :], xt_, gam, hTs[it % 2], b * 128, ident, dict(scr, pst=P.ps[7]))

    ngrp = 0
    for it in range(NTT):
        load_norm(it)
        xt = xts[it % 2]
        hT = hTs[it % 2]
        xb = P.db(f"x{it}")
        for b in range(2):
            for jg in range(2):
                banks = [P.ps[(ngrp % 2) * 3 + n] for n in range(3)]
                ngrp += 1
                for kc in range(8):
                    for n in range(3):
                        c0 = 3072 + (jg * 3 + n) * 512
                        pb = banks[n]
                        S.op("pe", lambda e: e.matmul(
                            pb.ap, lhsT=hT.ap[:, kc, b * 128:(b + 1) * 128], rhs=Win.ap[:, kc, c0:c0 + 512],
                            start=(kc == 0), stop=(kc == 7)),
                            reads=[hT, WA.sub(Win, kc * 6144 + c0, kc * 6144 + c0 + 512)], writes=[pb])
                for n in range(3):
                    jj = jg * 3 + n
                    pb = banks[n]
                    gs = A.sub(gv[b], jj * 512, jj * 512 + 512)
                    S.op("act", lambda e: e.activation(
                        out=gv[b].ap[:, jj * 512:(jj + 1) * 512], in_=pb.ap, func=ACTF.Gelu_apprx_tanh),
                        reads=[pb], writes=[gs])
                    S.op("dve", lambda e: e.scalar_tensor_tensor(
                        out=junk.ap[:, 0:512], in0=gv[b].ap[:, jj * 512:(jj + 1) * 512], scalar=1.0,
                        in1=gv[b].ap[:, jj * 512:(jj + 1) * 512], op0=ALU.mult, op1=ALU.mult,
                        accum_out=ssq[b].ap[:, jj:jj + 1]),
                        reads=[gs], writes=[junk, ssq[b]])
            S.op("dve", lambda e: e.tensor_reduce(out=sst[b].ap, in_=ssq[b].ap[:, 0:6], axis=AX.X, op=ALU.add),
                 reads=[ssq[b]], writes=[sst[b]])
            S.op("pool", lambda e: e.tensor_scalar(out=sst[b].ap, in0=sst[b].ap, scalar1=1.0 / 3072.0, scalar2=EPS,
                                                   op0=ALU.mult, op1=ALU.add),
                 reads=[sst[b]], writes=[sst[b]])
            S.op("pool", lambda e: e.tensor_tensor(out=rsv[b].ap, in0=sst[b].ap, in1=C["mhalf"].ap, op=ALU.pow),
                 reads=[sst[b], C["mhalf"]], writes=[rsv[b]])
            S.op("dve", lambda e: e.tensor_scalar(out=WmTs[b].ap, in0=WmT.ap, scalar1=rsv[b].ap, scalar2=None,
                                                  op0=ALU.mult),
                 reads=[WmT, rsv[b]], writes=[WmTs[b]])
        for hh in range(8):
            banks = [P.ps[(ngrp % 2) * 3 + n] for n in range(3)]
            ngrp += 1
            for kc in range(8):
                for n in range(3):
                    c = hh * 3 + n
                    pu = banks[n]
                    S.op("pe", lambda e: e.matmul(
                        pu.ap[:, 0:TT], lhsT=Win.ap[:, kc, c * 128:(c + 1) * 128], rhs=hT.ap[:, kc, :],
                        start=(kc == 0), stop=(kc == 7)),
                        reads=[hT, WA.sub(Win, kc * 6144 + c * 128, kc * 6144 + c * 128 + 128)], writes=[pu])
            for n in range(3):
                c = hh * 3 + n
                pu = banks[n]
                u_ = ug[c % 2]
                t_ = tmp[c % 2]
                S.op("act", lambda e: e.activation(out=u_.ap, in_=pu.ap[:, 0:TT], func=ACTF.Gelu_apprx_tanh),
                     reads=[pu], writes=[u_])
                pg = P.ps[6]
                pgo = (c % 2) * 256
                for b in range(2):
                    S.op("pe", lambda e: e.matmul(
                        pg.ap[:, pgo + b * 128:pgo + (b + 1) * 128], lhsT=gv[b].ap[:, c * 128:(c + 1) * 128],
                        rhs=WmTs[b].ap[:, hh, :], start=True, stop=True),
                        reads=[A.sub(gv[b], c * 128, c * 128 + 128), WmTs[b]], writes=[pg])
                for b in range(2):
                    S.op("dve", lambda e: e.scalar_tensor_tensor(
                        out=t_.ap[:, b * 128:(b + 1) * 128], in0=pg.ap[:, pgo + b * 128:pgo + (b + 1) * 128],
                        scalar=vg.ap[:, c:c + 1], in1=bs.ap[:, hh, :], op0=ALU.mult, op1=ALU.add),
                        reads=[pg, vg, bs], writes=[t_])
                S.op("pool", lambda e: e.tensor_tensor(out=uvT.ap[:, c, :], in0=t_.ap, in1=u_.ap, op=ALU.mult),
                     reads=[t_, u_], writes=[A.sub(uvT, c * TT, c * TT + TT)])
        residual_out(P, xt, None, Wout, 24,
                     lambda k, b: uvT.ap[:, k, b * 128:(b + 1) * 128],
                     lambda k: A.sub(uvT, k * TT, k * TT + TT), P.ps[0:4])
        S.dma("sp", out=xv[it], in_=xt.ap, reads=[xt], writes=[xb])


def emit_sincos(P, yy, sn, cs, ki, fr):
    S = P.S
    S.op("dve", lambda e: e.tensor_copy(out=ki.ap, in_=yy.ap), reads=[yy], writes=[ki])
    S.op("dve", lambda e: e.tensor_tensor(out=fr.ap, in0=yy.ap, in1=ki.ap, op=ALU.subtract),
         reads=[yy, ki], writes=[fr])
    S.op("act", lambda e: e.activation(out=sn.ap, in_=fr.ap, func=ACTF.Sin, scale=TWO_PI), reads=[fr], writes=[sn])
    S.op("dve", lambda e: e.tensor_scalar(out=ki.ap, in0=yy.ap, scalar1=0.25, scalar2=None, op0=ALU.add),
         reads=[yy, fr], writes=[ki])
    S.op("dve", lambda e: e.scalar_tensor_tensor(out=fr.ap, in0=yy.ap, scalar=0.25, in1=ki.ap,
                                                 op0=ALU.add, op1=ALU.subtract),
         reads=[yy, ki, sn], writes=[fr])
    S.op("act", lambda e: e.activation(out=cs.ap, in_=fr.ap, func=ACTF.Sin, scale=TWO_PI), reads=[fr], writes=[cs])


def phase_attn_qkv(P, xd, gamd, wqkvd, qgd, kgd, posd, sc):
    S, WA, A, C = P.S, P.WA, P.A, P.C
    TT = 512
    NB = TT // 128
    NTT = P.ntok // TT
    WA.reset(); A.reset(C["a0"])
    W = WA.alloc(8 * 3072, "p (k f) -> p k f", k=8)
    load_weight_fast(P, W, wqkvd, 8, 3072, piece=1536)
    gam = load_gamma(P, gamd)
    ident = C["ident"]
    gcol = A.alloc(2)
    for idx, gd in enumerate((qgd, kgd)):
        for half in range(2):
            S.dma("sp", out=gcol.ap[half * 64:(half + 1) * 64, idx:idx + 1], in_=gd, writes=[gcol])
    A.off = ((A.off + 255) // 256) * 256
    xts = [A.alloc(NB * 1024, "p (b d) -> p b d", b=NB), WA.alloc32(NB * 1024, "p (b d) -> p b d", b=NB)]
    sq = A.own(A.alloc16(1024)); ssn = A.alloc(1); rstdn = A.alloc(1)
    xns = [A.own(A.alloc16(1024)) for _ in range(2)]
    hTs = [A.alloc16(8 * TT, "p (k t) -> p k t", k=8) for _ in range(2)]
    posb = A.alloci(TT)
    yy = A.alloc(TT); ki = A.alloci(TT); fr = A.alloc(TT)
    sns = [WA.own(WA.alloc32(TT)) for _ in range(2)]
    css = [WA.own(WA.alloc32(TT)) for _ in range(2)]
    va = [A.alloc16(8 * 129, "p (h e) -> p h e", h=8) for _ in range(2)]
    NS = 4
    sqb = [WA.own(WA.alloc(TT)) for _ in range(NS)]
    sd = [WA.own(WA.alloc32(TT)) for _ in range(NS)]
    rs = [WA.own(WA.alloc32(TT)) for _ in range(NS)]
    qn = [WA.own(WA.alloc(TT)) for _ in range(NS)]
    t1 = [WA.own(WA.alloc32(TT)) for _ in range(NS)]
    t2 = [WA.own(WA.alloc32(TT)) for _ in range(NS)]
    qf = [WA.own(WA.alloc(TT)) for _ in range(NS)]
    for v in va:
        S.op("pool", lambda e, v=v: e.memset(v.ap, 1.0), writes=[v])
    xv = xd.rearrange("(n b p) d -> n p b d", p=128, b=NB)
    pqb = P.ps[0:3]; pmb = P.ps[3:5]; prb = P.ps[5:7]
    pst = P.ps[7]
    pv = ps_bf16(pst)

    def load_x(it):
        xb = [P.db(f"x{it * 2}"), P.db(f"x{it * 2 + 1}")]
        S.dma("sp", out=xts[it % 2].ap, in_=xv[it], reads=xb, writes=[xts[it % 2]])

    def norm_pre(it, b):
        xt = xts[it % 2]
        xn = xns[b % 2]
        S.op("act", lambda e: e.activation(out=sq.ap, in_=xt.ap[:, b, :], func=ACTF.Square, scale=1.0 / 32.0,
                                           accum_out=ssn.ap), reads=[xt], writes=[sq, ssn])
        S.op("pool", lambda e: e.tensor_scalar(out=ssn.ap, in0=ssn.ap, scalar1=EPS, scalar2=None, op0=ALU.add),
             reads=[ssn], writes=[ssn])
        S.op("pool", lambda e: e.tensor_tensor(out=rstdn.ap, in0=ssn.ap, in1=C["mhalf"].ap, op=ALU.pow),
             reads=[ssn, C["mhalf"]], writes=[rstdn])
        S.op("dve", lambda e: e.scalar_tensor_tensor(out=xn.ap, in0=xt.ap[:, b, :], scalar=rstdn.ap, in1=gam.ap,
                                                     op0=ALU.mult, op1=ALU.mult),
             reads=[xt, rstdn, gam], writes=[xn])

    def norm_post(it, b):
        xn = xns[b % 2]
        hT = hTs[it % 2]
        for kc in range(8):
            S.op("pe", lambda e: e.transpose(out=pv[:, kc * 128:(kc + 1) * 128], in_=xn.ap[:, kc * 128:(kc + 1) * 128],
                                             identity=ident.ap), reads=[xn, ident], writes=[pst])
        S.op("act", lambda e: e.activation(out=hT.ap[:, :, b * 128:(b + 1) * 128],
                                           in_=pv.rearrange("p (k t) -> p k t", k=8), func=ACTF.Copy),
             reads=[pst], writes=[hT])

    def sincos(it):
        S.dma("sp", out=posb.ap, in_=posd[:, it * TT:(it + 1) * TT].partition_broadcast(128), writes=[posb])
        S.op("dve", lambda e: e.tensor_scalar(out=yy.ap, in0=posb.ap, scalar1=C["invf"].ap, scalar2=None, op0=ALU.mult),
             reads=[posb, C["invf"]], writes=[yy])
        emit_sincos(P, yy, sns[it % 2], css[it % 2], ki, fr)

    items = [(it, which, h) for it in range(NTT) for which in range(2) for h in range(8)]
    n = len(items)

    def st0(i):
        it, which, h = items[i]
        hT = hTs[it % 2]
        col0 = which * 1024 + h * 128
        pq = pqb[i % 3]
        for kc in range(8):
            S.op("pe", lambda e: e.matmul(pq.ap, lhsT=W.ap[:, kc, col0:col0 + 128], rhs=hT.ap[:, kc, :],
                                          start=(kc == 0), stop=(kc == 7)),
                 reads=[hT, WA.sub(W, kc * 3072 + col0, kc * 3072 + col0 + 128)], writes=[pq])

    def st1(i):
        pq = pqb[i % 3]; pm = pmb[i % 2]; s_ = sqb[i % NS]
        S.op("act", lambda e: e.activation(out=s_.ap, in_=pq.ap, func=ACTF.Square), reads=[pq], writes=[s_])
        S.op("pe", lambda e: e.matmul(pm.ap, lhsT=C["bones"].ap, rhs=s_.ap, start=True, stop=True),
             reads=[s_, C["bones"]], writes=[pm])

    def st2(i):
        it, which, h = items[i]
        k = i % NS
        pq = pqb[i % 3]; pm = pmb[i % 2]
        S.op("act", lambda e: e.activation(out=sd[k].ap, in_=pm.ap, func=ACTF.Ln, bias=C["epscol"].ap),
             reads=[pm, C["epscol"]], writes=[sd[k]])
        S.op("act", lambda e: e.activation(out=rs[k].ap, in_=sd[k].ap, func=ACTF.Exp, scale=-0.5),
             reads=[sd[k]], writes=[rs[k]])
        S.op("dve", lambda e: e.scalar_tensor_tensor(out=qn[k].ap, in0=pq.ap, scalar=gcol.ap[:, which:which + 1],
                                                     in1=rs[k].ap, op0=ALU.mult, op1=ALU.mult),
             reads=[pq, gcol, rs[k]], writes=[qn[k]])

    def st2b(i):
        k = i % NS
        pr = prb[i % 2]
        S.op("pe", lambda e: e.matmul(pr.ap, lhsT=C["rrot"].ap, rhs=qn[k].ap, start=True, stop=True),
             reads=[qn[k], C["rrot"]], writes=[pr])

    def st3(i):
        it, which, h = items[i]
        k = i % NS
        pr = prb[i % 2]
        sn, cs = sns[it % 2], css[it % 2]
        S.op("pool", lambda e: e.tensor_tensor(out=t1[k].ap, in0=qn[k].ap, in1=cs.ap, op=ALU.mult),
             reads=[qn[k], cs], writes=[t1[k]])
        S.op("dve", lambda e: e.tensor_tensor(out=t2[k].ap, in0=pr.ap, in1=sn.ap, op=ALU.mult),
             reads=[pr, sn], writes=[t2[k]])
        S.op("dve", lambda e: e.tensor_tensor(out=qf[k].ap, in0=t1[k].ap, in1=t2[k].ap, op=ALU.add),
             reads=[t1[k], t2[k]], writes=[qf[k]])
        dst = sc["qT"] if which == 0 else sc["kT"]
        S.dma("sp", out=dst[h, :, it * TT:(it + 1) * TT], in_=qf[k].ap, reads=[qf[k]],
              writes=[P.db(f"{'qk'[which]}T{h}_{it}")])

    def vproj(it, b):
        hT = hTs[it % 2]
        v = va[b % 2]
        for jj in range(2):
            pvb = pst
            for kc in range(8):
                c0 = 2048 + jj * 512
                S.op("pe", lambda e: e.matmul(pvb.ap, lhsT=hT.ap[:, kc, b * 128:(b + 1) * 128], rhs=W.ap[:, kc, c0:c0 + 512],
                                              start=(kc == 0), stop=(kc == 7)),
                     reads=[hT, WA.sub(W, kc * 3072 + c0, kc * 3072 + c0 + 512)], writes=[pvb])
            S.op("act", lambda e: e.activation(
                out=v.ap[:, 4 * jj:4 * jj + 4, 0:128], in_=pvb.ap.rearrange("p (h e) -> p h e", h=4), func=ACTF.Copy),
                reads=[pvb], writes=[v])
        blk = it * NB + b
        S.dma("sp", out=sc["v"][blk], in_=v.ap, reads=[v], writes=[P.db(f"v{blk}")])

    load_x(0)
    for b in range(NB):
        norm_pre(0, b)
        norm_post(0, b)
    sincos(0)
    if NTT > 1:
        load_x(1)
    for s in range(n + 4):
        if s < n:
            it, which, h = items[s]
            j16 = s % 16
            st0(s)
        if 0 <= s - 2 < n:
            st2(s - 2)
        if 0 <= s - 1 < n:
            st1(s - 1)
        if 0 <= s - 3 < n:
            st2b(s - 3)
        if 0 <= s - 4 < n:
            st3(s - 4)
        if s < n:
            if j16 in (1, 5, 9, 13):
                vproj(it, j16 // 4)
            if it + 1 < NTT:
                if j16 in (0, 4, 8, 12):
                    norm_pre(it + 1, j16 // 4)
                if j16 in (3, 7, 11, 15):
                    norm_post(it + 1, j16 // 4)
                if j16 == 14:
                    sincos(it + 1)
                if j16 == 15 and it + 2 < NTT:
                    load_x(it + 2)


def phase_attn_core(P, lamd, sgd, lambda_init, sc):
    S, WA, A, C = P.S, P.WA, P.A, P.C
    ntok = P.ntok
    NB = ntok // 128
    NG = ntok // 512
    NT256 = ntok // 512
    WA.reset(); A.reset(C["a0"])
    K0 = [WA.alloc(ntok) for _ in range(2)]
    K1 = [WA.alloc(ntok) for _ in range(2)]
    QT = [WA.alloc(ntok) for _ in range(2)]
    VA = [WA.alloc(NB * 128, "p (n e) -> p n e", e=128) for _ in range(2)]
    ones16 = WA.alloc(128)
    onesb = WA.alloc(128)
    S.op("pool", lambda e: e.memset(ones16.ap, 1.0), writes=[ones16])
    S.op("pool", lambda e: e.memset(onesb.ap, 1.0 / 128.0), writes=[onesb])
    for i in range(2):
        S.op("pool", lambda e, i=i: e.memset(K0[i].ap[64:128, :], 0.0), writes=[K0[i]])
        S.op("pool", lambda e, i=i: e.memset(K1[i].ap[0:64, :], 0.0), writes=[K1[i]])
    L = A.alloc(256, "p (a d) -> p a d", a=4)
    S.dma("sp", out=L.ap, in_=lamd.rearrange("a d -> (a d)").partition_broadcast(128), writes=[L])
    lj = A.alloc(64); s12 = A.alloc(2); e12 = A.alloc(2); neglam = A.alloc(1)
    for a in range(2):
        S.op("dve", lambda e, a=a: e.scalar_tensor_tensor(
            out=lj.ap, in0=L.ap[:, 2 * a, :], scalar=1.0, in1=L.ap[:, 2 * a + 1, :], op0=ALU.mult, op1=ALU.mult,
            accum_out=s12.ap[:, a:a + 1]), reads=[L], writes=[lj, s12])
    S.op("act", lambda e: e.activation(out=e12.ap, in_=s12.ap, func=ACTF.Exp), reads=[s12], writes=[e12])
    S.op("dve", lambda e: e.tensor_tensor(out=neglam.ap, in0=e12.ap[:, 1:2], in1=e12.ap[:, 0:1], op=ALU.subtract),
         reads=[e12], writes=[neglam])
    S.op("dve", lambda e: e.tensor_scalar(out=neglam.ap, in0=neglam.ap, scalar1=-float(lambda_init), scalar2=None,
                                          op0=ALU.add), reads=[neglam], writes=[neglam])
    sgc = A.alloc(1)
    S.dma("sp", out=sgc.ap, in_=sgd.rearrange("o e -> e o"), writes=[sgc], allow_slow_non_contiguous=True)
    S.op("dve", lambda e: e.tensor_scalar(out=sgc.ap, in0=sgc.ap, scalar1=float(1.0 - lambda_init), scalar2=None,
                                          op0=ALU.mult), reads=[sgc], writes=[sgc])
    A.off = ((A.off + 255) // 256) * 256
    PT = [[A.alloc16(512) for _ in range(2)] for _ in range(2)]
    ob = [[A.alloc(512) for _ in range(2)] for _ in range(2)]
    rl = [A.alloc(512) for _ in range(2)]
    tt = A.alloc(512); uu = A.alloc(512); oo = A.alloc(512)
    osq = A.alloc16(512); msb = A.alloc(512); rs = A.alloc(512)
    oT = [A.alloc16(512) for _ in range(2)]
    sb3 = [P.ps[0], P.ps[1], P.ps[2]]
    otb = [P.ps[3], P.ps[4]]
    plb = [P.ps[5], P.ps[6]]
    pmb = P.ps[7]
    lsb = [[A.alloc(512) for _ in range(2)] for _ in range(2)]
    mhb = C["mhalf"].ap.broadcast_to([128, 512])
    ng = 0
    for h in range(8):
        buf = h % 2
        S.dma("sp", out=K0[buf].ap[0:64, :], in_=sc["kT"][h, 0:64, :],
              reads=[P.db(f"kT{h}_{it}") for it in range(NT256)], writes=[K0[buf]])
        S.dma("sp", out=K1[buf].ap[64:128, :], in_=sc["kT"][h, 64:128, :],
              reads=[P.db(f"kT{h}_{it}") for it in range(NT256)], writes=[K1[buf]])
        S.dma("sp", out=QT[buf].ap, in_=sc["qT"][h],
              reads=[P.db(f"qT{h}_{it}") for it in range(NT256)], writes=[QT[buf]])
        S.dma("sp", out=VA[buf].ap, in_=sc["v"].rearrange("n p (h e) -> h p n e", h=8)[h][:, :, 0:128],
              reads=[P.db(f"v{blk}") for blk in range(NB)], writes=[VA[buf]])
        for G in range(NG):
            gb = ng % 2
            ng += 1
            njb = 4 * G + 4

            def geom(jb):
                nq0 = max(0, jb - 4 * G)
                return nq0, (4 - nq0) * 128, G * 512 + nq0 * 128

            def stage_a(jb, cs_=(0, 1)):
                nq0, N, qc0 = geom(jb)
                for c in cs_:
                    Kc = (K0 if c == 0 else K1)[buf]
                    pss = sb3[(2 * jb + c) % 3]
                    S.op("pe", lambda e: e.matmul(
                        pss.ap[:, 0:N], lhsT=Kc.ap[:, jb * 128:(jb + 1) * 128], rhs=QT[buf].ap[:, qc0:qc0 + N],
                        start=True, stop=True),
                        reads=[WA.sub(Kc, jb * 128, jb * 128 + 128), WA.sub(QT[buf], qc0, qc0 + N)], writes=[pss])

            def stage_b(jb, cs_=(0, 1)):
                nq0, N, qc0 = geom(jb)
                c0 = nq0 * 128
                for c in cs_:
                    pss = sb3[(2 * jb + c) % 3]
                    pt = PT[jb % 2][c]
                    S.op("act", lambda e: e.activation(out=pt.ap[:, 0:N], in_=pss.ap[:, 0:N], func=ACTF.Exp, scale=0.125),
                         reads=[pss], writes=[pt])
                    eng = "dve" if c == 0 else "pool"
                    if jb >= 4 * G:
                        S.op("dve", lambda e: e.tensor_tensor(out=pt.ap[:, 0:128], in0=pt.ap[:, 0:128],
                                                               in1=C["triu"].ap, op=ALU.mult),
                             reads=[pt, C["triu"]], writes=[pt])

            def stage_c(jb):
                nq0, N, qc0 = geom(jb)
                c0 = nq0 * 128
                for c in range(2):
                    pt = PT[jb % 2][c]
                    S.op("pe", lambda e: e.matmul(
                        otb[c].ap[:, c0:512], lhsT=VA[buf].ap[:, jb, :], rhs=pt.ap[:, 0:N],
                        start=(jb == 0), stop=(jb == njb - 1)),
                        reads=[pt, WA.sub(VA[buf], jb * 128, jb * 128 + 128)], writes=[otb[c]])
                    S.op("pe", lambda e: e.matmul(
                        plb[c].ap[:, c0:512], lhsT=ones16.ap, rhs=pt.ap[:, 0:N],
                        start=(jb == 0), stop=(jb == njb - 1)),
                        reads=[pt, ones16], writes=[plb[c]])

            stage_a(0)
            for jb in range(njb):
                if jb + 1 < njb:
                    stage_a(jb + 1, (0,))
                stage_b(jb, (0,))
                if jb + 1 < njb:
                    stage_a(jb + 1, (1,))
                stage_b(jb, (1,))
                stage_c(jb)
            for c in range(2):
                S.op("act", lambda e, c=c: e.activation(out=lsb[gb][c].ap, in_=plb[c].ap, func=ACTF.Ln),
                     reads=[plb[c]], writes=[lsb[gb][c]])
                S.op("dve", lambda e, c=c: e.tensor_copy(out=ob[gb][c].ap, in_=otb[c].ap), reads=[otb[c]], writes=[ob[gb][c]])
            for c in range(2):
                S.op("act", lambda e, c=c: e.activation(out=rl[c].ap, in_=lsb[gb][c].ap, func=ACTF.Exp, scale=-1.0),
                     reads=[lsb[gb][c]], writes=[rl[c]])
            S.op("dve", lambda e: e.scalar_tensor_tensor(out=tt.ap, in0=ob[gb][1].ap, scalar=neglam.ap, in1=rl[1].ap,
                                                         op0=ALU.mult, op1=ALU.mult),
                 reads=[ob[gb][1], rl[1], neglam], writes=[tt])
            S.op("dve", lambda e: e.tensor_tensor(out=uu.ap, in0=ob[gb][0].ap, in1=rl[0].ap, op=ALU.mult),
                 reads=[ob[gb][0], rl[0]], writes=[uu])
            S.op("dve", lambda e: e.tensor_tensor(out=oo.ap, in0=uu.ap, in1=tt.ap, op=ALU.add), reads=[uu, tt], writes=[oo])
            S.op("dve", lambda e: e.tensor_tensor(out=osq.ap, in0=oo.ap, in1=oo.ap, op=ALU.mult), reads=[oo], writes=[osq])
            S.op("pe", lambda e: e.matmul(pmb.ap, lhsT=onesb.ap, rhs=osq.ap, start=True, stop=True),
                 reads=[onesb, osq], writes=[pmb])
            S.op("act", lambda e: e.activation(out=msb.ap, in_=pmb.ap, func=ACTF.Ln, bias=C["epscol"].ap),
                 reads=[pmb, C["epscol"]], writes=[msb])
            S.op("act", lambda e: e.activation(out=rs.ap, in_=msb.ap, func=ACTF.Exp, scale=-0.5),
                 reads=[msb], writes=[rs])
            oTt = oT[gb]
            S.op("dve", lambda e: e.scalar_tensor_tensor(out=oTt.ap, in0=oo.ap, scalar=sgc.ap, in1=rs.ap,
                                                         op0=ALU.mult, op1=ALU.mult),
                 reads=[oo, sgc, rs], writes=[oTt])
            S.dma("sp", out=sc["oT"][h, :, G * 512:(G + 1) * 512], in_=oTt.ap, reads=[oTt],
                  writes=[P.db(f"oT{h}_{G}")])


def phase_attn_out(P, xd, wod, sc, prefetch=None):
    S, WA, A, C = P.S, P.WA, P.A, P.C
    TT = 256
    NTT = P.ntok // TT
    WA.reset(); A.reset(C["a0"])
    WA.off = 64 * 1024
    Wo = WA.alloc(8 * 1024, "p (k f) -> p k f", k=8)
    load_weight_fast(P, Wo, wod, 8, 1024, piece=1024)
    if prefetch is not None:
        w1d, w2d = prefetch
        WA.off = 0
        W1 = WA.alloc(8 * 4096, "p (k f) -> p k f", k=8)
        W2 = WA.alloc(32 * 1024, "p (k f) -> p k f", k=32)
        load_weight(P, W1, w1d, 8, 4096)
        load_weight(P, W2, w2d, 32, 1024)
    xts = [A.alloc(2 * 1024, "p (b d) -> p b d", b=2) for _ in range(2)]
    oTs = [A.alloc16(8 * TT, "p (h t) -> p h t", h=8) for _ in range(2)]
    xv = xd.rearrange("(n b p) d -> n p b d", p=128, b=2)
    for it in range(NTT):
        xt = xts[it % 2]; ot = oTs[it % 2]
        xb = P.db(f"x{it}")
        S.dma("sp", out=xt.ap, in_=xv[it], reads=[xb], writes=[xt])
        S.dma("sp", out=ot.ap, in_=sc["oT"][:, :, it * TT:(it + 1) * TT].rearrange("h p t -> p h t"),
              reads=[P.db(f"oT{h}_{it // 2}") for h in range(8)], writes=[ot])
        residual_out(P, xt, None, Wo, 8, lambda k, b, ot=ot: ot.ap[:, k, b * 128:(b + 1) * 128],
                     lambda k, ot=ot: ot, P.ps[0:4])
        S.dma("sp", out=xv[it], in_=xt.ap, reads=[xt], writes=[xb])


def phase_s5(P, xd, gamd, wind, ared, aimd, bred, bimd, cred, cimd, dd, logdtd, wglud, bglud, woutd):
    S, WA, A, C = P.S, P.WA, P.A, P.C
    TT = 256
    NTT = P.ntok // TT
    WA.reset(); A.reset(C["a0"])
    ident = C["ident"]
    Win = WA.alloc(8 * 1024, "p (k f) -> p k f", k=8)
    Wglu = WA.alloc(8 * 1024, "p (k f) -> p k f", k=8)
    Wout = WA.alloc(8 * 1024, "p (k f) -> p k f", k=8)
    load_weight_fast(P, Win, wind, 8, 1024, piece=1024)
    load_weight_fast(P, Wglu, wglud, 8, 1024, piece=1024)
    load_weight_fast(P, Wout, woutd, 8, 1024, piece=1024)
    BL = WA.alloc(32 * 2 * 128, "p (q r m) -> p q r m", q=32, r=2)
    CBr = WA.alloc(32 * 3 * 128 + 2048)
    CBflat = CBr.ap
    Zr = [WA.alloc(512 + 256) for _ in range(2)]
    ZC = [WA.alloc(128, "p (g m) -> p g m", g=2) for _ in range(2)]
    rcol = A.alloc(32); ycol = A.alloc(32); carry = A.alloc(64, "p (q r) -> p q r", r=2)
    dcol = A.alloc(8); bgl = A.alloc(8)
    a_keep = A.off
    S.op("pool", lambda e: e.memset(carry.ap, 0.0), writes=[carry])
    S.dma("sp", out=dcol.ap, in_=dd.rearrange("o (c p) -> p (o c)", p=128), writes=[dcol], allow_slow_non_contiguous=True)
    S.dma("sp", out=bgl.ap, in_=bglud.rearrange("o (c p) -> p (o c)", p=128), writes=[bgl], allow_slow_non_contiguous=True)
    are = A.alloc(32); aim = A.alloc(32); ldt = A.alloc(32); dt = A.alloc(32)
    S.dma("sp", out=are.ap, in_=ared.rearrange("(q g2) p -> (g2 p) q", g2=2), writes=[are], allow_slow_non_contiguous=True)
    S.dma("sp", out=aim.ap, in_=aimd.rearrange("(q g2) p -> (g2 p) q", g2=2), writes=[aim], allow_slow_non_contiguous=True)
    ld2 = logdtd.rearrange("o (q g2) -> (o g2) q", g2=2)
    for g2 in range(2):
        S.dma("sp", out=ldt.ap[g2 * 64:(g2 + 1) * 64, :], in_=ld2[g2:g2 + 1, :].partition_broadcast(64), writes=[ldt],
              allow_slow_non_contiguous=True)
    S.op("act", lambda e: e.activation(out=dt.ap, in_=ldt.ap, func=ACTF.Exp), reads=[ldt], writes=[dt])
    S.op("dve", lambda e: e.tensor_scalar(out=are.ap, in0=are.ap, scalar1=-1e-4, scalar2=None, op0=ALU.min),
         reads=[are], writes=[are])
    rdt = A.alloc(32)
    S.op("dve", lambda e: e.tensor_tensor(out=rdt.ap, in0=are.ap, in1=dt.ap, op=ALU.mult), reads=[are, dt], writes=[rdt])
    S.op("act", lambda e: e.activation(out=rcol.ap, in_=rdt.ap, func=ACTF.Exp), reads=[rdt], writes=[rcol])
    S.op("dve", lambda e: e.scalar_tensor_tensor(out=ycol.ap, in0=aim.ap, scalar=float(1.0 / (2.0 * math.pi)), in1=dt.ap,
                                                 op0=ALU.mult, op1=ALU.mult), reads=[aim, dt], writes=[ycol])
    snt = A.alloc(32); cst = A.alloc(32); kit = A.alloci(32); frt = A.alloc(32)
    emit_sincos(P, ycol, snt, cst, kit, frt)
    nr = A.alloc(32); ni = A.alloc(32); den = A.alloc(32); t_a = A.alloc(32); t_b = A.alloc(32)
    zr = A.alloc(32); zi = A.alloc(32)
    S.op("dve", lambda e: e.tensor_tensor(out=nr.ap, in0=rcol.ap, in1=cst.ap, op=ALU.mult), reads=[rcol, cst], writes=[nr])
    S.op("dve", lambda e: e.tensor_scalar(out=nr.ap, in0=nr.ap, scalar1=-1.0, scalar2=None, op0=ALU.add), reads=[nr], writes=[nr])
    S.op("dve", lambda e: e.tensor_tensor(out=ni.ap, in0=rcol.ap, in1=snt.ap, op=ALU.mult), reads=[rcol, snt], writes=[ni])
    S.op("dve", lambda e: e.tensor_tensor(out=den.ap, in0=are.ap, in1=are.ap, op=ALU.mult), reads=[are], writes=[den])
    S.op("dve", lambda e: e.tensor_tensor(out=t_a.ap, in0=aim.ap, in1=aim.ap, op=ALU.mult), reads=[aim], writes=[t_a])
    S.op("dve", lambda e: e.tensor_tensor(out=den.ap, in0=den.ap, in1=t_a.ap, op=ALU.add), reads=[den, t_a], writes=[den])
    S.op("dve", lambda e: e.reciprocal(out=den.ap, in_=den.ap), reads=[den], writes=[den])
    S.op("dve", lambda e: e.tensor_tensor(out=t_a.ap, in0=nr.ap, in1=are.ap, op=ALU.mult), reads=[nr, are], writes=[t_a])
    S.op("dve", lambda e: e.tensor_tensor(out=t_b.ap, in0=ni.ap, in1=aim.ap, op=ALU.mult), reads=[ni, aim], writes=[t_b])
    S.op("dve", lambda e: e.tensor_tensor(out=t_a.ap, in0=t_a.ap, in1=t_b.ap, op=ALU.add), reads=[t_a, t_b], writes=[t_a])
    S.op("dve", lambda e: e.tensor_tensor(out=zr.ap, in0=t_a.ap, in1=den.ap, op=ALU.mult), reads=[t_a, den], writes=[zr])
    S.op("dve", lambda e: e.tensor_tensor(out=t_a.ap, in0=ni.ap, in1=are.ap, op=ALU.mult), reads=[ni, are], writes=[t_a])
    S.op("dve", lambda e: e.tensor_tensor(out=t_b.ap, in0=nr.ap, in1=aim.ap, op=ALU.mult), reads=[nr, aim], writes=[t_b])
    S.op("dve", lambda e: e.tensor_tensor(out=t_a.ap, in0=t_a.ap, in1=t_b.ap, op=ALU.subtract), reads=[t_a, t_b], writes=[t_a])
    S.op("dve", lambda e: e.tensor_tensor(out=zi.ap, in0=t_a.ap, in1=den.ap, op=ALU.mult), reads=[t_a, den], writes=[zi])
    Bre = A.alloc(512, "p (q h) -> p q h", h=16); Bim = A.alloc(512, "p (q h) -> p q h", h=16)
    S.dma("sp", out=Bre.ap, in_=bred.rearrange("(q g2) p h -> (g2 p) q h", g2=2), writes=[Bre])
    S.dma("sp", out=Bim.ap, in_=bimd.rearrange("(q g2) p h -> (g2 p) q h", g2=2), writes=[Bim])
    zrb = zr.ap.unsqueeze(2).broadcast_to([128, 32, 16])
    zib = zi.ap.unsqueeze(2).broadcast_to([128, 32, 16])
    M1 = A.alloc(512, "p (q h) -> p q h", h=16); M2 = A.alloc(512, "p (q h) -> p q h", h=16)
    Bb = [A.alloc(512, "p (q h) -> p q h", h=16) for _ in range(2)]
    S.op("dve", lambda e: e.tensor_tensor(out=M1.ap, in0=Bre.ap, in1=zrb, op=ALU.mult), reads=[Bre, zr], writes=[M1])
    S.op("dve", lambda e: e.tensor_tensor(out=M2.ap, in0=Bim.ap, in1=zib, op=ALU.mult), reads=[Bim, zi], writes=[M2])
    S.op("dve", lambda e: e.tensor_tensor(out=Bb[0].ap, in0=M1.ap, in1=M2.ap, op=ALU.subtract), reads=[M1, M2], writes=[Bb[0]])
    S.op("dve", lambda e: e.tensor_tensor(out=M1.ap, in0=Bim.ap, in1=zrb, op=ALU.mult), reads=[Bim, zr], writes=[M1])
    S.op("dve", lambda e: e.tensor_tensor(out=M2.ap, in0=Bre.ap, in1=zib, op=ALU.mult), reads=[Bre, zi], writes=[M2])
    S.op("dve", lambda e: e.tensor_tensor(out=Bb[1].ap, in0=M1.ap, in1=M2.ap, op=ALU.add), reads=[M1, M2], writes=[Bb[1]])
    pst = P.ps[7]
    pv = ps_bf16(pst)
    n = 0
    for k in range(8):
        for ri in range(2):
            Z = Zr[n % 2]
            n += 1
            S.op("pool", lambda e, Z=Z: e.memset(Z.ap, 0.0), writes=[Z])
            for g2 in range(2):
                dst = Z.ap[g2 * 64:(g2 + 1) * 64, g2 * 16:g2 * 16 + 640].rearrange("p (q m) -> p q m", m=160)[:, :, 0:16]
                S.op("dve", lambda e, dst=dst, g2=g2, ri=ri, k=k: e.tensor_copy(
                    out=dst, in_=Bb[ri].ap[g2 * 64:(g2 + 1) * 64, 4 * k:4 * k + 4, :]),
                    reads=[Bb[ri]], writes=[Z])
            for ql in range(4):
                S.op("pe", lambda e, Z=Z, ql=ql: e.transpose(out=pv[:, ql * 128:(ql + 1) * 128],
                                                            in_=Z.ap[:, ql * 128:(ql + 1) * 128], identity=ident.ap),
                     reads=[Z, ident], writes=[pst])
            S.op("act", lambda e, k=k, ri=ri: e.activation(out=BL.ap[:, 4 * k:4 * k + 4, ri, :],
                                                           in_=pv[:, 0:512].rearrange("p (q m) -> p q m", q=4),
                                                           func=ACTF.Copy),
                 reads=[pst], writes=[BL])
    Cn = [A.alloc(512, "p (k m) -> p k m", k=8) for _ in range(2)]
    S.dma("sp", out=Cn[0].ap, in_=cred.rearrange("(k gl) h p -> (gl h) k p", gl=8), writes=[Cn[0]])
    S.dma("sp", out=Cn[1].ap, in_=cimd.rearrange("(k gl) h p -> (gl h) k p", gl=8), writes=[Cn[1]])
    S.op("pool", lambda e: e.memset(CBflat, 0.0), writes=[CBr])
    pst2 = P.ps[6]
    pv2 = ps_bf16(pst2)
    n = 0
    for k in range(8):
        for ri in range(2):
            zc = ZC[n % 2]
            n += 1
            for g2 in range(2):
                S.op("dve", lambda e, zc=zc, g2=g2, ri=ri, k=k: e.tensor_scalar(
                    out=zc.ap[:, g2, :], in0=Cn[ri].ap[:, k, :], scalar1=C["par"].ap[:, g2:g2 + 1], scalar2=None,
                    op0=ALU.mult), reads=[Cn[ri], C["par"]], writes=[zc])
            S.op("pe", lambda e, zc=zc: e.transpose(out=pv2[:, 0:128], in_=zc.ap.rearrange("p g m -> p (g m)"),
                                                   identity=ident.ap), reads=[zc, ident], writes=[pst2])
            for var in ((0, 2) if ri == 0 else (1,)):
                base = 4 * k * 384 + var * 128
                dst = CBflat[:, base:base + 4 * 416].rearrange("p (q m) -> p q m", m=416)[:, :, 0:32]
                sgn = 1.0 if var == 0 else -1.0
                S.op("act", lambda e, dst=dst, sgn=sgn: e.activation(
                    out=dst, in_=pv2[:, 0:128].rearrange("p (q m) -> p q m", q=4), func=ACTF.Copy, scale=sgn),
                    reads=[pst2], writes=[CBr])
    CB = CBflat[:, 0:32 * 384].rearrange("p (q v m) -> p q v m", q=32, v=3)
    CS = WA.alloc(32 * TT, "p (q k) -> p q k", q=32)
    SN = WA.alloc(32 * TT, "p (q k) -> p q k", q=32)
    A.reset(a_keep)
    cT = A.alloc(32); sT = A.alloc(32)
    yT = A.alloc(32); kiT = A.alloci(32); frT = A.alloc(32)
    S.op("dve", lambda e: e.tensor_scalar(out=yT.ap, in0=ycol.ap, scalar1=float(TT), scalar2=None, op0=ALU.mult),
         reads=[ycol], writes=[yT])
    emit_sincos(P, yT, sT, cT, kiT, frT)
    a_keep2 = A.off
    yyt = [A.alloc(TT) for _ in range(2)]; kit2 = [A.alloci(TT) for _ in range(2)]; frt2 = [A.alloc(TT) for _ in range(2)]
    for q in range(32):
        yy_, ki_, fr_ = yyt[q % 2], kit2[q % 2], frt2[q % 2]
        S.op("dve", lambda e: e.tensor_scalar(out=yy_.ap, in0=C["tg0"].ap[:, 0:TT], scalar1=ycol.ap[:, q:q + 1], scalar2=None,
                                              op0=ALU.mult), reads=[C["tg0"], ycol], writes=[yy_])
        snq = Region(SN.ap[:, q, :], WA.sub(SN, q * TT, q * TT + TT))
        csq = Region(CS.ap[:, q, :], WA.sub(CS, q * TT, q * TT + TT))
        emit_sincos(P, yy_, snq, csq, ki_, fr_)
    A.reset(a_keep2)
    gam = load_gamma(P, gamd)
    xt = A.alloc(2 * 1024, "p (b d) -> p b d", b=2)
    scr = norm_scratch(P)
    hT = A.alloc16(8 * TT, "p (k t) -> p k t", k=8)
    uT = A.alloc16(8 * TT, "p (k t) -> p k t", k=8)
    gT = A.alloc16(8 * TT, "p (k t) -> p k t", k=8)
    ggT = A.alloc16(8 * TT, "p (k t) -> p k t", k=8)
    ydt = A.alloc(TT); sig = A.alloc(TT)
    NSET = 3

    def mkset(i):
        al = (lambda n: A.own(A.alloc(n))) if i == 0 else (lambda n: WA.own(WA.alloc32(n)))
        al16 = (lambda n: A.own(A.alloc16(n))) if i == 0 else (lambda n: WA.own(WA.alloc(n)))
        return dict(m=[al(TT) for _ in range(4)], w=[al(TT) for _ in range(2)], z=[al(TT) for _ in range(2)],
                    Y=[al16(TT) for _ in range(4)], ct=al(8))
    tsets = [mkset(0 if i < 2 else 1) for i in range(NSET)]
    cbufs = [Buf(f"carry{q}") for q in range(32)]
    xv = xd.rearrange("(n b p) d -> n p b d", p=128, b=2)
    gp = 0
    for it in range(NTT):
        xb = P.db(f"x{it}")
        S.dma("sp", out=xt.ap, in_=xv[it], reads=[xb], writes=[xt])
        for b in range(2):
            emit_norm_T(P, xt.ap[:, b, :], xt, gam, hT, b * 128, ident, dict(scr, pst=P.ps[6 + b]))
        for cc in range(8):
            pu = P.ps[cc % 2]
            for kc in range(8):
                S.op("pe", lambda e: e.matmul(
                    pu.ap[:, 0:TT], lhsT=Win.ap[:, kc, cc * 128:(cc + 1) * 128], rhs=hT.ap[:, kc, :],
                    start=(kc == 0), stop=(kc == 7)),
                    reads=[hT, WA.sub(Win, kc * 1024 + cc * 128, kc * 1024 + cc * 128 + 128)], writes=[pu])
            S.op("act", lambda e: e.activation(out=uT.ap[:, cc, :], in_=pu.ap[:, 0:TT], func=ACTF.Copy),
                 reads=[pu], writes=[A.sub(uT, cc * TT, cc * TT + TT)])

        def stB(q, part):
            cc = q // 4
            ts = tsets[(gp + q) % NSET]
            pb = [P.ps[2 + ((gp + q) % 2) * 2 + ri] for ri in range(2)]
            us = A.sub(uT, cc * TT, cc * TT + TT)
            m, w = ts["m"], ts["w"]
            csq = WA.sub(CS, q * TT, q * TT + TT); snq = WA.sub(SN, q * TT, q * TT + TT)
            if part == 0:
                for ri in range(2):
                    S.op("pe", lambda e: e.matmul(pb[ri].ap[:, 0:TT], lhsT=BL.ap[:, q, ri, :], rhs=uT.ap[:, cc, :],
                                                  start=True, stop=True), reads=[BL, us], writes=[pb[ri]])
                return
            for idx, (ri, tab, tb) in enumerate(((0, CS, csq), (1, SN, snq), (1, CS, csq), (0, SN, snq))):
                S.op("dve", lambda e: e.tensor_tensor(out=m[idx].ap, in0=pb[ri].ap[:, 0:TT], in1=tab.ap[:, q, :], op=ALU.mult),
                     reads=[pb[ri], tb], writes=[m[idx]])
            S.op("pool", lambda e: e.tensor_tensor(out=w[0].ap, in0=m[0].ap, in1=m[1].ap, op=ALU.add),
                 reads=[m[0], m[1]], writes=[w[0]])
            S.op("pool", lambda e: e.tensor_tensor(out=w[1].ap, in0=m[2].ap, in1=m[3].ap, op=ALU.subtract),
                 reads=[m[2], m[3]], writes=[w[1]])

        def stC(q):
            ts = tsets[(gp + q) % NSET]
            w, z, ct = ts["w"], ts["z"], ts["ct"]
            rbc = rcol.ap[:, q:q + 1].broadcast_to([128, TT])
            for ri in range(2):
                S.op("dve", lambda e: e.tensor_tensor_scan(
                    out=z[ri].ap, data0=rbc, data1=w[ri].ap, initial=carry.ap[:, q, ri:ri + 1],
                    op0=ALU.mult, op1=ALU.add), reads=[rcol, w[ri], cbufs[q]], writes=[z[ri]])
            zl = [z[0].ap[:, TT - 1:TT], z[1].ap[:, TT - 1:TT]]
            for idx, (ri, tab) in enumerate(((0, cT), (1, sT), (1, cT), (0, sT))):
                S.op("act", lambda e: e.activation(out=ct.ap[:, idx:idx + 1], in_=zl[ri], func=ACTF.Copy,
                                                   scale=tab.ap[:, q:q + 1]),
                     reads=[z[ri], tab], writes=[ct])
            S.op("pool", lambda e: e.tensor_tensor(out=carry.ap[:, q, 0:1], in0=ct.ap[:, 0:1], in1=ct.ap[:, 1:2], op=ALU.subtract),
                 reads=[ct], writes=[cbufs[q]])
            S.op("pool", lambda e: e.tensor_tensor(out=carry.ap[:, q, 1:2], in0=ct.ap[:, 2:3], in1=ct.ap[:, 3:4], op=ALU.add),
                 reads=[ct], writes=[cbufs[q]])

        def stD(q):
            cc, ql = q // 4, q % 4
            ts = tsets[(gp + q) % NSET]
            z, Y = ts["z"], ts["Y"]
            py = P.ps[cc % 2]
            csq = WA.sub(CS, q * TT, q * TT + TT); snq = WA.sub(SN, q * TT, q * TT + TT)
            S.op("pool", lambda e: e.tensor_tensor(out=Y[0].ap, in0=z[0].ap, in1=CS.ap[:, q, :], op=ALU.mult),
                 reads=[z[0], csq], writes=[Y[0]])
            S.op("pool", lambda e: e.tensor_tensor(out=Y[1].ap, in0=z[1].ap, in1=CS.ap[:, q, :], op=ALU.mult),
                 reads=[z[1], csq], writes=[Y[1]])
            S.op("dve", lambda e: e.tensor_tensor(out=Y[2].ap, in0=z[0].ap, in1=SN.ap[:, q, :], op=ALU.mult),
                 reads=[z[0], snq], writes=[Y[2]])
            S.op("dve", lambda e: e.tensor_tensor(out=Y[3].ap, in0=z[1].ap, in1=SN.ap[:, q, :], op=ALU.mult),
                 reads=[z[1], snq], writes=[Y[3]])
            for n_, (var, yi) in enumerate(((0, 0), (1, 1), (1, 2), (2, 3))):
                S.op("pe", lambda e: e.matmul(py.ap[:, 0:TT], lhsT=CB[:, q, var, :], rhs=Y[yi].ap,
                                              start=(ql == 0 and n_ == 0), stop=(ql == 3 and n_ == 3)),
                     reads=[CBr, Y[yi]], writes=[py])
            if ql == 3:
                us = A.sub(uT, cc * TT, cc * TT + TT)
                S.op("dve", lambda e: e.scalar_tensor_tensor(
                    out=ydt.ap, in0=uT.ap[:, cc, :], scalar=dcol.ap[:, cc:cc + 1], in1=py.ap[:, 0:TT],
                    op0=ALU.mult, op1=ALU.add), reads=[us, dcol, py], writes=[ydt])
                S.op("act", lambda e: e.activation(out=gT.ap[:, cc, :], in_=ydt.ap, func=ACTF.Gelu_apprx_tanh),
                     reads=[ydt], writes=[A.sub(gT, cc * TT, cc * TT + TT)])

        for s_ in range(32 + 2):
            if s_ < 32:
                stB(s_, 0)
            if 0 <= s_ - 2 < 32:
                stD(s_ - 2)
            if 0 <= s_ - 1 < 32:
                stC(s_ - 1)
            if s_ < 32:
                stB(s_, 1)
        gp += 32
        for c2 in range(8):
            pz = P.ps[6 + c2 % 2]
            for cc in range(8):
                S.op("pe", lambda e: e.matmul(
                    pz.ap[:, 0:TT], lhsT=Wglu.ap[:, cc, c2 * 128:(c2 + 1) * 128], rhs=gT.ap[:, cc, :],
                    start=(cc == 0), stop=(cc == 7)),
                    reads=[A.sub(gT, cc * TT, cc * TT + TT), WA.sub(Wglu, cc * 1024 + c2 * 128, cc * 1024 + c2 * 128 + 128)],
                    writes=[pz])
            S.op("act", lambda e: e.activation(out=sig.ap, in_=pz.ap[:, 0:TT], func=ACTF.Sigmoid, bias=bgl.ap[:, c2:c2 + 1]),
                 reads=[pz, bgl], writes=[sig])
            S.op("pool", lambda e: e.tensor_tensor(out=ggT.ap[:, c2, :], in0=gT.ap[:, c2, :], in1=sig.ap, op=ALU.mult),
                 reads=[A.sub(gT, c2 * TT, c2 * TT + TT), sig], writes=[A.sub(ggT, c2 * TT, c2 * TT + TT)])
        residual_out(P, xt, None, Wout, 8, lambda k, b: ggT.ap[:, k, b * 128:(b + 1) * 128],
                     lambda k: A.sub(ggT, k * TT, k * TT + TT), P.ps[2:6])
        S.dma("sp", out=xv[it], in_=xt.ap, reads=[xt], writes=[xb])


def host_consts():
    bf = ml_dtypes.bfloat16
    c = {}
    c["c_ident"] = np.eye(128, dtype=np.float32).astype(bf)
    p = np.arange(128)
    c["c_bones"] = ((p[:, None] // 64) == (p[None, :] // 64)).astype(np.float32).astype(bf) * np.float32(1.0 / 64.0)
    c["c_bones"] = c["c_bones"].astype(bf)
    rr = np.zeros((128, 128), np.float32)
    for cc in range(2):
        for d in range(64):
            mcol = cc * 64 + d
            if d < 32:
                rr[cc * 64 + d + 32, mcol] = -1.0
            else:
                rr[cc * 64 + d - 32, mcol] = 1.0
    c["c_rrot"] = rr.astype(bf)
    invf = (10000.0 ** (-np.arange(0, 64, 2, dtype=np.float32) / 64.0)).astype(np.float32)
    c["c_invf"] = (invf[(p % 64) % 32] / np.float32(2.0 * math.pi)).astype(np.float32).reshape(128, 1)
    c["c_triu"] = (p[:, None] <= p[None, :]).astype(np.float32).astype(bf)
    c["c_tril"] = (p[None, :] <= p[:, None]).astype(np.float32).astype(bf)
    c["c_tg0"] = np.broadcast_to(np.arange(256, dtype=np.float32)[None, :], (128, 256)).copy()
    par = np.zeros((128, 2), np.float32)
    par[:, 0] = ((p // 16) % 2 == 0)
    par[:, 1] = ((p // 16) % 2 == 1)
    c["c_par"] = par
    return c


def setup_consts(P):
    S, A = P.S, P.A
    C = {}
    A.reset()

    def ld(name, n, dtype, shape):
        d = P.din("c_" + name, shape, dtype)
        r = A.alloc16(n) if dtype == BF16 else A.alloc(n)
        S.dma("sp", out=r.ap, in_=d, writes=[r])
        C[name] = r

    ld("ident", 128, BF16, [128, 128])
    ld("bones", 128, BF16, [128, 128])
    ld("rrot", 128, BF16, [128, 128])
    ld("triu", 128, BF16, [128, 128])
    ld("tril", 128, BF16, [128, 128])
    ld("invf", 1, F32, [128, 1])
    ld("tg0", 256, F32, [128, 256])
    ld("par", 2, F32, [128, 2])
    C["mhalf"] = A.alloc(1)
    S.op("pool", lambda e: e.memset(C["mhalf"].ap, -0.5), writes=[C["mhalf"]])
    C["epscol"] = A.alloc(1)
    S.op("pool", lambda e: e.memset(C["epscol"].ap, EPS), writes=[C["epscol"]])
    A.off = ((A.off + 255) // 256) * 256
    C["a0"] = A.off
    P.C = C
    return C


def lambda_init_of(li):
    return 0.8 - 0.6 * math.exp(-0.3 * li)


def build(spec, ntok=SEQ, debug=False):
    P = Prog(ntok)
    P.DEBUG = debug
    S = P.S
    xin = P.din("x", [ntok, D])
    xout = P.dout("y", [ntok, D])
    setup_consts(P)
    nt = ntok // 256
    xiv = xin.rearrange("(n t) d -> n t d", t=256)
    xov = xout.rearrange("(n t) d -> n t d", t=256)
    for it in range(nt):
        S.dma("sp", out=xov[it], in_=xiv[it], writes=[P.db(f"x{it}")])
    sc = None
    posd = None
    preloaded = set()
    for si, (kind, li) in enumerate(spec):
        if kind == "mlp":
            phase_mlp(P, xout, P.din(f"mlp_w1_{li}", [D, 4 * D]) if li not in preloaded else None,
                      P.din(f"mlp_w2_{li}", [4 * D, D]) if li not in preloaded else None,
                      P.din(f"norm_mlp_{li}", [1, D]), preloaded=(li in preloaded))
            continue
        gamd = P.din(f"norm_mix_{li}", [1, D])
        mk = li % 3
        j = li // 3
        if mk == 0:
            if sc is None:
                sc = dict(qT=P.dscratch("sc_qT", [8, 128, ntok], BF16), kT=P.dscratch("sc_kT", [8, 128, ntok], BF16),
                          v=P.dscratch("sc_v", [ntok // 128, 128, 8 * 129], BF16),
                          oT=P.dscratch("sc_oT", [8, 128, ntok], BF16))
                posd = P.din("positions", [1, ntok], I32)
            phase_attn_qkv(P, xout, gamd, P.din(f"attn_w_qkv_{j}", [D, 3 * D]), P.din(f"attn_q_norm_{j}", [64, 1]),
                           P.din(f"attn_k_norm_{j}", [64, 1]), posd, sc)
            phase_attn_core(P, P.din(f"attn_lambda_{j}", [4, 64]), P.din(f"attn_sub_norm_{j}", [1, 128]),
                            lambda_init_of(li), sc)
            pf = None
            if si + 1 < len(spec) and spec[si + 1] == ("mlp", li):
                pf = (P.din(f"mlp_w1_{li}", [D, 4 * D]), P.din(f"mlp_w2_{li}", [4 * D, D]))
                preloaded.add(li)
            phase_attn_out(P, xout, P.din(f"attn_w_o_{j}", [D, D]), sc, prefetch=pf)
        elif mk == 1:
            phase_s5(P, xout, gamd, P.din(f"ssm_w_in_{j}", [D, D]), P.din(f"ssm_a_re_{j}", [64, 64]),
                     P.din(f"ssm_a_im_{j}", [64, 64]), P.din(f"ssm_b_re_{j}", [64, 64, 16]),
                     P.din(f"ssm_b_im_{j}", [64, 64, 16]), P.din(f"ssm_c_re_{j}", [64, 16, 64]),
                     P.din(f"ssm_c_im_{j}", [64, 16, 64]), P.din(f"ssm_d_{j}", [1, D]),
                     P.din(f"ssm_log_dt_{j}", [1, 64]), P.din(f"ssm_w_glu_{j}", [D, D]),
                     P.din(f"ssm_b_glu_{j}", [1, D]), P.din(f"ssm_w_out_{j}", [D, D]))
        else:
            phase_gmlp(P, xout, gamd, P.din(f"gm_w_in_{j}", [D, 6 * D]), P.din(f"gm_v_norm_{j}", [1, 3 * D]),
                       P.din(f"gm_w_s_{j}", [8, 128, 128]), P.din(f"gm_b_s_{j}", [8, 128]),
                       P.din(f"gm_w_out_{j}", [3 * D, D]))
    S.wait_bufs("sp", [P.db(f"x{it}") for it in range(nt)] + getattr(P, "dbg", []))
    return P


FULL_SPEC = [("mix", 0), ("mlp", 0), ("mix", 1), ("mlp", 1), ("mix", 2), ("mlp", 2), ("mix", 3), ("mlp", 3)]

_SHAPES = {
    "norm_mix": (1, D), "norm_mlp": (1, D), "attn_q_norm": (64, 1), "attn_k_norm": (64, 1), "attn_sub_norm": (1, 128),
    "ssm_d": (1, D), "ssm_log_dt": (1, 64), "ssm_b_glu": (1, D), "gm_v_norm": (1, 3 * D),
}


def make_in_map(P, inputs, b, ntok=SEQ):
    m = dict(host_consts())
    out = {}
    for name in P.dram_in:
        if name in m:
            out[name] = m[name]
        elif name == "x":
            out[name] = np.ascontiguousarray(inputs["x"][b, :ntok])
        elif name == "positions":
            out[name] = np.ascontiguousarray(inputs["positions"][b, :ntok].reshape(1, ntok)).astype(np.int32)
        else:
            base, idx = name.rsplit("_", 1)
            arr = np.asarray(inputs[base])[int(idx)]
            shp = _SHAPES.get(base)
            if shp is not None:
                arr = arr.reshape(shp)
            out[name] = np.ascontiguousarray(arr, dtype=np.float32)
    return out


_PROG = {}


def kernel(**inputs):
    if "full" not in _PROG:
        _PROG["full"] = build(FULL_SPEC, SEQ)
    P = _PROG["full"]
    inputs = {k: np.asarray(v) for k, v in inputs.items()}
    maps = [make_in_map(P, inputs, c % BATCH) for c in range(NCORES)]
    res = run_bass_kernel_spmd(P.nc, maps, core_ids=list(range(NCORES)))
    return np.stack([np.asarray(res.results[b]["y"], dtype=np.float32) for b in range(BATCH)], axis=0)
```

```python
import math
from contextlib import ExitStack

import numpy as np
import ml_dtypes

import concourse.bass as bass
import concourse.mybir as mybir
from concourse.bass_utils import run_bass_kernel_spmd

F32 = mybir.dt.float32
BF16 = mybir.dt.bfloat16
I32 = mybir.dt.int32
ALU = mybir.AluOpType
ACTF = mybir.ActivationFunctionType
AX = mybir.AxisListType

D = 1024
SEQ = 4096
BATCH = 4
DEPTH = 4
EPS = 1e-6
NCORES = 8


class Buf:
    __slots__ = ("name", "w", "r")

    def __init__(self, name):
        self.name = name
        self.w = None
        self.r = {}


class Region:
    def __init__(self, ap, bufs):
        self.ap = ap
        self.bufs = bufs


def _flat(items):
    out = []
    for it in items:
        if it is None:
            continue
        if isinstance(it, Buf):
            out.append(it)
        elif isinstance(it, Region):
            out.extend(it.bufs)
        else:
            out.extend(_flat(it))
    return out


class Sched:
    GEN = 24000
    POOL_INFLIGHT = 8

    def __init__(self, nc, stack, ndma=32):
        self.nc = nc
        self.stack = stack
        self.eng = dict(pe=nc.tensor, act=nc.scalar, dve=nc.vector, pool=nc.gpsimd, sp=nc.sync)
        self.stream = {k: [] for k in self.eng}
        self.cnt = {k: 0 for k in self.eng}
        self.gen = {k: 0 for k in self.eng}
        self.semh = {}
        for k in ("pe", "act", "dve", "pool"):
            self.semh[(k, 0)] = stack.enter_context(nc.semaphore(f"s_{k}0"))
        self.ndma = ndma
        for j in range(ndma):
            self.semh[("d", j)] = stack.enter_context(nc.semaphore(f"s_d{j}"))
        self.dcnt = [0] * ndma
        self.dnext = 0
        self.known = {k: {} for k in self.eng}
        self.ninstr = 0
        self.pool_hist = []

    def _collect(self, reads, writes):
        need = {}
        for b in reads:
            ev = b.w
            if ev is not None and need.get(ev[0], 0) < ev[1]:
                need[ev[0]] = ev[1]
        for b in writes:
            ev = b.w
            if ev is not None and need.get(ev[0], 0) < ev[1]:
                need[ev[0]] = ev[1]
            for k, v in b.r.items():
                if need.get(k, 0) < v:
                    need[k] = v
        return need

    def _waits(self, e, need, skip_self=False):
        kn = self.known[e]
        st = self.stream[e]
        for k, v in need.items():
            if skip_self and k[0] == e:
                continue
            if kn.get(k, 0) >= v:
                continue
            kn[k] = v
            sem = self.semh[k]
            self.eng[e].wait_ge(sem, v)
            self.ninstr += 1

    def op(self, e, fn, reads=(), writes=()):
        reads = _flat(reads)
        writes = _flat(writes)
        need = self._collect(reads, writes)
        self._waits(e, need, skip_self=(e == "pe"))
        if self.cnt[e] >= self.GEN:
            self.gen[e] += 1
            self.cnt[e] = 0
            self.semh[(e, self.gen[e])] = self.stack.enter_context(
                self.nc.semaphore(f"s_{e}{self.gen[e]}"))
        self.cnt[e] += 1
        key = (e, self.gen[e])
        v = self.cnt[e]
        sem = self.semh[key]
        fn(self.eng[e]).then_inc(sem, 1)
        self.ninstr += 1
        for b in reads:
            b.r[key] = v
        for b in writes:
            b.w = (key, v)
            b.r = {}

    def dma(self, q, out, in_, reads=(), writes=(), **kw):
        reads = _flat(reads)
        writes = _flat(writes)
        need = self._collect(reads, writes)
        if q == "pool":
            hist = self.pool_hist
            if len(hist) >= self.POOL_INFLIGHT:
                k0, v0 = hist[-self.POOL_INFLIGHT]
                if need.get(k0, 0) < v0:
                    need[k0] = v0
        j = self.dnext
        self.dnext = (j + 1) % self.ndma
        if self.dcnt[j] > 0 and need.get(("d", j), 0) < self.dcnt[j]:
            need[("d", j)] = self.dcnt[j]
        self._waits(q, need)
        self.dcnt[j] += 16
        key = ("d", j)
        v = self.dcnt[j]
        sem = self.semh[key]
        self.eng[q].dma_start(out=out, in_=in_, **kw).then_inc(sem, 16)
        self.ninstr += 1
        if q == "pool":
            self.pool_hist.append((key, v))
        for b in reads:
            b.r[key] = v
        for b in writes:
            b.w = (key, v)
            b.r = {}

    def wait_bufs(self, e, bufs):
        bufs = _flat(bufs)
        need = self._collect((), bufs)
        self._waits(e, need)

    def finish(self):
        return
        nc = self.nc
        with nc.Block() as block:
            for name, deco in (("pe", block.tensor), ("act", block.scalar), ("dve", block.vector),
                               ("pool", block.gpsimd), ("sp", block.sync)):
                lst = self.stream[name]

                @deco
                def _(eng, lst=lst):
                    for th in lst:
                        th(eng)


class Arena:
    def __init__(self, nc, stack, name, nelem, dtype, chunk):
        self.t = stack.enter_context(nc.sbuf_tensor(name, [128, nelem], dtype))
        self.n = nelem
        self.chunk = chunk
        self.bufs = [Buf(f"{name}{i}") for i in range((nelem + chunk - 1) // chunk)]
        self.off = 0
        self.name = name

    def reset(self, off=0):
        for b, cbs in getattr(self, "owned", []):
            for cb in cbs:
                if b.w is not None:
                    cb.r[b.w[0]] = max(cb.r.get(b.w[0], 0), b.w[1])
                for k, v in b.r.items():
                    cb.r[k] = max(cb.r.get(k, 0), v)
        self.owned = []
        self.off = off

    def own(self, reg):
        b = Buf("own")
        for cb in reg.bufs:
            if cb.w is not None:
                b.r[cb.w[0]] = max(b.r.get(cb.w[0], 0), cb.w[1])
            for k, v in cb.r.items():
                b.r[k] = max(b.r.get(k, 0), v)
        if not hasattr(self, "owned"):
            self.owned = []
        self.owned.append((b, reg.bufs))
        reg.bufs = [b]
        return reg

    def alloc(self, n, pattern=None, **kw):
        off = self.off
        assert off + n <= self.n, f"arena {self.name} overflow: {off}+{n} > {self.n}"
        self.off = off + n
        return self.view(off, n, pattern, **kw)

    def view(self, off, n, pattern=None, **kw):
        ap = self.t[:, off:off + n]
        if pattern is not None:
            ap = ap.rearrange(pattern, **kw)
        c0 = off // self.chunk
        c1 = (off + n - 1) // self.chunk
        r = Region(ap, self.bufs[c0:c1 + 1])
        r.off = off
        r.n = n
        r.arena = self
        return r

    def sub(self, reg, lo, hi):
        m = getattr(reg, "mul", 1)
        c0 = (reg.off + lo // m) // self.chunk
        c1 = (reg.off + (hi - 1) // m) // self.chunk
        return self.bufs[c0:c1 + 1]

    def alloc16(self, n, pattern=None, **kw):
        n32 = (n + 1) // 2
        off = self.off
        assert off + n32 <= self.n, f"arena {self.name} overflow: {off}+{n32} > {self.n}"
        self.off = off + n32
        ap = self.t[:, off:off + n32].bitcast(BF16)[:, 0:n]
        if pattern is not None:
            ap = ap.rearrange(pattern, **kw)
        c0 = off // self.chunk
        c1 = (off + n32 - 1) // self.chunk
        r = Region(ap, self.bufs[c0:c1 + 1])
        r.off = off; r.n = n32; r.arena = self; r.mul = 2
        return r

    def alloc32(self, n, pattern=None, **kw):
        r = self.alloc(2 * n)
        ap = r.ap.bitcast(F32)
        if pattern is not None:
            ap = ap.rearrange(pattern, **kw)
        r.ap = ap
        return r

    def alloci(self, n, pattern=None, **kw):
        r = self.alloc(n)
        ap = r.ap.bitcast(I32)
        if pattern is not None:
            ap = ap.rearrange(pattern, **kw)
        r.ap = ap
        return r


class Prog:
    def __init__(self, ntok=SEQ):
        self.ntok = ntok
        self.nc = bass.Bass("TRN2", target_bir_lowering=False)
        self.stack = ExitStack()
        nc = self.nc
        self.S = Sched(nc, self.stack)
        st = self.stack
        self.WA = Arena(nc, st, "wa", 72 * 1024, BF16, 2048)
        self.A = Arena(nc, st, "aa", 15 * 1024 + 512, F32, 256)
        self.psum_t = st.enter_context(nc.psum_tensor("ps", [128, 8, 512], F32))
        self.ps = [Region(self.psum_t[:, b, :], [Buf(f"ps{b}")]) for b in range(8)]
        self.dram_in = {}
        self.consts = {}
        self.dbuf = {}

    def din(self, name, shape, dtype=F32):
        t = self.nc.dram_tensor(name, list(shape), dtype, kind="ExternalInput").ap()
        self.dram_in[name] = t
        return t

    def dout(self, name, shape, dtype=F32):
        return self.nc.dram_tensor(name, list(shape), dtype, kind="ExternalOutput").ap()

    def dscratch(self, name, shape, dtype):
        return self.nc.dram_tensor(name, list(shape), dtype, kind="Internal").ap()

    def debug(self, name, reg, shape, dtype):
        d = self.nc.dram_tensor("dbg_" + name, list(shape), dtype, kind="ExternalOutput").ap()
        self.S.dma("sp", out=d, in_=reg.ap, reads=[reg], writes=[self.db("dbg_" + name)])
        self.dbg = getattr(self, "dbg", [])
        self.dbg.append(self.db("dbg_" + name))

    def db(self, name):
        b = self.dbuf.get(name)
        if b is None:
            b = self.dbuf[name] = Buf(name)
        return b


def ps_bf16(reg):
    return reg.ap.bitcast(BF16)


def emit_norm_T(P, xt_ap, xt_bufs, gam, hT, tok_off, ident, scr):
    S = P.S
    sq, ss, rstd, xn, pst = scr["sq"], scr["ss"], scr["rstd"], scr["xn"], scr["pst"]
    S.op("act", lambda e: e.activation(out=sq.ap, in_=xt_ap, func=ACTF.Square, scale=1.0 / 32.0,
                                       accum_out=ss.ap),
         reads=[xt_bufs], writes=[sq, ss])
    S.op("pool", lambda e: e.tensor_scalar(out=ss.ap, in0=ss.ap, scalar1=EPS, scalar2=None, op0=ALU.add),
         reads=[ss], writes=[ss])
    S.op("pool", lambda e: e.tensor_tensor(out=rstd.ap, in0=ss.ap, in1=P.C["mhalf"].ap, op=ALU.pow),
         reads=[ss, P.C["mhalf"]], writes=[rstd])
    S.op("dve", lambda e: e.scalar_tensor_tensor(out=xn.ap, in0=xt_ap, scalar=rstd.ap, in1=gam.ap,
                                                 op0=ALU.mult, op1=ALU.mult),
         reads=[xt_bufs, rstd, gam], writes=[xn])
    pv = ps_bf16(pst)
    for kc in range(8):
        S.op("pe", lambda e, kc=kc: e.transpose(out=pv[:, kc * 128:(kc + 1) * 128],
                                                in_=xn.ap[:, kc * 128:(kc + 1) * 128],
                                                identity=ident.ap),
             reads=[xn, ident], writes=[pst])
    S.op("act", lambda e: e.activation(out=hT.ap[:, :, tok_off:tok_off + 128],
                                       in_=pv.rearrange("p (k t) -> p k t", k=8), func=ACTF.Copy),
         reads=[pst], writes=[hT])


def load_weight_fast(P, reg, dram_ap, nk, ncols, nstage=4, piece=2048):
    S, A = P.S, P.A
    piece = min(piece, ncols)
    off_keep = A.off
    A.off = A.n - nstage * piece
    stg = [A.alloc(piece) for _ in range(nstage)]
    A.off = off_keep
    i = 0
    for kc in range(nk):
        for c0 in range(0, ncols, piece):
            c1 = min(ncols, c0 + piece)
            st = stg[i % nstage]
            S.dma("sp", out=st.ap[:, 0:c1 - c0], in_=dram_ap[kc * 128:(kc + 1) * 128, c0:c1], writes=[st])
            lo = kc * ncols + c0
            dst = reg.arena.sub(reg, lo, lo + (c1 - c0))
            if i % 2 == 0:
                S.op("act", lambda e: e.activation(out=reg.ap[:, kc, c0:c1], in_=st.ap[:, 0:c1 - c0], func=ACTF.Copy),
                     reads=[st], writes=[dst])
            else:
                S.op("dve", lambda e: e.tensor_copy(out=reg.ap[:, kc, c0:c1], in_=st.ap[:, 0:c1 - c0]),
                     reads=[st], writes=[dst])
            i += 1


def load_weight(P, reg, dram_ap, rows_per_part_dim, ncols, q="pool"):
    S = P.S
    nk = rows_per_part_dim
    for kc in range(nk):
        for c0 in range(0, ncols, 2048):
            c1 = min(ncols, c0 + 2048)
            lo = kc * ncols + c0
            S.dma(q, out=reg.ap[:, kc, c0:c1], in_=dram_ap[kc * 128:(kc + 1) * 128, c0:c1],
                  writes=[reg.arena.sub(reg, lo, lo + (c1 - c0))])


TWO_PI = 6.283179


def norm_scratch(P):
    A = P.A
    return dict(sq=A.alloc16(1024), ss=A.alloc(1), rstd=A.alloc(1), xn=A.alloc16(1024))


def load_gamma(P, gamd):
    gam = P.A.alloc(1024)
    P.S.dma("sp", out=gam.ap, in_=gamd.partition_broadcast(128), writes=[gam])
    return gam


def residual_out(P, xt, xb_ap_fn, W, nk, lhs_fn, lhs_reads_fn, banks):
    S = P.S
    nb = xt.ap.shape[1]
    i = 0
    for b in range(nb):
        for oc in range(2):
            pb = banks[i % len(banks)]
            i += 1
            for k in range(nk):
                S.op("pe", lambda e, k=k, b=b, oc=oc, pb=pb: e.matmul(
                    pb.ap, lhsT=lhs_fn(k, b), rhs=W.ap[:, k, oc * 512:(oc + 1) * 512],
                    start=(k == 0), stop=(k == nk - 1)),
                    reads=[lhs_reads_fn(k), P.WA.sub(W, k * 1024 + oc * 512, k * 1024 + oc * 512 + 512)],
                    writes=[pb])
            S.op("dve", lambda e, b=b, oc=oc, pb=pb: e.tensor_tensor(
                out=xt.ap[:, b, oc * 512:(oc + 1) * 512], in0=pb.ap,
                in1=xt.ap[:, b, oc * 512:(oc + 1) * 512], op=ALU.add),
                reads=[pb, xt], writes=[xt])


def phase_mlp(P, xd, w1d, w2d, gamd, preloaded=False):
    S, WA, A, C = P.S, P.WA, P.A, P.C
    TT = 256
    NTT = P.ntok // TT
    WA.reset(); A.reset(C["a0"])
    W1 = WA.alloc(8 * 4096, "p (k f) -> p k f", k=8)
    W2 = WA.alloc(32 * 1024, "p (k f) -> p k f", k=32)
    if not preloaded:
        load_weight_fast(P, W1, w1d, 8, 4096)
        load_weight_fast(P, W2, w2d, 32, 1024, piece=1024)
    gam = load_gamma(P, gamd)
    xts = [A.alloc(2 * 1024, "p (b d) -> p b d", b=2) for _ in range(2)]
    sqfs = [A.own(A.alloc(256)) for _ in range(3)]
    hTs = [A.alloc16(8 * TT, "p (k t) -> p k t", k=8) for _ in range(2)]
    actT = A.alloc16(32 * TT, "p (f t) -> p f t", f=32)
    ident = C["ident"]
    xv = xd.rearrange("(n b p) d -> n p b d", p=128, b=2)
    groups = [list(range(g, min(g + 3, 32))) for g in range(0, 32, 3)]

    xns = [A.own(A.alloc16(1024)) for _ in range(2)]
    sqj = A.own(A.alloc16(1024)); ssn = A.alloc(1); rstdn = A.alloc(1)

    def load_x(it):
        S.dma("sp", out=xts[it % 2].ap, in_=xv[it], reads=[P.db(f"x{it}")], writes=[xts[it % 2]])

    def norm_pre(it, b):
        xt_ = xts[it % 2]; xn = xns[b]
        S.op("act", lambda e: e.activation(out=sqj.ap, in_=xt_.ap[:, b, :], func=ACTF.Square, scale=1.0 / 32.0,
                                           accum_out=ssn.ap), reads=[xt_], writes=[sqj, ssn])
        S.op("pool", lambda e: e.tensor_scalar(out=ssn.ap, in0=ssn.ap, scalar1=EPS, scalar2=None, op0=ALU.add),
             reads=[ssn], writes=[ssn])
        S.op("pool", lambda e: e.tensor_tensor(out=rstdn.ap, in0=ssn.ap, in1=C["mhalf"].ap, op=ALU.pow),
             reads=[ssn, C["mhalf"]], writes=[rstdn])
        S.op("dve", lambda e: e.scalar_tensor_tensor(out=xn.ap, in0=xt_.ap[:, b, :], scalar=rstdn.ap, in1=gam.ap,
                                                     op0=ALU.mult, op1=ALU.mult),
             reads=[xt_, rstdn, gam], writes=[xn])

    def norm_post(it, b):
        xn = xns[b]; hT_ = hTs[it % 2]
        pst = P.ps[6 + b]
        pv = ps_bf16(pst)
        for kc in range(8):
            S.op("pe", lambda e: e.transpose(out=pv[:, kc * 128:(kc + 1) * 128], in_=xn.ap[:, kc * 128:(kc + 1) * 128],
                                             identity=ident.ap), reads=[xn, ident], writes=[pst])
        S.op("act", lambda e: e.activation(out=hT_.ap[:, :, b * 128:(b + 1) * 128],
                                           in_=pv.rearrange("p (k t) -> p k t", k=8), func=ACTF.Copy),
             reads=[pst], writes=[hT_])

    def load_norm(it):
        load_x(it)
        for b in range(2):
            norm_pre(it, b)
            norm_post(it, b)

    load_norm(0)
    nsq = 0
    for it in range(NTT):
        xt = xts[it % 2]; hT = hTs[it % 2]
        xb = P.db(f"x{it}")
        if it + 1 < NTT:
            load_x(it + 1)
        for gi, grp in enumerate(groups):
            if it + 1 < NTT and gi in (5, 7):
                norm_pre(it + 1, (gi - 5) // 2)
            banks = [P.ps[(gi % 2) * 3 + n] for n in range(len(grp))]
            for kc in range(8):
                for n, fc in enumerate(grp):
                    pb = banks[n]
                    S.op("pe", lambda e: e.matmul(
                        pb.ap[:, 0:TT], lhsT=W1.ap[:, kc, fc * 128:(fc + 1) * 128], rhs=hT.ap[:, kc, :],
                        start=(kc == 0), stop=(kc == 7)),
                        reads=[WA.sub(W1, kc * 4096 + fc * 128, kc * 4096 + fc * 128 + 128), hT], writes=[pb])
            for n, fc in enumerate(grp):
                pb = banks[n]
                sqf = sqfs[nsq % 3]
                nsq += 1
                S.op("act", lambda e: e.activation(out=sqf.ap, in_=pb.ap[:, 0:TT], func=ACTF.Relu),
                     reads=[pb], writes=[sqf])
                S.op("dve", lambda e: e.tensor_tensor(out=actT.ap[:, fc, :], in0=sqf.ap, in1=sqf.ap, op=ALU.mult),
                     reads=[sqf], writes=[A.sub(actT, fc * TT, fc * TT + TT)])
        if it + 1 < NTT:
            norm_post(it + 1, 0)
            norm_post(it + 1, 1)
        residual_out(P, xt, None, W2, 32,
                     lambda k, b: actT.ap[:, k, b * 128:(b + 1) * 128],
                     lambda k: A.sub(actT, k * TT, k * TT + TT), P.ps[0:4])
        S.dma("sp", out=xv[it], in_=xt.ap, reads=[xt], writes=[xb])


def phase_gmlp(P, xd, gamd, wind, vgd, wsd, bsd, woutd):
    S, WA, A, C = P.S, P.WA, P.A, P.C
    TT = 256
    NTT = P.ntok // TT
    WA.reset(); A.reset(C["a0"])
    Win = WA.alloc(8 * 6144, "p (k f) -> p k f", k=8)
    Wout = WA.alloc(24 * 1024, "p (k f) -> p k f", k=24)
    load_weight_fast(P, Win, wind, 8, 6144)
    load_weight_fast(P, Wout, woutd, 24, 1024, piece=1024)
    gam = load_gamma(P, gamd)
    ident = C["ident"]
    WmT = A.alloc16(1024, "p (h t) -> p h t", h=8)
    vg = A.alloc(24)
    bs = A.alloc(1024, "p (h t) -> p h t", h=8)
    off_keep = A.off
    A.off = A.n - 512
    wn = A.alloc16(1024, "p (h s) -> p h s", h=8)
    A.off = off_keep
    S.dma("pool", out=wn.ap, in_=wsd.rearrange("h t s -> t h s"), writes=[wn])
    S.op("dve", lambda e: e.tensor_tensor(out=wn.ap, in0=wn.ap,
                                          in1=C["tril"].ap.unsqueeze(1).broadcast_to([128, 8, 128]), op=ALU.mult),
         reads=[wn, C["tril"]], writes=[wn])
    pst = P.ps[7]
    pv = ps_bf16(pst)
    for h in range(8):
        S.op("pe", lambda e, h=h: e.transpose(out=pv[:, h * 128:(h + 1) * 128], in_=wn.ap[:, h, :], identity=ident.ap),
             reads=[wn, ident], writes=[pst])
    S.op("act", lambda e: e.activation(out=WmT.ap, in_=pv.rearrange("p (h t) -> p h t", h=8), func=ACTF.Copy),
         reads=[pst], writes=[WmT])
    S.dma("sp", out=vg.ap, in_=vgd.rearrange("o (c p) -> p (o c)", p=128), writes=[vg], allow_slow_non_contiguous=True)
    S.dma("sp", out=bs.ap, in_=bsd.rearrange("h t -> (h t)").partition_broadcast(128), writes=[bs])
    xts = [A.alloc(2 * 1024, "p (b d) -> p b d", b=2)] * 2
    scr = norm_scratch(P)
    hT = A.alloc16(8 * TT, "p (k t) -> p k t", k=8)
    gv = [A.alloc16(3072) for _ in range(2)]
    ssq = [A.alloc(8) for _ in range(2)]
    sst = [A.alloc(1) for _ in range(2)]
    rsv = [A.alloc(1) for _ in range(2)]
    junk = scr["sq"]
    WmTs = [A.alloc16(1024, "p (h t) -> p h t", h=8) for _ in range(2)]
    ug = [A.own(A.alloc(TT)) for _ in range(2)]
    tmp = [A.own(A.alloc(TT)) for _ in range(2)]
    uvT = A.alloc16(24 * TT, "p (c t) -> p c t", c=24)
    xv = xd.rearrange("(n b p) d -> n p b d", p=128, b=2)
    hTs = [hT, hT]

    def load_norm(it):
        xt_ = xts[it % 2]
        S.dma("sp", out=xt_.ap, in_=xv[it], reads=[P.db(f"x{it}")], writes=[xt_])
        for b in range(2):
            emit_norm_T(P, xt_.ap[:, b, :], xt_, gam, hTs[it % 2], b * 128, ident, dict(scr, pst=P.ps[7]))

    ngrp = 0
    for it in range(NTT):
        load_norm(it)
        xt = xts[it % 2]
        hT = hTs[it % 2]
        xb = P.db(f"x{it}")
        for b in range(2):
            for jg in range(2):
                banks = [P.ps[(ngrp % 2) * 3 + n] for n in range(3)]
                ngrp += 1
                for kc in range(8):
                    for n in range(3):
                        c0 = 3072 + (jg * 3 + n) * 512
                        pb = banks[n]
                        S.op("pe", lambda e: e.matmul(
                            pb.ap, lhsT=hT.ap[:, kc, b * 128:(b + 1) * 128], rhs=Win.ap[:, kc, c0:c0 + 512],
                            start=(kc == 0), stop=(kc == 7)),
                            reads=[hT, WA.sub(Win, kc * 6144 + c0, kc * 6144 + c0 + 512)], writes=[pb])
                for n in range(3):
                    jj = jg * 3 + n
                    pb = banks[n]
                    gs = A.sub(gv[b], jj * 512, jj * 512 + 512)
                    S.op("act", lambda e: e.activation(
                        out=gv[b].ap[:, jj * 512:(jj + 1) * 512], in_=pb.ap, func=ACTF.Gelu_apprx_tanh),
                        reads=[pb], writes=[gs])
                    S.op("dve", lambda e: e.scalar_tensor_tensor(
                        out=junk.ap[:, 0:512], in0=gv[b].ap[:, jj * 512:(jj + 1) * 512], scalar=1.0,
                        in1=gv[b].ap[:, jj * 512:(jj + 1) * 512], op0=ALU.mult, op1=ALU.mult,
                        accum_out=ssq[b].ap[:, jj:jj + 1]),
                        reads=[gs], writes=[junk, ssq[b]])
            S.op("dve", lambda e: e.tensor_reduce(out=sst[b].ap, in_=ssq[b].ap[:, 0:6], axis=AX.X, op=ALU.add),
                 reads=[ssq[b]], writes=[sst[b]])
            S.op("pool", lambda e: e.tensor_scalar(out=sst[b].ap, in0=sst[b].ap, scalar1=1.0 / 3072.0, scalar2=EPS,
                                                   op0=ALU.mult, op1=ALU.add),
                 reads=[sst[b]], writes=[sst[b]])
            S.op("pool", lambda e: e.tensor_tensor(out=rsv[b].ap, in0=sst[b].ap, in1=C["mhalf"].ap, op=ALU.pow),
                 reads=[sst[b], C["mhalf"]], writes=[rsv[b]])
            S.op("dve", lambda e: e.tensor_scalar(out=WmTs[b].ap, in0=WmT.ap, scalar1=rsv[b].ap, scalar2=None,
                                                  op0=ALU.mult),
                 reads=[WmT, rsv[b]], writes=[WmTs[b]])
        for hh in range(8):
            banks = [P.ps[(ngrp % 2) * 3 + n] for n in range(3)]
            ngrp += 1
            for kc in range(8):
                for n in range(3):
                    c = hh * 3 + n
                    pu = banks[n]
                    S.op("pe", lambda e: e.matmul(
                        pu.ap[:, 0:TT], lhsT=Win.ap[:, kc, c * 128:(c + 1) * 128], rhs=hT.ap[:, kc, :],
                        start=(kc == 0), stop=(kc == 7)),
                        reads=[hT, WA.sub(Win, kc * 6144 + c * 128, kc * 6144 + c * 128 + 128)], writes=[pu])
            for n in range(3):
                c = hh * 3 + n
                pu = banks[n]
                u_ = ug[c % 2]
                t_ = tmp[c % 2]
                S.op("act", lambda e: e.activation(out=u_.ap, in_=pu.ap[:, 0:TT], func=ACTF.Gelu_apprx_tanh),
                     reads=[pu], writes=[u_])
                pg = P.ps[6]
                pgo = (c % 2) * 256
                for b in range(2):
                    S.op("pe", lambda e: e.matmul(
                        pg.ap[:, pgo + b * 128:pgo + (b + 1) * 128], lhsT=gv[b].ap[:, c * 128:(c + 1) * 128],
                        rhs=WmTs[b].ap[:, hh, :], start=True, stop=True),
                        reads=[A.sub(gv[b], c * 128, c * 128 + 128), WmTs[b]], writes=[pg])
                for b in range(2):
                    S.op("dve", lambda e: e.scalar_tensor_tensor(
                        out=t_.ap[:, b * 128:(b + 1) * 128], in0=pg.ap[:, pgo + b * 128:pgo + (b + 1) * 128],
                        scalar=vg.ap[:, c:c + 1], in1=bs.ap[:, hh, :], op0=ALU.mult, op1=ALU.add),
                        reads=[pg, vg, bs], writes=[t_])
                S.op("pool", lambda e: e.tensor_tensor(out=uvT.ap[:, c, :], in0=t_.ap, in1=u_.ap, op=ALU.mult),
                     reads=[t_, u_], writes=[A.sub(uvT, c * TT, c * TT + TT)])
        residual_out(P, xt, None, Wout, 24,
                     lambda k, b: uvT.ap[:, k, b * 128:(b + 1) * 128],
                     lambda k: A.sub(uvT, k * TT, k * TT + TT), P.ps[0:4])
        S.dma("sp", out=xv[it], in_=xt.ap, reads=[xt], writes=[xb])


def emit_sincos(P, yy, sn, cs, ki, fr):
    S = P.S
    S.op("dve", lambda e: e.tensor_copy(out=ki.ap, in_=yy.ap), reads=[yy], writes=[ki])
    S.op("dve", lambda e: e.tensor_tensor(out=fr.ap, in0=yy.ap, in1=ki.ap, op=ALU.subtract),
         reads=[yy, ki], writes=[fr])
    S.op("act", lambda e: e.activation(out=sn.ap, in_=fr.ap, func=ACTF.Sin, scale=TWO_PI), reads=[fr], writes=[sn])
    S.op("dve", lambda e: e.tensor_scalar(out=ki.ap, in0=yy.ap, scalar1=0.25, scalar2=None, op0=ALU.add),
         reads=[yy, fr], writes=[ki])
    S.op("dve", lambda e: e.scalar_tensor_tensor(out=fr.ap, in0=yy.ap, scalar=0.25, in1=ki.ap,
                                                 op0=ALU.add, op1=ALU.subtract),
         reads=[yy, ki, sn], writes=[fr])
    S.op("act", lambda e: e.activation(out=cs.ap, in_=fr.ap, func=ACTF.Sin, scale=TWO_PI), reads=[fr], writes=[cs])


def phase_attn_qkv(P, xd, gamd, wqkvd, qgd, kgd, posd, sc):
    S, WA, A, C = P.S, P.WA, P.A, P.C
    TT = 512
    NB = TT // 128
    NTT = P.ntok // TT
    WA.reset(); A.reset(C["a0"])
    W = WA.alloc(8 * 3072, "p (k f) -> p k f", k=8)
    load_weight_fast(P, W, wqkvd, 8, 3072, piece=1536)
    gam = load_gamma(P, gamd)
    ident = C["ident"]
    gcol = A.alloc(2)
    for idx, gd in enumerate((qgd, kgd)):
        for half in range(2):
            S.dma("sp", out=gcol.ap[half * 64:(half + 1) * 64, idx:idx + 1], in_=gd, writes=[gcol])
    A.off = ((A.off + 255) // 256) * 256
    xts = [A.alloc(NB * 1024, "p (b d) -> p b d", b=NB), WA.alloc32(NB * 1024, "p (b d) -> p b d", b=NB)]
    sq = A.own(A.alloc16(1024)); ssn = A.alloc(1); rstdn = A.alloc(1)
    xns = [A.own(A.alloc16(1024)) for _ in range(2)]
    hTs = [A.alloc16(8 * TT, "p (k t) -> p k t", k=8) for _ in range(2)]
    posb = A.alloci(TT)
    yy = A.alloc(TT); ki = A.alloci(TT); fr = A.alloc(TT)
    sns = [WA.own(WA.alloc32(TT)) for _ in range(2)]
    css = [WA.own(WA.alloc32(TT)) for _ in range(2)]
    va = [A.alloc16(8 * 129, "p (h e) -> p h e", h=8) for _ in range(2)]
    NS = 4
    sqb = [WA.own(WA.alloc(TT)) for _ in range(NS)]
    sd = [WA.own(WA.alloc32(TT)) for _ in range(NS)]
    rs = [WA.own(WA.alloc32(TT)) for _ in range(NS)]
    qn = [WA.own(WA.alloc(TT)) for _ in range(NS)]
    t1 = [WA.own(WA.alloc32(TT)) for _ in range(NS)]
    t2 = [WA.own(WA.alloc32(TT)) for _ in range(NS)]
    qf = [WA.own(WA.alloc(TT)) for _ in range(NS)]
    for v in va:
        S.op("pool", lambda e, v=v: e.memset(v.ap, 1.0), writes=[v])
    xv = xd.rearrange("(n b p) d -> n p b d", p=128, b=NB)
    pqb = P.ps[0:3]; pmb = P.ps[3:5]; prb = P.ps[5:7]
    pst = P.ps[7]
    pv = ps_bf16(pst)

    def load_x(it):
        xb = [P.db(f"x{it * 2}"), P.db(f"x{it * 2 + 1}")]
        S.dma("sp", out=xts[it % 2].ap, in_=xv[it], reads=xb, writes=[xts[it % 2]])

    def norm_pre(it, b):
        xt = xts[it % 2]
        xn = xns[b % 2]
        S.op("act", lambda e: e.activation(out=sq.ap, in_=xt.ap[:, b, :], func=ACTF.Square, scale=1.0 / 32.0,
                                           accum_out=ssn.ap), reads=[xt], writes=[sq, ssn])
        S.op("pool", lambda e: e.tensor_scalar(out=ssn.ap, in0=ssn.ap, scalar1=EPS, scalar2=None, op0=ALU.add),
             reads=[ssn], writes=[ssn])
        S.op("pool", lambda e: e.tensor_tensor(out=rstdn.ap, in0=ssn.ap, in1=C["mhalf"].ap, op=ALU.pow),
             reads=[ssn, C["mhalf"]], writes=[rstdn])
        S.op("dve", lambda e: e.scalar_tensor_tensor(out=xn.ap, in0=xt.ap[:, b, :], scalar=rstdn.ap, in1=gam.ap,
                                                     op0=ALU.mult, op1=ALU.mult),
             reads=[xt, rstdn, gam], writes=[xn])

    def norm_post(it, b):
        xn = xns[b % 2]
        hT = hTs[it % 2]
        for kc in range(8):
            S.op("pe", lambda e: e.transpose(out=pv[:, kc * 128:(kc + 1) * 128], in_=xn.ap[:, kc * 128:(kc + 1) * 128],
                                             identity=ident.ap), reads=[xn, ident], writes=[pst])
        S.op("act", lambda e: e.activation(out=hT.ap[:, :, b * 128:(b + 1) * 128],
                                           in_=pv.rearrange("p (k t) -> p k t", k=8), func=ACTF.Copy),
             reads=[pst], writes=[hT])

    def sincos(it):
        S.dma("sp", out=posb.ap, in_=posd[:, it * TT:(it + 1) * TT].partition_broadcast(128), writes=[posb])
        S.op("dve", lambda e: e.tensor_scalar(out=yy.ap, in0=posb.ap, scalar1=C["invf"].ap, scalar2=None, op0=ALU.mult),
             reads=[posb, C["invf"]], writes=[yy])
        emit_sincos(P, yy, sns[it % 2], css[it % 2], ki, fr)

    items = [(it, which, h) for it in range(NTT) for which in range(2) for h in range(8)]
    n = len(items)

    def st0(i):
        it, which, h = items[i]
        hT = hTs[it % 2]
        col0 = which * 1024 + h * 128
        pq = pqb[i % 3]
        for kc in range(8):
            S.op("pe", lambda e: e.matmul(pq.ap, lhsT=W.ap[:, kc, col0:col0 + 128], rhs=hT.ap[:, kc, :],
                                          start=(kc == 0), stop=(kc == 7)),
                 reads=[hT, WA.sub(W, kc * 3072 + col0, kc * 3072 + col0 + 128)], writes=[pq])

    def st1(i):
        pq = pqb[i % 3]; pm = pmb[i % 2]; s_ = sqb[i % NS]
        S.op("act", lambda e: e.activation(out=s_.ap, in_=pq.ap, func=ACTF.Square), reads=[pq], writes=[s_])
        S.op("pe", lambda e: e.matmul(pm.ap, lhsT=C["bones"].ap, rhs=s_.ap, start=True, stop=True),
             reads=[s_, C["bones"]], writes=[pm])

    def st2(i):
        it, which, h = items[i]
        k = i % NS
        pq = pqb[i % 3]; pm = pmb[i % 2]
        S.op("act", lambda e: e.activation(out=sd[k].ap, in_=pm.ap, func=ACTF.Ln, bias=C["epscol"].ap),
             reads=[pm, C["epscol"]], writes=[sd[k]])
        S.op("act", lambda e: e.activation(out=rs[k].ap, in_=sd[k].ap, func=ACTF.Exp, scale=-0.5),
             reads=[sd[k]], writes=[rs[k]])
        S.op("dve", lambda e: e.scalar_tensor_tensor(out=qn[k].ap, in0=pq.ap, scalar=gcol.ap[:, which:which + 1],
                                                     in1=rs[k].ap, op0=ALU.mult, op1=ALU.mult),
             reads=[pq, gcol, rs[k]], writes=[qn[k]])

    def st2b(i):
        k = i % NS
        pr = prb[i % 2]
        S.op("pe", lambda e: e.matmul(pr.ap, lhsT=C["rrot"].ap, rhs=qn[k].ap, start=True, stop=True),
             reads=[qn[k], C["rrot"]], writes=[pr])

    def st3(i):
        it, which, h = items[i]
        k = i % NS
        pr = prb[i % 2]
        sn, cs = sns[it % 2], css[it % 2]
        S.op("pool", lambda e: e.tensor_tensor(out=t1[k].ap, in0=qn[k].ap, in1=cs.ap, op=ALU.mult),
             reads=[qn[k], cs], writes=[t1[k]])
        S.op("dve", lambda e: e.tensor_tensor(out=t2[k].ap, in0=pr.ap, in1=sn.ap, op=ALU.mult),
             reads=[pr, sn], writes=[t2[k]])
        S.op("dve", lambda e: e.tensor_tensor(out=qf[k].ap, in0=t1[k].ap, in1=t2[k].ap, op=ALU.add),
             reads=[t1[k], t2[k]], writes=[qf[k]])
        dst = sc["qT"] if which == 0 else sc["kT"]
        S.dma("sp", out=dst[h, :, it * TT:(it + 1) * TT], in_=qf[k].ap, reads=[qf[k]],
              writes=[P.db(f"{'qk'[which]}T{h}_{it}")])

    def vproj(it, b):
        hT = hTs[it % 2]
        v = va[b % 2]
        for jj in range(2):
            pvb = pst
            for kc in range(8):
                c0 = 2048 + jj * 512
                S.op("pe", lambda e: e.matmul(pvb.ap, lhsT=hT.ap[:, kc, b * 128:(b + 1) * 128], rhs=W.ap[:, kc, c0:c0 + 512],
                                              start=(kc == 0), stop=(kc == 7)),
                     reads=[hT, WA.sub(W, kc * 3072 + c0, kc * 3072 + c0 + 512)], writes=[pvb])
            S.op("act", lambda e: e.activation(
                out=v.ap[:, 4 * jj:4 * jj + 4, 0:128], in_=pvb.ap.rearrange("p (h e) -> p h e", h=4), func=ACTF.Copy),
                reads=[pvb], writes=[v])
        blk = it * NB + b
        S.dma("sp", out=sc["v"][blk], in_=v.ap, reads=[v], writes=[P.db(f"v{blk}")])

    load_x(0)
    for b in range(NB):
        norm_pre(0, b)
        norm_post(0, b)
    sincos(0)
    if NTT > 1:
        load_x(1)
    for s in range(n + 4):
        if s < n:
            it, which, h = items[s]
            j16 = s % 16
            st0(s)
        if 0 <= s - 2 < n:
            st2(s - 2)
        if 0 <= s - 1 < n:
            st1(s - 1)
        if 0 <= s - 3 < n:
            st2b(s - 3)
        if 0 <= s - 4 < n:
            st3(s - 4)
        if s < n:
            if j16 in (1, 5, 9, 13):
                vproj(it, j16 // 4)
            if it + 1 < NTT:
                if j16 in (0, 4, 8, 12):
                    norm_pre(it + 1, j16 // 4)
                if j16 in (3, 7, 11, 15):
                    norm_post(it + 1, j16 // 4)
                if j16 == 14:
                    sincos(it + 1)
                if j16 == 15 and it + 2 < NTT:
                    load_x(it + 2)


def phase_attn_core(P, lamd, sgd, lambda_init, sc):
    S, WA, A, C = P.S, P.WA, P.A, P.C
    ntok = P.ntok
    NB = ntok // 128
    NG = ntok // 512
    NT256 = ntok // 512
    WA.reset(); A.reset(C["a0"])
    K0 = [WA.alloc(ntok) for _ in range(2)]
    K1 = [WA.alloc(ntok) for _ in range(2)]
    QT = [WA.alloc(ntok) for _ in range(2)]
    VA = [WA.alloc(NB * 128, "p (n e) -> p n e", e=128) for _ in range(2)]
    ones16 = WA.alloc(128)
    onesb = WA.alloc(128)
    S.op("pool", lambda e: e.memset(ones16.ap, 1.0), writes=[ones16])
    S.op("pool", lambda e: e.memset(onesb.ap, 1.0 / 128.0), writes=[onesb])
    for i in range(2):
        S.op("pool", lambda e, i=i: e.memset(K0[i].ap[64:128, :], 0.0), writes=[K0[i]])
        S.op("pool", lambda e, i=i: e.memset(K1[i].ap[0:64, :], 0.0), writes=[K1[i]])
    L = A.alloc(256, "p (a d) -> p a d", a=4)
    S.dma("sp", out=L.ap, in_=lamd.rearrange("a d -> (a d)").partition_broadcast(128), writes=[L])
    lj = A.alloc(64); s12 = A.alloc(2); e12 = A.alloc(2); neglam = A.alloc(1)
    for a in range(2):
        S.op("dve", lambda e, a=a: e.scalar_tensor_tensor(
            out=lj.ap, in0=L.ap[:, 2 * a, :], scalar=1.0, in1=L.ap[:, 2 * a + 1, :], op0=ALU.mult, op1=ALU.mult,
            accum_out=s12.ap[:, a:a + 1]), reads=[L], writes=[lj, s12])
    S.op("act", lambda e: e.activation(out=e12.ap, in_=s12.ap, func=ACTF.Exp), reads=[s12], writes=[e12])
    S.op("dve", lambda e: e.tensor_tensor(out=neglam.ap, in0=e12.ap[:, 1:2], in1=e12.ap[:, 0:1], op=ALU.subtract),
         reads=[e12], writes=[neglam])
    S.op("dve", lambda e: e.tensor_scalar(out=neglam.ap, in0=neglam.ap, scalar1=-float(lambda_init), scalar2=None,
                                          op0=ALU.add), reads=[neglam], writes=[neglam])
    sgc = A.alloc(1)
    S.dma("sp", out=sgc.ap, in_=sgd.rearrange("o e -> e o"), writes=[sgc], allow_slow_non_contiguous=True)
    S.op("dve", lambda e: e.tensor_scalar(out=sgc.ap, in0=sgc.ap, scalar1=float(1.0 - lambda_init), scalar2=None,
                                          op0=ALU.mult), reads=[sgc], writes=[sgc])
    A.off = ((A.off + 255) // 256) * 256
    PT = [[A.alloc16(512) for _ in range(2)] for _ in range(2)]
    ob = [[A.alloc(512) for _ in range(2)] for _ in range(2)]
    rl = [A.alloc(512) for _ in range(2)]
    tt = A.alloc(512); uu = A.alloc(512); oo = A.alloc(512)
    osq = A.alloc16(512); msb = A.alloc(512); rs = A.alloc(512)
    oT = [A.alloc16(512) for _ in range(2)]
    sb3 = [P.ps[0], P.ps[1], P.ps[2]]
    otb = [P.ps[3], P.ps[4]]
    plb = [P.ps[5], P.ps[6]]
    pmb = P.ps[7]
    lsb = [[A.alloc(512) for _ in range(2)] for _ in range(2)]
    mhb = C["mhalf"].ap.broadcast_to([128, 512])
    ng = 0
    for h in range(8):
        buf = h % 2
        S.dma("sp", out=K0[buf].ap[0:64, :], in_=sc["kT"][h, 0:64, :],
              reads=[P.db(f"kT{h}_{it}") for it in range(NT256)], writes=[K0[buf]])
        S.dma("sp", out=K1[buf].ap[64:128, :], in_=sc["kT"][h, 64:128, :],
              reads=[P.db(f"kT{h}_{it}") for it in range(NT256)], writes=[K1[buf]])
        S.dma("sp", out=QT[buf].ap, in_=sc["qT"][h],
              reads=[P.db(f"qT{h}_{it}") for it in range(NT256)], writes=[QT[buf]])
        S.dma("sp", out=VA[buf].ap, in_=sc["v"].rearrange("n p (h e) -> h p n e", h=8)[h][:, :, 0:128],
              reads=[P.db(f"v{blk}") for blk in range(NB)], writes=[VA[buf]])
        for G in range(NG):
            gb = ng % 2
            ng += 1
            njb = 4 * G + 4

            def geom(jb):
                nq0 = max(0, jb - 4 * G)
                return nq0, (4 - nq0) * 128, G * 512 + nq0 * 128

            def stage_a(jb, cs_=(0, 1)):
                nq0, N, qc0 = geom(jb)
                for c in cs_:
                    Kc = (K0 if c == 0 else K1)[buf]
                    pss = sb3[(2 * jb + c) % 3]
                    S.op("pe", lambda e: e.matmul(
                        pss.ap[:, 0:N], lhsT=Kc.ap[:, jb * 128:(jb + 1) * 128], rhs=QT[buf].ap[:, qc0:qc0 + N],
                        start=True, stop=True),
                        reads=[WA.sub(Kc, jb * 128, jb * 128 + 128), WA.sub(QT[buf], qc0, qc0 + N)], writes=[pss])

            def stage_b(jb, cs_=(0, 1)):
                nq0, N, qc0 = geom(jb)
                c0 = nq0 * 128
                for c in cs_:
                    pss = sb3[(2 * jb + c) % 3]
                    pt = PT[jb % 2][c]
                    S.op("act", lambda e: e.activation(out=pt.ap[:, 0:N], in_=pss.ap[:, 0:N], func=ACTF.Exp, scale=0.125),
                         reads=[pss], writes=[pt])
                    eng = "dve" if c == 0 else "pool"
                    if jb >= 4 * G:
                        S.op("dve", lambda e: e.tensor_tensor(out=pt.ap[:, 0:128], in0=pt.ap[:, 0:128],
                                                               in1=C["triu"].ap, op=ALU.mult),
                             reads=[pt, C["triu"]], writes=[pt])

            def stage_c(jb):
                nq0, N, qc0 = geom(jb)
                c0 = nq0 * 128
                for c in range(2):
                    pt = PT[jb % 2][c]
                    S.op("pe", lambda e: e.matmul(
                        otb[c].ap[:, c0:512], lhsT=VA[buf].ap[:, jb, :], rhs=pt.ap[:, 0:N],
                        start=(jb == 0), stop=(jb == njb - 1)),
                        reads=[pt, WA.sub(VA[buf], jb * 128, jb * 128 + 128)], writes=[otb[c]])
                    S.op("pe", lambda e: e.matmul(
                        plb[c].ap[:, c0:512], lhsT=ones16.ap, rhs=pt.ap[:, 0:N],
                        start=(jb == 0), stop=(jb == njb - 1)),
                        reads=[pt, ones16], writes=[plb[c]])

            stage_a(0)
            for jb in range(njb):
                if jb + 1 < njb:
                    stage_a(jb + 1, (0,))
                stage_b(jb, (0,))
                if jb + 1 < njb:
                    stage_a(jb + 1, (1,))
                stage_b(jb, (1,))
                stage_c(jb)
            for c in range(2):
                S.op("act", lambda e, c=c: e.activation(out=lsb[gb][c].ap, in_=plb[c].ap, func=ACTF.Ln),
                     reads=[plb[c]], writes=[lsb[gb][c]])
                S.op("dve", lambda e, c=c: e.tensor_copy(out=ob[gb][c].ap, in_=otb[c].ap), reads=[otb[c]], writes=[ob[gb][c]])
            for c in range(2):
                S.op("act", lambda e, c=c: e.activation(out=rl[c].ap, in_=lsb[gb][c].ap, func=ACTF.Exp, scale=-1.0),
                     reads=[lsb[gb][c]], writes=[rl[c]])
            S.op("dve", lambda e: e.scalar_tensor_tensor(out=tt.ap, in0=ob[gb][1].ap, scalar=neglam.ap, in1=rl[1].ap,
                                                         op0=ALU.mult, op1=ALU.mult),
                 reads=[ob[gb][1], rl[1], neglam], writes=[tt])
            S.op("dve", lambda e: e.tensor_tensor(out=uu.ap, in0=ob[gb][0].ap, in1=rl[0].ap, op=ALU.mult),
                 reads=[ob[gb][0], rl[0]], writes=[uu])
            S.op("dve", lambda e: e.tensor_tensor(out=oo.ap, in0=uu.ap, in1=tt.ap, op=ALU.add), reads=[uu, tt], writes=[oo])
            S.op("dve", lambda e: e.tensor_tensor(out=osq.ap, in0=oo.ap, in1=oo.ap, op=ALU.mult), reads=[oo], writes=[osq])
            S.op("pe", lambda e: e.matmul(pmb.ap, lhsT=onesb.ap, rhs=osq.ap, start=True, stop=True),
                 reads=[onesb, osq], writes=[pmb])
            S.op("act", lambda e: e.activation(out=msb.ap, in_=pmb.ap, func=ACTF.Ln, bias=C["epscol"].ap),
                 reads=[pmb, C["epscol"]], writes=[msb])
            S.op("act", lambda e: e.activation(out=rs.ap, in_=msb.ap, func=ACTF.Exp, scale=-0.5),
                 reads=[msb], writes=[rs])
            oTt = oT[gb]
            S.op("dve", lambda e: e.scalar_tensor_tensor(out=oTt.ap, in0=oo.ap, scalar=sgc.ap, in1=rs.ap,
                                                         op0=ALU.mult, op1=ALU.mult),
                 reads=[oo, sgc, rs], writes=[oTt])
            S.dma("sp", out=sc["oT"][h, :, G * 512:(G + 1) * 512], in_=oTt.ap, reads=[oTt],
                  writes=[P.db(f"oT{h}_{G}")])


def phase_attn_out(P, xd, wod, sc, prefetch=None):
    S, WA, A, C = P.S, P.WA, P.A, P.C
    TT = 256
    NTT = P.ntok // TT
    WA.reset(); A.reset(C["a0"])
    WA.off = 64 * 1024
    Wo = WA.alloc(8 * 1024, "p (k f) -> p k f", k=8)
    load_weight_fast(P, Wo, wod, 8, 1024, piece=1024)
    if prefetch is not None:
        w1d, w2d = prefetch
        WA.off = 0
        W1 = WA.alloc(8 * 4096, "p (k f) -> p k f", k=8)
        W2 = WA.alloc(32 * 1024, "p (k f) -> p k f", k=32)
        load_weight(P, W1, w1d, 8, 4096)
        load_weight(P, W2, w2d, 32, 1024)
    NBK = 4
    TT = 512
    NTT = P.ntok // TT
    xts = [A.alloc(NBK * 1024, "p (b d) -> p b d", b=NBK) for _ in range(2)]
    oTs = [A.alloc16(8 * TT, "p (h t) -> p h t", h=8) for _ in range(2)]
    xv = xd.rearrange("(n b p) d -> n p b d", p=128, b=NBK)

    def loads(it):
        xt = xts[it % 2]; ot = oTs[it % 2]
        S.dma("sp", out=xt.ap, in_=xv[it], reads=[P.db(f"x{2 * it}"), P.db(f"x{2 * it + 1}")], writes=[xt])
        S.dma("sp", out=ot.ap, in_=sc["oT"][:, :, it * TT:(it + 1) * TT].rearrange("h p t -> p h t"),
              reads=[P.db(f"oT{h}_{it}") for h in range(8)], writes=[ot])

    loads(0)
    for it in range(NTT):
        xt = xts[it % 2]; ot = oTs[it % 2]
        if it + 1 < NTT:
            loads(it + 1)
        residual_out(P, xt, None, Wo, 8, lambda k, b, ot=ot: ot.ap[:, k, b * 128:(b + 1) * 128],
                     lambda k, ot=ot: ot, P.ps[0:6])
        S.dma("sp", out=xv[it], in_=xt.ap, reads=[xt], writes=[P.db(f"x{2 * it}"), P.db(f"x{2 * it + 1}")])


def phase_s5(P, xd, gamd, wind, ared, aimd, bred, bimd, cred, cimd, dd, logdtd, wglud, bglud, woutd):
    S, WA, A, C = P.S, P.WA, P.A, P.C
    TT = 256
    NTT = P.ntok // TT
    WA.reset(); A.reset(C["a0"])
    ident = C["ident"]
    Win = WA.alloc(8 * 1024, "p (k f) -> p k f", k=8)
    Wglu = WA.alloc(8 * 1024, "p (k f) -> p k f", k=8)
    Wout = WA.alloc(8 * 1024, "p (k f) -> p k f", k=8)
    load_weight_fast(P, Win, wind, 8, 1024, piece=1024)
    load_weight_fast(P, Wglu, wglud, 8, 1024, piece=1024)
    load_weight_fast(P, Wout, woutd, 8, 1024, piece=1024)
    BL = WA.alloc(32 * 2 * 128, "p (q r m) -> p q r m", q=32, r=2)
    CBr = WA.alloc(32 * 3 * 128 + 2048)
    CBflat = CBr.ap
    Zr = [WA.alloc(512 + 256) for _ in range(2)]
    ZC = [WA.alloc(128, "p (g m) -> p g m", g=2) for _ in range(2)]
    rcol = A.alloc(32); ycol = A.alloc(32); carry = A.alloc(64, "p (q r) -> p q r", r=2)
    dcol = A.alloc(8); bgl = A.alloc(8)
    a_keep = A.off
    S.op("pool", lambda e: e.memset(carry.ap, 0.0), writes=[carry])
    S.dma("sp", out=dcol.ap, in_=dd.rearrange("o (c p) -> p (o c)", p=128), writes=[dcol], allow_slow_non_contiguous=True)
    S.dma("sp", out=bgl.ap, in_=bglud.rearrange("o (c p) -> p (o c)", p=128), writes=[bgl], allow_slow_non_contiguous=True)
    are = A.alloc(32); aim = A.alloc(32); ldt = A.alloc(32); dt = A.alloc(32)
    S.dma("sp", out=are.ap, in_=ared.rearrange("(q g2) p -> (g2 p) q", g2=2), writes=[are], allow_slow_non_contiguous=True)
    S.dma("sp", out=aim.ap, in_=aimd.rearrange("(q g2) p -> (g2 p) q", g2=2), writes=[aim], allow_slow_non_contiguous=True)
    ld2 = logdtd.rearrange("o (q g2) -> (o g2) q", g2=2)
    for g2 in range(2):
        S.dma("sp", out=ldt.ap[g2 * 64:(g2 + 1) * 64, :], in_=ld2[g2:g2 + 1, :].partition_broadcast(64), writes=[ldt],
              allow_slow_non_contiguous=True)
    S.op("act", lambda e: e.activation(out=dt.ap, in_=ldt.ap, func=ACTF.Exp), reads=[ldt], writes=[dt])
    S.op("dve", lambda e: e.tensor_scalar(out=are.ap, in0=are.ap, scalar1=-1e-4, scalar2=None, op0=ALU.min),
         reads=[are], writes=[are])
    rdt = A.alloc(32)
    S.op("dve", lambda e: e.tensor_tensor(out=rdt.ap, in0=are.ap, in1=dt.ap, op=ALU.mult), reads=[are, dt], writes=[rdt])
    S.op("act", lambda e: e.activation(out=rcol.ap, in_=rdt.ap, func=ACTF.Exp), reads=[rdt], writes=[rcol])
    S.op("dve", lambda e: e.scalar_tensor_tensor(out=ycol.ap, in0=aim.ap, scalar=float(1.0 / (2.0 * math.pi)), in1=dt.ap,
                                                 op0=ALU.mult, op1=ALU.mult), reads=[aim, dt], writes=[ycol])
    snt = A.alloc(32); cst = A.alloc(32); kit = A.alloci(32); frt = A.alloc(32)
    emit_sincos(P, ycol, snt, cst, kit, frt)
    nr = A.alloc(32); ni = A.alloc(32); den = A.alloc(32); t_a = A.alloc(32); t_b = A.alloc(32)
    zr = A.alloc(32); zi = A.alloc(32)
    S.op("dve", lambda e: e.tensor_tensor(out=nr.ap, in0=rcol.ap, in1=cst.ap, op=ALU.mult), reads=[rcol, cst], writes=[nr])
    S.op("dve", lambda e: e.tensor_scalar(out=nr.ap, in0=nr.ap, scalar1=-1.0, scalar2=None, op0=ALU.add), reads=[nr], writes=[nr])
    S.op("dve", lambda e: e.tensor_tensor(out=ni.ap, in0=rcol.ap, in1=snt.ap, op=ALU.mult), reads=[rcol, snt], writes=[ni])
    S.op("dve", lambda e: e.tensor_tensor(out=den.ap, in0=are.ap, in1=are.ap, op=ALU.mult), reads=[are], writes=[den])
    S.op("dve", lambda e: e.tensor_tensor(out=t_a.ap, in0=aim.ap, in1=aim.ap, op=ALU.mult), reads=[aim], writes=[t_a])
    S.op("dve", lambda e: e.tensor_tensor(out=den.ap, in0=den.ap, in1=t_a.ap, op=ALU.add), reads=[den, t_a], writes=[den])
    S.op("dve", lambda e: e.reciprocal(out=den.ap, in_=den.ap), reads=[den], writes=[den])
    S.op("dve", lambda e: e.tensor_tensor(out=t_a.ap, in0=nr.ap, in1=are.ap, op=ALU.mult), reads=[nr, are], writes=[t_a])
    S.op("dve", lambda e: e.tensor_tensor(out=t_b.ap, in0=ni.ap, in1=aim.ap, op=ALU.mult), reads=[ni, aim], writes=[t_b])
    S.op("dve", lambda e: e.tensor_tensor(out=t_a.ap, in0=t_a.ap, in1=t_b.ap, op=ALU.add), reads=[t_a, t_b], writes=[t_a])
    S.op("dve", lambda e: e.tensor_tensor(out=zr.ap, in0=t_a.ap, in1=den.ap, op=ALU.mult), reads=[t_a, den], writes=[zr])
    S.op("dve", lambda e: e.tensor_tensor(out=t_a.ap, in0=ni.ap, in1=are.ap, op=ALU.mult), reads=[ni, are], writes=[t_a])
    S.op("dve", lambda e: e.tensor_tensor(out=t_b.ap, in0=nr.ap, in1=aim.ap, op=ALU.mult), reads=[nr, aim], writes=[t_b])
    S.op("dve", lambda e: e.tensor_tensor(out=t_a.ap, in0=t_a.ap, in1=t_b.ap, op=ALU.subtract), reads=[t_a, t_b], writes=[t_a])
    S.op("dve", lambda e: e.tensor_tensor(out=zi.ap, in0=t_a.ap, in1=den.ap, op=ALU.mult), reads=[t_a, den], writes=[zi])
    Bre = A.alloc(512, "p (q h) -> p q h", h=16); Bim = A.alloc(512, "p (q h) -> p q h", h=16)
    S.dma("sp", out=Bre.ap, in_=bred.rearrange("(q g2) p h -> (g2 p) q h", g2=2), writes=[Bre])
    S.dma("sp", out=Bim.ap, in_=bimd.rearrange("(q g2) p h -> (g2 p) q h", g2=2), writes=[Bim])
    zrb = zr.ap.unsqueeze(2).broadcast_to([128, 32, 16])
    zib = zi.ap.unsqueeze(2).broadcast_to([128, 32, 16])
    M1 = A.alloc(512, "p (q h) -> p q h", h=16); M2 = A.alloc(512, "p (q h) -> p q h", h=16)
    Bb = [A.alloc(512, "p (q h) -> p q h", h=16) for _ in range(2)]
    S.op("dve", lambda e: e.tensor_tensor(out=M1.ap, in0=Bre.ap, in1=zrb, op=ALU.mult), reads=[Bre, zr], writes=[M1])
    S.op("dve", lambda e: e.tensor_tensor(out=M2.ap, in0=Bim.ap, in1=zib, op=ALU.mult), reads=[Bim, zi], writes=[M2])
    S.op("dve", lambda e: e.tensor_tensor(out=Bb[0].ap, in0=M1.ap, in1=M2.ap, op=ALU.subtract), reads=[M1, M2], writes=[Bb[0]])
    S.op("dve", lambda e: e.tensor_tensor(out=M1.ap, in0=Bim.ap, in1=zrb, op=ALU.mult), reads=[Bim, zr], writes=[M1])
    S.op("dve", lambda e: e.tensor_tensor(out=M2.ap, in0=Bre.ap, in1=zib, op=ALU.mult), reads=[Bre, zi], writes=[M2])
    S.op("dve", lambda e: e.tensor_tensor(out=Bb[1].ap, in0=M1.ap, in1=M2.ap, op=ALU.add), reads=[M1, M2], writes=[Bb[1]])
    pst = P.ps[7]
    pv = ps_bf16(pst)
    n = 0
    for k in range(8):
        for ri in range(2):
            Z = Zr[n % 2]
            n += 1
            S.op("pool", lambda e, Z=Z: e.memset(Z.ap, 0.0), writes=[Z])
            for g2 in range(2):
                dst = Z.ap[g2 * 64:(g2 + 1) * 64, g2 * 16:g2 * 16 + 640].rearrange("p (q m) -> p q m", m=160)[:, :, 0:16]
                S.op("dve", lambda e, dst=dst, g2=g2, ri=ri, k=k: e.tensor_copy(
                    out=dst, in_=Bb[ri].ap[g2 * 64:(g2 + 1) * 64, 4 * k:4 * k + 4, :]),
                    reads=[Bb[ri]], writes=[Z])
            for ql in range(4):
                S.op("pe", lambda e, Z=Z, ql=ql: e.transpose(out=pv[:, ql * 128:(ql + 1) * 128],
                                                            in_=Z.ap[:, ql * 128:(ql + 1) * 128], identity=ident.ap),
                     reads=[Z, ident], writes=[pst])
            S.op("act", lambda e, k=k, ri=ri: e.activation(out=BL.ap[:, 4 * k:4 * k + 4, ri, :],
                                                           in_=pv[:, 0:512].rearrange("p (q m) -> p q m", q=4),
                                                           func=ACTF.Copy),
                 reads=[pst], writes=[BL])
    Cn = [A.alloc(512, "p (k m) -> p k m", k=8) for _ in range(2)]
    S.dma("sp", out=Cn[0].ap, in_=cred.rearrange("(k gl) h p -> (gl h) k p", gl=8), writes=[Cn[0]])
    S.dma("sp", out=Cn[1].ap, in_=cimd.rearrange("(k gl) h p -> (gl h) k p", gl=8), writes=[Cn[1]])
    S.op("pool", lambda e: e.memset(CBflat, 0.0), writes=[CBr])
    pst2 = P.ps[6]
    pv2 = ps_bf16(pst2)
    n = 0
    for k in range(8):
        for ri in range(2):
            zc = ZC[n % 2]
            n += 1
            for g2 in range(2):
                S.op("dve", lambda e, zc=zc, g2=g2, ri=ri, k=k: e.tensor_scalar(
                    out=zc.ap[:, g2, :], in0=Cn[ri].ap[:, k, :], scalar1=C["par"].ap[:, g2:g2 + 1], scalar2=None,
                    op0=ALU.mult), reads=[Cn[ri], C["par"]], writes=[zc])
            S.op("pe", lambda e, zc=zc: e.transpose(out=pv2[:, 0:128], in_=zc.ap.rearrange("p g m -> p (g m)"),
                                                   identity=ident.ap), reads=[zc, ident], writes=[pst2])
            for var in ((0, 2) if ri == 0 else (1,)):
                base = 4 * k * 384 + var * 128
                dst = CBflat[:, base:base + 4 * 416].rearrange("p (q m) -> p q m", m=416)[:, :, 0:32]
                sgn = 1.0 if var == 0 else -1.0
                S.op("act", lambda e, dst=dst, sgn=sgn: e.activation(
                    out=dst, in_=pv2[:, 0:128].rearrange("p (q m) -> p q m", q=4), func=ACTF.Copy, scale=sgn),
                    reads=[pst2], writes=[CBr])
    CB = CBflat[:, 0:32 * 384].rearrange("p (q v m) -> p q v m", q=32, v=3)
    CS = WA.alloc(32 * TT, "p (q k) -> p q k", q=32)
    SN = WA.alloc(32 * TT, "p (q k) -> p q k", q=32)
    A.reset(a_keep)
    cT = A.alloc(32); sT = A.alloc(32)
    yT = A.alloc(32); kiT = A.alloci(32); frT = A.alloc(32)
    S.op("dve", lambda e: e.tensor_scalar(out=yT.ap, in0=ycol.ap, scalar1=float(TT), scalar2=None, op0=ALU.mult),
         reads=[ycol], writes=[yT])
    emit_sincos(P, yT, sT, cT, kiT, frT)
    a_keep2 = A.off
    yyt = [A.alloc(TT) for _ in range(2)]; kit2 = [A.alloci(TT) for _ in range(2)]; frt2 = [A.alloc(TT) for _ in range(2)]
    for q in range(32):
        yy_, ki_, fr_ = yyt[q % 2], kit2[q % 2], frt2[q % 2]
        S.op("dve", lambda e: e.tensor_scalar(out=yy_.ap, in0=C["tg0"].ap[:, 0:TT], scalar1=ycol.ap[:, q:q + 1], scalar2=None,
                                              op0=ALU.mult), reads=[C["tg0"], ycol], writes=[yy_])
        snq = Region(SN.ap[:, q, :], WA.sub(SN, q * TT, q * TT + TT))
        csq = Region(CS.ap[:, q, :], WA.sub(CS, q * TT, q * TT + TT))
        emit_sincos(P, yy_, snq, csq, ki_, fr_)
    A.reset(a_keep2)
    gam = load_gamma(P, gamd)
    xt = A.alloc(2 * 1024, "p (b d) -> p b d", b=2)
    scr = norm_scratch(P)
    hT = A.alloc16(8 * TT, "p (k t) -> p k t", k=8)
    uT = A.alloc16(8 * TT, "p (k t) -> p k t", k=8)
    gT = A.alloc16(8 * TT, "p (k t) -> p k t", k=8)
    ggT = A.alloc16(8 * TT, "p (k t) -> p k t", k=8)
    ydt = A.alloc(TT); sig = A.alloc(TT)
    NSET = 3

    def mkset(i):
        al = (lambda n: A.own(A.alloc(n))) if i == 0 else (lambda n: WA.own(WA.alloc32(n)))
        al16 = (lambda n: A.own(A.alloc16(n))) if i == 0 else (lambda n: WA.own(WA.alloc(n)))
        return dict(m=[al(TT) for _ in range(4)], w=[al(TT) for _ in range(2)], z=[al(TT) for _ in range(2)],
                    Y=[al16(TT) for _ in range(4)], ct=al(8))
    tsets = [mkset(0 if i < 2 else 1) for i in range(NSET)]
    cbufs = [Buf(f"carry{q}") for q in range(32)]
    xv = xd.rearrange("(n b p) d -> n p b d", p=128, b=2)
    gp = 0
    for it in range(NTT):
        xb = P.db(f"x{it}")
        S.dma("sp", out=xt.ap, in_=xv[it], reads=[xb], writes=[xt])
        for b in range(2):
            emit_norm_T(P, xt.ap[:, b, :], xt, gam, hT, b * 128, ident, dict(scr, pst=P.ps[6 + b]))
        for cc in range(8):
            pu = P.ps[cc % 2]
            for kc in range(8):
                S.op("pe", lambda e: e.matmul(
                    pu.ap[:, 0:TT], lhsT=Win.ap[:, kc, cc * 128:(cc + 1) * 128], rhs=hT.ap[:, kc, :],
                    start=(kc == 0), stop=(kc == 7)),
                    reads=[hT, WA.sub(Win, kc * 1024 + cc * 128, kc * 1024 + cc * 128 + 128)], writes=[pu])
            S.op("act", lambda e: e.activation(out=uT.ap[:, cc, :], in_=pu.ap[:, 0:TT], func=ACTF.Copy),
                 reads=[pu], writes=[A.sub(uT, cc * TT, cc * TT + TT)])

        def stB(q, part):
            cc = q // 4
            ts = tsets[(gp + q) % NSET]
            pb = [P.ps[2 + ((gp + q) % 2) * 2 + ri] for ri in range(2)]
            us = A.sub(uT, cc * TT, cc * TT + TT)
            m, w = ts["m"], ts["w"]
            csq = WA.sub(CS, q * TT, q * TT + TT); snq = WA.sub(SN, q * TT, q * TT + TT)
            if part == 0:
                for ri in range(2):
                    S.op("pe", lambda e: e.matmul(pb[ri].ap[:, 0:TT], lhsT=BL.ap[:, q, ri, :], rhs=uT.ap[:, cc, :],
                                                  start=True, stop=True), reads=[BL, us], writes=[pb[ri]])
                return
            for idx, (ri, tab, tb) in enumerate(((0, CS, csq), (1, SN, snq), (1, CS, csq), (0, SN, snq))):
                S.op("dve", lambda e: e.tensor_tensor(out=m[idx].ap, in0=pb[ri].ap[:, 0:TT], in1=tab.ap[:, q, :], op=ALU.mult),
                     reads=[pb[ri], tb], writes=[m[idx]])
            S.op("pool", lambda e: e.tensor_tensor(out=w[0].ap, in0=m[0].ap, in1=m[1].ap, op=ALU.add),
                 reads=[m[0], m[1]], writes=[w[0]])
            S.op("pool", lambda e: e.tensor_tensor(out=w[1].ap, in0=m[2].ap, in1=m[3].ap, op=ALU.subtract),
                 reads=[m[2], m[3]], writes=[w[1]])

        def stC(q):
            ts = tsets[(gp + q) % NSET]
            w, z, ct = ts["w"], ts["z"], ts["ct"]
            rbc = rcol.ap[:, q:q + 1].broadcast_to([128, TT])
            for ri in range(2):
                S.op("dve", lambda e: e.tensor_tensor_scan(
                    out=z[ri].ap, data0=rbc, data1=w[ri].ap, initial=carry.ap[:, q, ri:ri + 1],
                    op0=ALU.mult, op1=ALU.add), reads=[rcol, w[ri], cbufs[q]], writes=[z[ri]])
            zl = [z[0].ap[:, TT - 1:TT], z[1].ap[:, TT - 1:TT]]
            for idx, (ri, tab) in enumerate(((0, cT), (1, sT), (1, cT), (0, sT))):
                S.op("act", lambda e: e.activation(out=ct.ap[:, idx:idx + 1], in_=zl[ri], func=ACTF.Copy,
                                                   scale=tab.ap[:, q:q + 1]),
                     reads=[z[ri], tab], writes=[ct])
            S.op("pool", lambda e: e.tensor_tensor(out=carry.ap[:, q, 0:1], in0=ct.ap[:, 0:1], in1=ct.ap[:, 1:2], op=ALU.subtract),
                 reads=[ct], writes=[cbufs[q]])
            S.op("pool", lambda e: e.tensor_tensor(out=carry.ap[:, q, 1:2], in0=ct.ap[:, 2:3], in1=ct.ap[:, 3:4], op=ALU.add),
                 reads=[ct], writes=[cbufs[q]])

        def stD(q):
            cc, ql = q // 4, q % 4
            ts = tsets[(gp + q) % NSET]
            z, Y = ts["z"], ts["Y"]
            py = P.ps[cc % 2]
            csq = WA.sub(CS, q * TT, q * TT + TT); snq = WA.sub(SN, q * TT, q * TT + TT)
            S.op("pool", lambda e: e.tensor_tensor(out=Y[0].ap, in0=z[0].ap, in1=CS.ap[:, q, :], op=ALU.mult),
                 reads=[z[0], csq], writes=[Y[0]])
            S.op("pool", lambda e: e.tensor_tensor(out=Y[1].ap, in0=z[1].ap, in1=CS.ap[:, q, :], op=ALU.mult),
                 reads=[z[1], csq], writes=[Y[1]])
            S.op("dve", lambda e: e.tensor_tensor(out=Y[2].ap, in0=z[0].ap, in1=SN.ap[:, q, :], op=ALU.mult),
                 reads=[z[0], snq], writes=[Y[2]])
            S.op("dve", lambda e: e.tensor_tensor(out=Y[3].ap, in0=z[1].ap, in1=SN.ap[:, q, :], op=ALU.mult),
                 reads=[z[1], snq], writes=[Y[3]])
            for n_, (var, yi) in enumerate(((0, 0), (1, 1), (1, 2), (2, 3))):
                S.op("pe", lambda e: e.matmul(py.ap[:, 0:TT], lhsT=CB[:, q, var, :], rhs=Y[yi].ap,
                                              start=(ql == 0 and n_ == 0), stop=(ql == 3 and n_ == 3)),
                     reads=[CBr, Y[yi]], writes=[py])
            if ql == 3:
                us = A.sub(uT, cc * TT, cc * TT + TT)
                S.op("dve", lambda e: e.scalar_tensor_tensor(
                    out=ydt.ap, in0=uT.ap[:, cc, :], scalar=dcol.ap[:, cc:cc + 1], in1=py.ap[:, 0:TT],
                    op0=ALU.mult, op1=ALU.add), reads=[us, dcol, py], writes=[ydt])
                S.op("act", lambda e: e.activation(out=gT.ap[:, cc, :], in_=ydt.ap, func=ACTF.Gelu_apprx_tanh),
                     reads=[ydt], writes=[A.sub(gT, cc * TT, cc * TT + TT)])

        for s_ in range(32 + 2):
            if s_ < 32:
                stB(s_, 0)
            if 0 <= s_ - 2 < 32:
                stD(s_ - 2)
            if 0 <= s_ - 1 < 32:
                stC(s_ - 1)
            if s_ < 32:
                stB(s_, 1)
        gp += 32
        for c2 in range(8):
            pz = P.ps[6 + c2 % 2]
            for cc in range(8):
                S.op("pe", lambda e: e.matmul(
                    pz.ap[:, 0:TT], lhsT=Wglu.ap[:, cc, c2 * 128:(c2 + 1) * 128], rhs=gT.ap[:, cc, :],
                    start=(cc == 0), stop=(cc == 7)),
                    reads=[A.sub(gT, cc * TT, cc * TT + TT), WA.sub(Wglu, cc * 1024 + c2 * 128, cc * 1024 + c2 * 128 + 128)],
                    writes=[pz])
            S.op("act", lambda e: e.activation(out=sig.ap, in_=pz.ap[:, 0:TT], func=ACTF.Sigmoid, bias=bgl.ap[:, c2:c2 + 1]),
                 reads=[pz, bgl], writes=[sig])
            S.op("pool", lambda e: e.tensor_tensor(out=ggT.ap[:, c2, :], in0=gT.ap[:, c2, :], in1=sig.ap, op=ALU.mult),
                 reads=[A.sub(gT, c2 * TT, c2 * TT + TT), sig], writes=[A.sub(ggT, c2 * TT, c2 * TT + TT)])
        residual_out(P, xt, None, Wout, 8, lambda k, b: ggT.ap[:, k, b * 128:(b + 1) * 128],
                     lambda k: A.sub(ggT, k * TT, k * TT + TT), P.ps[2:6])
        S.dma("sp", out=xv[it], in_=xt.ap, reads=[xt], writes=[xb])


def host_consts():
    bf = ml_dtypes.bfloat16
    c = {}
    c["c_ident"] = np.eye(128, dtype=np.float32).astype(bf)
    p = np.arange(128)
    c["c_bones"] = ((p[:, None] // 64) == (p[None, :] // 64)).astype(np.float32).astype(bf) * np.float32(1.0 / 64.0)
    c["c_bones"] = c["c_bones"].astype(bf)
    rr = np.zeros((128, 128), np.float32)
    for cc in range(2):
        for d in range(64):
            mcol = cc * 64 + d
            if d < 32:
                rr[cc * 64 + d + 32, mcol] = -1.0
            else:
                rr[cc * 64 + d - 32, mcol] = 1.0
    c["c_rrot"] = rr.astype(bf)
    invf = (10000.0 ** (-np.arange(0, 64, 2, dtype=np.float32) / 64.0)).astype(np.float32)
    c["c_invf"] = (invf[(p % 64) % 32] / np.float32(2.0 * math.pi)).astype(np.float32).reshape(128, 1)
    c["c_triu"] = (p[:, None] <= p[None, :]).astype(np.float32).astype(bf)
    c["c_tril"] = (p[None, :] <= p[:, None]).astype(np.float32).astype(bf)
    c["c_tg0"] = np.broadcast_to(np.arange(256, dtype=np.float32)[None, :], (128, 256)).copy()
    par = np.zeros((128, 2), np.float32)
    par[:, 0] = ((p // 16) % 2 == 0)
    par[:, 1] = ((p // 16) % 2 == 1)
    c["c_par"] = par
    return c


def setup_consts(P):
    S, A = P.S, P.A
    C = {}
    A.reset()

    def ld(name, n, dtype, shape):
        d = P.din("c_" + name, shape, dtype)
        r = A.alloc16(n) if dtype == BF16 else A.alloc(n)
        S.dma("sp", out=r.ap, in_=d, writes=[r])
        C[name] = r

    ld("ident", 128, BF16, [128, 128])
    ld("bones", 128, BF16, [128, 128])
    ld("rrot", 128, BF16, [128, 128])
    ld("triu", 128, BF16, [128, 128])
    ld("tril", 128, BF16, [128, 128])
    ld("invf", 1, F32, [128, 1])
    ld("tg0", 256, F32, [128, 256])
    ld("par", 2, F32, [128, 2])
    C["mhalf"] = A.alloc(1)
    S.op("pool", lambda e: e.memset(C["mhalf"].ap, -0.5), writes=[C["mhalf"]])
    C["epscol"] = A.alloc(1)
    S.op("pool", lambda e: e.memset(C["epscol"].ap, EPS), writes=[C["epscol"]])
    A.off = ((A.off + 255) // 256) * 256
    C["a0"] = A.off
    P.C = C
    return C


def lambda_init_of(li):
    return 0.8 - 0.6 * math.exp(-0.3 * li)


def build(spec, ntok=SEQ, debug=False):
    P = Prog(ntok)
    P.DEBUG = debug
    S = P.S
    xin = P.din("x", [ntok, D])
    xout = P.dout("y", [ntok, D])
    setup_consts(P)
    nt = ntok // 256
    xiv = xin.rearrange("(n t) d -> n t d", t=256)
    xov = xout.rearrange("(n t) d -> n t d", t=256)
    for it in range(nt):
        S.dma("sp", out=xov[it], in_=xiv[it], writes=[P.db(f"x{it}")])
    sc = None
    posd = None
    preloaded = set()
    for si, (kind, li) in enumerate(spec):
        if kind == "mlp":
            phase_mlp(P, xout, P.din(f"mlp_w1_{li}", [D, 4 * D]) if li not in preloaded else None,
                      P.din(f"mlp_w2_{li}", [4 * D, D]) if li not in preloaded else None,
                      P.din(f"norm_mlp_{li}", [1, D]), preloaded=(li in preloaded))
            continue
        gamd = P.din(f"norm_mix_{li}", [1, D])
        mk = li % 3
        j = li // 3
        if mk == 0:
            if sc is None:
                sc = dict(qT=P.dscratch("sc_qT", [8, 128, ntok], BF16), kT=P.dscratch("sc_kT", [8, 128, ntok], BF16),
                          v=P.dscratch("sc_v", [ntok // 128, 128, 8 * 129], BF16),
                          oT=P.dscratch("sc_oT", [8, 128, ntok], BF16))
                posd = P.din("positions", [1, ntok], I32)
            phase_attn_qkv(P, xout, gamd, P.din(f"attn_w_qkv_{j}", [D, 3 * D]), P.din(f"attn_q_norm_{j}", [64, 1]),
                           P.din(f"attn_k_norm_{j}", [64, 1]), posd, sc)
            phase_attn_core(P, P.din(f"attn_lambda_{j}", [4, 64]), P.din(f"attn_sub_norm_{j}", [1, 128]),
                            lambda_init_of(li), sc)
            pf = None
            if si + 1 < len(spec) and spec[si + 1] == ("mlp", li):
                pf = (P.din(f"mlp_w1_{li}", [D, 4 * D]), P.din(f"mlp_w2_{li}", [4 * D, D]))
                preloaded.add(li)
            phase_attn_out(P, xout, P.din(f"attn_w_o_{j}", [D, D]), sc, prefetch=pf)
        elif mk == 1:
            phase_s5(P, xout, gamd, P.din(f"ssm_w_in_{j}", [D, D]), P.din(f"ssm_a_re_{j}", [64, 64]),
                     P.din(f"ssm_a_im_{j}", [64, 64]), P.din(f"ssm_b_re_{j}", [64, 64, 16]),
                     P.din(f"ssm_b_im_{j}", [64, 64, 16]), P.din(f"ssm_c_re_{j}", [64, 16, 64]),
                     P.din(f"ssm_c_im_{j}", [64, 16, 64]), P.din(f"ssm_d_{j}", [1, D]),
                     P.din(f"ssm_log_dt_{j}", [1, 64]), P.din(f"ssm_w_glu_{j}", [D, D]),
                     P.din(f"ssm_b_glu_{j}", [1, D]), P.din(f"ssm_w_out_{j}", [D, D]))
        else:
            phase_gmlp(P, xout, gamd, P.din(f"gm_w_in_{j}", [D, 6 * D]), P.din(f"gm_v_norm_{j}", [1, 3 * D]),
                       P.din(f"gm_w_s_{j}", [8, 128, 128]), P.din(f"gm_b_s_{j}", [8, 128]),
                       P.din(f"gm_w_out_{j}", [3 * D, D]))
    S.wait_bufs("sp", [P.db(f"x{it}") for it in range(nt)] + getattr(P, "dbg", []))
    return P


FULL_SPEC = [("mix", 0), ("mlp", 0), ("mix", 1), ("mlp", 1), ("mix", 2), ("mlp", 2), ("mix", 3), ("mlp", 3)]

_SHAPES = {
    "norm_mix": (1, D), "norm_mlp": (1, D), "attn_q_norm": (64, 1), "attn_k_norm": (64, 1), "attn_sub_norm": (1, 128),
    "ssm_d": (1, D), "ssm_log_dt": (1, 64), "ssm_b_glu": (1, D), "gm_v_norm": (1, 3 * D),
}


def make_in_map(P, inputs, b, ntok=SEQ):
    m = dict(host_consts())
    out = {}
    for name in P.dram_in:
        if name in m:
            out[name] = m[name]
        elif name == "x":
            out[name] = np.ascontiguousarray(inputs["x"][b, :ntok])
        elif name == "positions":
            out[name] = np.ascontiguousarray(inputs["positions"][b, :ntok].reshape(1, ntok)).astype(np.int32)
        else:
            base, idx = name.rsplit("_", 1)
            arr = np.asarray(inputs[base])[int(idx)]
            shp = _SHAPES.get(base)
            if shp is not None:
                arr = arr.reshape(shp)
            out[name] = np.ascontiguousarray(arr, dtype=np.float32)
    return out


_PROG = {}


def kernel(**inputs):
    if "full" not in _PROG:
        _PROG["full"] = build(FULL_SPEC, SEQ)
    P = _PROG["full"]
    inputs = {k: np.asarray(v) for k, v in inputs.items()}
    maps = [make_in_map(P, inputs, c % BATCH) for c in range(NCORES)]
    res = run_bass_kernel_spmd(P.nc, maps, core_ids=list(range(NCORES)))
    return np.stack([np.asarray(res.results[b]["y"], dtype=np.float32) for b in range(BATCH)], axis=0)
```

```python
import math
from contextlib import ExitStack

import numpy as np
import ml_dtypes

import concourse.bass as bass
import concourse.mybir as mybir
from concourse.bass_utils import run_bass_kernel_spmd

F32 = mybir.dt.float32
BF16 = mybir.dt.bfloat16
I32 = mybir.dt.int32
ALU = mybir.AluOpType
ACTF = mybir.ActivationFunctionType
AX = mybir.AxisListType

D = 1024
SEQ = 4096
BATCH = 4
DEPTH = 4
EPS = 1e-6
NCORES = 8


class Buf:
    __slots__ = ("name", "w", "r")

    def __init__(self, name):
        self.name = name
        self.w = None
        self.r = {}


class Region:
    def __init__(self, ap, bufs):
        self.ap = ap
        self.bufs = bufs


def _flat(items):
    out = []
    for it in items:
        if it is None:
            continue
        if isinstance(it, Buf):
            out.append(it)
        elif isinstance(it, Region):
            out.extend(it.bufs)
        else:
            out.extend(_flat(it))
    return out


class Sched:
    GEN = 24000
    POOL_INFLIGHT = 8

    def __init__(self, nc, stack, ndma=32):
        self.nc = nc
        self.stack = stack
        self.eng = dict(pe=nc.tensor, act=nc.scalar, dve=nc.vector, pool=nc.gpsimd, sp=nc.sync)
        self.stream = {k: [] for k in self.eng}
        self.cnt = {k: 0 for k in self.eng}
        self.gen = {k: 0 for k in self.eng}
        self.semh = {}
        for k in ("pe", "act", "dve", "pool"):
            self.semh[(k, 0)] = stack.enter_context(nc.semaphore(f"s_{k}0"))
        self.ndma = ndma
        for j in range(ndma):
            self.semh[("d", j)] = stack.enter_context(nc.semaphore(f"s_d{j}"))
        self.dcnt = [0] * ndma
        self.dnext = 0
        self.known = {k: {} for k in self.eng}
        self.ninstr = 0
        self.pool_hist = []

    def _collect(self, reads, writes):
        need = {}
        for b in reads:
            ev = b.w
            if ev is not None and need.get(ev[0], 0) < ev[1]:
                need[ev[0]] = ev[1]
        for b in writes:
            ev = b.w
            if ev is not None and need.get(ev[0], 0) < ev[1]:
                need[ev[0]] = ev[1]
            for k, v in b.r.items():
                if need.get(k, 0) < v:
                    need[k] = v
        return need

    def _waits(self, e, need, skip_self=False):
        kn = self.known[e]
        st = self.stream[e]
        for k, v in need.items():
            if skip_self and k[0] == e:
                continue
            if kn.get(k, 0) >= v:
                continue
            kn[k] = v
            sem = self.semh[k]
            self.eng[e].wait_ge(sem, v)
            self.ninstr += 1

    def op(self, e, fn, reads=(), writes=()):
        reads = _flat(reads)
        writes = _flat(writes)
        need = self._collect(reads, writes)
        self._waits(e, need, skip_self=(e == "pe"))
        if self.cnt[e] >= self.GEN:
            self.gen[e] += 1
            self.cnt[e] = 0
            self.semh[(e, self.gen[e])] = self.stack.enter_context(
                self.nc.semaphore(f"s_{e}{self.gen[e]}"))
        self.cnt[e] += 1
        key = (e, self.gen[e])
        v = self.cnt[e]
        sem = self.semh[key]
        fn(self.eng[e]).then_inc(sem, 1)
        self.ninstr += 1
        for b in reads:
            b.r[key] = v
        for b in writes:
            b.w = (key, v)
            b.r = {}

    def dma(self, q, out, in_, reads=(), writes=(), **kw):
        reads = _flat(reads)
        writes = _flat(writes)
        need = self._collect(reads, writes)
        if q == "pool":
            hist = self.pool_hist
            if len(hist) >= self.POOL_INFLIGHT:
                k0, v0 = hist[-self.POOL_INFLIGHT]
                if need.get(k0, 0) < v0:
                    need[k0] = v0
        j = self.dnext
        self.dnext = (j + 1) % self.ndma
        if self.dcnt[j] > 0 and need.get(("d", j), 0) < self.dcnt[j]:
            need[("d", j)] = self.dcnt[j]
        self._waits(q, need)
        self.dcnt[j] += 16
        key = ("d", j)
        v = self.dcnt[j]
        sem = self.semh[key]
        self.eng[q].dma_start(out=out, in_=in_, **kw).then_inc(sem, 16)
        self.ninstr += 1
        if q == "pool":
            self.pool_hist.append((key, v))
        for b in reads:
            b.r[key] = v
        for b in writes:
            b.w = (key, v)
            b.r = {}

    def wait_bufs(self, e, bufs):
        bufs = _flat(bufs)
        need = self._collect((), bufs)
        self._waits(e, need)

    def finish(self):
        return
        nc = self.nc
        with nc.Block() as block:
            for name, deco in (("pe", block.tensor), ("act", block.scalar), ("dve", block.vector),
                               ("pool", block.gpsimd), ("sp", block.sync)):
                lst = self.stream[name]

                @deco
                def _(eng, lst=lst):
                    for th in lst:
                        th(eng)


class Arena:
    def __init__(self, nc, stack, name, nelem, dtype, chunk):
        self.t = stack.enter_context(nc.sbuf_tensor(name, [128, nelem], dtype))
        self.n = nelem
        self.chunk = chunk
        self.bufs = [Buf(f"{name}{i}") for i in range((nelem + chunk - 1) // chunk)]
        self.off = 0
        self.name = name

    def reset(self, off=0):
        for b, cbs in getattr(self, "owned", []):
            for cb in cbs:
                if b.w is not None:
                    cb.r[b.w[0]] = max(cb.r.get(b.w[0], 0), b.w[1])
                for k, v in b.r.items():
                    cb.r[k] = max(cb.r.get(k, 0), v)
        self.owned = []
        self.off = off

    def own(self, reg):
        b = Buf("own")
        for cb in reg.bufs:
            if cb.w is not None:
                b.r[cb.w[0]] = max(b.r.get(cb.w[0], 0), cb.w[1])
            for k, v in cb.r.items():
                b.r[k] = max(b.r.get(k, 0), v)
        if not hasattr(self, "owned"):
            self.owned = []
        self.owned.append((b, reg.bufs))
        reg.bufs = [b]
        return reg

    def alloc(self, n, pattern=None, **kw):
        off = self.off
        assert off + n <= self.n, f"arena {self.name} overflow: {off}+{n} > {self.n}"
        self.off = off + n
        return self.view(off, n, pattern, **kw)

    def view(self, off, n, pattern=None, **kw):
        ap = self.t[:, off:off + n]
        if pattern is not None:
            ap = ap.rearrange(pattern, **kw)
        c0 = off // self.chunk
        c1 = (off + n - 1) // self.chunk
        r = Region(ap, self.bufs[c0:c1 + 1])
        r.off = off
        r.n = n
        r.arena = self
        return r

    def sub(self, reg, lo, hi):
        m = getattr(reg, "mul", 1)
        c0 = (reg.off + lo // m) // self.chunk
        c1 = (reg.off + (hi - 1) // m) // self.chunk
        return self.bufs[c0:c1 + 1]

    def alloc16(self, n, pattern=None, **kw):
        n32 = (n + 1) // 2
        off = self.off
        assert off + n32 <= self.n, f"arena {self.name} overflow: {off}+{n32} > {self.n}"
        self.off = off + n32
        ap = self.t[:, off:off + n32].bitcast(BF16)[:, 0:n]
        if pattern is not None:
            ap = ap.rearrange(pattern, **kw)
        c0 = off // self.chunk
        c1 = (off + n32 - 1) // self.chunk
        r = Region(ap, self.bufs[c0:c1 + 1])
        r.off = off; r.n = n32; r.arena = self; r.mul = 2
        return r

    def alloc32(self, n, pattern=None, **kw):
        r = self.alloc(2 * n)
        ap = r.ap.bitcast(F32)
        if pattern is not None:
            ap = ap.rearrange(pattern, **kw)
        r.ap = ap
        return r

    def alloci(self, n, pattern=None, **kw):
        r = self.alloc(n)
        ap = r.ap.bitcast(I32)
        if pattern is not None:
            ap = ap.rearrange(pattern, **kw)
        r.ap = ap
        return r


class Prog:
    def __init__(self, ntok=SEQ):
        self.ntok = ntok
        self.nc = bass.Bass("TRN2", target_bir_lowering=False)
        self.stack = ExitStack()
        nc = self.nc
        self.S = Sched(nc, self.stack)
        st = self.stack
        self.WA = Arena(nc, st, "wa", 72 * 1024, BF16, 2048)
        self.A = Arena(nc, st, "aa", 15 * 1024 + 512, F32, 256)
        self.psum_t = st.enter_context(nc.psum_tensor("ps", [128, 8, 512], F32))
        self.ps = [Region(self.psum_t[:, b, :], [Buf(f"ps{b}")]) for b in range(8)]
        self.dram_in = {}
        self.consts = {}
        self.dbuf = {}

    def din(self, name, shape, dtype=F32):
        t = self.nc.dram_tensor(name, list(shape), dtype, kind="ExternalInput").ap()
        self.dram_in[name] = t
        return t

    def dout(self, name, shape, dtype=F32):
        return self.nc.dram_tensor(name, list(shape), dtype, kind="ExternalOutput").ap()

    def dscratch(self, name, shape, dtype):
        return self.nc.dram_tensor(name, list(shape), dtype, kind="Internal").ap()

    def debug(self, name, reg, shape, dtype):
        d = self.nc.dram_tensor("dbg_" + name, list(shape), dtype, kind="ExternalOutput").ap()
        self.S.dma("sp", out=d, in_=reg.ap, reads=[reg], writes=[self.db("dbg_" + name)])
        self.dbg = getattr(self, "dbg", [])
        self.dbg.append(self.db("dbg_" + name))

    def db(self, name):
        b = self.dbuf.get(name)
        if b is None:
            b = self.dbuf[name] = Buf(name)
        return b


def ps_bf16(reg):
    return reg.ap.bitcast(BF16)


def emit_norm_T(P, xt_ap, xt_bufs, gam, hT, tok_off, ident, scr):
    S = P.S
    sq, ss, rstd, xn, pst = scr["sq"], scr["ss"], scr["rstd"], scr["xn"], scr["pst"]
    S.op("act", lambda e: e.activation(out=sq.ap, in_=xt_ap, func=ACTF.Square, scale=1.0 / 32.0,
                                       accum_out=ss.ap),
         reads=[xt_bufs], writes=[sq, ss])
    S.op("pool", lambda e: e.tensor_scalar(out=ss.ap, in0=ss.ap, scalar1=EPS, scalar2=None, op0=ALU.add),
         reads=[ss], writes=[ss])
    S.op("pool", lambda e: e.tensor_tensor(out=rstd.ap, in0=ss.ap, in1=P.C["mhalf"].ap, op=ALU.pow),
         reads=[ss, P.C["mhalf"]], writes=[rstd])
    S.op("dve", lambda e: e.scalar_tensor_tensor(out=xn.ap, in0=xt_ap, scalar=rstd.ap, in1=gam.ap,
                                                 op0=ALU.mult, op1=ALU.mult),
         reads=[xt_bufs, rstd, gam], writes=[xn])
    pv = ps_bf16(pst)
    for kc in range(8):
        S.op("pe", lambda e, kc=kc: e.transpose(out=pv[:, kc * 128:(kc + 1) * 128],
                                                in_=xn.ap[:, kc * 128:(kc + 1) * 128],
                                                identity=ident.ap),
             reads=[xn, ident], writes=[pst])
    S.op("act", lambda e: e.activation(out=hT.ap[:, :, tok_off:tok_off + 128],
                                       in_=pv.rearrange("p (k t) -> p k t", k=8), func=ACTF.Copy),
         reads=[pst], writes=[hT])


def load_weight_fast(P, reg, dram_ap, nk, ncols, nstage=4, piece=2048):
    S, A = P.S, P.A
    piece = min(piece, ncols)
    off_keep = A.off
    A.off = A.n - nstage * piece
    stg = [A.alloc(piece) for _ in range(nstage)]
    A.off = off_keep
    i = 0
    for kc in range(nk):
        for c0 in range(0, ncols, piece):
            c1 = min(ncols, c0 + piece)
            st = stg[i % nstage]
            S.dma("sp", out=st.ap[:, 0:c1 - c0], in_=dram_ap[kc * 128:(kc + 1) * 128, c0:c1], writes=[st])
            lo = kc * ncols + c0
            dst = reg.arena.sub(reg, lo, lo + (c1 - c0))
            if i % 2 == 0:
                S.op("act", lambda e: e.activation(out=reg.ap[:, kc, c0:c1], in_=st.ap[:, 0:c1 - c0], func=ACTF.Copy),
                     reads=[st], writes=[dst])
            else:
                S.op("dve", lambda e: e.tensor_copy(out=reg.ap[:, kc, c0:c1], in_=st.ap[:, 0:c1 - c0]),
                     reads=[st], writes=[dst])
            i += 1


def load_weight(P, reg, dram_ap, rows_per_part_dim, ncols, q="pool"):
    S = P.S
    nk = rows_per_part_dim
    for kc in range(nk):
        for c0 in range(0, ncols, 2048):
            c1 = min(ncols, c0 + 2048)
            lo = kc * ncols + c0
            S.dma(q, out=reg.ap[:, kc, c0:c1], in_=dram_ap[kc * 128:(kc + 1) * 128, c0:c1],
                  writes=[reg.arena.sub(reg, lo, lo + (c1 - c0))])


TWO_PI = 6.283179


def norm_scratch(P):
    A = P.A
    return dict(sq=A.alloc16(1024), ss=A.alloc(1), rstd=A.alloc(1), xn=A.alloc16(1024))


def load_gamma(P, gamd):
    gam = P.A.alloc(1024)
    P.S.dma("sp", out=gam.ap, in_=gamd.partition_broadcast(128), writes=[gam])
    return gam


def residual_out(P, xt, xb_ap_fn, W, nk, lhs_fn, lhs_reads_fn, banks):
    S = P.S
    nb = xt.ap.shape[1]
    i = 0
    for b in range(nb):
        for oc in range(2):
            pb = banks[i % len(banks)]
            i += 1
            for k in range(nk):
                S.op("pe", lambda e, k=k, b=b, oc=oc, pb=pb: e.matmul(
                    pb.ap, lhsT=lhs_fn(k, b), rhs=W.ap[:, k, oc * 512:(oc + 1) * 512],
                    start=(k == 0), stop=(k == nk - 1)),
                    reads=[lhs_reads_fn(k), P.WA.sub(W, k * 1024 + oc * 512, k * 1024 + oc * 512 + 512)],
                    writes=[pb])
            S.op("dve", lambda e, b=b, oc=oc, pb=pb: e.tensor_tensor(
                out=xt.ap[:, b, oc * 512:(oc + 1) * 512], in0=pb.ap,
                in1=xt.ap[:, b, oc * 512:(oc + 1) * 512], op=ALU.add),
                reads=[pb, xt], writes=[xt])


def phase_mlp(P, xd, w1d, w2d, gamd, preloaded=False):
    S, WA, A, C = P.S, P.WA, P.A, P.C
    TT = 256
    NTT = P.ntok // TT
    WA.reset(); A.reset(C["a0"])
    W1 = WA.alloc(8 * 4096, "p (k f) -> p k f", k=8)
    W2 = WA.alloc(32 * 1024, "p (k f) -> p k f", k=32)
    if not preloaded:
        load_weight_fast(P, W1, w1d, 8, 4096)
        load_weight_fast(P, W2, w2d, 32, 1024, piece=1024)
    gam = load_gamma(P, gamd)
    xts = [A.alloc(2 * 1024, "p (b d) -> p b d", b=2) for _ in range(2)]
    sqfs = [A.own(A.alloc(256)) for _ in range(3)]
    hTs = [A.alloc16(8 * TT, "p (k t) -> p k t", k=8) for _ in range(2)]
    actT = A.alloc16(32 * TT, "p (f t) -> p f t", f=32)
    ident = C["ident"]
    xv = xd.rearrange("(n b p) d -> n p b d", p=128, b=2)
    groups = [list(range(g, min(g + 3, 32))) for g in range(0, 32, 3)]

    xns = [A.own(A.alloc16(1024)) for _ in range(2)]
    sqj = A.own(A.alloc16(1024)); ssn = A.alloc(1); rstdn = A.alloc(1)

    def load_x(it):
        S.dma("sp", out=xts[it % 2].ap, in_=xv[it], reads=[P.db(f"x{it}")], writes=[xts[it % 2]])

    def norm_pre(it, b):
        xt_ = xts[it % 2]; xn = xns[b]
        S.op("act", lambda e: e.activation(out=sqj.ap, in_=xt_.ap[:, b, :], func=ACTF.Square, scale=1.0 / 32.0,
                                           accum_out=ssn.ap), reads=[xt_], writes=[sqj, ssn])
        S.op("pool", lambda e: e.tensor_scalar(out=ssn.ap, in0=ssn.ap, scalar1=EPS, scalar2=None, op0=ALU.add),
             reads=[ssn], writes=[ssn])
        S.op("pool", lambda e: e.tensor_tensor(out=rstdn.ap, in0=ssn.ap, in1=C["mhalf"].ap, op=ALU.pow),
             reads=[ssn, C["mhalf"]], writes=[rstdn])
        S.op("dve", lambda e: e.scalar_tensor_tensor(out=xn.ap, in0=xt_.ap[:, b, :], scalar=rstdn.ap, in1=gam.ap,
                                                     op0=ALU.mult, op1=ALU.mult),
             reads=[xt_, rstdn, gam], writes=[xn])

    def norm_post(it, b):
        xn = xns[b]; hT_ = hTs[it % 2]
        pst = P.ps[6 + b]
        pv = ps_bf16(pst)
        for kc in range(8):
            S.op("pe", lambda e: e.transpose(out=pv[:, kc * 128:(kc + 1) * 128], in_=xn.ap[:, kc * 128:(kc + 1) * 128],
                                             identity=ident.ap), reads=[xn, ident], writes=[pst])
        S.op("act", lambda e: e.activation(out=hT_.ap[:, :, b * 128:(b + 1) * 128],
                                           in_=pv.rearrange("p (k t) -> p k t", k=8), func=ACTF.Copy),
             reads=[pst], writes=[hT_])

    def load_norm(it):
        load_x(it)
        for b in range(2):
            norm_pre(it, b)
            norm_post(it, b)

    load_norm(0)
    nsq = 0
    for it in range(NTT):
        xt = xts[it % 2]; hT = hTs[it % 2]
        xb = P.db(f"x{it}")
        if it + 1 < NTT:
            load_x(it + 1)
        for gi, grp in enumerate(groups):
            if it + 1 < NTT and gi in (5, 7):
                norm_pre(it + 1, (gi - 5) // 2)
            banks = [P.ps[(gi % 2) * 3 + n] for n in range(len(grp))]
            for kc in range(8):
                for n, fc in enumerate(grp):
                    pb = banks[n]
                    S.op("pe", lambda e: e.matmul(
                        pb.ap[:, 0:TT], lhsT=W1.ap[:, kc, fc * 128:(fc + 1) * 128], rhs=hT.ap[:, kc, :],
                        start=(kc == 0), stop=(kc == 7)),
                        reads=[WA.sub(W1, kc * 4096 + fc * 128, kc * 4096 + fc * 128 + 128), hT], writes=[pb])
            for n, fc in enumerate(grp):
                pb = banks[n]
                sqf = sqfs[nsq % 3]
                nsq += 1
                S.op("act", lambda e: e.activation(out=sqf.ap, in_=pb.ap[:, 0:TT], func=ACTF.Relu),
                     reads=[pb], writes=[sqf])
                S.op("dve", lambda e: e.tensor_tensor(out=actT.ap[:, fc, :], in0=sqf.ap, in1=sqf.ap, op=ALU.mult),
                     reads=[sqf], writes=[A.sub(actT, fc * TT, fc * TT + TT)])
        if it + 1 < NTT:
            norm_post(it + 1, 0)
            norm_post(it + 1, 1)
        residual_out(P, xt, None, W2, 32,
                     lambda k, b: actT.ap[:, k, b * 128:(b + 1) * 128],
                     lambda k: A.sub(actT, k * TT, k * TT + TT), P.ps[0:4])
        S.dma("sp", out=xv[it], in_=xt.ap, reads=[xt], writes=[xb])


def phase_gmlp(P, xd, gamd, wind, vgd, wsd, bsd, woutd):
    S, WA, A, C = P.S, P.WA, P.A, P.C
    TT = 256
    NTT = P.ntok // TT
    WA.reset(); A.reset(C["a0"])
    Win = WA.alloc(8 * 6144, "p (k f) -> p k f", k=8)
    Wout = WA.alloc(24 * 1024, "p (k f) -> p k f", k=24)
    load_weight_fast(P, Win, wind, 8, 6144)
    load_weight_fast(P, Wout, woutd, 24, 1024, piece=1024)
    gam = load_gamma(P, gamd)
    ident = C["ident"]
    WmT = A.alloc16(1024, "p (h t) -> p h t", h=8)
    vg = A.alloc(24)
    bs = A.alloc(1024, "p (h t) -> p h t", h=8)
    off_keep = A.off
    A.off = A.n - 512
    wn = A.alloc16(1024, "p (h s) -> p h s", h=8)
    A.off = off_keep
    off_keep2 = A.off
    A.off = A.n - 512 - 1024
    wnf = A.alloc(1024, "p (h s) -> p h s", h=8)
    A.off = off_keep2
    S.dma("sp", out=wnf.ap, in_=wsd.rearrange("h t s -> t h s"), writes=[wnf])
    S.op("dve", lambda e: e.tensor_tensor(out=wn.ap, in0=wnf.ap,
                                          in1=C["tril"].ap.unsqueeze(1).broadcast_to([128, 8, 128]), op=ALU.mult),
         reads=[wnf, C["tril"]], writes=[wn])
    pst = P.ps[7]
    pv = ps_bf16(pst)
    for h in range(8):
        S.op("pe", lambda e, h=h: e.transpose(out=pv[:, h * 128:(h + 1) * 128], in_=wn.ap[:, h, :], identity=ident.ap),
             reads=[wn, ident], writes=[pst])
    S.op("act", lambda e: e.activation(out=WmT.ap, in_=pv.rearrange("p (h t) -> p h t", h=8), func=ACTF.Copy),
         reads=[pst], writes=[WmT])
    S.dma("sp", out=vg.ap, in_=vgd.rearrange("o (c p) -> p (o c)", p=128), writes=[vg], allow_slow_non_contiguous=True)
    S.dma("sp", out=bs.ap, in_=bsd.rearrange("h t -> (h t)").partition_broadcast(128), writes=[bs])
    xts = [A.alloc(2 * 1024, "p (b d) -> p b d", b=2)] * 2
    scr = norm_scratch(P)
    hT = A.alloc16(8 * TT, "p (k t) -> p k t", k=8)
    gv = [A.alloc16(3072) for _ in range(2)]
    ssq = [A.alloc(8) for _ in range(2)]
    sst = [A.alloc(1) for _ in range(2)]
    rsv = [A.alloc(1) for _ in range(2)]
    junk = scr["sq"]
    WmTs = [A.alloc16(1024, "p (h t) -> p h t", h=8) for _ in range(2)]
    ug = [A.own(A.alloc(TT)) for _ in range(2)]
    tmp = [A.own(A.alloc(TT)) for _ in range(2)]
    uvT = A.alloc16(24 * TT, "p (c t) -> p c t", c=24)
    xv = xd.rearrange("(n b p) d -> n p b d", p=128, b=2)
    hTs = [hT, hT]

    def load_norm(it):
        xt_ = xts[it % 2]
        S.dma("sp", out=xt_.ap, in_=xv[it], reads=[P.db(f"x{it}")], writes=[xt_])
        for b in range(2):
            emit_norm_T(P, xt_.ap[:, b, :], xt_, gam, hTs[it % 2], b * 128, ident, dict(scr, pst=P.ps[7]))

    ngrp = 0
    for it in range(NTT):
        load_norm(it)
        xt = xts[it % 2]
        hT = hTs[it % 2]
        xb = P.db(f"x{it}")
        for b in range(2):
            for jg in range(2):
                banks = [P.ps[(ngrp % 2) * 3 + n] for n in range(3)]
                ngrp += 1
                for kc in range(8):
                    for n in range(3):
                        c0 = 3072 + (jg * 3 + n) * 512
                        pb = banks[n]
                        S.op("pe", lambda e: e.matmul(
                            pb.ap, lhsT=hT.ap[:, kc, b * 128:(b + 1) * 128], rhs=Win.ap[:, kc, c0:c0 + 512],
                            start=(kc == 0), stop=(kc == 7)),
                            reads=[hT, WA.sub(Win, kc * 6144 + c0, kc * 6144 + c0 + 512)], writes=[pb])
                for n in range(3):
                    jj = jg * 3 + n
                    pb = banks[n]
                    gs = A.sub(gv[b], jj * 512, jj * 512 + 512)
                    S.op("act", lambda e: e.activation(
                        out=gv[b].ap[:, jj * 512:(jj + 1) * 512], in_=pb.ap, func=ACTF.Gelu_apprx_tanh),
                        reads=[pb], writes=[gs])
                    S.op("dve", lambda e: e.scalar_tensor_tensor(
                        out=junk.ap[:, 0:512], in0=gv[b].ap[:, jj * 512:(jj + 1) * 512], scalar=1.0,
                        in1=gv[b].ap[:, jj * 512:(jj + 1) * 512], op0=ALU.mult, op1=ALU.mult,
                        accum_out=ssq[b].ap[:, jj:jj + 1]),
                        reads=[gs], writes=[junk, ssq[b]])
            S.op("dve", lambda e: e.tensor_reduce(out=sst[b].ap, in_=ssq[b].ap[:, 0:6], axis=AX.X, op=ALU.add),
                 reads=[ssq[b]], writes=[sst[b]])
            S.op("pool", lambda e: e.tensor_scalar(out=sst[b].ap, in0=sst[b].ap, scalar1=1.0 / 3072.0, scalar2=EPS,
                                                   op0=ALU.mult, op1=ALU.add),
                 reads=[sst[b]], writes=[sst[b]])
            S.op("pool", lambda e: e.tensor_tensor(out=rsv[b].ap, in0=sst[b].ap, in1=C["mhalf"].ap, op=ALU.pow),
                 reads=[sst[b], C["mhalf"]], writes=[rsv[b]])
            S.op("dve", lambda e: e.tensor_scalar(out=WmTs[b].ap, in0=WmT.ap, scalar1=rsv[b].ap, scalar2=None,
                                                  op0=ALU.mult),
                 reads=[WmT, rsv[b]], writes=[WmTs[b]])
        for hh in range(8):
            banks = [P.ps[(ngrp % 2) * 3 + n] for n in range(3)]
            ngrp += 1
            for kc in range(8):
                for n in range(3):
                    c = hh * 3 + n
                    pu = banks[n]
                    S.op("pe", lambda e: e.matmul(
                        pu.ap[:, 0:TT], lhsT=Win.ap[:, kc, c * 128:(c + 1) * 128], rhs=hT.ap[:, kc, :],
                        start=(kc == 0), stop=(kc == 7)),
                        reads=[hT, WA.sub(Win, kc * 6144 + c * 128, kc * 6144 + c * 128 + 128)], writes=[pu])
            for n in range(3):
                c = hh * 3 + n
                pu = banks[n]
                u_ = ug[c % 2]
                t_ = tmp[c % 2]
                S.op("act", lambda e: e.activation(out=u_.ap, in_=pu.ap[:, 0:TT], func=ACTF.Gelu_apprx_tanh),
                     reads=[pu], writes=[u_])
                pg = P.ps[6]
                pgo = (c % 2) * 256
                for b in range(2):
                    S.op("pe", lambda e: e.matmul(
                        pg.ap[:, pgo + b * 128:pgo + (b + 1) * 128], lhsT=gv[b].ap[:, c * 128:(c + 1) * 128],
                        rhs=WmTs[b].ap[:, hh, :], start=True, stop=True),
                        reads=[A.sub(gv[b], c * 128, c * 128 + 128), WmTs[b]], writes=[pg])
                for b in range(2):
                    S.op("dve", lambda e: e.scalar_tensor_tensor(
                        out=t_.ap[:, b * 128:(b + 1) * 128], in0=pg.ap[:, pgo + b * 128:pgo + (b + 1) * 128],
                        scalar=vg.ap[:, c:c + 1], in1=bs.ap[:, hh, :], op0=ALU.mult, op1=ALU.add),
                        reads=[pg, vg, bs], writes=[t_])
                S.op("pool", lambda e: e.tensor_tensor(out=uvT.ap[:, c, :], in0=t_.ap, in1=u_.ap, op=ALU.mult),
                     reads=[t_, u_], writes=[A.sub(uvT, c * TT, c * TT + TT)])
        residual_out(P, xt, None, Wout, 24,
                     lambda k, b: uvT.ap[:, k, b * 128:(b + 1) * 128],
                     lambda k: A.sub(uvT, k * TT, k * TT + TT), P.ps[0:4])
        S.dma("sp", out=xv[it], in_=xt.ap, reads=[xt], writes=[xb])


def emit_sincos(P, yy, sn, cs, ki, fr):
    S = P.S
    S.op("dve", lambda e: e.tensor_copy(out=ki.ap, in_=yy.ap), reads=[yy], writes=[ki])
    S.op("dve", lambda e: e.tensor_tensor(out=fr.ap, in0=yy.ap, in1=ki.ap, op=ALU.subtract),
         reads=[yy, ki], writes=[fr])
    S.op("act", lambda e: e.activation(out=sn.ap, in_=fr.ap, func=ACTF.Sin, scale=TWO_PI), reads=[fr], writes=[sn])
    S.op("dve", lambda e: e.tensor_scalar(out=ki.ap, in0=yy.ap, scalar1=0.25, scalar2=None, op0=ALU.add),
         reads=[yy, fr], writes=[ki])
    S.op("dve", lambda e: e.scalar_tensor_tensor(out=fr.ap, in0=yy.ap, scalar=0.25, in1=ki.ap,
                                                 op0=ALU.add, op1=ALU.subtract),
         reads=[yy, ki, sn], writes=[fr])
    S.op("act", lambda e: e.activation(out=cs.ap, in_=fr.ap, func=ACTF.Sin, scale=TWO_PI), reads=[fr], writes=[cs])


def phase_attn_qkv(P, xd, gamd, wqkvd, qgd, kgd, posd, sc):
    S, WA, A, C = P.S, P.WA, P.A, P.C
    TT = 512
    NB = TT // 128
    NTT = P.ntok // TT
    WA.reset(); A.reset(C["a0"])
    W = WA.alloc(8 * 3072, "p (k f) -> p k f", k=8)
    load_weight_fast(P, W, wqkvd, 8, 3072, piece=1536)
    gam = load_gamma(P, gamd)
    ident = C["ident"]
    gcol = A.alloc(2)
    for idx, gd in enumerate((qgd, kgd)):
        for half in range(2):
            S.dma("sp", out=gcol.ap[half * 64:(half + 1) * 64, idx:idx + 1], in_=gd, writes=[gcol])
    A.off = ((A.off + 255) // 256) * 256
    xts = [A.alloc(NB * 1024, "p (b d) -> p b d", b=NB), WA.alloc32(NB * 1024, "p (b d) -> p b d", b=NB)]
    sq = A.own(A.alloc16(1024)); ssn = A.alloc(1); rstdn = A.alloc(1)
    xns = [A.own(A.alloc16(1024)) for _ in range(2)]
    hTs = [A.alloc16(8 * TT, "p (k t) -> p k t", k=8) for _ in range(2)]
    posb = A.alloci(TT)
    yy = A.alloc(TT); ki = A.alloci(TT); fr = A.alloc(TT)
    sns = [WA.own(WA.alloc32(TT)) for _ in range(2)]
    css = [WA.own(WA.alloc32(TT)) for _ in range(2)]
    va = [A.alloc16(8 * 129, "p (h e) -> p h e", h=8) for _ in range(2)]
    NS = 4
    sqb = [WA.own(WA.alloc(TT)) for _ in range(NS)]
    sd = [WA.own(WA.alloc32(TT)) for _ in range(NS)]
    rs = [WA.own(WA.alloc32(TT)) for _ in range(NS)]
    qn = [WA.own(WA.alloc(TT)) for _ in range(NS)]
    t1 = [WA.own(WA.alloc32(TT)) for _ in range(NS)]
    t2 = [WA.own(WA.alloc32(TT)) for _ in range(NS)]
    qf = [WA.own(WA.alloc(TT)) for _ in range(NS)]
    for v in va:
        S.op("pool", lambda e, v=v: e.memset(v.ap, 1.0), writes=[v])
    xv = xd.rearrange("(n b p) d -> n p b d", p=128, b=NB)
    pqb = P.ps[0:3]; pmb = P.ps[3:5]; prb = P.ps[5:7]
    pst = P.ps[7]
    pv = ps_bf16(pst)

    def load_x(it):
        xb = [P.db(f"x{it * 2}"), P.db(f"x{it * 2 + 1}")]
        S.dma("sp", out=xts[it % 2].ap, in_=xv[it], reads=xb, writes=[xts[it % 2]])

    def norm_pre(it, b):
        xt = xts[it % 2]
        xn = xns[b % 2]
        S.op("act", lambda e: e.activation(out=sq.ap, in_=xt.ap[:, b, :], func=ACTF.Square, scale=1.0 / 32.0,
                                           accum_out=ssn.ap), reads=[xt], writes=[sq, ssn])
        S.op("pool", lambda e: e.tensor_scalar(out=ssn.ap, in0=ssn.ap, scalar1=EPS, scalar2=None, op0=ALU.add),
             reads=[ssn], writes=[ssn])
        S.op("pool", lambda e: e.tensor_tensor(out=rstdn.ap, in0=ssn.ap, in1=C["mhalf"].ap, op=ALU.pow),
             reads=[ssn, C["mhalf"]], writes=[rstdn])
        S.op("dve", lambda e: e.scalar_tensor_tensor(out=xn.ap, in0=xt.ap[:, b, :], scalar=rstdn.ap, in1=gam.ap,
                                                     op0=ALU.mult, op1=ALU.mult),
             reads=[xt, rstdn, gam], writes=[xn])

    def norm_post(it, b):
        xn = xns[b % 2]
        hT = hTs[it % 2]
        for kc in range(8):
            S.op("pe", lambda e: e.transpose(out=pv[:, kc * 128:(kc + 1) * 128], in_=xn.ap[:, kc * 128:(kc + 1) * 128],
                                             identity=ident.ap), reads=[xn, ident], writes=[pst])
        S.op("act", lambda e: e.activation(out=hT.ap[:, :, b * 128:(b + 1) * 128],
                                           in_=pv.rearrange("p (k t) -> p k t", k=8), func=ACTF.Copy),
             reads=[pst], writes=[hT])

    def sincos(it):
        S.dma("sp", out=posb.ap, in_=posd[:, it * TT:(it + 1) * TT].partition_broadcast(128), writes=[posb])
        S.op("dve", lambda e: e.tensor_scalar(out=yy.ap, in0=posb.ap, scalar1=C["invf"].ap, scalar2=None, op0=ALU.mult),
             reads=[posb, C["invf"]], writes=[yy])
        emit_sincos(P, yy, sns[it % 2], css[it % 2], ki, fr)

    items = [(it, which, h) for it in range(NTT) for which in range(2) for h in range(8)]
    n = len(items)

    def st0(i):
        it, which, h = items[i]
        hT = hTs[it % 2]
        col0 = which * 1024 + h * 128
        pq = pqb[i % 3]
        for kc in range(8):
            S.op("pe", lambda e: e.matmul(pq.ap, lhsT=W.ap[:, kc, col0:col0 + 128], rhs=hT.ap[:, kc, :],
                                          start=(kc == 0), stop=(kc == 7)),
                 reads=[hT, WA.sub(W, kc * 3072 + col0, kc * 3072 + col0 + 128)], writes=[pq])

    def st1(i):
        pq = pqb[i % 3]; pm = pmb[i % 2]; s_ = sqb[i % NS]
        S.op("act", lambda e: e.activation(out=s_.ap, in_=pq.ap, func=ACTF.Square), reads=[pq], writes=[s_])
        S.op("pe", lambda e: e.matmul(pm.ap, lhsT=C["bones"].ap, rhs=s_.ap, start=True, stop=True),
             reads=[s_, C["bones"]], writes=[pm])

    def st2(i):
        it, which, h = items[i]
        k = i % NS
        pq = pqb[i % 3]; pm = pmb[i % 2]
        S.op("act", lambda e: e.activation(out=sd[k].ap, in_=pm.ap, func=ACTF.Ln, bias=C["epscol"].ap),
             reads=[pm, C["epscol"]], writes=[sd[k]])
        S.op("act", lambda e: e.activation(out=rs[k].ap, in_=sd[k].ap, func=ACTF.Exp, scale=-0.5),
             reads=[sd[k]], writes=[rs[k]])
        S.op("dve", lambda e: e.scalar_tensor_tensor(out=qn[k].ap, in0=pq.ap, scalar=gcol.ap[:, which:which + 1],
                                                     in1=rs[k].ap, op0=ALU.mult, op1=ALU.mult),
             reads=[pq, gcol, rs[k]], writes=[qn[k]])

    def st2b(i):
        k = i % NS
        pr = prb[i % 2]
        S.op("pe", lambda e: e.matmul(pr.ap, lhsT=C["rrot"].ap, rhs=qn[k].ap, start=True, stop=True),
             reads=[qn[k], C["rrot"]], writes=[pr])

    def st3(i):
        it, which, h = items[i]
        k = i % NS
        pr = prb[i % 2]
        sn, cs = sns[it % 2], css[it % 2]
        S.op("pool", lambda e: e.tensor_tensor(out=t1[k].ap, in0=qn[k].ap, in1=cs.ap, op=ALU.mult),
             reads=[qn[k], cs], writes=[t1[k]])
        S.op("dve", lambda e: e.tensor_tensor(out=t2[k].ap, in0=pr.ap, in1=sn.ap, op=ALU.mult),
             reads=[pr, sn], writes=[t2[k]])
        S.op("dve", lambda e: e.tensor_tensor(out=qf[k].ap, in0=t1[k].ap, in1=t2[k].ap, op=ALU.add),
             reads=[t1[k], t2[k]], writes=[qf[k]])
        dst = sc["qT"] if which == 0 else sc["kT"]
        S.dma("sp", out=dst[h, :, it * TT:(it + 1) * TT], in_=qf[k].ap, reads=[qf[k]],
              writes=[P.db(f"{'qk'[which]}T{h}_{it}")])

    def vproj(it, b):
        hT = hTs[it % 2]
        v = va[b % 2]
        for jj in range(2):
            pvb = pst
            for kc in range(8):
                c0 = 2048 + jj * 512
                S.op("pe", lambda e: e.matmul(pvb.ap, lhsT=hT.ap[:, kc, b * 128:(b + 1) * 128], rhs=W.ap[:, kc, c0:c0 + 512],
                                              start=(kc == 0), stop=(kc == 7)),
                     reads=[hT, WA.sub(W, kc * 3072 + c0, kc * 3072 + c0 + 512)], writes=[pvb])
            S.op("act", lambda e: e.activation(
                out=v.ap[:, 4 * jj:4 * jj + 4, 0:128], in_=pvb.ap.rearrange("p (h e) -> p h e", h=4), func=ACTF.Copy),
                reads=[pvb], writes=[v])
        blk = it * NB + b
        S.dma("sp", out=sc["v"][blk], in_=v.ap, reads=[v], writes=[P.db(f"v{blk}")])

    load_x(0)
    for b in range(NB):
        norm_pre(0, b)
        norm_post(0, b)
    sincos(0)
    if NTT > 1:
        load_x(1)
    for s in range(n + 4):
        if s < n:
            it, which, h = items[s]
            j16 = s % 16
            st0(s)
        if 0 <= s - 2 < n:
            st2(s - 2)
        if 0 <= s - 1 < n:
            st1(s - 1)
        if 0 <= s - 3 < n:
            st2b(s - 3)
        if 0 <= s - 4 < n:
            st3(s - 4)
        if s < n:
            if j16 in (1, 5, 9, 13):
                vproj(it, j16 // 4)
            if it + 1 < NTT:
                if j16 in (0, 4, 8, 12):
                    norm_pre(it + 1, j16 // 4)
                if j16 in (3, 7, 11, 15):
                    norm_post(it + 1, j16 // 4)
                if j16 == 14:
                    sincos(it + 1)
                if j16 == 15 and it + 2 < NTT:
                    load_x(it + 2)


def phase_attn_core(P, lamd, sgd, lambda_init, sc):
    S, WA, A, C = P.S, P.WA, P.A, P.C
    ntok = P.ntok
    NB = ntok // 128
    NG = ntok // 512
    NT256 = ntok // 512
    WA.reset(); A.reset(C["a0"])
    K0 = [WA.alloc(ntok) for _ in range(2)]
    K1 = [WA.alloc(ntok) for _ in range(2)]
    QT = [WA.alloc(ntok) for _ in range(2)]
    VA = [WA.alloc(NB * 128, "p (n e) -> p n e", e=128) for _ in range(2)]
    ones16 = WA.alloc(128)
    onesb = WA.alloc(128)
    S.op("pool", lambda e: e.memset(ones16.ap, 1.0), writes=[ones16])
    S.op("pool", lambda e: e.memset(onesb.ap, 1.0 / 128.0), writes=[onesb])
    for i in range(2):
        S.op("pool", lambda e, i=i: e.memset(K0[i].ap[64:128, :], 0.0), writes=[K0[i]])
        S.op("pool", lambda e, i=i: e.memset(K1[i].ap[0:64, :], 0.0), writes=[K1[i]])
    L = A.alloc(256, "p (a d) -> p a d", a=4)
    S.dma("sp", out=L.ap, in_=lamd.rearrange("a d -> (a d)").partition_broadcast(128), writes=[L])
    lj = A.alloc(64); s12 = A.alloc(2); e12 = A.alloc(2); neglam = A.alloc(1)
    for a in range(2):
        S.op("dve", lambda e, a=a: e.scalar_tensor_tensor(
            out=lj.ap, in0=L.ap[:, 2 * a, :], scalar=1.0, in1=L.ap[:, 2 * a + 1, :], op0=ALU.mult, op1=ALU.mult,
            accum_out=s12.ap[:, a:a + 1]), reads=[L], writes=[lj, s12])
    S.op("act", lambda e: e.activation(out=e12.ap, in_=s12.ap, func=ACTF.Exp), reads=[s12], writes=[e12])
    S.op("dve", lambda e: e.tensor_tensor(out=neglam.ap, in0=e12.ap[:, 1:2], in1=e12.ap[:, 0:1], op=ALU.subtract),
         reads=[e12], writes=[neglam])
    S.op("dve", lambda e: e.tensor_scalar(out=neglam.ap, in0=neglam.ap, scalar1=-float(lambda_init), scalar2=None,
                                          op0=ALU.add), reads=[neglam], writes=[neglam])
    sgc = A.alloc(1)
    S.dma("sp", out=sgc.ap, in_=sgd.rearrange("o e -> e o"), writes=[sgc], allow_slow_non_contiguous=True)
    S.op("dve", lambda e: e.tensor_scalar(out=sgc.ap, in0=sgc.ap, scalar1=float(1.0 - lambda_init), scalar2=None,
                                          op0=ALU.mult), reads=[sgc], writes=[sgc])
    A.off = ((A.off + 255) // 256) * 256
    PT = [[A.alloc16(512) for _ in range(2)] for _ in range(2)]
    ob = [[A.alloc(512) for _ in range(2)] for _ in range(2)]
    rl = [A.alloc(512) for _ in range(2)]
    tt = A.alloc(512); uu = A.alloc(512); oo = A.alloc(512)
    osq = A.alloc16(512); msb = A.alloc(512); rs = A.alloc(512)
    oT = [A.alloc16(512) for _ in range(2)]
    sb3 = [P.ps[0], P.ps[1], P.ps[2]]
    otb = [P.ps[3], P.ps[4]]
    plb = [P.ps[5], P.ps[6]]
    pmb = P.ps[7]
    lsb = [[A.alloc(512) for _ in range(2)] for _ in range(2)]
    mhb = C["mhalf"].ap.broadcast_to([128, 512])
    ng = 0
    for h in range(8):
        buf = h % 2
        S.dma("sp", out=K0[buf].ap[0:64, :], in_=sc["kT"][h, 0:64, :],
              reads=[P.db(f"kT{h}_{it}") for it in range(NT256)], writes=[K0[buf]])
        S.dma("sp", out=K1[buf].ap[64:128, :], in_=sc["kT"][h, 64:128, :],
              reads=[P.db(f"kT{h}_{it}") for it in range(NT256)], writes=[K1[buf]])
        S.dma("sp", out=QT[buf].ap, in_=sc["qT"][h],
              reads=[P.db(f"qT{h}_{it}") for it in range(NT256)], writes=[QT[buf]])
        S.dma("sp", out=VA[buf].ap, in_=sc["v"].rearrange("n p (h e) -> h p n e", h=8)[h][:, :, 0:128],
              reads=[P.db(f"v{blk}") for blk in range(NB)], writes=[VA[buf]])
        for G in range(NG):
            gb = ng % 2
            ng += 1
            njb = 4 * G + 4

            def geom(jb):
                nq0 = max(0, jb - 4 * G)
                return nq0, (4 - nq0) * 128, G * 512 + nq0 * 128

            def stage_a(jb, cs_=(0, 1)):
                nq0, N, qc0 = geom(jb)
                for c in cs_:
                    Kc = (K0 if c == 0 else K1)[buf]
                    pss = sb3[(2 * jb + c) % 3]
                    S.op("pe", lambda e: e.matmul(
                        pss.ap[:, 0:N], lhsT=Kc.ap[:, jb * 128:(jb + 1) * 128], rhs=QT[buf].ap[:, qc0:qc0 + N],
                        start=True, stop=True),
                        reads=[WA.sub(Kc, jb * 128, jb * 128 + 128), WA.sub(QT[buf], qc0, qc0 + N)], writes=[pss])

            def stage_b(jb, cs_=(0, 1)):
                nq0, N, qc0 = geom(jb)
                c0 = nq0 * 128
                for c in cs_:
                    pss = sb3[(2 * jb + c) % 3]
                    pt = PT[jb % 2][c]
                    S.op("act", lambda e: e.activation(out=pt.ap[:, 0:N], in_=pss.ap[:, 0:N], func=ACTF.Exp, scale=0.125),
                         reads=[pss], writes=[pt])
                    eng = "dve" if c == 0 else "pool"
                    if jb >= 4 * G:
                        S.op("dve", lambda e: e.tensor_tensor(out=pt.ap[:, 0:128], in0=pt.ap[:, 0:128],
                                                               in1=C["triu"].ap, op=ALU.mult),
                             reads=[pt, C["triu"]], writes=[pt])

            def stage_c(jb):
                nq0, N, qc0 = geom(jb)
                c0 = nq0 * 128
                for c in range(2):
                    pt = PT[jb % 2][c]
                    S.op("pe", lambda e: e.matmul(
                        otb[c].ap[:, c0:512], lhsT=VA[buf].ap[:, jb, :], rhs=pt.ap[:, 0:N],
                        start=(jb == 0), stop=(jb == njb - 1)),
                        reads=[pt, WA.sub(VA[buf], jb * 128, jb * 128 + 128)], writes=[otb[c]])
                    S.op("pe", lambda e: e.matmul(
                        plb[c].ap[:, c0:512], lhsT=ones16.ap, rhs=pt.ap[:, 0:N],
                        start=(jb == 0), stop=(jb == njb - 1)),
                        reads=[pt, ones16], writes=[plb[c]])

            stage_a(0)
            for jb in range(njb):
                if jb + 1 < njb:
                    stage_a(jb + 1, (0,))
                stage_b(jb, (0,))
                if jb + 1 < njb:
                    stage_a(jb + 1, (1,))
                stage_b(jb, (1,))
                stage_c(jb)
            for c in range(2):
                S.op("act", lambda e, c=c: e.activation(out=lsb[gb][c].ap, in_=plb[c].ap, func=ACTF.Ln),
                     reads=[plb[c]], writes=[lsb[gb][c]])
                S.op("dve", lambda e, c=c: e.tensor_copy(out=ob[gb][c].ap, in_=otb[c].ap), reads=[otb[c]], writes=[ob[gb][c]])
            for c in range(2):
                S.op("act", lambda e, c=c: e.activation(out=rl[c].ap, in_=lsb[gb][c].ap, func=ACTF.Exp, scale=-1.0),
                     reads=[lsb[gb][c]], writes=[rl[c]])
            S.op("dve", lambda e: e.scalar_tensor_tensor(out=tt.ap, in0=ob[gb][1].ap, scalar=neglam.ap, in1=rl[1].ap,
                                                         op0=ALU.mult, op1=ALU.mult),
                 reads=[ob[gb][1], rl[1], neglam], writes=[tt])
            S.op("dve", lambda e: e.tensor_tensor(out=uu.ap, in0=ob[gb][0].ap, in1=rl[0].ap, op=ALU.mult),
                 reads=[ob[gb][0], rl[0]], writes=[uu])
            S.op("dve", lambda e: e.tensor_tensor(out=oo.ap, in0=uu.ap, in1=tt.ap, op=ALU.add), reads=[uu, tt], writes=[oo])
            S.op("dve", lambda e: e.tensor_tensor(out=osq.ap, in0=oo.ap, in1=oo.ap, op=ALU.mult), reads=[oo], writes=[osq])
            S.op("pe", lambda e: e.matmul(pmb.ap, lhsT=onesb.ap, rhs=osq.ap, start=True, stop=True),
                 reads=[onesb, osq], writes=[pmb])
            S.op("act", lambda e: e.activation(out=msb.ap, in_=pmb.ap, func=ACTF.Ln, bias=C["epscol"].ap),
                 reads=[pmb, C["epscol"]], writes=[msb])
            S.op("act", lambda e: e.activation(out=rs.ap, in_=msb.ap, func=ACTF.Exp, scale=-0.5),
                 reads=[msb], writes=[rs])
            oTt = oT[gb]
            S.op("dve", lambda e: e.scalar_tensor_tensor(out=oTt.ap, in0=oo.ap, scalar=sgc.ap, in1=rs.ap,
                                                         op0=ALU.mult, op1=ALU.mult),
                 reads=[oo, sgc, rs], writes=[oTt])
            S.dma("sp", out=sc["oT"][h, :, G * 512:(G + 1) * 512], in_=oTt.ap, reads=[oTt],
                  writes=[P.db(f"oT{h}_{G}")])


def phase_attn_out(P, xd, wod, sc, prefetch=None):
    S, WA, A, C = P.S, P.WA, P.A, P.C
    TT = 256
    NTT = P.ntok // TT
    WA.reset(); A.reset(C["a0"])
    WA.off = 64 * 1024
    Wo = WA.alloc(8 * 1024, "p (k f) -> p k f", k=8)
    load_weight_fast(P, Wo, wod, 8, 1024, piece=1024)
    if prefetch is not None:
        w1d, w2d = prefetch
        WA.off = 0
        W1 = WA.alloc(8 * 4096, "p (k f) -> p k f", k=8)
        W2 = WA.alloc(32 * 1024, "p (k f) -> p k f", k=32)
        load_weight_fast(P, W1, w1d, 8, 4096, nstage=2, piece=1024)
        load_weight_fast(P, W2, w2d, 32, 1024, nstage=2, piece=1024)
    NBK = 4
    TT = 512
    NTT = P.ntok // TT
    xts = [A.alloc(NBK * 1024, "p (b d) -> p b d", b=NBK) for _ in range(2)]
    oTs = [A.alloc16(8 * TT, "p (h t) -> p h t", h=8) for _ in range(2)]
    xv = xd.rearrange("(n b p) d -> n p b d", p=128, b=NBK)

    def loads(it):
        xt = xts[it % 2]; ot = oTs[it % 2]
        S.dma("sp", out=xt.ap, in_=xv[it], reads=[P.db(f"x{2 * it}"), P.db(f"x{2 * it + 1}")], writes=[xt])
        S.dma("sp", out=ot.ap, in_=sc["oT"][:, :, it * TT:(it + 1) * TT].rearrange("h p t -> p h t"),
              reads=[P.db(f"oT{h}_{it}") for h in range(8)], writes=[ot])

    loads(0)
    for it in range(NTT):
        xt = xts[it % 2]; ot = oTs[it % 2]
        if it + 1 < NTT:
            loads(it + 1)
        residual_out(P, xt, None, Wo, 8, lambda k, b, ot=ot: ot.ap[:, k, b * 128:(b + 1) * 128],
                     lambda k, ot=ot: ot, P.ps[0:6])
        S.dma("sp", out=xv[it], in_=xt.ap, reads=[xt], writes=[P.db(f"x{2 * it}"), P.db(f"x{2 * it + 1}")])


def phase_s5(P, xd, gamd, wind, ared, aimd, bred, bimd, cred, cimd, dd, logdtd, wglud, bglud, woutd):
    S, WA, A, C = P.S, P.WA, P.A, P.C
    TT = 256
    NTT = P.ntok // TT
    WA.reset(); A.reset(C["a0"])
    ident = C["ident"]
    Win = WA.alloc(8 * 1024, "p (k f) -> p k f", k=8)
    Wglu = WA.alloc(8 * 1024, "p (k f) -> p k f", k=8)
    Wout = WA.alloc(8 * 1024, "p (k f) -> p k f", k=8)
    load_weight_fast(P, Win, wind, 8, 1024, piece=1024)
    load_weight_fast(P, Wglu, wglud, 8, 1024, piece=1024)
    load_weight_fast(P, Wout, woutd, 8, 1024, piece=1024)
    BL = WA.alloc(32 * 2 * 128, "p (q r m) -> p q r m", q=32, r=2)
    CBr = WA.alloc(32 * 3 * 128 + 2048)
    CBflat = CBr.ap
    Zr = [WA.alloc(512 + 256) for _ in range(2)]
    ZC = [WA.alloc(128, "p (g m) -> p g m", g=2) for _ in range(2)]
    rcol = A.alloc(32); ycol = A.alloc(32); carry = A.alloc(64, "p (q r) -> p q r", r=2)
    dcol = A.alloc(8); bgl = A.alloc(8)
    a_keep = A.off
    S.op("pool", lambda e: e.memset(carry.ap, 0.0), writes=[carry])
    S.dma("sp", out=dcol.ap, in_=dd.rearrange("o (c p) -> p (o c)", p=128), writes=[dcol], allow_slow_non_contiguous=True)
    S.dma("sp", out=bgl.ap, in_=bglud.rearrange("o (c p) -> p (o c)", p=128), writes=[bgl], allow_slow_non_contiguous=True)
    are = A.alloc(32); aim = A.alloc(32); ldt = A.alloc(32); dt = A.alloc(32)
    S.dma("sp", out=are.ap, in_=ared.rearrange("(q g2) p -> (g2 p) q", g2=2), writes=[are], allow_slow_non_contiguous=True)
    S.dma("sp", out=aim.ap, in_=aimd.rearrange("(q g2) p -> (g2 p) q", g2=2), writes=[aim], allow_slow_non_contiguous=True)
    ld2 = logdtd.rearrange("o (q g2) -> (o g2) q", g2=2)
    for g2 in range(2):
        S.dma("sp", out=ldt.ap[g2 * 64:(g2 + 1) * 64, :], in_=ld2[g2:g2 + 1, :].partition_broadcast(64), writes=[ldt],
              allow_slow_non_contiguous=True)
    S.op("act", lambda e: e.activation(out=dt.ap, in_=ldt.ap, func=ACTF.Exp), reads=[ldt], writes=[dt])
    S.op("dve", lambda e: e.tensor_scalar(out=are.ap, in0=are.ap, scalar1=-1e-4, scalar2=None, op0=ALU.min),
         reads=[are], writes=[are])
    rdt = A.alloc(32)
    S.op("dve", lambda e: e.tensor_tensor(out=rdt.ap, in0=are.ap, in1=dt.ap, op=ALU.mult), reads=[are, dt], writes=[rdt])
    S.op("act", lambda e: e.activation(out=rcol.ap, in_=rdt.ap, func=ACTF.Exp), reads=[rdt], writes=[rcol])
    S.op("dve", lambda e: e.scalar_tensor_tensor(out=ycol.ap, in0=aim.ap, scalar=float(1.0 / (2.0 * math.pi)), in1=dt.ap,
                                                 op0=ALU.mult, op1=ALU.mult), reads=[aim, dt], writes=[ycol])
    snt = A.alloc(32); cst = A.alloc(32); kit = A.alloci(32); frt = A.alloc(32)
    emit_sincos(P, ycol, snt, cst, kit, frt)
    nr = A.alloc(32); ni = A.alloc(32); den = A.alloc(32); t_a = A.alloc(32); t_b = A.alloc(32)
    zr = A.alloc(32); zi = A.alloc(32)
    S.op("dve", lambda e: e.tensor_tensor(out=nr.ap, in0=rcol.ap, in1=cst.ap, op=ALU.mult), reads=[rcol, cst], writes=[nr])
    S.op("dve", lambda e: e.tensor_scalar(out=nr.ap, in0=nr.ap, scalar1=-1.0, scalar2=None, op0=ALU.add), reads=[nr], writes=[nr])
    S.op("dve", lambda e: e.tensor_tensor(out=ni.ap, in0=rcol.ap, in1=snt.ap, op=ALU.mult), reads=[rcol, snt], writes=[ni])
    S.op("dve", lambda e: e.tensor_tensor(out=den.ap, in0=are.ap, in1=are.ap, op=ALU.mult), reads=[are], writes=[den])
    S.op("dve", lambda e: e.tensor_tensor(out=t_a.ap, in0=aim.ap, in1=aim.ap, op=ALU.mult), reads=[aim], writes=[t_a])
    S.op("dve", lambda e: e.tensor_tensor(out=den.ap, in0=den.ap, in1=t_a.ap, op=ALU.add), reads=[den, t_a], writes=[den])
    S.op("dve", lambda e: e.reciprocal(out=den.ap, in_=den.ap), reads=[den], writes=[den])
    S.op("dve", lambda e: e.tensor_tensor(out=t_a.ap, in0=nr.ap, in1=are.ap, op=ALU.mult), reads=[nr, are], writes=[t_a])
    S.op("dve", lambda e: e.tensor_tensor(out=t_b.ap, in0=ni.ap, in1=aim.ap, op=ALU.mult), reads=[ni, aim], writes=[t_b])
    S.op("dve", lambda e: e.tensor_tensor(out=t_a.ap, in0=t_a.ap, in1=t_b.ap, op=ALU.add), reads=[t_a, t_b], writes=[t_a])
    S.op("dve", lambda e: e.tensor_tensor(out=zr.ap, in0=t_a.ap, in1=den.ap, op=ALU.mult), reads=[t_a, den], writes=[zr])
    S.op("dve", lambda e: e.tensor_tensor(out=t_a.ap, in0=ni.ap, in1=are.ap, op=ALU.mult), reads=[ni, are], writes=[t_a])
    S.op("dve", lambda e: e.tensor_tensor(out=t_b.ap, in0=nr.ap, in1=aim.ap, op=ALU.mult), reads=[nr, aim], writes=[t_b])
    S.op("dve", lambda e: e.tensor_tensor(out=t_a.ap, in0=t_a.ap, in1=t_b.ap, op=ALU.subtract), reads=[t_a, t_b], writes=[t_a])
    S.op("dve", lambda e: e.tensor_tensor(out=zi.ap, in0=t_a.ap, in1=den.ap, op=ALU.mult), reads=[t_a, den], writes=[zi])
    Bre = A.alloc(512, "p (q h) -> p q h", h=16); Bim = A.alloc(512, "p (q h) -> p q h", h=16)
    S.dma("sp", out=Bre.ap, in_=bred.rearrange("(q g2) p h -> (g2 p) q h", g2=2), writes=[Bre])
    S.dma("sp", out=Bim.ap, in_=bimd.rearrange("(q g2) p h -> (g2 p) q h", g2=2), writes=[Bim])
    zrb = zr.ap.unsqueeze(2).broadcast_to([128, 32, 16])
    zib = zi.ap.unsqueeze(2).broadcast_to([128, 32, 16])
    M1 = A.alloc(512, "p (q h) -> p q h", h=16); M2 = A.alloc(512, "p (q h) -> p q h", h=16)
    Bb = [A.alloc(512, "p (q h) -> p q h", h=16) for _ in range(2)]
    S.op("dve", lambda e: e.tensor_tensor(out=M1.ap, in0=Bre.ap, in1=zrb, op=ALU.mult), reads=[Bre, zr], writes=[M1])
    S.op("dve", lambda e: e.tensor_tensor(out=M2.ap, in0=Bim.ap, in1=zib, op=ALU.mult), reads=[Bim, zi], writes=[M2])
    S.op("dve", lambda e: e.tensor_tensor(out=Bb[0].ap, in0=M1.ap, in1=M2.ap, op=ALU.subtract), reads=[M1, M2], writes=[Bb[0]])
    S.op("dve", lambda e: e.tensor_tensor(out=M1.ap, in0=Bim.ap, in1=zrb, op=ALU.mult), reads=[Bim, zr], writes=[M1])
    S.op("dve", lambda e: e.tensor_tensor(out=M2.ap, in0=Bre.ap, in1=zib, op=ALU.mult), reads=[Bre, zi], writes=[M2])
    S.op("dve", lambda e: e.tensor_tensor(out=Bb[1].ap, in0=M1.ap, in1=M2.ap, op=ALU.add), reads=[M1, M2], writes=[Bb[1]])
    pst = P.ps[7]
    pv = ps_bf16(pst)
    n = 0
    for k in range(8):
        for ri in range(2):
            Z = Zr[n % 2]
            n += 1
            S.op("pool", lambda e, Z=Z: e.memset(Z.ap, 0.0), writes=[Z])
            for g2 in range(2):
                dst = Z.ap[g2 * 64:(g2 + 1) * 64, g2 * 16:g2 * 16 + 640].rearrange("p (q m) -> p q m", m=160)[:, :, 0:16]
                S.op("dve", lambda e, dst=dst, g2=g2, ri=ri, k=k: e.tensor_copy(
                    out=dst, in_=Bb[ri].ap[g2 * 64:(g2 + 1) * 64, 4 * k:4 * k + 4, :]),
                    reads=[Bb[ri]], writes=[Z])
            for ql in range(4):
                S.op("pe", lambda e, Z=Z, ql=ql: e.transpose(out=pv[:, ql * 128:(ql + 1) * 128],
                                                            in_=Z.ap[:, ql * 128:(ql + 1) * 128], identity=ident.ap),
                     reads=[Z, ident], writes=[pst])
            S.op("act", lambda e, k=k, ri=ri: e.activation(out=BL.ap[:, 4 * k:4 * k + 4, ri, :],
                                                           in_=pv[:, 0:512].rearrange("p (q m) -> p q m", q=4),
                                                           func=ACTF.Copy),
                 reads=[pst], writes=[BL])
    Cn = [A.alloc(512, "p (k m) -> p k m", k=8) for _ in range(2)]
    S.dma("sp", out=Cn[0].ap, in_=cred.rearrange("(k gl) h p -> (gl h) k p", gl=8), writes=[Cn[0]])
    S.dma("sp", out=Cn[1].ap, in_=cimd.rearrange("(k gl) h p -> (gl h) k p", gl=8), writes=[Cn[1]])
    S.op("pool", lambda e: e.memset(CBflat, 0.0), writes=[CBr])
    pst2 = P.ps[6]
    pv2 = ps_bf16(pst2)
    n = 0
    for k in range(8):
        for ri in range(2):
            zc = ZC[n % 2]
            n += 1
            for g2 in range(2):
                S.op("dve", lambda e, zc=zc, g2=g2, ri=ri, k=k: e.tensor_scalar(
                    out=zc.ap[:, g2, :], in0=Cn[ri].ap[:, k, :], scalar1=C["par"].ap[:, g2:g2 + 1], scalar2=None,
                    op0=ALU.mult), reads=[Cn[ri], C["par"]], writes=[zc])
            S.op("pe", lambda e, zc=zc: e.transpose(out=pv2[:, 0:128], in_=zc.ap.rearrange("p g m -> p (g m)"),
                                                   identity=ident.ap), reads=[zc, ident], writes=[pst2])
            for var in ((0, 2) if ri == 0 else (1,)):
                base = 4 * k * 384 + var * 128
                dst = CBflat[:, base:base + 4 * 416].rearrange("p (q m) -> p q m", m=416)[:, :, 0:32]
                sgn = 1.0 if var == 0 else -1.0
                S.op("act", lambda e, dst=dst, sgn=sgn: e.activation(
                    out=dst, in_=pv2[:, 0:128].rearrange("p (q m) -> p q m", q=4), func=ACTF.Copy, scale=sgn),
                    reads=[pst2], writes=[CBr])
    CB = CBflat[:, 0:32 * 384].rearrange("p (q v m) -> p q v m", q=32, v=3)
    CS = WA.alloc(32 * TT, "p (q k) -> p q k", q=32)
    SN = WA.alloc(32 * TT, "p (q k) -> p q k", q=32)
    A.reset(a_keep)
    cT = A.alloc(32); sT = A.alloc(32)
    yT = A.alloc(32); kiT = A.alloci(32); frT = A.alloc(32)
    S.op("dve", lambda e: e.tensor_scalar(out=yT.ap, in0=ycol.ap, scalar1=float(TT), scalar2=None, op0=ALU.mult),
         reads=[ycol], writes=[yT])
    emit_sincos(P, yT, sT, cT, kiT, frT)
    a_keep2 = A.off
    yyt = [A.alloc(TT) for _ in range(2)]; kit2 = [A.alloci(TT) for _ in range(2)]; frt2 = [A.alloc(TT) for _ in range(2)]
    for q in range(32):
        yy_, ki_, fr_ = yyt[q % 2], kit2[q % 2], frt2[q % 2]
        S.op("dve", lambda e: e.tensor_scalar(out=yy_.ap, in0=C["tg0"].ap[:, 0:TT], scalar1=ycol.ap[:, q:q + 1], scalar2=None,
                                              op0=ALU.mult), reads=[C["tg0"], ycol], writes=[yy_])
        snq = Region(SN.ap[:, q, :], WA.sub(SN, q * TT, q * TT + TT))
        csq = Region(CS.ap[:, q, :], WA.sub(CS, q * TT, q * TT + TT))
        emit_sincos(P, yy_, snq, csq, ki_, fr_)
    A.reset(a_keep2)
    gam = load_gamma(P, gamd)
    xts = [A.alloc(2 * 1024, "p (b d) -> p b d", b=2) for _ in range(2)]
    sqj = A.alloc16(1024); ssn = A.alloc(1); rstdn = A.alloc(1)
    xns = [A.own(A.alloc16(1024)), WA.own(WA.alloc(1024))]
    hT = A.alloc16(8 * TT, "p (k t) -> p k t", k=8)
    uT = A.alloc16(8 * TT, "p (k t) -> p k t", k=8)
    gT = A.alloc16(8 * TT, "p (k t) -> p k t", k=8)
    ggT = uT
    ydt = WA.alloc32(TT); sig = WA.alloc32(TT)

    def load_x(it):
        S.dma("sp", out=xts[it % 2].ap, in_=xv[it], reads=[P.db(f"x{it}")], writes=[xts[it % 2]])

    def norm_pre(it, b):
        xt_ = xts[it % 2]; xn = xns[b]
        S.op("act", lambda e: e.activation(out=sqj.ap, in_=xt_.ap[:, b, :], func=ACTF.Square, scale=1.0 / 32.0,
                                           accum_out=ssn.ap), reads=[xt_], writes=[sqj, ssn])
        S.op("pool", lambda e: e.tensor_scalar(out=ssn.ap, in0=ssn.ap, scalar1=EPS, scalar2=None, op0=ALU.add),
             reads=[ssn], writes=[ssn])
        S.op("pool", lambda e: e.tensor_tensor(out=rstdn.ap, in0=ssn.ap, in1=C["mhalf"].ap, op=ALU.pow),
             reads=[ssn, C["mhalf"]], writes=[rstdn])
        S.op("dve", lambda e: e.scalar_tensor_tensor(out=xn.ap, in0=xt_.ap[:, b, :], scalar=rstdn.ap, in1=gam.ap,
                                                     op0=ALU.mult, op1=ALU.mult),
             reads=[xt_, rstdn, gam], writes=[xn])

    def norm_post(it, b):
        xn = xns[b]
        pst = P.ps[6 + b]
        pvv = ps_bf16(pst)
        for kc in range(8):
            S.op("pe", lambda e: e.transpose(out=pvv[:, kc * 128:(kc + 1) * 128], in_=xn.ap[:, kc * 128:(kc + 1) * 128],
                                             identity=ident.ap), reads=[xn, ident], writes=[pst])
        S.op("act", lambda e: e.activation(out=hT.ap[:, :, b * 128:(b + 1) * 128],
                                           in_=pvv.rearrange("p (k t) -> p k t", k=8), func=ACTF.Copy),
             reads=[pst], writes=[hT])
    NSET = 3

    def mkset(i):
        al = (lambda n: A.own(A.alloc(n))) if i == 0 else (lambda n: WA.own(WA.alloc32(n)))
        al16 = (lambda n: A.own(A.alloc16(n))) if i == 0 else (lambda n: WA.own(WA.alloc(n)))
        return dict(m=[al(TT) for _ in range(4)], w=[al(TT) for _ in range(2)], z=[al(TT) for _ in range(2)],
                    Y=[al16(TT) for _ in range(4)], ct=al(8))
    tsets = [mkset(0 if i < 2 else 1) for i in range(NSET)]
    cbufs = [Buf(f"carry{q}") for q in range(32)]
    xv = xd.rearrange("(n b p) d -> n p b d", p=128, b=2)
    gp = 0
    load_x(0)
    for b in range(2):
        norm_pre(0, b)
        norm_post(0, b)
    for it in range(NTT):
        xb = P.db(f"x{it}")
        xt = xts[it % 2]
        if it + 1 < NTT:
            load_x(it + 1)
        for cc in range(8):
            pu = P.ps[cc % 2]
            for kc in range(8):
                S.op("pe", lambda e: e.matmul(
                    pu.ap[:, 0:TT], lhsT=Win.ap[:, kc, cc * 128:(cc + 1) * 128], rhs=hT.ap[:, kc, :],
                    start=(kc == 0), stop=(kc == 7)),
                    reads=[hT, WA.sub(Win, kc * 1024 + cc * 128, kc * 1024 + cc * 128 + 128)], writes=[pu])
            S.op("act", lambda e: e.activation(out=uT.ap[:, cc, :], in_=pu.ap[:, 0:TT], func=ACTF.Copy),
                 reads=[pu], writes=[A.sub(uT, cc * TT, cc * TT + TT)])

        def stB(q, part):
            cc = q // 4
            ts = tsets[(gp + q) % NSET]
            pb = [P.ps[2 + ((gp + q) % 2) * 2 + ri] for ri in range(2)]
            us = A.sub(uT, cc * TT, cc * TT + TT)
            m, w = ts["m"], ts["w"]
            csq = WA.sub(CS, q * TT, q * TT + TT); snq = WA.sub(SN, q * TT, q * TT + TT)
            if part == 0:
                for ri in range(2):
                    S.op("pe", lambda e: e.matmul(pb[ri].ap[:, 0:TT], lhsT=BL.ap[:, q, ri, :], rhs=uT.ap[:, cc, :],
                                                  start=True, stop=True), reads=[BL, us], writes=[pb[ri]])
                return
            for idx, (ri, tab, tb) in enumerate(((0, CS, csq), (1, SN, snq), (1, CS, csq), (0, SN, snq))):
                S.op("dve", lambda e: e.tensor_tensor(out=m[idx].ap, in0=pb[ri].ap[:, 0:TT], in1=tab.ap[:, q, :], op=ALU.mult),
                     reads=[pb[ri], tb], writes=[m[idx]])
            S.op("pool", lambda e: e.tensor_tensor(out=w[0].ap, in0=m[0].ap, in1=m[1].ap, op=ALU.add),
                 reads=[m[0], m[1]], writes=[w[0]])
            S.op("pool", lambda e: e.tensor_tensor(out=w[1].ap, in0=m[2].ap, in1=m[3].ap, op=ALU.subtract),
                 reads=[m[2], m[3]], writes=[w[1]])

        def stC(q):
            ts = tsets[(gp + q) % NSET]
            w, z, ct = ts["w"], ts["z"], ts["ct"]
            rbc = rcol.ap[:, q:q + 1].broadcast_to([128, TT])
            for ri in range(2):
                S.op("dve", lambda e: e.tensor_tensor_scan(
                    out=z[ri].ap, data0=rbc, data1=w[ri].ap, initial=carry.ap[:, q, ri:ri + 1],
                    op0=ALU.mult, op1=ALU.add), reads=[rcol, w[ri], cbufs[q]], writes=[z[ri]])
            zl = [z[0].ap[:, TT - 1:TT], z[1].ap[:, TT - 1:TT]]
            for idx, (ri, tab) in enumerate(((0, cT), (1, sT), (1, cT), (0, sT))):
                S.op("act", lambda e: e.activation(out=ct.ap[:, idx:idx + 1], in_=zl[ri], func=ACTF.Copy,
                                                   scale=tab.ap[:, q:q + 1]),
                     reads=[z[ri], tab], writes=[ct])
            S.op("pool", lambda e: e.tensor_tensor(out=carry.ap[:, q, 0:1], in0=ct.ap[:, 0:1], in1=ct.ap[:, 1:2], op=ALU.subtract),
                 reads=[ct], writes=[cbufs[q]])
            S.op("pool", lambda e: e.tensor_tensor(out=carry.ap[:, q, 1:2], in0=ct.ap[:, 2:3], in1=ct.ap[:, 3:4], op=ALU.add),
                 reads=[ct], writes=[cbufs[q]])

        def stD(q):
            cc, ql = q // 4, q % 4
            ts = tsets[(gp + q) % NSET]
            z, Y = ts["z"], ts["Y"]
            py = P.ps[cc % 2]
            csq = WA.sub(CS, q * TT, q * TT + TT); snq = WA.sub(SN, q * TT, q * TT + TT)
            S.op("pool", lambda e: e.tensor_tensor(out=Y[0].ap, in0=z[0].ap, in1=CS.ap[:, q, :], op=ALU.mult),
                 reads=[z[0], csq], writes=[Y[0]])
            S.op("pool", lambda e: e.tensor_tensor(out=Y[1].ap, in0=z[1].ap, in1=CS.ap[:, q, :], op=ALU.mult),
                 reads=[z[1], csq], writes=[Y[1]])
            S.op("dve", lambda e: e.tensor_tensor(out=Y[2].ap, in0=z[0].ap, in1=SN.ap[:, q, :], op=ALU.mult),
                 reads=[z[0], snq], writes=[Y[2]])
            S.op("dve", lambda e: e.tensor_tensor(out=Y[3].ap, in0=z[1].ap, in1=SN.ap[:, q, :], op=ALU.mult),
                 reads=[z[1], snq], writes=[Y[3]])
            for n_, (var, yi) in enumerate(((0, 0), (1, 1), (1, 2), (2, 3))):
                S.op("pe", lambda e: e.matmul(py.ap[:, 0:TT], lhsT=CB[:, q, var, :], rhs=Y[yi].ap,
                                              start=(ql == 0 and n_ == 0), stop=(ql == 3 and n_ == 3)),
                     reads=[CBr, Y[yi]], writes=[py])
            if ql == 3:
                us = A.sub(uT, cc * TT, cc * TT + TT)
                S.op("dve", lambda e: e.scalar_tensor_tensor(
                    out=ydt.ap, in0=uT.ap[:, cc, :], scalar=dcol.ap[:, cc:cc + 1], in1=py.ap[:, 0:TT],
                    op0=ALU.mult, op1=ALU.add), reads=[us, dcol, py], writes=[ydt])
                S.op("act", lambda e: e.activation(out=gT.ap[:, cc, :], in_=ydt.ap, func=ACTF.Gelu_apprx_tanh),
                     reads=[ydt], writes=[A.sub(gT, cc * TT, cc * TT + TT)])

        for s_ in range(32 + 2):
            if s_ < 32:
                stB(s_, 0)
            if 0 <= s_ - 2 < 32:
                stD(s_ - 2)
            if 0 <= s_ - 1 < 32:
                stC(s_ - 1)
            if s_ < 32:
                stB(s_, 1)
            if it + 1 < NTT:
                if s_ in (8, 14):
                    norm_pre(it + 1, (s_ - 8) // 6)
                if s_ in (20, 26):
                    norm_post(it + 1, (s_ - 20) // 6)
        gp += 32
        for c2 in range(8):
            pz = P.ps[6 + c2 % 2]
            for cc in range(8):
                S.op("pe", lambda e: e.matmul(
                    pz.ap[:, 0:TT], lhsT=Wglu.ap[:, cc, c2 * 128:(c2 + 1) * 128], rhs=gT.ap[:, cc, :],
                    start=(cc == 0), stop=(cc == 7)),
                    reads=[A.sub(gT, cc * TT, cc * TT + TT), WA.sub(Wglu, cc * 1024 + c2 * 128, cc * 1024 + c2 * 128 + 128)],
                    writes=[pz])
            S.op("act", lambda e: e.activation(out=sig.ap, in_=pz.ap[:, 0:TT], func=ACTF.Sigmoid, bias=bgl.ap[:, c2:c2 + 1]),
                 reads=[pz, bgl], writes=[sig])
            S.op("pool", lambda e: e.tensor_tensor(out=ggT.ap[:, c2, :], in0=gT.ap[:, c2, :], in1=sig.ap, op=ALU.mult),
                 reads=[A.sub(gT, c2 * TT, c2 * TT + TT), sig], writes=[A.sub(ggT, c2 * TT, c2 * TT + TT)])
        residual_out(P, xt, None, Wout, 8, lambda k, b: ggT.ap[:, k, b * 128:(b + 1) * 128],
                     lambda k: A.sub(ggT, k * TT, k * TT + TT), P.ps[2:6])
        S.dma("sp", out=xv[it], in_=xt.ap, reads=[xt], writes=[xb])


def host_consts():
    bf = ml_dtypes.bfloat16
    c = {}
    c["c_ident"] = np.eye(128, dtype=np.float32).astype(bf)
    p = np.arange(128)
    c["c_bones"] = ((p[:, None] // 64) == (p[None, :] // 64)).astype(np.float32).astype(bf) * np.float32(1.0 / 64.0)
    c["c_bones"] = c["c_bones"].astype(bf)
    rr = np.zeros((128, 128), np.float32)
    for cc in range(2):
        for d in range(64):
            mcol = cc * 64 + d
            if d < 32:
                rr[cc * 64 + d + 32, mcol] = -1.0
            else:
                rr[cc * 64 + d - 32, mcol] = 1.0
    c["c_rrot"] = rr.astype(bf)
    invf = (10000.0 ** (-np.arange(0, 64, 2, dtype=np.float32) / 64.0)).astype(np.float32)
    c["c_invf"] = (invf[(p % 64) % 32] / np.float32(2.0 * math.pi)).astype(np.float32).reshape(128, 1)
    c["c_triu"] = (p[:, None] <= p[None, :]).astype(np.float32).astype(bf)
    c["c_tril"] = (p[None, :] <= p[:, None]).astype(np.float32).astype(bf)
    c["c_tg0"] = np.broadcast_to(np.arange(256, dtype=np.float32)[None, :], (128, 256)).copy()
    par = np.zeros((128, 2), np.float32)
    par[:, 0] = ((p // 16) % 2 == 0)
    par[:, 1] = ((p // 16) % 2 == 1)
    c["c_par"] = par
    return c


def setup_consts(P):
    S, A = P.S, P.A
    C = {}
    A.reset()

    def ld(name, n, dtype, shape):
        d = P.din("c_" + name, shape, dtype)
        r = A.alloc16(n) if dtype == BF16 else A.alloc(n)
        S.dma("sp", out=r.ap, in_=d, writes=[r])
        C[name] = r

    ld("ident", 128, BF16, [128, 128])
    ld("bones", 128, BF16, [128, 128])
    ld("rrot", 128, BF16, [128, 128])
    ld("triu", 128, BF16, [128, 128])
    ld("tril", 128, BF16, [128, 128])
    ld("invf", 1, F32, [128, 1])
    ld("tg0", 256, F32, [128, 256])
    ld("par", 2, F32, [128, 2])
    C["mhalf"] = A.alloc(1)
    S.op("pool", lambda e: e.memset(C["mhalf"].ap, -0.5), writes=[C["mhalf"]])
    C["epscol"] = A.alloc(1)
    S.op("pool", lambda e: e.memset(C["epscol"].ap, EPS), writes=[C["epscol"]])
    A.off = ((A.off + 255) // 256) * 256
    C["a0"] = A.off
    P.C = C
    return C


def lambda_init_of(li):
    return 0.8 - 0.6 * math.exp(-0.3 * li)


def build(spec, ntok=SEQ, debug=False):
    P = Prog(ntok)
    P.DEBUG = debug
    S = P.S
    xin = P.din("x", [ntok, D])
    xout = P.dout("y", [ntok, D])
    setup_consts(P)
    nt = ntok // 256
    xiv = xin.rearrange("(n t) d -> n t d", t=256)
    xov = xout.rearrange("(n t) d -> n t d", t=256)
    for it in range(nt):
        S.dma("sp", out=xov[it], in_=xiv[it], writes=[P.db(f"x{it}")])
    sc = None
    posd = None
    preloaded = set()
    for si, (kind, li) in enumerate(spec):
        if kind == "mlp":
            phase_mlp(P, xout, P.din(f"mlp_w1_{li}", [D, 4 * D]) if li not in preloaded else None,
                      P.din(f"mlp_w2_{li}", [4 * D, D]) if li not in preloaded else None,
                      P.din(f"norm_mlp_{li}", [1, D]), preloaded=(li in preloaded))
            continue
        gamd = P.din(f"norm_mix_{li}", [1, D])
        mk = li % 3
        j = li // 3
        if mk == 0:
            if sc is None:
                sc = dict(qT=P.dscratch("sc_qT", [8, 128, ntok], BF16), kT=P.dscratch("sc_kT", [8, 128, ntok], BF16),
                          v=P.dscratch("sc_v", [ntok // 128, 128, 8 * 129], BF16),
                          oT=P.dscratch("sc_oT", [8, 128, ntok], BF16))
                posd = P.din("positions", [1, ntok], I32)
            phase_attn_qkv(P, xout, gamd, P.din(f"attn_w_qkv_{j}", [D, 3 * D]), P.din(f"attn_q_norm_{j}", [64, 1]),
                           P.din(f"attn_k_norm_{j}", [64, 1]), posd, sc)
            phase_attn_core(P, P.din(f"attn_lambda_{j}", [4, 64]), P.din(f"attn_sub_norm_{j}", [1, 128]),
                            lambda_init_of(li), sc)
            pf = None
            if si + 1 < len(spec) and spec[si + 1] == ("mlp", li):
                pf = (P.din(f"mlp_w1_{li}", [D, 4 * D]), P.din(f"mlp_w2_{li}", [4 * D, D]))
                preloaded.add(li)
            phase_attn_out(P, xout, P.din(f"attn_w_o_{j}", [D, D]), sc, prefetch=pf)
        elif mk == 1:
            phase_s5(P, xout, gamd, P.din(f"ssm_w_in_{j}", [D, D]), P.din(f"ssm_a_re_{j}", [64, 64]),
                     P.din(f"ssm_a_im_{j}", [64, 64]), P.din(f"ssm_b_re_{j}", [64, 64, 16]),
                     P.din(f"ssm_b_im_{j}", [64, 64, 16]), P.din(f"ssm_c_re_{j}", [64, 16, 64]),
                     P.din(f"ssm_c_im_{j}", [64, 16, 64]), P.din(f"ssm_d_{j}", [1, D]),
                     P.din(f"ssm_log_dt_{j}", [1, 64]), P.din(f"ssm_w_glu_{j}", [D, D]),
                     P.din(f"ssm_b_glu_{j}", [1, D]), P.din(f"ssm_w_out_{j}", [D, D]))
        else:
            phase_gmlp(P, xout, gamd, P.din(f"gm_w_in_{j}", [D, 6 * D]), P.din(f"gm_v_norm_{j}", [1, 3 * D]),
                       P.din(f"gm_w_s_{j}", [8, 128, 128]), P.din(f"gm_b_s_{j}", [8, 128]),
                       P.din(f"gm_w_out_{j}", [3 * D, D]))
    S.wait_bufs("sp", [P.db(f"x{it}") for it in range(nt)] + getattr(P, "dbg", []))
    return P


FULL_SPEC = [("mix", 0), ("mlp", 0), ("mix", 1), ("mlp", 1), ("mix", 2), ("mlp", 2), ("mix", 3), ("mlp", 3)]

_SHAPES = {
    "norm_mix": (1, D), "norm_mlp": (1, D), "attn_q_norm": (64, 1), "attn_k_norm": (64, 1), "attn_sub_norm": (1, 128),
    "ssm_d": (1, D), "ssm_log_dt": (1, 64), "ssm_b_glu": (1, D), "gm_v_norm": (1, 3 * D),
}


def make_in_map(P, inputs, b, ntok=SEQ):
    m = dict(host_consts())
    out = {}
    for name in P.dram_in:
        if name in m:
            out[name] = m[name]
        elif name == "x":
            out[name] = np.ascontiguousarray(inputs["x"][b, :ntok])
        elif name == "positions":
            out[name] = np.ascontiguousarray(inputs["positions"][b, :ntok].reshape(1, ntok)).astype(np.int32)
        else:
            base, idx = name.rsplit("_", 1)
            arr = np.asarray(inputs[base])[int(idx)]
            shp = _SHAPES.get(base)
            if shp is not None:
                arr = arr.reshape(shp)
            out[name] = np.ascontiguousarray(arr, dtype=np.float32)
    return out


_PROG = {}


def kernel(**inputs):
    if "full" not in _PROG:
        _PROG["full"] = build(FULL_SPEC, SEQ)
    P = _PROG["full"]
    inputs = {k: np.asarray(v) for k, v in inputs.items()}
    maps = [make_in_map(P, inputs, c % BATCH) for c in range(NCORES)]
    res = run_bass_kernel_spmd(P.nc, maps, core_ids=list(range(NCORES)))
    return np.stack([np.asarray(res.results[b]["y"], dtype=np.float32) for b in range(BATCH)], axis=0)
```

```python
import math
from contextlib import ExitStack

import numpy as np
import ml_dtypes

import concourse.bass as bass
import concourse.mybir as mybir
from concourse.bass_utils import run_bass_kernel_spmd

F32 = mybir.dt.float32
BF16 = mybir.dt.bfloat16
I32 = mybir.dt.int32
ALU = mybir.AluOpType
ACTF = mybir.ActivationFunctionType
AX = mybir.AxisListType

D = 1024
SEQ = 4096
BATCH = 4
DEPTH = 4
EPS = 1e-6
NCORES = 8


class Buf:
    __slots__ = ("name", "w", "r")

    def __init__(self, name):
        self.name = name
        self.w = None
        self.r = {}


class Region:
    def __init__(self, ap, bufs):
        self.ap = ap
        self.bufs = bufs


def _flat(items):
    out = []
    for it in items:
        if it is None:
            continue
        if isinstance(it, Buf):
            out.append(it)
        elif isinstance(it, Region):
            out.extend(it.bufs)
        else:
            out.extend(_flat(it))
    return out


class Sched:
    GEN = 24000
    POOL_INFLIGHT = 8

    def __init__(self, nc, stack, ndma=32):
        self.nc = nc
        self.stack = stack
        self.eng = dict(pe=nc.tensor, act=nc.scalar, dve=nc.vector, pool=nc.gpsimd, sp=nc.sync)
        self.stream = {k: [] for k in self.eng}
        self.cnt = {k: 0 for k in self.eng}
        self.gen = {k: 0 for k in self.eng}
        self.semh = {}
        for k in ("pe", "act", "dve", "pool"):
            self.semh[(k, 0)] = stack.enter_context(nc.semaphore(f"s_{k}0"))
        self.ndma = ndma
        for j in range(ndma):
            self.semh[("d", j)] = stack.enter_context(nc.semaphore(f"s_d{j}"))
        self.dcnt = [0] * ndma
        self.dnext = 0
        self.known = {k: {} for k in self.eng}
        self.ninstr = 0
        self.pool_hist = []

    def _collect(self, reads, writes):
        need = {}
        for b in reads:
            ev = b.w
            if ev is not None and need.get(ev[0], 0) < ev[1]:
                need[ev[0]] = ev[1]
        for b in writes:
            ev = b.w
            if ev is not None and need.get(ev[0], 0) < ev[1]:
                need[ev[0]] = ev[1]
            for k, v in b.r.items():
                if need.get(k, 0) < v:
                    need[k] = v
        return need

    def _waits(self, e, need, skip_self=False):
        kn = self.known[e]
        st = self.stream[e]
        for k, v in need.items():
            if skip_self and k[0] == e:
                continue
            if kn.get(k, 0) >= v:
                continue
            kn[k] = v
            sem = self.semh[k]
            self.eng[e].wait_ge(sem, v)
            self.ninstr += 1

    def op(self, e, fn, reads=(), writes=()):
        reads = _flat(reads)
        writes = _flat(writes)
        need = self._collect(reads, writes)
        self._waits(e, need, skip_self=(e == "pe"))
        if self.cnt[e] >= self.GEN:
            self.gen[e] += 1
            self.cnt[e] = 0
            self.semh[(e, self.gen[e])] = self.stack.enter_context(
                self.nc.semaphore(f"s_{e}{self.gen[e]}"))
        self.cnt[e] += 1
        key = (e, self.gen[e])
        v = self.cnt[e]
        sem = self.semh[key]
        fn(self.eng[e]).then_inc(sem, 1)
        self.ninstr += 1
        for b in reads:
            b.r[key] = v
        for b in writes:
            b.w = (key, v)
            b.r = {}

    def dma(self, q, out, in_, reads=(), writes=(), **kw):
        reads = _flat(reads)
        writes = _flat(writes)
        need = self._collect(reads, writes)
        if q == "pool":
            hist = self.pool_hist
            if len(hist) >= self.POOL_INFLIGHT:
                k0, v0 = hist[-self.POOL_INFLIGHT]
                if need.get(k0, 0) < v0:
                    need[k0] = v0
        j = self.dnext
        self.dnext = (j + 1) % self.ndma
        if self.dcnt[j] > 0 and need.get(("d", j), 0) < self.dcnt[j]:
            need[("d", j)] = self.dcnt[j]
        self._waits(q, need)
        self.dcnt[j] += 16
        key = ("d", j)
        v = self.dcnt[j]
        sem = self.semh[key]
        self.eng[q].dma_start(out=out, in_=in_, **kw).then_inc(sem, 16)
        self.ninstr += 1
        if q == "pool":
            self.pool_hist.append((key, v))
        for b in reads:
            b.r[key] = v
        for b in writes:
            b.w = (key, v)
            b.r = {}

    def wait_bufs(self, e, bufs):
        bufs = _flat(bufs)
        need = self._collect((), bufs)
        self._waits(e, need)

    def finish(self):
        return
        nc = self.nc
        with nc.Block() as block:
            for name, deco in (("pe", block.tensor), ("act", block.scalar), ("dve", block.vector),
                               ("pool", block.gpsimd), ("sp", block.sync)):
                lst = self.stream[name]

                @deco
                def _(eng, lst=lst):
                    for th in lst:
                        th(eng)


class Arena:
    def __init__(self, nc, stack, name, nelem, dtype, chunk):
        self.t = stack.enter_context(nc.sbuf_tensor(name, [128, nelem], dtype))
        self.n = nelem
        self.chunk = chunk
        self.bufs = [Buf(f"{name}{i}") for i in range((nelem + chunk - 1) // chunk)]
        self.off = 0
        self.name = name

    def reset(self, off=0):
        for b, cbs in getattr(self, "owned", []):
            for cb in cbs:
                if b.w is not None:
                    cb.r[b.w[0]] = max(cb.r.get(b.w[0], 0), b.w[1])
                for k, v in b.r.items():
                    cb.r[k] = max(cb.r.get(k, 0), v)
        self.owned = []
        self.off = off

    def own(self, reg):
        b = Buf("own")
        for cb in reg.bufs:
            if cb.w is not None:
                b.r[cb.w[0]] = max(b.r.get(cb.w[0], 0), cb.w[1])
            for k, v in cb.r.items():
                b.r[k] = max(b.r.get(k, 0), v)
        if not hasattr(self, "owned"):
            self.owned = []
        self.owned.append((b, reg.bufs))
        reg.bufs = [b]
        return reg

    def alloc(self, n, pattern=None, **kw):
        off = self.off
        assert off + n <= self.n, f"arena {self.name} overflow: {off}+{n} > {self.n}"
        self.off = off + n
        return self.view(off, n, pattern, **kw)

    def view(self, off, n, pattern=None, **kw):
        ap = self.t[:, off:off + n]
        if pattern is not None:
            ap = ap.rearrange(pattern, **kw)
        c0 = off // self.chunk
        c1 = (off + n - 1) // self.chunk
        r = Region(ap, self.bufs[c0:c1 + 1])
        r.off = off
        r.n = n
        r.arena = self
        return r

    def sub(self, reg, lo, hi):
        m = getattr(reg, "mul", 1)
        c0 = (reg.off + lo // m) // self.chunk
        c1 = (reg.off + (hi - 1) // m) // self.chunk
        return self.bufs[c0:c1 + 1]

    def alloc16(self, n, pattern=None, **kw):
        n32 = (n + 1) // 2
        off = self.off
        assert off + n32 <= self.n, f"arena {self.name} overflow: {off}+{n32} > {self.n}"
        self.off = off + n32
        ap = self.t[:, off:off + n32].bitcast(BF16)[:, 0:n]
        if pattern is not None:
            ap = ap.rearrange(pattern, **kw)
        c0 = off // self.chunk
        c1 = (off + n32 - 1) // self.chunk
        r = Region(ap, self.bufs[c0:c1 + 1])
        r.off = off; r.n = n32; r.arena = self; r.mul = 2
        return r

    def alloc32(self, n, pattern=None, **kw):
        r = self.alloc(2 * n)
        ap = r.ap.bitcast(F32)
        if pattern is not None:
            ap = ap.rearrange(pattern, **kw)
        r.ap = ap
        return r

    def alloci(self, n, pattern=None, **kw):
        r = self.alloc(n)
        ap = r.ap.bitcast(I32)
        if pattern is not None:
            ap = ap.rearrange(pattern, **kw)
        r.ap = ap
        return r


class Prog:
    def __init__(self, ntok=SEQ):
        self.ntok = ntok
        self.nc = bass.Bass("TRN2", target_bir_lowering=False)
        self.stack = ExitStack()
        nc = self.nc
        self.S = Sched(nc, self.stack)
        st = self.stack
        self.WA = Arena(nc, st, "wa", 72 * 1024, BF16, 2048)
        self.A = Arena(nc, st, "aa", 15 * 1024 + 512, F32, 256)
        self.psum_t = st.enter_context(nc.psum_tensor("ps", [128, 8, 512], F32))
        self.ps = [Region(self.psum_t[:, b, :], [Buf(f"ps{b}")]) for b in range(8)]
        self.dram_in = {}
        self.consts = {}
        self.dbuf = {}

    def din(self, name, shape, dtype=F32):
        t = self.nc.dram_tensor(name, list(shape), dtype, kind="ExternalInput").ap()
        self.dram_in[name] = t
        return t

    def dout(self, name, shape, dtype=F32):
        return self.nc.dram_tensor(name, list(shape), dtype, kind="ExternalOutput").ap()

    def dscratch(self, name, shape, dtype):
        return self.nc.dram_tensor(name, list(shape), dtype, kind="Internal").ap()

    def debug(self, name, reg, shape, dtype):
        d = self.nc.dram_tensor("dbg_" + name, list(shape), dtype, kind="ExternalOutput").ap()
        self.S.dma("sp", out=d, in_=reg.ap, reads=[reg], writes=[self.db("dbg_" + name)])
        self.dbg = getattr(self, "dbg", [])
        self.dbg.append(self.db("dbg_" + name))

    def db(self, name):
        b = self.dbuf.get(name)
        if b is None:
            b = self.dbuf[name] = Buf(name)
        return b


def ps_bf16(reg):
    return reg.ap.bitcast(BF16)


def emit_norm_T(P, xt_ap, xt_bufs, gam, hT, tok_off, ident, scr):
    S = P.S
    sq, ss, rstd, xn, pst = scr["sq"], scr["ss"], scr["rstd"], scr["xn"], scr["pst"]
    S.op("act", lambda e: e.activation(out=sq.ap, in_=xt_ap, func=ACTF.Square, scale=1.0 / 32.0,
                                       accum_out=ss.ap),
         reads=[xt_bufs], writes=[sq, ss])
    S.op("pool", lambda e: e.tensor_scalar(out=ss.ap, in0=ss.ap, scalar1=EPS, scalar2=None, op0=ALU.add),
         reads=[ss], writes=[ss])
    S.op("pool", lambda e: e.tensor_tensor(out=rstd.ap, in0=ss.ap, in1=P.C["mhalf"].ap, op=ALU.pow),
         reads=[ss, P.C["mhalf"]], writes=[rstd])
    S.op("dve", lambda e: e.scalar_tensor_tensor(out=xn.ap, in0=xt_ap, scalar=rstd.ap, in1=gam.ap,
                                                 op0=ALU.mult, op1=ALU.mult),
         reads=[xt_bufs, rstd, gam], writes=[xn])
    pv = ps_bf16(pst)
    for kc in range(8):
        S.op("pe", lambda e, kc=kc: e.transpose(out=pv[:, kc * 128:(kc + 1) * 128],
                                                in_=xn.ap[:, kc * 128:(kc + 1) * 128],
                                                identity=ident.ap),
             reads=[xn, ident], writes=[pst])
    S.op("act", lambda e: e.activation(out=hT.ap[:, :, tok_off:tok_off + 128],
                                       in_=pv.rearrange("p (k t) -> p k t", k=8), func=ACTF.Copy),
         reads=[pst], writes=[hT])


def load_weight_fast(P, reg, dram_ap, nk, ncols, nstage=4, piece=2048):
    S, A = P.S, P.A
    piece = min(piece, ncols)
    off_keep = A.off
    A.off = A.n - nstage * piece
    stg = [A.alloc(piece) for _ in range(nstage)]
    A.off = off_keep
    i = 0
    for kc in range(nk):
        for c0 in range(0, ncols, piece):
            c1 = min(ncols, c0 + piece)
            st = stg[i % nstage]
            S.dma("sp", out=st.ap[:, 0:c1 - c0], in_=dram_ap[kc * 128:(kc + 1) * 128, c0:c1], writes=[st])
            lo = kc * ncols + c0
            dst = reg.arena.sub(reg, lo, lo + (c1 - c0))
            if i % 2 == 0:
                S.op("act", lambda e: e.activation(out=reg.ap[:, kc, c0:c1], in_=st.ap[:, 0:c1 - c0], func=ACTF.Copy),
                     reads=[st], writes=[dst])
            else:
                S.op("dve", lambda e: e.tensor_copy(out=reg.ap[:, kc, c0:c1], in_=st.ap[:, 0:c1 - c0]),
                     reads=[st], writes=[dst])
            i += 1


def load_weight(P, reg, dram_ap, rows_per_part_dim, ncols, q="pool"):
    S = P.S
    nk = rows_per_part_dim
    for kc in range(nk):
        for c0 in range(0, ncols, 2048):
            c1 = min(ncols, c0 + 2048)
            lo = kc * ncols + c0
            S.dma(q, out=reg.ap[:, kc, c0:c1], in_=dram_ap[kc * 128:(kc + 1) * 128, c0:c1],
                  writes=[reg.arena.sub(reg, lo, lo + (c1 - c0))])


TWO_PI = 6.283179


def norm_scratch(P):
    A = P.A
    return dict(sq=A.alloc16(1024), ss=A.alloc(1), rstd=A.alloc(1), xn=A.alloc16(1024))


def load_gamma(P, gamd):
    gam = P.A.alloc(1024)
    P.S.dma("sp", out=gam.ap, in_=gamd.partition_broadcast(128), writes=[gam])
    return gam


def residual_out(P, xt, xb_ap_fn, W, nk, lhs_fn, lhs_reads_fn, banks):
    S = P.S
    nb = xt.ap.shape[1]
    i = 0
    for b in range(nb):
        for oc in range(2):
            pb = banks[i % len(banks)]
            i += 1
            for k in range(nk):
                S.op("pe", lambda e, k=k, b=b, oc=oc, pb=pb: e.matmul(
                    pb.ap, lhsT=lhs_fn(k, b), rhs=W.ap[:, k, oc * 512:(oc + 1) * 512],
                    start=(k == 0), stop=(k == nk - 1)),
                    reads=[lhs_reads_fn(k), P.WA.sub(W, k * 1024 + oc * 512, k * 1024 + oc * 512 + 512)],
                    writes=[pb])
            S.op("dve", lambda e, b=b, oc=oc, pb=pb: e.tensor_tensor(
                out=xt.ap[:, b, oc * 512:(oc + 1) * 512], in0=pb.ap,
                in1=xt.ap[:, b, oc * 512:(oc + 1) * 512], op=ALU.add),
                reads=[pb, xt], writes=[xt])


def phase_mlp(P, xd, w1d, w2d, gamd, preloaded=False):
    S, WA, A, C = P.S, P.WA, P.A, P.C
    TT = 256
    NTT = P.ntok // TT
    WA.reset(); A.reset(C["a0"])
    W1 = WA.alloc(8 * 4096, "p (k f) -> p k f", k=8)
    W2 = WA.alloc(32 * 1024, "p (k f) -> p k f", k=32)
    if not preloaded:
        load_weight_fast(P, W1, w1d, 8, 4096)
        load_weight_fast(P, W2, w2d, 32, 1024, piece=1024)
    gam = load_gamma(P, gamd)
    xts = [A.alloc(2 * 1024, "p (b d) -> p b d", b=2) for _ in range(2)]
    sqfs = [A.own(A.alloc(256)) for _ in range(3)]
    hTs = [A.alloc16(8 * TT, "p (k t) -> p k t", k=8) for _ in range(2)]
    actT = A.alloc16(32 * TT, "p (f t) -> p f t", f=32)
    ident = C["ident"]
    xv = xd.rearrange("(n b p) d -> n p b d", p=128, b=2)
    groups = [list(range(g, min(g + 3, 32))) for g in range(0, 32, 3)]

    xns = [A.own(A.alloc16(1024)) for _ in range(2)]
    sqj = A.own(A.alloc16(1024)); ssn = A.alloc(1); rstdn = A.alloc(1)

    def load_x(it):
        S.dma("sp", out=xts[it % 2].ap, in_=xv[it], reads=[P.db(f"x{it}")], writes=[xts[it % 2]])

    def norm_pre(it, b):
        xt_ = xts[it % 2]; xn = xns[b]
        S.op("act", lambda e: e.activation(out=sqj.ap, in_=xt_.ap[:, b, :], func=ACTF.Square, scale=1.0 / 32.0,
                                           accum_out=ssn.ap), reads=[xt_], writes=[sqj, ssn])
        S.op("pool", lambda e: e.tensor_scalar(out=ssn.ap, in0=ssn.ap, scalar1=EPS, scalar2=None, op0=ALU.add),
             reads=[ssn], writes=[ssn])
        S.op("pool", lambda e: e.tensor_tensor(out=rstdn.ap, in0=ssn.ap, in1=C["mhalf"].ap, op=ALU.pow),
             reads=[ssn, C["mhalf"]], writes=[rstdn])
        S.op("dve", lambda e: e.scalar_tensor_tensor(out=xn.ap, in0=xt_.ap[:, b, :], scalar=rstdn.ap, in1=gam.ap,
                                                     op0=ALU.mult, op1=ALU.mult),
             reads=[xt_, rstdn, gam], writes=[xn])

    def norm_post(it, b):
        xn = xns[b]; hT_ = hTs[it % 2]
        pst = P.ps[6 + b]
        pv = ps_bf16(pst)
        for kc in range(8):
            S.op("pe", lambda e: e.transpose(out=pv[:, kc * 128:(kc + 1) * 128], in_=xn.ap[:, kc * 128:(kc + 1) * 128],
                                             identity=ident.ap), reads=[xn, ident], writes=[pst])
        S.op("act", lambda e: e.activation(out=hT_.ap[:, :, b * 128:(b + 1) * 128],
                                           in_=pv.rearrange("p (k t) -> p k t", k=8), func=ACTF.Copy),
             reads=[pst], writes=[hT_])

    def load_norm(it):
        load_x(it)
        for b in range(2):
            norm_pre(it, b)
            norm_post(it, b)

    load_norm(0)
    nsq = 0
    for it in range(NTT):
        xt = xts[it % 2]; hT = hTs[it % 2]
        xb = P.db(f"x{it}")
        if it + 1 < NTT:
            load_x(it + 1)
        for gi, grp in enumerate(groups):
            if it + 1 < NTT and gi in (5, 7):
                norm_pre(it + 1, (gi - 5) // 2)
            banks = [P.ps[(gi % 2) * 3 + n] for n in range(len(grp))]
            for kc in range(8):
                for n, fc in enumerate(grp):
                    pb = banks[n]
                    S.op("pe", lambda e: e.matmul(
                        pb.ap[:, 0:TT], lhsT=W1.ap[:, kc, fc * 128:(fc + 1) * 128], rhs=hT.ap[:, kc, :],
                        start=(kc == 0), stop=(kc == 7)),
                        reads=[WA.sub(W1, kc * 4096 + fc * 128, kc * 4096 + fc * 128 + 128), hT], writes=[pb])
            for n, fc in enumerate(grp):
                pb = banks[n]
                sqf = sqfs[nsq % 3]
                nsq += 1
                S.op("act", lambda e: e.activation(out=sqf.ap, in_=pb.ap[:, 0:TT], func=ACTF.Relu),
                     reads=[pb], writes=[sqf])
                S.op("dve", lambda e: e.tensor_tensor(out=actT.ap[:, fc, :], in0=sqf.ap, in1=sqf.ap, op=ALU.mult),
                     reads=[sqf], writes=[A.sub(actT, fc * TT, fc * TT + TT)])
        if it + 1 < NTT:
            norm_post(it + 1, 0)
            norm_post(it + 1, 1)
        residual_out(P, xt, None, W2, 32,
                     lambda k, b: actT.ap[:, k, b * 128:(b + 1) * 128],
                     lambda k: A.sub(actT, k * TT, k * TT + TT), P.ps[0:4])
        S.dma("sp", out=xv[it], in_=xt.ap, reads=[xt], writes=[xb])


def phase_gmlp(P, xd, gamd, wind, vgd, wsd, bsd, woutd):
    S, WA, A, C = P.S, P.WA, P.A, P.C
    TT = 256
    NTT = P.ntok // TT
    WA.reset(); A.reset(C["a0"])
    Win = WA.alloc(8 * 6144, "p (k f) -> p k f", k=8)
    Wout = WA.alloc(24 * 1024, "p (k f) -> p k f", k=24)
    load_weight_fast(P, Win, wind, 8, 6144)
    load_weight_fast(P, Wout, woutd, 24, 1024, piece=1024)
    gam = load_gamma(P, gamd)
    ident = C["ident"]
    WmT = A.alloc16(1024, "p (h t) -> p h t", h=8)
    vg = A.alloc(24)
    bs = A.alloc(1024, "p (h t) -> p h t", h=8)
    off_keep = A.off
    A.off = A.n - 512
    wn = A.alloc16(1024, "p (h s) -> p h s", h=8)
    A.off = off_keep
    off_keep2 = A.off
    A.off = A.n - 512 - 1024
    wnf = A.alloc(1024, "p (h s) -> p h s", h=8)
    A.off = off_keep2
    S.dma("sp", out=wnf.ap, in_=wsd.rearrange("h t s -> t h s"), writes=[wnf])
    S.op("dve", lambda e: e.tensor_tensor(out=wn.ap, in0=wnf.ap,
                                          in1=C["tril"].ap.unsqueeze(1).broadcast_to([128, 8, 128]), op=ALU.mult),
         reads=[wnf, C["tril"]], writes=[wn])
    pst = P.ps[7]
    pv = ps_bf16(pst)
    for h in range(8):
        S.op("pe", lambda e, h=h: e.transpose(out=pv[:, h * 128:(h + 1) * 128], in_=wn.ap[:, h, :], identity=ident.ap),
             reads=[wn, ident], writes=[pst])
    S.op("act", lambda e: e.activation(out=WmT.ap, in_=pv.rearrange("p (h t) -> p h t", h=8), func=ACTF.Copy),
         reads=[pst], writes=[WmT])
    S.dma("sp", out=vg.ap, in_=vgd.rearrange("o (c p) -> p (o c)", p=128), writes=[vg], allow_slow_non_contiguous=True)
    S.dma("sp", out=bs.ap, in_=bsd.rearrange("h t -> (h t)").partition_broadcast(128), writes=[bs])
    xts = [A.alloc(2 * 1024, "p (b d) -> p b d", b=2)] * 2
    scr = norm_scratch(P)
    hT = A.alloc16(8 * TT, "p (k t) -> p k t", k=8)
    gv = [A.alloc16(3072) for _ in range(2)]
    ssq = [A.alloc(8) for _ in range(2)]
    sst = [A.alloc(1) for _ in range(2)]
    rsv = [A.alloc(1) for _ in range(2)]
    junk = scr["sq"]
    WmTs = [A.alloc16(1024, "p (h t) -> p h t", h=8) for _ in range(2)]
    ug = [A.own(A.alloc(TT)) for _ in range(2)]
    tmp = [A.own(A.alloc(TT)) for _ in range(2)]
    uvT = A.alloc16(24 * TT, "p (c t) -> p c t", c=24)
    xv = xd.rearrange("(n b p) d -> n p b d", p=128, b=2)
    hTs = [hT, hT]

    def load_norm(it):
        xt_ = xts[it % 2]
        S.dma("sp", out=xt_.ap, in_=xv[it], reads=[P.db(f"x{it}")], writes=[xt_])
        for b in range(2):
            emit_norm_T(P, xt_.ap[:, b, :], xt_, gam, hTs[it % 2], b * 128, ident, dict(scr, pst=P.ps[7]))

    ngrp = 0
    for it in range(NTT):
        load_norm(it)
        xt = xts[it % 2]
        hT = hTs[it % 2]
        xb = P.db(f"x{it}")
        for b in range(2):
            for jg in range(2):
                banks = [P.ps[(ngrp % 2) * 3 + n] for n in range(3)]
                ngrp += 1
                for kc in range(8):
                    for n in range(3):
                        c0 = 3072 + (jg * 3 + n) * 512
                        pb = banks[n]
                        S.op("pe", lambda e: e.matmul(
                            pb.ap, lhsT=hT.ap[:, kc, b * 128:(b + 1) * 128], rhs=Win.ap[:, kc, c0:c0 + 512],
                            start=(kc == 0), stop=(kc == 7)),
                            reads=[hT, WA.sub(Win, kc * 6144 + c0, kc * 6144 + c0 + 512)], writes=[pb])
                for n in range(3):
                    jj = jg * 3 + n
                    pb = banks[n]
                    gs = A.sub(gv[b], jj * 512, jj * 512 + 512)
                    S.op("act", lambda e: e.activation(
                        out=gv[b].ap[:, jj * 512:(jj + 1) * 512], in_=pb.ap, func=ACTF.Gelu_apprx_tanh),
                        reads=[pb], writes=[gs])
                    S.op("dve", lambda e: e.scalar_tensor_tensor(
                        out=junk.ap[:, 0:512], in0=gv[b].ap[:, jj * 512:(jj + 1) * 512], scalar=1.0,
                        in1=gv[b].ap[:, jj * 512:(jj + 1) * 512], op0=ALU.mult, op1=ALU.mult,
                        accum_out=ssq[b].ap[:, jj:jj + 1]),
                        reads=[gs], writes=[junk, ssq[b]])
            S.op("dve", lambda e: e.tensor_reduce(out=sst[b].ap, in_=ssq[b].ap[:, 0:6], axis=AX.X, op=ALU.add),
                 reads=[ssq[b]], writes=[sst[b]])
            S.op("pool", lambda e: e.tensor_scalar(out=sst[b].ap, in0=sst[b].ap, scalar1=1.0 / 3072.0, scalar2=EPS,
                                                   op0=ALU.mult, op1=ALU.add),
                 reads=[sst[b]], writes=[sst[b]])
            S.op("pool", lambda e: e.tensor_tensor(out=rsv[b].ap, in0=sst[b].ap, in1=C["mhalf"].ap, op=ALU.pow),
                 reads=[sst[b], C["mhalf"]], writes=[rsv[b]])
            S.op("dve", lambda e: e.tensor_scalar(out=WmTs[b].ap, in0=WmT.ap, scalar1=rsv[b].ap, scalar2=None,
                                                  op0=ALU.mult),
                 reads=[WmT, rsv[b]], writes=[WmTs[b]])
        for hh in range(8):
            banks = [P.ps[(ngrp % 2) * 3 + n] for n in range(3)]
            ngrp += 1
            for kc in range(8):
                for n in range(3):
                    c = hh * 3 + n
                    pu = banks[n]
                    S.op("pe", lambda e: e.matmul(
                        pu.ap[:, 0:TT], lhsT=Win.ap[:, kc, c * 128:(c + 1) * 128], rhs=hT.ap[:, kc, :],
                        start=(kc == 0), stop=(kc == 7)),
                        reads=[hT, WA.sub(Win, kc * 6144 + c * 128, kc * 6144 + c * 128 + 128)], writes=[pu])
            for n in range(3):
                c = hh * 3 + n
                pu = banks[n]
                u_ = ug[c % 2]
                t_ = tmp[c % 2]
                S.op("act", lambda e: e.activation(out=u_.ap, in_=pu.ap[:, 0:TT], func=ACTF.Gelu_apprx_tanh),
                     reads=[pu], writes=[u_])
                pg = P.ps[6]
                pgo = (c % 2) * 256
                for b in range(2):
                    S.op("pe", lambda e: e.matmul(
                        pg.ap[:, pgo + b * 128:pgo + (b + 1) * 128], lhsT=gv[b].ap[:, c * 128:(c + 1) * 128],
                        rhs=WmTs[b].ap[:, hh, :], start=True, stop=True),
                        reads=[A.sub(gv[b], c * 128, c * 128 + 128), WmTs[b]], writes=[pg])
                for b in range(2):
                    S.op("dve", lambda e: e.scalar_tensor_tensor(
                        out=t_.ap[:, b * 128:(b + 1) * 128], in0=pg.ap[:, pgo + b * 128:pgo + (b + 1) * 128],
                        scalar=vg.ap[:, c:c + 1], in1=bs.ap[:, hh, :], op0=ALU.mult, op1=ALU.add),
                        reads=[pg, vg, bs], writes=[t_])
                S.op("pool", lambda e: e.tensor_tensor(out=uvT.ap[:, c, :], in0=t_.ap, in1=u_.ap, op=ALU.mult),
                     reads=[t_, u_], writes=[A.sub(uvT, c * TT, c * TT + TT)])
        residual_out(P, xt, None, Wout, 24,
                     lambda k, b: uvT.ap[:, k, b * 128:(b + 1) * 128],
                     lambda k: A.sub(uvT, k * TT, k * TT + TT), P.ps[0:4])
        S.dma("sp", out=xv[it], in_=xt.ap, reads=[xt], writes=[xb])


def emit_sincos(P, yy, sn, cs, ki, fr):
    S = P.S
    S.op("dve", lambda e: e.tensor_copy(out=ki.ap, in_=yy.ap), reads=[yy], writes=[ki])
    S.op("dve", lambda e: e.tensor_tensor(out=fr.ap, in0=yy.ap, in1=ki.ap, op=ALU.subtract),
         reads=[yy, ki], writes=[fr])
    S.op("act", lambda e: e.activation(out=sn.ap, in_=fr.ap, func=ACTF.Sin, scale=TWO_PI), reads=[fr], writes=[sn])
    S.op("dve", lambda e: e.tensor_scalar(out=ki.ap, in0=yy.ap, scalar1=0.25, scalar2=None, op0=ALU.add),
         reads=[yy, fr], writes=[ki])
    S.op("dve", lambda e: e.scalar_tensor_tensor(out=fr.ap, in0=yy.ap, scalar=0.25, in1=ki.ap,
                                                 op0=ALU.add, op1=ALU.subtract),
         reads=[yy, ki, sn], writes=[fr])
    S.op("act", lambda e: e.activation(out=cs.ap, in_=fr.ap, func=ACTF.Sin, scale=TWO_PI), reads=[fr], writes=[cs])


def phase_attn_qkv(P, xd, gamd, wqkvd, qgd, kgd, posd, sc):
    S, WA, A, C = P.S, P.WA, P.A, P.C
    TT = 512
    NB = TT // 128
    NTT = P.ntok // TT
    WA.reset(); A.reset(C["a0"])
    W = WA.alloc(8 * 3072, "p (k f) -> p k f", k=8)
    load_weight_fast(P, W, wqkvd, 8, 3072, piece=1536)
    gam = load_gamma(P, gamd)
    ident = C["ident"]
    gcol = A.alloc(2)
    for idx, gd in enumerate((qgd, kgd)):
        for half in range(2):
            S.dma("sp", out=gcol.ap[half * 64:(half + 1) * 64, idx:idx + 1], in_=gd, writes=[gcol])
    A.off = ((A.off + 255) // 256) * 256
    xts = [A.alloc(NB * 1024, "p (b d) -> p b d", b=NB), WA.alloc32(NB * 1024, "p (b d) -> p b d", b=NB)]
    sq = A.own(A.alloc16(1024)); ssn = A.alloc(1); rstdn = A.alloc(1)
    xns = [A.own(A.alloc16(1024)) for _ in range(2)]
    hTs = [A.alloc16(8 * TT, "p (k t) -> p k t", k=8) for _ in range(2)]
    posb = A.alloci(TT)
    yy = A.alloc(TT); ki = A.alloci(TT); fr = A.alloc(TT)
    sns = [WA.own(WA.alloc32(TT)) for _ in range(2)]
    css = [WA.own(WA.alloc32(TT)) for _ in range(2)]
    va = [A.alloc16(8 * 129, "p (h e) -> p h e", h=8) for _ in range(2)]
    NS = 4
    sqb = [WA.own(WA.alloc(TT)) for _ in range(NS)]
    sd = [WA.own(WA.alloc32(TT)) for _ in range(NS)]
    rs = [WA.own(WA.alloc32(TT)) for _ in range(NS)]
    qn = [WA.own(WA.alloc(TT)) for _ in range(NS)]
    t1 = [WA.own(WA.alloc32(TT)) for _ in range(NS)]
    t2 = [WA.own(WA.alloc32(TT)) for _ in range(NS)]
    qf = [WA.own(WA.alloc(TT)) for _ in range(NS)]
    for v in va:
        S.op("pool", lambda e, v=v: e.memset(v.ap, 1.0), writes=[v])
    xv = xd.rearrange("(n b p) d -> n p b d", p=128, b=NB)
    pqb = P.ps[0:3]; pmb = P.ps[3:5]; prb = P.ps[5:7]
    pst = P.ps[7]
    pv = ps_bf16(pst)

    def load_x(it):
        xb = [P.db(f"x{it * 2}"), P.db(f"x{it * 2 + 1}")]
        S.dma("sp", out=xts[it % 2].ap, in_=xv[it], reads=xb, writes=[xts[it % 2]])

    def norm_pre(it, b):
        xt = xts[it % 2]
        xn = xns[b % 2]
        S.op("act", lambda e: e.activation(out=sq.ap, in_=xt.ap[:, b, :], func=ACTF.Square, scale=1.0 / 32.0,
                                           accum_out=ssn.ap), reads=[xt], writes=[sq, ssn])
        S.op("pool", lambda e: e.tensor_scalar(out=ssn.ap, in0=ssn.ap, scalar1=EPS, scalar2=None, op0=ALU.add),
             reads=[ssn], writes=[ssn])
        S.op("pool", lambda e: e.tensor_tensor(out=rstdn.ap, in0=ssn.ap, in1=C["mhalf"].ap, op=ALU.pow),
             reads=[ssn, C["mhalf"]], writes=[rstdn])
        S.op("dve", lambda e: e.scalar_tensor_tensor(out=xn.ap, in0=xt.ap[:, b, :], scalar=rstdn.ap, in1=gam.ap,
                                                     op0=ALU.mult, op1=ALU.mult),
             reads=[xt, rstdn, gam], writes=[xn])

    def norm_post(it, b):
        xn = xns[b % 2]
        hT = hTs[it % 2]
        for kc in range(8):
            S.op("pe", lambda e: e.transpose(out=pv[:, kc * 128:(kc + 1) * 128], in_=xn.ap[:, kc * 128:(kc + 1) * 128],
                                             identity=ident.ap), reads=[xn, ident], writes=[pst])
        S.op("act", lambda e: e.activation(out=hT.ap[:, :, b * 128:(b + 1) * 128],
                                           in_=pv.rearrange("p (k t) -> p k t", k=8), func=ACTF.Copy),
             reads=[pst], writes=[hT])

    def sincos(it):
        S.dma("sp", out=posb.ap, in_=posd[:, it * TT:(it + 1) * TT].partition_broadcast(128), writes=[posb])
        S.op("dve", lambda e: e.tensor_scalar(out=yy.ap, in0=posb.ap, scalar1=C["invf"].ap, scalar2=None, op0=ALU.mult),
             reads=[posb, C["invf"]], writes=[yy])
        emit_sincos(P, yy, sns[it % 2], css[it % 2], ki, fr)

    items = [(it, which, h) for it in range(NTT) for which in range(2) for h in range(8)]
    n = len(items)

    def st0(i):
        it, which, h = items[i]
        hT = hTs[it % 2]
        col0 = which * 1024 + h * 128
        pq = pqb[i % 3]
        for kc in range(8):
            S.op("pe", lambda e: e.matmul(pq.ap, lhsT=W.ap[:, kc, col0:col0 + 128], rhs=hT.ap[:, kc, :],
                                          start=(kc == 0), stop=(kc == 7)),
                 reads=[hT, WA.sub(W, kc * 3072 + col0, kc * 3072 + col0 + 128)], writes=[pq])

    def st1(i):
        pq = pqb[i % 3]; pm = pmb[i % 2]; s_ = sqb[i % NS]
        S.op("act", lambda e: e.activation(out=s_.ap, in_=pq.ap, func=ACTF.Square), reads=[pq], writes=[s_])
        S.op("pe", lambda e: e.matmul(pm.ap, lhsT=C["bones"].ap, rhs=s_.ap, start=True, stop=True),
             reads=[s_, C["bones"]], writes=[pm])

    def st2(i):
        it, which, h = items[i]
        k = i % NS
        pq = pqb[i % 3]; pm = pmb[i % 2]
        S.op("act", lambda e: e.activation(out=sd[k].ap, in_=pm.ap, func=ACTF.Ln, bias=C["epscol"].ap),
             reads=[pm, C["epscol"]], writes=[sd[k]])
        S.op("act", lambda e: e.activation(out=rs[k].ap, in_=sd[k].ap, func=ACTF.Exp, scale=-0.5),
             reads=[sd[k]], writes=[rs[k]])
        S.op("dve", lambda e: e.scalar_tensor_tensor(out=qn[k].ap, in0=pq.ap, scalar=gcol.ap[:, which:which + 1],
                                                     in1=rs[k].ap, op0=ALU.mult, op1=ALU.mult),
             reads=[pq, gcol, rs[k]], writes=[qn[k]])

    def st2b(i):
        k = i % NS
        pr = prb[i % 2]
        S.op("pe", lambda e: e.matmul(pr.ap, lhsT=C["rrot"].ap, rhs=qn[k].ap, start=True, stop=True),
             reads=[qn[k], C["rrot"]], writes=[pr])

    def st3(i):
        it, which, h = items[i]
        k = i % NS
        pr = prb[i % 2]
        sn, cs = sns[it % 2], css[it % 2]
        S.op("pool", lambda e: e.tensor_tensor(out=t1[k].ap, in0=qn[k].ap, in1=cs.ap, op=ALU.mult),
             reads=[qn[k], cs], writes=[t1[k]])
        S.op("dve", lambda e: e.tensor_tensor(out=t2[k].ap, in0=pr.ap, in1=sn.ap, op=ALU.mult),
             reads=[pr, sn], writes=[t2[k]])
        S.op("dve", lambda e: e.tensor_tensor(out=qf[k].ap, in0=t1[k].ap, in1=t2[k].ap, op=ALU.add),
             reads=[t1[k], t2[k]], writes=[qf[k]])
        dst = sc["qT"] if which == 0 else sc["kT"]
        S.dma("sp", out=dst[h, :, it * TT:(it + 1) * TT], in_=qf[k].ap, reads=[qf[k]],
              writes=[P.db(f"{'qk'[which]}T{h}_{it}")])

    def vproj(it, b):
        hT = hTs[it % 2]
        v = va[b % 2]
        for jj in range(2):
            pvb = pst
            for kc in range(8):
                c0 = 2048 + jj * 512
                S.op("pe", lambda e: e.matmul(pvb.ap, lhsT=hT.ap[:, kc, b * 128:(b + 1) * 128], rhs=W.ap[:, kc, c0:c0 + 512],
                                              start=(kc == 0), stop=(kc == 7)),
                     reads=[hT, WA.sub(W, kc * 3072 + c0, kc * 3072 + c0 + 512)], writes=[pvb])
            S.op("act", lambda e: e.activation(
                out=v.ap[:, 4 * jj:4 * jj + 4, 0:128], in_=pvb.ap.rearrange("p (h e) -> p h e", h=4), func=ACTF.Copy),
                reads=[pvb], writes=[v])
        blk = it * NB + b
        S.dma("sp", out=sc["v"][blk], in_=v.ap, reads=[v], writes=[P.db(f"v{blk}")])

    load_x(0)
    for b in range(NB):
        norm_pre(0, b)
        norm_post(0, b)
    sincos(0)
    if NTT > 1:
        load_x(1)
    for s in range(n + 4):
        if s < n:
            it, which, h = items[s]
            j16 = s % 16
            st0(s)
        if 0 <= s - 2 < n:
            st2(s - 2)
        if 0 <= s - 1 < n:
            st1(s - 1)
        if 0 <= s - 3 < n:
            st2b(s - 3)
        if 0 <= s - 4 < n:
            st3(s - 4)
        if s < n:
            if j16 in (1, 5, 9, 13):
                vproj(it, j16 // 4)
            if it + 1 < NTT:
                if j16 in (0, 4, 8, 12):
                    norm_pre(it + 1, j16 // 4)
                if j16 in (3, 7, 11, 15):
                    norm_post(it + 1, j16 // 4)
                if j16 == 14:
                    sincos(it + 1)
                if j16 == 15 and it + 2 < NTT:
                    load_x(it + 2)


def phase_attn_core(P, lamd, sgd, lambda_init, sc):
    S, WA, A, C = P.S, P.WA, P.A, P.C
    ntok = P.ntok
    NB = ntok // 128
    NG = ntok // 512
    NT256 = ntok // 512
    WA.reset(); A.reset(C["a0"])
    K0 = [WA.alloc(ntok) for _ in range(2)]
    K1 = [WA.alloc(ntok) for _ in range(2)]
    QT = [WA.alloc(ntok) for _ in range(2)]
    VA = [WA.alloc(NB * 128, "p (n e) -> p n e", e=128) for _ in range(2)]
    ones16 = WA.alloc(128)
    onesb = WA.alloc(128)
    S.op("pool", lambda e: e.memset(ones16.ap, 1.0), writes=[ones16])
    S.op("pool", lambda e: e.memset(onesb.ap, 1.0 / 128.0), writes=[onesb])
    for i in range(2):
        S.op("pool", lambda e, i=i: e.memset(K0[i].ap[64:128, :], 0.0), writes=[K0[i]])
        S.op("pool", lambda e, i=i: e.memset(K1[i].ap[0:64, :], 0.0), writes=[K1[i]])
    L = A.alloc(256, "p (a d) -> p a d", a=4)
    S.dma("sp", out=L.ap, in_=lamd.rearrange("a d -> (a d)").partition_broadcast(128), writes=[L])
    lj = A.alloc(64); s12 = A.alloc(2); e12 = A.alloc(2); neglam = A.alloc(1)
    for a in range(2):
        S.op("dve", lambda e, a=a: e.scalar_tensor_tensor(
            out=lj.ap, in0=L.ap[:, 2 * a, :], scalar=1.0, in1=L.ap[:, 2 * a + 1, :], op0=ALU.mult, op1=ALU.mult,
            accum_out=s12.ap[:, a:a + 1]), reads=[L], writes=[lj, s12])
    S.op("act", lambda e: e.activation(out=e12.ap, in_=s12.ap, func=ACTF.Exp), reads=[s12], writes=[e12])
    S.op("dve", lambda e: e.tensor_tensor(out=neglam.ap, in0=e12.ap[:, 1:2], in1=e12.ap[:, 0:1], op=ALU.subtract),
         reads=[e12], writes=[neglam])
    S.op("dve", lambda e: e.tensor_scalar(out=neglam.ap, in0=neglam.ap, scalar1=-float(lambda_init), scalar2=None,
                                          op0=ALU.add), reads=[neglam], writes=[neglam])
    sgc = A.alloc(1)
    S.dma("sp", out=sgc.ap, in_=sgd.rearrange("o e -> e o"), writes=[sgc], allow_slow_non_contiguous=True)
    S.op("dve", lambda e: e.tensor_scalar(out=sgc.ap, in0=sgc.ap, scalar1=float(1.0 - lambda_init), scalar2=None,
                                          op0=ALU.mult), reads=[sgc], writes=[sgc])
    A.off = ((A.off + 255) // 256) * 256
    PT = [[A.alloc16(512) for _ in range(2)] for _ in range(2)]
    ob = [[A.alloc(512) for _ in range(2)] for _ in range(2)]
    rl = [A.alloc(512) for _ in range(2)]
    tt = A.alloc(512); uu = A.alloc(512); oo = A.alloc(512)
    osq = A.alloc16(512); msb = A.alloc(512); rs = A.alloc(512)
    oT = [A.alloc16(512) for _ in range(2)]
    sb3 = [P.ps[0], P.ps[1], P.ps[2]]
    otb = [P.ps[3], P.ps[4]]
    plb = [P.ps[5], P.ps[6]]
    pmb = P.ps[7]
    lsb = [[A.alloc(512) for _ in range(2)] for _ in range(2)]
    mhb = C["mhalf"].ap.broadcast_to([128, 512])
    ng = 0
    for h in range(8):
        buf = h % 2
        S.dma("sp", out=K0[buf].ap[0:64, :], in_=sc["kT"][h, 0:64, :],
              reads=[P.db(f"kT{h}_{it}") for it in range(NT256)], writes=[K0[buf]])
        S.dma("sp", out=K1[buf].ap[64:128, :], in_=sc["kT"][h, 64:128, :],
              reads=[P.db(f"kT{h}_{it}") for it in range(NT256)], writes=[K1[buf]])
        S.dma("sp", out=QT[buf].ap, in_=sc["qT"][h],
              reads=[P.db(f"qT{h}_{it}") for it in range(NT256)], writes=[QT[buf]])
        S.dma("sp", out=VA[buf].ap, in_=sc["v"].rearrange("n p (h e) -> h p n e", h=8)[h][:, :, 0:128],
              reads=[P.db(f"v{blk}") for blk in range(NB)], writes=[VA[buf]])
        for G in range(NG):
            gb = ng % 2
            ng += 1
            njb = 4 * G + 4

            def geom(jb):
                nq0 = max(0, jb - 4 * G)
                return nq0, (4 - nq0) * 128, G * 512 + nq0 * 128

            def stage_a(jb, cs_=(0, 1)):
                nq0, N, qc0 = geom(jb)
                for c in cs_:
                    Kc = (K0 if c == 0 else K1)[buf]
                    pss = sb3[(2 * jb + c) % 3]
                    S.op("pe", lambda e: e.matmul(
                        pss.ap[:, 0:N], lhsT=Kc.ap[:, jb * 128:(jb + 1) * 128], rhs=QT[buf].ap[:, qc0:qc0 + N],
                        start=True, stop=True),
                        reads=[WA.sub(Kc, jb * 128, jb * 128 + 128), WA.sub(QT[buf], qc0, qc0 + N)], writes=[pss])

            def stage_b(jb, cs_=(0, 1)):
                nq0, N, qc0 = geom(jb)
                c0 = nq0 * 128
                for c in cs_:
                    pss = sb3[(2 * jb + c) % 3]
                    pt = PT[jb % 2][c]
                    S.op("act", lambda e: e.activation(out=pt.ap[:, 0:N], in_=pss.ap[:, 0:N], func=ACTF.Exp, scale=0.125),
                         reads=[pss], writes=[pt])
                    eng = "dve" if c == 0 else "pool"
                    if jb >= 4 * G:
                        S.op("dve", lambda e: e.tensor_tensor(out=pt.ap[:, 0:128], in0=pt.ap[:, 0:128],
                                                               in1=C["triu"].ap, op=ALU.mult),
                             reads=[pt, C["triu"]], writes=[pt])

            def stage_c(jb):
                nq0, N, qc0 = geom(jb)
                c0 = nq0 * 128
                for c in range(2):
                    pt = PT[jb % 2][c]
                    S.op("pe", lambda e: e.matmul(
                        otb[c].ap[:, c0:512], lhsT=VA[buf].ap[:, jb, :], rhs=pt.ap[:, 0:N],
                        start=(jb == 0), stop=(jb == njb - 1)),
                        reads=[pt, WA.sub(VA[buf], jb * 128, jb * 128 + 128)], writes=[otb[c]])
                    S.op("pe", lambda e: e.matmul(
                        plb[c].ap[:, c0:512], lhsT=ones16.ap, rhs=pt.ap[:, 0:N],
                        start=(jb == 0), stop=(jb == njb - 1)),
                        reads=[pt, ones16], writes=[plb[c]])

            stage_a(0)
            for jb in range(njb):
                if jb + 1 < njb:
                    stage_a(jb + 1, (0,))
                stage_b(jb, (0,))
                if jb + 1 < njb:
                    stage_a(jb + 1, (1,))
                stage_b(jb, (1,))
                stage_c(jb)
            for c in range(2):
                S.op("act", lambda e, c=c: e.activation(out=lsb[gb][c].ap, in_=plb[c].ap, func=ACTF.Ln),
                     reads=[plb[c]], writes=[lsb[gb][c]])
                S.op("dve", lambda e, c=c: e.tensor_copy(out=ob[gb][c].ap, in_=otb[c].ap), reads=[otb[c]], writes=[ob[gb][c]])
            for c in range(2):
                S.op("act", lambda e, c=c: e.activation(out=rl[c].ap, in_=lsb[gb][c].ap, func=ACTF.Exp, scale=-1.0),
                     reads=[lsb[gb][c]], writes=[rl[c]])
            S.op("dve", lambda e: e.scalar_tensor_tensor(out=tt.ap, in0=ob[gb][1].ap, scalar=neglam.ap, in1=rl[1].ap,
                                                         op0=ALU.mult, op1=ALU.mult),
                 reads=[ob[gb][1], rl[1], neglam], writes=[tt])
            S.op("dve", lambda e: e.tensor_tensor(out=uu.ap, in0=ob[gb][0].ap, in1=rl[0].ap, op=ALU.mult),
                 reads=[ob[gb][0], rl[0]], writes=[uu])
            S.op("dve", lambda e: e.tensor_tensor(out=oo.ap, in0=uu.ap, in1=tt.ap, op=ALU.add), reads=[uu, tt], writes=[oo])
            S.op("dve", lambda e: e.tensor_tensor(out=osq.ap, in0=oo.ap, in1=oo.ap, op=ALU.mult), reads=[oo], writes=[osq])
            S.op("pe", lambda e: e.matmul(pmb.ap, lhsT=onesb.ap, rhs=osq.ap, start=True, stop=True),
                 reads=[onesb, osq], writes=[pmb])
            S.op("act", lambda e: e.activation(out=msb.ap, in_=pmb.ap, func=ACTF.Ln, bias=C["epscol"].ap),
                 reads=[pmb, C["epscol"]], writes=[msb])
            S.op("act", lambda e: e.activation(out=rs.ap, in_=msb.ap, func=ACTF.Exp, scale=-0.5),
                 reads=[msb], writes=[rs])
            oTt = oT[gb]
            S.op("dve", lambda e: e.scalar_tensor_tensor(out=oTt.ap, in0=oo.ap, scalar=sgc.ap, in1=rs.ap,
                                                         op0=ALU.mult, op1=ALU.mult),
                 reads=[oo, sgc, rs], writes=[oTt])
            S.dma("sp", out=sc["oT"][h, :, G * 512:(G + 1) * 512], in_=oTt.ap, reads=[oTt],
                  writes=[P.db(f"oT{h}_{G}")])


def phase_attn_out(P, xd, wod, sc, prefetch=None):
    S, WA, A, C = P.S, P.WA, P.A, P.C
    TT = 256
    NTT = P.ntok // TT
    WA.reset(); A.reset(C["a0"])
    WA.off = 64 * 1024
    Wo = WA.alloc(8 * 1024, "p (k f) -> p k f", k=8)
    load_weight_fast(P, Wo, wod, 8, 1024, piece=1024)
    if prefetch is not None:
        w1d, w2d = prefetch
        WA.off = 0
        W1 = WA.alloc(8 * 4096, "p (k f) -> p k f", k=8)
        W2 = WA.alloc(32 * 1024, "p (k f) -> p k f", k=32)
        load_weight(P, W1, w1d, 8, 4096)
        load_weight(P, W2, w2d, 32, 1024)
    NBK = 4
    TT = 512
    NTT = P.ntok // TT
    xts = [A.alloc(NBK * 1024, "p (b d) -> p b d", b=NBK) for _ in range(2)]
    oTs = [A.alloc16(8 * TT, "p (h t) -> p h t", h=8) for _ in range(2)]
    xv = xd.rearrange("(n b p) d -> n p b d", p=128, b=NBK)

    def loads(it):
        xt = xts[it % 2]; ot = oTs[it % 2]
        S.dma("sp", out=xt.ap, in_=xv[it], reads=[P.db(f"x{2 * it}"), P.db(f"x{2 * it + 1}")], writes=[xt])
        S.dma("sp", out=ot.ap, in_=sc["oT"][:, :, it * TT:(it + 1) * TT].rearrange("h p t -> p h t"),
              reads=[P.db(f"oT{h}_{it}") for h in range(8)], writes=[ot])

    loads(0)
    for it in range(NTT):
        xt = xts[it % 2]; ot = oTs[it % 2]
        if it + 1 < NTT:
            loads(it + 1)
        residual_out(P, xt, None, Wo, 8, lambda k, b, ot=ot: ot.ap[:, k, b * 128:(b + 1) * 128],
                     lambda k, ot=ot: ot, P.ps[0:6])
        S.dma("sp", out=xv[it], in_=xt.ap, reads=[xt], writes=[P.db(f"x{2 * it}"), P.db(f"x{2 * it + 1}")])


def phase_s5(P, xd, gamd, wind, ared, aimd, bred, bimd, cred, cimd, dd, logdtd, wglud, bglud, woutd):
    S, WA, A, C = P.S, P.WA, P.A, P.C
    TT = 256
    NTT = P.ntok // TT
    WA.reset(); A.reset(C["a0"])
    ident = C["ident"]
    Win = WA.alloc(8 * 1024, "p (k f) -> p k f", k=8)
    Wglu = WA.alloc(8 * 1024, "p (k f) -> p k f", k=8)
    Wout = WA.alloc(8 * 1024, "p (k f) -> p k f", k=8)
    load_weight_fast(P, Win, wind, 8, 1024, piece=1024)
    load_weight_fast(P, Wglu, wglud, 8, 1024, piece=1024)
    load_weight_fast(P, Wout, woutd, 8, 1024, piece=1024)
    BL = WA.alloc(32 * 2 * 128, "p (q r m) -> p q r m", q=32, r=2)
    CBr = WA.alloc(32 * 3 * 128 + 2048)
    CBflat = CBr.ap
    Zr = [WA.alloc(512 + 256) for _ in range(2)]
    ZC = [WA.alloc(128, "p (g m) -> p g m", g=2) for _ in range(2)]
    rcol = A.alloc(32); ycol = A.alloc(32); carry = A.alloc(64, "p (q r) -> p q r", r=2)
    dcol = A.alloc(8); bgl = A.alloc(8)
    a_keep = A.off
    S.op("pool", lambda e: e.memset(carry.ap, 0.0), writes=[carry])
    S.dma("sp", out=dcol.ap, in_=dd.rearrange("o (c p) -> p (o c)", p=128), writes=[dcol], allow_slow_non_contiguous=True)
    S.dma("sp", out=bgl.ap, in_=bglud.rearrange("o (c p) -> p (o c)", p=128), writes=[bgl], allow_slow_non_contiguous=True)
    are = A.alloc(32); aim = A.alloc(32); ldt = A.alloc(32); dt = A.alloc(32)
    S.dma("sp", out=are.ap, in_=ared.rearrange("(q g2) p -> (g2 p) q", g2=2), writes=[are], allow_slow_non_contiguous=True)
    S.dma("sp", out=aim.ap, in_=aimd.rearrange("(q g2) p -> (g2 p) q", g2=2), writes=[aim], allow_slow_non_contiguous=True)
    ld2 = logdtd.rearrange("o (q g2) -> (o g2) q", g2=2)
    for g2 in range(2):
        S.dma("sp", out=ldt.ap[g2 * 64:(g2 + 1) * 64, :], in_=ld2[g2:g2 + 1, :].partition_broadcast(64), writes=[ldt],
              allow_slow_non_contiguous=True)
    S.op("act", lambda e: e.activation(out=dt.ap, in_=ldt.ap, func=ACTF.Exp), reads=[ldt], writes=[dt])
    S.op("dve", lambda e: e.tensor_scalar(out=are.ap, in0=are.ap, scalar1=-1e-4, scalar2=None, op0=ALU.min),
         reads=[are], writes=[are])
    rdt = A.alloc(32)
    S.op("dve", lambda e: e.tensor_tensor(out=rdt.ap, in0=are.ap, in1=dt.ap, op=ALU.mult), reads=[are, dt], writes=[rdt])
    S.op("act", lambda e: e.activation(out=rcol.ap, in_=rdt.ap, func=ACTF.Exp), reads=[rdt], writes=[rcol])
    S.op("dve", lambda e: e.scalar_tensor_tensor(out=ycol.ap, in0=aim.ap, scalar=float(1.0 / (2.0 * math.pi)), in1=dt.ap,
                                                 op0=ALU.mult, op1=ALU.mult), reads=[aim, dt], writes=[ycol])
    snt = A.alloc(32); cst = A.alloc(32); kit = A.alloci(32); frt = A.alloc(32)
    emit_sincos(P, ycol, snt, cst, kit, frt)
    nr = A.alloc(32); ni = A.alloc(32); den = A.alloc(32); t_a = A.alloc(32); t_b = A.alloc(32)
    zr = A.alloc(32); zi = A.alloc(32)
    S.op("dve", lambda e: e.tensor_tensor(out=nr.ap, in0=rcol.ap, in1=cst.ap, op=ALU.mult), reads=[rcol, cst], writes=[nr])
    S.op("dve", lambda e: e.tensor_scalar(out=nr.ap, in0=nr.ap, scalar1=-1.0, scalar2=None, op0=ALU.add), reads=[nr], writes=[nr])
    S.op("dve", lambda e: e.tensor_tensor(out=ni.ap, in0=rcol.ap, in1=snt.ap, op=ALU.mult), reads=[rcol, snt], writes=[ni])
    S.op("dve", lambda e: e.tensor_tensor(out=den.ap, in0=are.ap, in1=are.ap, op=ALU.mult), reads=[are], writes=[den])
    S.op("dve", lambda e: e.tensor_tensor(out=t_a.ap, in0=aim.ap, in1=aim.ap, op=ALU.mult), reads=[aim], writes=[t_a])
    S.op("dve", lambda e: e.tensor_tensor(out=den.ap, in0=den.ap, in1=t_a.ap, op=ALU.add), reads=[den, t_a], writes=[den])
    S.op("dve", lambda e: e.reciprocal(out=den.ap, in_=den.ap), reads=[den], writes=[den])
    S.op("dve", lambda e: e.tensor_tensor(out=t_a.ap, in0=nr.ap, in1=are.ap, op=ALU.mult), reads=[nr, are], writes=[t_a])
    S.op("dve", lambda e: e.tensor_tensor(out=t_b.ap, in0=ni.ap, in1=aim.ap, op=ALU.mult), reads=[ni, aim], writes=[t_b])
    S.op("dve", lambda e: e.tensor_tensor(out=t_a.ap, in0=t_a.ap, in1=t_b.ap, op=ALU.add), reads=[t_a, t_b], writes=[t_a])
    S.op("dve", lambda e: e.tensor_tensor(out=zr.ap, in0=t_a.ap, in1=den.ap, op=ALU.mult), reads=[t_a, den], writes=[zr])
    S.op("dve", lambda e: e.tensor_tensor(out=t_a.ap, in0=ni.ap, in1=are.ap, op=ALU.mult), reads=[ni, are], writes=[t_a])
    S.op("dve", lambda e: e.tensor_tensor(out=t_b.ap, in0=nr.ap, in1=aim.ap, op=ALU.mult), reads=[nr, aim], writes=[t_b])
    S.op("dve", lambda e: e.tensor_tensor(out=t_a.ap, in0=t_a.ap, in1=t_b.ap, op=ALU.subtract), reads=[t_a, t_b], writes=[t_a])
    S.op("dve", lambda e: e.tensor_tensor(out=zi.ap, in0=t_a.ap, in1=den.ap, op=ALU.mult), reads=[t_a, den], writes=[zi])
    Bre = A.alloc(512, "p (q h) -> p q h", h=16); Bim = A.alloc(512, "p (q h) -> p q h", h=16)
    S.dma("sp", out=Bre.ap, in_=bred.rearrange("(q g2) p h -> (g2 p) q h", g2=2), writes=[Bre])
    S.dma("sp", out=Bim.ap, in_=bimd.rearrange("(q g2) p h -> (g2 p) q h", g2=2), writes=[Bim])
    zrb = zr.ap.unsqueeze(2).broadcast_to([128, 32, 16])
    zib = zi.ap.unsqueeze(2).broadcast_to([128, 32, 16])
    M1 = A.alloc(512, "p (q h) -> p q h", h=16); M2 = A.alloc(512, "p (q h) -> p q h", h=16)
    Bb = [A.alloc(512, "p (q h) -> p q h", h=16) for _ in range(2)]
    S.op("dve", lambda e: e.tensor_tensor(out=M1.ap, in0=Bre.ap, in1=zrb, op=ALU.mult), reads=[Bre, zr], writes=[M1])
    S.op("dve", lambda e: e.tensor_tensor(out=M2.ap, in0=Bim.ap, in1=zib, op=ALU.mult), reads=[Bim, zi], writes=[M2])
    S.op("dve", lambda e: e.tensor_tensor(out=Bb[0].ap, in0=M1.ap, in1=M2.ap, op=ALU.subtract), reads=[M1, M2], writes=[Bb[0]])
    S.op("dve", lambda e: e.tensor_tensor(out=M1.ap, in0=Bim.ap, in1=zrb, op=ALU.mult), reads=[Bim, zr], writes=[M1])
    S.op("dve", lambda e: e.tensor_tensor(out=M2.ap, in0=Bre.ap, in1=zib, op=ALU.mult), reads=[Bre, zi], writes=[M2])
    S.op("dve", lambda e: e.tensor_tensor(out=Bb[1].ap, in0=M1.ap, in1=M2.ap, op=ALU.add), reads=[M1, M2], writes=[Bb[1]])
    pst = P.ps[7]
    pv = ps_bf16(pst)
    n = 0
    for k in range(8):
        for ri in range(2):
            Z = Zr[n % 2]
            n += 1
            S.op("pool", lambda e, Z=Z: e.memset(Z.ap, 0.0), writes=[Z])
            for g2 in range(2):
                dst = Z.ap[g2 * 64:(g2 + 1) * 64, g2 * 16:g2 * 16 + 640].rearrange("p (q m) -> p q m", m=160)[:, :, 0:16]
                S.op("dve", lambda e, dst=dst, g2=g2, ri=ri, k=k: e.tensor_copy(
                    out=dst, in_=Bb[ri].ap[g2 * 64:(g2 + 1) * 64, 4 * k:4 * k + 4, :]),
                    reads=[Bb[ri]], writes=[Z])
            for ql in range(4):
                S.op("pe", lambda e, Z=Z, ql=ql: e.transpose(out=pv[:, ql * 128:(ql + 1) * 128],
                                                            in_=Z.ap[:, ql * 128:(ql + 1) * 128], identity=ident.ap),
                     reads=[Z, ident], writes=[pst])
            S.op("act", lambda e, k=k, ri=ri: e.activation(out=BL.ap[:, 4 * k:4 * k + 4, ri, :],
                                                           in_=pv[:, 0:512].rearrange("p (q m) -> p q m", q=4),
                                                           func=ACTF.Copy),
                 reads=[pst], writes=[BL])
    Cn = [A.alloc(512, "p (k m) -> p k m", k=8) for _ in range(2)]
    S.dma("sp", out=Cn[0].ap, in_=cred.rearrange("(k gl) h p -> (gl h) k p", gl=8), writes=[Cn[0]])
    S.dma("sp", out=Cn[1].ap, in_=cimd.rearrange("(k gl) h p -> (gl h) k p", gl=8), writes=[Cn[1]])
    S.op("pool", lambda e: e.memset(CBflat, 0.0), writes=[CBr])
    pst2 = P.ps[6]
    pv2 = ps_bf16(pst2)
    n = 0
    for k in range(8):
        for ri in range(2):
            zc = ZC[n % 2]
            n += 1
            for g2 in range(2):
                S.op("dve", lambda e, zc=zc, g2=g2, ri=ri, k=k: e.tensor_scalar(
                    out=zc.ap[:, g2, :], in0=Cn[ri].ap[:, k, :], scalar1=C["par"].ap[:, g2:g2 + 1], scalar2=None,
                    op0=ALU.mult), reads=[Cn[ri], C["par"]], writes=[zc])
            S.op("pe", lambda e, zc=zc: e.transpose(out=pv2[:, 0:128], in_=zc.ap.rearrange("p g m -> p (g m)"),
                                                   identity=ident.ap), reads=[zc, ident], writes=[pst2])
            for var in ((0, 2) if ri == 0 else (1,)):
                base = 4 * k * 384 + var * 128
                dst = CBflat[:, base:base + 4 * 416].rearrange("p (q m) -> p q m", m=416)[:, :, 0:32]
                sgn = 1.0 if var == 0 else -1.0
                S.op("act", lambda e, dst=dst, sgn=sgn: e.activation(
                    out=dst, in_=pv2[:, 0:128].rearrange("p (q m) -> p q m", q=4), func=ACTF.Copy, scale=sgn),
                    reads=[pst2], writes=[CBr])
    CB = CBflat[:, 0:32 * 384].rearrange("p (q v m) -> p q v m", q=32, v=3)
    CS = WA.alloc(32 * TT, "p (q k) -> p q k", q=32)
    SN = WA.alloc(32 * TT, "p (q k) -> p q k", q=32)
    A.reset(a_keep)
    cT = A.alloc(32); sT = A.alloc(32)
    yT = A.alloc(32); kiT = A.alloci(32); frT = A.alloc(32)
    S.op("dve", lambda e: e.tensor_scalar(out=yT.ap, in0=ycol.ap, scalar1=float(TT), scalar2=None, op0=ALU.mult),
         reads=[ycol], writes=[yT])
    emit_sincos(P, yT, sT, cT, kiT, frT)
    a_keep2 = A.off
    yyt = [A.alloc(TT) for _ in range(2)]; kit2 = [A.alloci(TT) for _ in range(2)]; frt2 = [A.alloc(TT) for _ in range(2)]
    for q in range(32):
        yy_, ki_, fr_ = yyt[q % 2], kit2[q % 2], frt2[q % 2]
        S.op("dve", lambda e: e.tensor_scalar(out=yy_.ap, in0=C["tg0"].ap[:, 0:TT], scalar1=ycol.ap[:, q:q + 1], scalar2=None,
                                              op0=ALU.mult), reads=[C["tg0"], ycol], writes=[yy_])
        snq = Region(SN.ap[:, q, :], WA.sub(SN, q * TT, q * TT + TT))
        csq = Region(CS.ap[:, q, :], WA.sub(CS, q * TT, q * TT + TT))
        emit_sincos(P, yy_, snq, csq, ki_, fr_)
    A.reset(a_keep2)
    gam = load_gamma(P, gamd)
    xts = [A.alloc(2 * 1024, "p (b d) -> p b d", b=2) for _ in range(2)]
    sqj = A.alloc16(1024); ssn = A.alloc(1); rstdn = A.alloc(1)
    xns = [A.own(A.alloc16(1024)), WA.own(WA.alloc(1024))]
    hT = A.alloc16(8 * TT, "p (k t) -> p k t", k=8)
    uT = A.alloc16(8 * TT, "p (k t) -> p k t", k=8)
    gT = A.alloc16(8 * TT, "p (k t) -> p k t", k=8)
    ggT = uT
    ydt = WA.alloc32(TT); sig = WA.alloc32(TT)

    def load_x(it):
        S.dma("sp", out=xts[it % 2].ap, in_=xv[it], reads=[P.db(f"x{it}")], writes=[xts[it % 2]])

    def norm_pre(it, b):
        xt_ = xts[it % 2]; xn = xns[b]
        S.op("act", lambda e: e.activation(out=sqj.ap, in_=xt_.ap[:, b, :], func=ACTF.Square, scale=1.0 / 32.0,
                                           accum_out=ssn.ap), reads=[xt_], writes=[sqj, ssn])
        S.op("pool", lambda e: e.tensor_scalar(out=ssn.ap, in0=ssn.ap, scalar1=EPS, scalar2=None, op0=ALU.add),
             reads=[ssn], writes=[ssn])
        S.op("pool", lambda e: e.tensor_tensor(out=rstdn.ap, in0=ssn.ap, in1=C["mhalf"].ap, op=ALU.pow),
             reads=[ssn, C["mhalf"]], writes=[rstdn])
        S.op("dve", lambda e: e.scalar_tensor_tensor(out=xn.ap, in0=xt_.ap[:, b, :], scalar=rstdn.ap, in1=gam.ap,
                                                     op0=ALU.mult, op1=ALU.mult),
             reads=[xt_, rstdn, gam], writes=[xn])

    def norm_post(it, b):
        xn = xns[b]
        pst = P.ps[6 + b]
        pvv = ps_bf16(pst)
        for kc in range(8):
            S.op("pe", lambda e: e.transpose(out=pvv[:, kc * 128:(kc + 1) * 128], in_=xn.ap[:, kc * 128:(kc + 1) * 128],
                                             identity=ident.ap), reads=[xn, ident], writes=[pst])
        S.op("act", lambda e: e.activation(out=hT.ap[:, :, b * 128:(b + 1) * 128],
                                           in_=pvv.rearrange("p (k t) -> p k t", k=8), func=ACTF.Copy),
             reads=[pst], writes=[hT])
    NSET = 3

    def mkset(i):
        al = (lambda n: A.own(A.alloc(n))) if i == 0 else (lambda n: WA.own(WA.alloc32(n)))
        al16 = (lambda n: A.own(A.alloc16(n))) if i == 0 else (lambda n: WA.own(WA.alloc(n)))
        return dict(m=[al(TT) for _ in range(4)], w=[al(TT) for _ in range(2)], z=[al(TT) for _ in range(2)],
                    Y=[al16(TT) for _ in range(4)], ct=al(8))
    tsets = [mkset(0 if i < 2 else 1) for i in range(NSET)]
    cbufs = [Buf(f"carry{q}") for q in range(32)]
    xv = xd.rearrange("(n b p) d -> n p b d", p=128, b=2)
    gp = 0
    load_x(0)
    for b in range(2):
        norm_pre(0, b)
        norm_post(0, b)
    for it in range(NTT):
        xb = P.db(f"x{it}")
        xt = xts[it % 2]
        if it + 1 < NTT:
            load_x(it + 1)
        for cc in range(8):
            pu = P.ps[cc % 2]
            for kc in range(8):
                S.op("pe", lambda e: e.matmul(
                    pu.ap[:, 0:TT], lhsT=Win.ap[:, kc, cc * 128:(cc + 1) * 128], rhs=hT.ap[:, kc, :],
                    start=(kc == 0), stop=(kc == 7)),
                    reads=[hT, WA.sub(Win, kc * 1024 + cc * 128, kc * 1024 + cc * 128 + 128)], writes=[pu])
            S.op("act", lambda e: e.activation(out=uT.ap[:, cc, :], in_=pu.ap[:, 0:TT], func=ACTF.Copy),
                 reads=[pu], writes=[A.sub(uT, cc * TT, cc * TT + TT)])

        def stB(q, part):
            cc = q // 4
            ts = tsets[(gp + q) % NSET]
            pb = [P.ps[2 + ((gp + q) % 2) * 2 + ri] for ri in range(2)]
            us = A.sub(uT, cc * TT, cc * TT + TT)
            m, w = ts["m"], ts["w"]
            csq = WA.sub(CS, q * TT, q * TT + TT); snq = WA.sub(SN, q * TT, q * TT + TT)
            if part == 0:
                for ri in range(2):
                    S.op("pe", lambda e: e.matmul(pb[ri].ap[:, 0:TT], lhsT=BL.ap[:, q, ri, :], rhs=uT.ap[:, cc, :],
                                                  start=True, stop=True), reads=[BL, us], writes=[pb[ri]])
                return
            for idx, (ri, tab, tb) in enumerate(((0, CS, csq), (1, SN, snq), (1, CS, csq), (0, SN, snq))):
                S.op("dve", lambda e: e.tensor_tensor(out=m[idx].ap, in0=pb[ri].ap[:, 0:TT], in1=tab.ap[:, q, :], op=ALU.mult),
                     reads=[pb[ri], tb], writes=[m[idx]])
            S.op("pool", lambda e: e.tensor_tensor(out=w[0].ap, in0=m[0].ap, in1=m[1].ap, op=ALU.add),
                 reads=[m[0], m[1]], writes=[w[0]])
            S.op("pool", lambda e: e.tensor_tensor(out=w[1].ap, in0=m[2].ap, in1=m[3].ap, op=ALU.subtract),
                 reads=[m[2], m[3]], writes=[w[1]])

        def stC(q):
            ts = tsets[(gp + q) % NSET]
            w, z, ct = ts["w"], ts["z"], ts["ct"]
            rbc = rcol.ap[:, q:q + 1].broadcast_to([128, TT])
            for ri in range(2):
                S.op("dve", lambda e: e.tensor_tensor_scan(
                    out=z[ri].ap, data0=rbc, data1=w[ri].ap, initial=carry.ap[:, q, ri:ri + 1],
                    op0=ALU.mult, op1=ALU.add), reads=[rcol, w[ri], cbufs[q]], writes=[z[ri]])
            zl = [z[0].ap[:, TT - 1:TT], z[1].ap[:, TT - 1:TT]]
            for idx, (ri, tab) in enumerate(((0, cT), (1, sT), (1, cT), (0, sT))):
                S.op("act", lambda e: e.activation(out=ct.ap[:, idx:idx + 1], in_=zl[ri], func=ACTF.Copy,
                                                   scale=tab.ap[:, q:q + 1]),
                     reads=[z[ri], tab], writes=[ct])
            S.op("pool", lambda e: e.tensor_tensor(out=carry.ap[:, q, 0:1], in0=ct.ap[:, 0:1], in1=ct.ap[:, 1:2], op=ALU.subtract),
                 reads=[ct], writes=[cbufs[q]])
            S.op("pool", lambda e: e.tensor_tensor(out=carry.ap[:, q, 1:2], in0=ct.ap[:, 2:3], in1=ct.ap[:, 3:4], op=ALU.add),
                 reads=[ct], writes=[cbufs[q]])

        def stD(q):
            cc, ql = q // 4, q % 4
            ts = tsets[(gp + q) % NSET]
            z, Y = ts["z"], ts["Y"]
            py = P.ps[cc % 2]
            csq = WA.sub(CS, q * TT, q * TT + TT); snq = WA.sub(SN, q * TT, q * TT + TT)
            S.op("pool", lambda e: e.tensor_tensor(out=Y[0].ap, in0=z[0].ap, in1=CS.ap[:, q, :], op=ALU.mult),
                 reads=[z[0], csq], writes=[Y[0]])
            S.op("pool", lambda e: e.tensor_tensor(out=Y[1].ap, in0=z[1].ap, in1=CS.ap[:, q, :], op=ALU.mult),
                 reads=[z[1], csq], writes=[Y[1]])
            S.op("dve", lambda e: e.tensor_tensor(out=Y[2].ap, in0=z[0].ap, in1=SN.ap[:, q, :], op=ALU.mult),
                 reads=[z[0], snq], writes=[Y[2]])
            S.op("dve", lambda e: e.tensor_tensor(out=Y[3].ap, in0=z[1].ap, in1=SN.ap[:, q, :], op=ALU.mult),
                 reads=[z[1], snq], writes=[Y[3]])
            for n_, (var, yi) in enumerate(((0, 0), (1, 1), (1, 2), (2, 3))):
                S.op("pe", lambda e: e.matmul(py.ap[:, 0:TT], lhsT=CB[:, q, var, :], rhs=Y[yi].ap,
                                              start=(ql == 0 and n_ == 0), stop=(ql == 3 and n_ == 3)),
                     reads=[CBr, Y[yi]], writes=[py])
            if ql == 3:
                us = A.sub(uT, cc * TT, cc * TT + TT)
                S.op("dve", lambda e: e.scalar_tensor_tensor(
                    out=ydt.ap, in0=uT.ap[:, cc, :], scalar=dcol.ap[:, cc:cc + 1], in1=py.ap[:, 0:TT],
                    op0=ALU.mult, op1=ALU.add), reads=[us, dcol, py], writes=[ydt])
                S.op("act", lambda e: e.activation(out=gT.ap[:, cc, :], in_=ydt.ap, func=ACTF.Gelu_apprx_tanh),
                     reads=[ydt], writes=[A.sub(gT, cc * TT, cc * TT + TT)])

        for s_ in range(32 + 2):
            if s_ < 32:
                stB(s_, 0)
            if 0 <= s_ - 2 < 32:
                stD(s_ - 2)
            if 0 <= s_ - 1 < 32:
                stC(s_ - 1)
            if s_ < 32:
                stB(s_, 1)
            if it + 1 < NTT:
                if s_ in (8, 14):
                    norm_pre(it + 1, (s_ - 8) // 6)
                if s_ in (20, 26):
                    norm_post(it + 1, (s_ - 20) // 6)
        gp += 32
        for c2 in range(8):
            pz = P.ps[6 + c2 % 2]
            for cc in range(8):
                S.op("pe", lambda e: e.matmul(
                    pz.ap[:, 0:TT], lhsT=Wglu.ap[:, cc, c2 * 128:(c2 + 1) * 128], rhs=gT.ap[:, cc, :],
                    start=(cc == 0), stop=(cc == 7)),
                    reads=[A.sub(gT, cc * TT, cc * TT + TT), WA.sub(Wglu, cc * 1024 + c2 * 128, cc * 1024 + c2 * 128 + 128)],
                    writes=[pz])
            S.op("act", lambda e: e.activation(out=sig.ap, in_=pz.ap[:, 0:TT], func=ACTF.Sigmoid, bias=bgl.ap[:, c2:c2 + 1]),
                 reads=[pz, bgl], writes=[sig])
            S.op("pool", lambda e: e.tensor_tensor(out=ggT.ap[:, c2, :], in0=gT.ap[:, c2, :], in1=sig.ap, op=ALU.mult),
                 reads=[A.sub(gT, c2 * TT, c2 * TT + TT), sig], writes=[A.sub(ggT, c2 * TT, c2 * TT + TT)])
        residual_out(P, xt, None, Wout, 8, lambda k, b: ggT.ap[:, k, b * 128:(b + 1) * 128],
                     lambda k: A.sub(ggT, k * TT, k * TT + TT), P.ps[2:6])
        S.dma("sp", out=xv[it], in_=xt.ap, reads=[xt], writes=[xb])


def host_consts():
    bf = ml_dtypes.bfloat16
    c = {}
    c["c_ident"] = np.eye(128, dtype=np.float32).astype(bf)
    p = np.arange(128)
    c["c_bones"] = ((p[:, None] // 64) == (p[None, :] // 64)).astype(np.float32).astype(bf) * np.float32(1.0 / 64.0)
    c["c_bones"] = c["c_bones"].astype(bf)
    rr = np.zeros((128, 128), np.float32)
    for cc in range(2):
        for d in range(64):
            mcol = cc * 64 + d
            if d < 32:
                rr[cc * 64 + d + 32, mcol] = -1.0
            else:
                rr[cc * 64 + d - 32, mcol] = 1.0
    c["c_rrot"] = rr.astype(bf)
    invf = (10000.0 ** (-np.arange(0, 64, 2, dtype=np.float32) / 64.0)).astype(np.float32)
    c["c_invf"] = (invf[(p % 64) % 32] / np.float32(2.0 * math.pi)).astype(np.float32).reshape(128, 1)
    c["c_triu"] = (p[:, None] <= p[None, :]).astype(np.float32).astype(bf)
    c["c_tril"] = (p[None, :] <= p[:, None]).astype(np.float32).astype(bf)
    c["c_tg0"] = np.broadcast_to(np.arange(256, dtype=np.float32)[None, :], (128, 256)).copy()
    par = np.zeros((128, 2), np.float32)
    par[:, 0] = ((p // 16) % 2 == 0)
    par[:, 1] = ((p // 16) % 2 == 1)
    c["c_par"] = par
    return c


def setup_consts(P):
    S, A = P.S, P.A
    C = {}
    A.reset()

    def ld(name, n, dtype, shape):
        d = P.din("c_" + name, shape, dtype)
        r = A.alloc16(n) if dtype == BF16 else A.alloc(n)
        S.dma("sp", out=r.ap, in_=d, writes=[r])
        C[name] = r

    ld("ident", 128, BF16, [128, 128])
    ld("bones", 128, BF16, [128, 128])
    ld("rrot", 128, BF16, [128, 128])
    ld("triu", 128, BF16, [128, 128])
    ld("tril", 128, BF16, [128, 128])
    ld("invf", 1, F32, [128, 1])
    ld("tg0", 256, F32, [128, 256])
    ld("par", 2, F32, [128, 2])
    C["mhalf"] = A.alloc(1)
    S.op("pool", lambda e: e.memset(C["mhalf"].ap, -0.5), writes=[C["mhalf"]])
    C["epscol"] = A.alloc(1)
    S.op("pool", lambda e: e.memset(C["epscol"].ap, EPS), writes=[C["epscol"]])
    A.off = ((A.off + 255) // 256) * 256
    C["a0"] = A.off
    P.C = C
    return C


def lambda_init_of(li):
    return 0.8 - 0.6 * math.exp(-0.3 * li)


def build(spec, ntok=SEQ, debug=False):
    P = Prog(ntok)
    P.DEBUG = debug
    S = P.S
    xin = P.din("x", [ntok, D])
    xout = P.dout("y", [ntok, D])
    setup_consts(P)
    nt = ntok // 256
    xiv = xin.rearrange("(n t) d -> n t d", t=256)
    xov = xout.rearrange("(n t) d -> n t d", t=256)
    for it in range(nt):
        S.dma("sp", out=xov[it], in_=xiv[it], writes=[P.db(f"x{it}")])
    sc = None
    posd = None
    preloaded = set()
    for si, (kind, li) in enumerate(spec):
        if kind == "mlp":
            phase_mlp(P, xout, P.din(f"mlp_w1_{li}", [D, 4 * D]) if li not in preloaded else None,
                      P.din(f"mlp_w2_{li}", [4 * D, D]) if li not in preloaded else None,
                      P.din(f"norm_mlp_{li}", [1, D]), preloaded=(li in preloaded))
            continue
        gamd = P.din(f"norm_mix_{li}", [1, D])
        mk = li % 3
        j = li // 3
        if mk == 0:
            if sc is None:
                sc = dict(qT=P.dscratch("sc_qT", [8, 128, ntok], BF16), kT=P.dscratch("sc_kT", [8, 128, ntok], BF16),
                          v=P.dscratch("sc_v", [ntok // 128, 128, 8 * 129], BF16),
                          oT=P.dscratch("sc_oT", [8, 128, ntok], BF16))
                posd = P.din("positions", [1, ntok], I32)
            phase_attn_qkv(P, xout, gamd, P.din(f"attn_w_qkv_{j}", [D, 3 * D]), P.din(f"attn_q_norm_{j}", [64, 1]),
                           P.din(f"attn_k_norm_{j}", [64, 1]), posd, sc)
            phase_attn_core(P, P.din(f"attn_lambda_{j}", [4, 64]), P.din(f"attn_sub_norm_{j}", [1, 128]),
                            lambda_init_of(li), sc)
            pf = None
            if si + 1 < len(spec) and spec[si + 1] == ("mlp", li):
                pf = (P.din(f"mlp_w1_{li}", [D, 4 * D]), P.din(f"mlp_w2_{li}", [4 * D, D]))
                preloaded.add(li)
            phase_attn_out(P, xout, P.din(f"attn_w_o_{j}", [D, D]), sc, prefetch=pf)
        elif mk == 1:
            phase_s5(P, xout, gamd, P.din(f"ssm_w_in_{j}", [D, D]), P.din(f"ssm_a_re_{j}", [64, 64]),
                     P.din(f"ssm_a_im_{j}", [64, 64]), P.din(f"ssm_b_re_{j}", [64, 64, 16]),
                     P.din(f"ssm_b_im_{j}", [64, 64, 16]), P.din(f"ssm_c_re_{j}", [64, 16, 64]),
                     P.din(f"ssm_c_im_{j}", [64, 16, 64]), P.din(f"ssm_d_{j}", [1, D]),
                     P.din(f"ssm_log_dt_{j}", [1, 64]), P.din(f"ssm_w_glu_{j}", [D, D]),
                     P.din(f"ssm_b_glu_{j}", [1, D]), P.din(f"ssm_w_out_{j}", [D, D]))
        else:
            phase_gmlp(P, xout, gamd, P.din(f"gm_w_in_{j}", [D, 6 * D]), P.din(f"gm_v_norm_{j}", [1, 3 * D]),
                       P.din(f"gm_w_s_{j}", [8, 128, 128]), P.din(f"gm_b_s_{j}", [8, 128]),
                       P.din(f"gm_w_out_{j}", [3 * D, D]))
    S.wait_bufs("sp", [P.db(f"x{it}") for it in range(nt)] + getattr(P, "dbg", []))
    return P


FULL_SPEC = [("mix", 0), ("mlp", 0), ("mix", 1), ("mlp", 1), ("mix", 2), ("mlp", 2), ("mix", 3), ("mlp", 3)]

_SHAPES = {
    "norm_mix": (1, D), "norm_mlp": (1, D), "attn_q_norm": (64, 1), "attn_k_norm": (64, 1), "attn_sub_norm": (1, 128),
    "ssm_d": (1, D), "ssm_log_dt": (1, 64), "ssm_b_glu": (1, D), "gm_v_norm": (1, 3 * D),
}


def make_in_map(P, inputs, b, ntok=SEQ):
    m = dict(host_consts())
    out = {}
    for name in P.dram_in:
        if name in m:
            out[name] = m[name]
        elif name == "x":
            out[name] = np.ascontiguousarray(inputs["x"][b, :ntok])
        elif name == "positions":
            out[name] = np.ascontiguousarray(inputs["positions"][b, :ntok].reshape(1, ntok)).astype(np.int32)
        else:
            base, idx = name.rsplit("_", 1)
            arr = np.asarray(inputs[base])[int(idx)]
            shp = _SHAPES.get(base)
            if shp is not None:
                arr = arr.reshape(shp)
            out[name] = np.ascontiguousarray(arr, dtype=np.float32)
    return out


_PROG = {}


def kernel(**inputs):
    if "full" not in _PROG:
        _PROG["full"] = build(FULL_SPEC, SEQ)
    P = _PROG["full"]
    inputs = {k: np.asarray(v) for k, v in inputs.items()}
    maps = [make_in_map(P, inputs, c % BATCH) for c in range(NCORES)]
    res = run_bass_kernel_spmd(P.nc, maps, core_ids=list(range(NCORES)))
    return np.stack([np.asarray(res.results[b]["y"], dtype=np.float32) for b in range(BATCH)], axis=0)
```
